# Optimizing a Trainium2 kernel written in Bass

```python
import jax
import jax.numpy as jnp
from jax import lax
import numpy as np

D_MODEL = 1024
BATCH = 8
SEQ = 2048
DEPTH = 2
DEC_BATCH = 32
DEC_SEQ = 1
PAST_LEN = 8192
PAGE_SIZE = 128

EXPAND = 2
D_MIX = EXPAND * D_MODEL
C_A = 3 * D_MIX // 8
HD_A = 64
H_A = C_A // HD_A
PATTERNS = ((128, 1), (512, 4), (2048, 16))
MAX_WINDOW = 2048
DIL_BLOCK = 128
C_B = D_MIX // 4
G_B = 8
SGU_CHUNK = 128
C_C = D_MIX - C_A - C_B
K_C = 128
H_C = C_C // K_C
V_C = C_C // H_C
HG_CHUNK = 16
SPLITS = (C_A,) * 4 + (C_B,) * 3 + (C_C,) * 4
N_IN = sum(SPLITS)
EPS = 1e-6
F32 = jnp.float32

kernel_name = 'hybrid_dilated_sgu_hgrn2_step'


def _rms(x, g):
    xf = x.astype(F32)
    y = xf * lax.rsqrt(jnp.mean(xf * xf, axis=-1, keepdims=True) + EPS)
    return (y * g.astype(F32)).astype(x.dtype)


def _layernorm(x, g, b):
    xf = x.astype(F32)
    mu = jnp.mean(xf, axis=-1, keepdims=True)
    xc = xf - mu
    var = jnp.mean(xc * xc, axis=-1, keepdims=True)
    return (xc * lax.rsqrt(var + EPS) * g.astype(F32) + b.astype(F32)).astype(x.dtype)


def _softmax_stats(s):
    m = jnp.max(s, axis=-1, keepdims=True)
    p = jnp.exp(s - m)
    den = jnp.sum(p, axis=-1, keepdims=True)
    return p / den, (m + jnp.log(den))[..., 0]


def _combine_patterns(outs, lses):
    w = jax.nn.softmax(jnp.stack(lses), axis=0)
    return jnp.sum(w[..., None] * jnp.stack(outs), axis=0)


def _dilated_prompt(q, k, v):
    B, S, H, E = q.shape
    outs, lses = [], []
    for win, dil in PATTERNS:
        n_back = win // dil
        L = S // dil
        nb = -(-L // DIL_BLOCK)
        Lp = nb * DIL_BLOCK

        def to_blocks(t):
            t = t.reshape(B, L, dil, H, E).transpose(0, 2, 1, 3, 4)
            t = jnp.pad(t, ((0, 0), (0, 0), (0, Lp - L), (0, 0), (0, 0)))
            return t.reshape(B, dil, nb, DIL_BLOCK, H, E)

        def with_prev(t):
            prev = jnp.pad(t, ((0, 0), (0, 0), (1, 0), (0, 0), (0, 0), (0, 0)))[:, :, :nb]
            return jnp.concatenate([prev, t], axis=3)

        qb = to_blocks(q)
        kb = with_prev(to_blocks(k))
        vb = with_prev(to_blocks(v))
        s = jnp.einsum('brnqhe,brnkhe->brnhqk', qb, kb, preferred_element_type=F32)
        qi = jnp.arange(DIL_BLOCK)[:, None]
        ki = jnp.arange(2 * DIL_BLOCK)[None, :]
        dist = qi + DIL_BLOCK - ki
        exists = jnp.arange(nb)[:, None, None] * DIL_BLOCK - DIL_BLOCK + ki[None] >= 0
        valid = (dist >= 0) & (dist <= n_back) & exists
        s = jnp.where(valid[:, None], s, -jnp.inf)
        p, lse = _softmax_stats(s)
        o = jnp.einsum('brnhqk,brnkhe->brnqhe', p, vb.astype(F32))
        lse = lse.transpose(0, 1, 2, 4, 3)

        def from_blocks(t):
            t = t.reshape((B, dil, Lp) + t.shape[4:])[:, :, :L]
            return jnp.moveaxis(t, 1, 2).reshape((B, S) + t.shape[3:])

        outs.append(from_blocks(o))
        lses.append(from_blocks(lse))
    return _combine_patterns(outs, lses)


def _dilated_step(q, k_new, v_new, k_buf, v_buf):
    T = q.shape[1]
    W = k_buf.shape[1]
    kk = jnp.concatenate([k_buf.astype(k_new.dtype), k_new], axis=1)
    vv = jnp.concatenate([v_buf.astype(v_new.dtype), v_new], axis=1)
    t = jnp.arange(T)[:, None]
    outs, lses = [], []
    for win, dil in PATTERNS:
        back = jnp.arange(win // dil + 1)[None, :] * dil
        valid = PAST_LEN + t - back >= 0
        idx = jnp.clip(W + t - back, 0, W + T - 1)
        kg = kk[:, idx]
        vg = vv[:, idx]
        s = jnp.einsum('bthe,btnhe->bthn', q, kg, preferred_element_type=F32)
        s = jnp.where(valid[None, :, None, :], s, -jnp.inf)
        p, lse = _softmax_stats(s)
        outs.append(jnp.einsum('bthn,btnhe->bthe', p, vg.astype(F32)))
        lses.append(lse)
    return _combine_patterns(outs, lses)


def _sgu(u, v, ln_g, ln_b, w_s, b_s):
    B, L, _ = v.shape
    vn = _layernorm(v, ln_g, ln_b)
    nc = -(-L // SGU_CHUNK)
    Lp = nc * SGU_CHUNK
    vp = jnp.pad(vn, ((0, 0), (0, Lp - L), (0, 0))).reshape(B, nc, SGU_CHUNK, G_B, C_B // G_B)
    tri = jnp.tril(jnp.ones((SGU_CHUNK, SGU_CHUNK), bool))
    w = jnp.where(tri[None], w_s, jnp.zeros_like(w_s))
    mix = jnp.einsum('gts,bnsgc->bntgc', w, vp) + b_s.T[None, None, :, :, None]
    mix = mix.reshape(B, Lp, C_B)[:, :L]
    return u * mix, vn


def _hgrn2(q_pre, f_pre, i, lb, norm_g, S0):
    B, L, _ = q_pre.shape
    nc = -(-L // HG_CHUNK)
    Lp = nc * HG_CHUNK
    fp = f_pre.astype(F32)
    f = lb + (1.0 - lb) * jax.nn.sigmoid(fp)
    k = (1.0 - lb) * jax.nn.sigmoid(-fp)
    g = jnp.log(f)
    q = jax.nn.silu(q_pre.astype(F32))

    def chunks(t, d):
        t = jnp.pad(t.reshape(B, L, H_C, d), ((0, 0), (0, Lp - L), (0, 0), (0, 0)))
        return jnp.moveaxis(t.reshape(B, nc, HG_CHUNK, H_C, d), 1, 0)

    qs, ks, gs = chunks(q, K_C), chunks(k, K_C), chunks(g, K_C)
    vs = chunks(i.astype(F32), V_C)
    tri = jnp.tril(jnp.ones((HG_CHUNK, HG_CHUNK), bool))[None, :, :, None, None]

    def body(S, xs):
        qc, kc, vc, gc = xs
        G = jnp.cumsum(gc, axis=1)
        o_inter = jnp.einsum('bchk,bhkv->bchv', qc * jnp.exp(G), S)
        diff = G[:, :, None] - G[:, None, :]
        D = jnp.exp(jnp.where(tri, diff, -jnp.inf))
        A = jnp.einsum('bthk,btshk,bshk->bths', qc, D, kc)
        o = o_inter + jnp.einsum('bths,bshv->bthv', A, vc)
        G_last = G[:, -1]
        S_new = jnp.exp(G_last)[..., None] * S + jnp.einsum(
            'bshk,bshv->bhkv', kc * jnp.exp(G_last[:, None] - G), vc)
        return S_new, o

    S_fin, o = lax.scan(body, S0, (qs, ks, vs, gs))
    o = jnp.moveaxis(o, 0, 1).reshape(B, Lp, H_C, V_C)[:, :L]
    o = _rms(o, norm_g.reshape(H_C, V_C))
    return o.reshape(B, L, C_C), S_fin


def _project(x, norm_g, w_in, qn_g, kn_g):
    B, L, _ = x.shape
    h = _rms(x, norm_g)
    proj = jnp.einsum('bld,dn->bln', h, w_in)
    cuts = [int(c) for c in np.cumsum(SPLITS)[:-1]]
    aq, ak, av, az, bu, bv, bz, cq, cf, ci, cz = jnp.split(proj, cuts, axis=-1)
    aq = _rms(aq.reshape(B, L, H_A, HD_A), qn_g.reshape(H_A, HD_A)) * (HD_A ** -0.5)
    ak = _rms(ak.reshape(B, L, H_A, HD_A), kn_g.reshape(H_A, HD_A))
    av = av.reshape(B, L, H_A, HD_A)
    return aq, ak, av, az, bu, bv, bz, cq, cf, ci, cz


def _merge(x, o_a, z_a, o_b, z_b, o_c, z_c, w_out):
    B, L, _ = x.shape
    y = jnp.concatenate([
        o_a.reshape(B, L, C_A) * jax.nn.silu(z_a.astype(F32)),
        o_b.astype(F32) * jax.nn.silu(z_b.astype(F32)),
        o_c.astype(F32) * jax.nn.silu(z_c.astype(F32)),
    ], axis=-1).astype(x.dtype)
    return x + jnp.einsum('bln,nd->bld', y, w_out)


def setup_inputs(seed: int = 0) -> dict:
    key = jax.random.key(seed)
    ks = jax.random.split(key, 18)
    nrm = jax.random.normal
    wbuf = min(MAX_WINDOW, PAST_LEN)
    return {
        'x_prompt': nrm(ks[0], (BATCH, SEQ, D_MODEL), F32),
        'x_sample': nrm(ks[1], (DEC_BATCH, DEC_SEQ, D_MODEL), F32),
        'cache_k': nrm(ks[2], (DEPTH, DEC_BATCH, wbuf, H_A, HD_A), F32),
        'cache_v': nrm(ks[3], (DEPTH, DEC_BATCH, wbuf, H_A, HD_A), F32),
        'state_hgrn': 0.5 * nrm(ks[4], (DEPTH, DEC_BATCH, H_C, K_C, V_C), F32),
        'norm_g': 1.0 + 0.02 * nrm(ks[5], (DEPTH, D_MODEL), F32),
        'w_in': nrm(ks[6], (DEPTH, D_MODEL, N_IN), F32) * D_MODEL ** -0.5,
        'q_norm_g': 1.0 + 0.02 * nrm(ks[7], (DEPTH, C_A), F32),
        'k_norm_g': 1.0 + 0.02 * nrm(ks[8], (DEPTH, C_A), F32),
        'sgu_ln_g': 1.0 + 0.02 * nrm(ks[9], (DEPTH, C_B), F32),
        'sgu_ln_b': 0.02 * nrm(ks[10], (DEPTH, C_B), F32),
        'sgu_w': nrm(ks[11], (DEPTH, G_B, SGU_CHUNK, SGU_CHUNK), F32) * SGU_CHUNK ** -0.5,
        'sgu_b': 0.02 * nrm(ks[12], (DEPTH, G_B, SGU_CHUNK), F32),
        'hgrn_lb_logits': 0.5 * nrm(ks[13], (DEPTH, C_C), F32),
        'hgrn_norm_g': 1.0 + 0.02 * nrm(ks[14], (DEPTH, C_C), F32),
        'w_out': nrm(ks[15], (DEPTH, D_MIX, D_MODEL), F32) * D_MIX ** -0.5,
    }


def reference(x_prompt, x_sample, cache_k, cache_v, state_hgrn, norm_g, w_in, q_norm_g, k_norm_g,
              sgu_ln_g, sgu_ln_b, sgu_w, sgu_b, hgrn_lb_logits, hgrn_norm_g, w_out):
    lb_p = jax.nn.softmax(hgrn_lb_logits.astype(F32), axis=0)
    lb_all = jnp.cumsum(lb_p, axis=0) - lb_p[0]
    xp, xs = x_prompt, x_sample
    w_keep = min(MAX_WINDOW, x_prompt.shape[1])
    kp_l, vp_l, ks_l, vs_l, sgu_l, hp_l, hs_l = [], [], [], [], [], [], []
    for l in range(DEPTH):
        aq, ak, av, az, bu, bv, bz, cq, cf, ci, cz = _project(xp, norm_g[l], w_in[l], q_norm_g[l], k_norm_g[l])
        o_a = _dilated_prompt(aq, ak, av)
        o_b, _ = _sgu(bu, bv, sgu_ln_g[l], sgu_ln_b[l], sgu_w[l], sgu_b[l])
        S0 = jnp.zeros((xp.shape[0], H_C, K_C, V_C), F32)
        o_c, S_p = _hgrn2(cq, cf, ci, lb_all[l], hgrn_norm_g[l], S0)
        xp = _merge(xp, o_a, az, o_b, bz, o_c, cz, w_out[l])
        kp_l.append(ak[:, -w_keep:])
        vp_l.append(av[:, -w_keep:])
        hp_l.append(S_p)
        aq, ak, av, az, bu, bv, bz, cq, cf, ci, cz = _project(xs, norm_g[l], w_in[l], q_norm_g[l], k_norm_g[l])
        o_a = _dilated_step(aq, ak, av, cache_k[l], cache_v[l])
        o_b, vn_s = _sgu(bu, bv, sgu_ln_g[l], sgu_ln_b[l], sgu_w[l], sgu_b[l])
        o_c, S_s = _hgrn2(cq, cf, ci, lb_all[l], hgrn_norm_g[l], state_hgrn[l].astype(F32))
        xs = _merge(xs, o_a, az, o_b, bz, o_c, cz, w_out[l])
        ks_l.append(ak)
        vs_l.append(av)
        sgu_l.append(vn_s)
        hs_l.append(S_s)
    y_prompt = xp
    y_sample = xs
    new_k_prompt = jnp.stack(kp_l)
    new_v_prompt = jnp.stack(vp_l)
    new_k_sample = jnp.stack(ks_l)
    new_v_sample = jnp.stack(vs_l)
    new_sgu_v_sample = jnp.stack(sgu_l)
    new_hgrn_prompt = jnp.stack(hp_l)
    new_hgrn_sample = jnp.stack(hs_l)
    return (y_prompt, y_sample, new_k_prompt, new_v_prompt, new_k_sample, new_v_sample,
            new_sgu_v_sample, new_hgrn_prompt, new_hgrn_sample)
```

```python
import bisect
import contextlib
import os

import numpy as np

import concourse.bass as bass
import concourse.mybir as mybir
from concourse.bass_utils import run_bass_kernel_spmd

F32 = mybir.dt.float32
BF16 = mybir.dt.bfloat16
AF = mybir.ActivationFunctionType
ALU = mybir.AluOpType
AX = mybir.AxisListType

NCORES = 8
NT = 17
TOK = NT * 128
EPS = 1e-6


class Eng:
    def __init__(self, name, eng, sem):
        self.name, self.eng, self.sem = name, eng, sem
        self.nseq = 0
        self.cnt = 0
        self.last = None
        self.inc_seq = []
        self.waited = {}
        self.slots = []
        self.rr = 0


class Slot:
    def __init__(self, sem):
        self.sem = sem
        self.val = 0


class Buf:
    __slots__ = ("name", "w", "r")

    def __init__(self, name):
        self.name = name
        self.w = None
        self.r = {}


class Tracker:
    def __init__(self):
        self.engs = []
        self.stopped = False
        self.force = False
        self.stop_at = os.environ.get("KSTOP", "")

    def ck(self, name):
        if self.stop_at and name == self.stop_at:
            self.stopped = True

    def resolve(self, ev):
        if ev[0] == "s":
            return ev[1], ev[2]
        E, seq = ev[1], ev[2]
        i = bisect.bisect_left(E.inc_seq, seq)
        if i < len(E.inc_seq):
            return E.sem, i + 1
        E.last.then_inc(E.sem, 1)
        E.cnt += 1
        E.inc_seq.append(E.nseq)
        return E.sem, E.cnt

    def _wait(self, E, sem, val):
        if E.waited.get(sem.num, 0) < val:
            E.eng.wait_ge(sem, val)
            E.waited[sem.num] = val

    def deps(self, E, reads, writes):
        evs = []
        for b in reads:
            if b.w is not None:
                evs.append(b.w)
        for b in writes:
            if b.w is not None:
                evs.append(b.w)
            for k, ev in b.r.items():
                if ev[0] == "e" and ev[1] is E:
                    continue
                evs.append(ev)
        need = {}
        for ev in evs:
            if ev[0] == "e" and ev[1] is E and E.name == "pe":
                continue
            sem, val = self.resolve(ev)
            if need.get(sem.num, (None, 0))[1] < val:
                need[sem.num] = (sem, val)
        for num, (sem, val) in need.items():
            self._wait(E, sem, val)

    def op(self, E, fn, reads=(), writes=(), inc=None):
        if self.stopped and not self.force:
            return None
        self.deps(E, reads, writes)
        ins = fn()
        E.nseq += 1
        E.last = ins
        if inc and os.environ.get("KINC", "0") == "1":
            ins.then_inc(E.sem, 1)
            E.cnt += 1
            E.inc_seq.append(E.nseq)
        ev = ("e", E, E.nseq)
        for b in writes:
            b.w = ev
            b.r = {}
        for b in reads:
            b.r[E.name] = ev
        return ins

    def dma(self, Q, out, in_, reads=(), writes=()):
        if self.stopped and not self.force:
            return
        self.deps(Q, reads, writes)
        slot = Q.slots[Q.rr % len(Q.slots)]
        Q.rr += 1
        if slot.val > 0:
            self._wait(Q, slot.sem, slot.val)
        Q.eng.dma_start(out=out, in_=in_).then_inc(slot.sem, 16)
        slot.val += 16
        ev = ("s", slot.sem, slot.val)
        for b in writes:
            b.w = ev
            b.r = {}
        for b in reads:
            b.r[("d", slot.sem.num)] = ev

    def barrier(self):
        if self.stopped and not self.force:
            return
        pts = []
        for F in self.engs:
            if F.nseq > 0:
                pts.append(self.resolve(("e", F, F.nseq)))
            for s in F.slots:
                if s.val > 0:
                    pts.append((s.sem, s.val))
        for E in self.engs:
            for sem, val in pts:
                if sem is E.sem and E.name == "pe":
                    continue
                self._wait(E, sem, val)


def build_nc():
    nc = bass.Bass("TRN2", target_bir_lowering=False)

    def din(name, shape):
        return nc.dram_tensor(name, shape, F32, kind="ExternalInput").ap()

    def dout(name, shape):
        return nc.dram_tensor(name, shape, F32, kind="ExternalOutput").ap()

    xp = din("xp", [2048, 1024])
    xs = din("xs", [4, 1024])
    ck = din("ck", [2, 4, 2048, 768])
    cv = din("cv", [2, 4, 2048, 768])
    st = din("st", [2, 4, 6, 128, 128])
    norm_g = din("norm_g", [2, 1024])
    w_in = din("w_in", [2, 1024, 7680])
    qng = din("q_norm_g", [2, 768])
    kng = din("k_norm_g", [2, 768])
    lng = din("sgu_ln_g", [2, 512])
    lnb = din("sgu_ln_b", [2, 512])
    sgw = din("sgu_w", [2, 8, 128, 128])
    sgb = din("sgu_b", [2, 8, 128])
    lbl = din("hgrn_lb_logits", [2, 768])
    hng = din("hgrn_norm_g", [2, 768])
    w_out = din("w_out", [2, 2048, 1024])
    yp = dout("yp", [2048, 1024])
    ys = dout("ys", [4, 1024])
    okp = dout("okp", [2, 2048, 768])
    ovp = dout("ovp", [2, 2048, 768])
    oks = dout("oks", [2, 4, 768])
    ovs = dout("ovs", [2, 4, 768])
    osg = dout("osg", [2, 4, 512])
    ohp = dout("ohp", [2, 6, 128, 128])
    ohs = dout("ohs", [2, 4, 6, 128, 128])

    T = Tracker()
    es = contextlib.ExitStack()
    with es:
        nmctr = [0]

        def sbt(stack, name, shape, dt):
            nmctr[0] += 1
            return stack.enter_context(nc.sbuf_tensor(f"{name}_{nmctr[0]}", shape, dt))

        sems = [es.enter_context(nc.semaphore(f"sem{i}")) for i in range(20)]
        PE = Eng("pe", nc.tensor, sems[0])
        ACT = Eng("act", nc.scalar, sems[1])
        DVE = Eng("dve", nc.vector, sems[2])
        POOL = Eng("pool", nc.gpsimd, sems[3])
        SP = Eng("sp", nc.sync, None)
        SP.slots = [Slot(s) for s in sems[4:12]]
        POOL.slots = [Slot(s) for s in sems[12:20]]
        T.engs = [PE, ACT, DVE, POOL, SP]
        op, dma = T.op, T.dma
        V, A, G, TE = nc.vector, nc.scalar, nc.gpsimd, nc.tensor

        PS = [es.enter_context(nc.psum_tensor(f"ps{i}", [128, 512], F32)) for i in range(6)]
        PB = [es.enter_context(nc.psum_tensor(f"pb{i}", [128, 1024], BF16)) for i in range(2)]
        BPS = [Buf(f"ps{i}") for i in range(6)]
        BPB = [Buf(f"pb{i}") for i in range(2)]

        xres = sbt(es, "xres", [128, NT, 1024], F32)
        hT = sbt(es, "hT", [128, 8, TOK], BF16)
        yT = sbt(es, "yT", [128, 6, TOK], BF16)
        WA = [sbt(es, f"wA{i}", [128, 8, 512], BF16) for i in range(2)]
        wo = sbt(es, "wo", [128, 6, 1024], BF16)
        ident_bf = sbt(es, "ident_bf", [128, 128], BF16)
        ident_f = sbt(es, "ident_f", [128, 128], F32)
        ones_bf = sbt(es, "ones_bf", [128, 128], BF16)
        ones_f = sbt(es, "ones_f", [128, 128], F32)
        mask01 = sbt(es, "mask01", [128, 512], BF16)
        maskcur = sbt(es, "maskcur", [128, 128], BF16)
        blockmask = sbt(es, "blockmask", [128, 128], F32)
        rowmask = sbt(es, "rowmask", [128, 4], F32)
        rowmask_s = sbt(es, "rowmask_s", [128, 4], F32)
        rowmask_s3 = sbt(es, "rowmask_s3", [128, 4], F32)
        epsc = sbt(es, "epsc", [128, 1], F32)
        lbT = sbt(es, "lbT", [128, 12], F32)
        omlbT = sbt(es, "omlbT", [128, 12], F32)
        hgT = sbt(es, "hgT", [128, 12], F32)

        Bx = [Buf(f"x{i}") for i in range(NT)]
        BhT = [Buf(f"hT{i}") for i in range(NT)]
        ByTp = [Buf(f"yTp{i}") for i in range(6)]
        ByTs = [Buf(f"yTs{i}") for i in range(6)]
        BW = [Buf("wA0"), Buf("wA1")]
        Bwo = Buf("wo")
        Bc = Buf("consts")

        def tcols(i):
            return slice(i * 128, (i + 1) * 128)

        units = []
        for l in range(2):
            for c in range(6):
                units.append((l, [(0, c * 128), (128, 768 + c * 128), (256, 1536 + c * 128), (384, 2304 + c * 128)], 128))
            units.append((l, [(0, 3584)], 512))
            for cb in range(4):
                units.append((l, [(0, 3072 + cb * 128), (128, 4096 + cb * 128)], 128))
            for hd in range(6):
                units.append((l, [(0, 4608 + hd * 128), (128, 5376 + hd * 128), (256, 6144 + hd * 128), (384, 6912 + hd * 128)], 128))
        ustate = {"loaded": 0}

        def load_unit(u):
            if u >= len(units) or u < ustate["loaded"]:
                return
            assert u == ustate["loaded"]
            ustate["loaded"] = u + 1
            l, parts, wdt = units[u]
            wt = WA[u % 2]
            for (dst, src, ) in [(p[0], p[1]) for p in parts]:
                dma(POOL, wt[:, :, dst:dst + wdt],
                    w_in[l, :, src:src + wdt].rearrange("(kc p) n -> p kc n", p=128), writes=[BW[u % 2]])

        def load_wo(l, r0, nch):
            dma(POOL, wo[:, 0:nch, :], w_out[l, r0:r0 + nch * 128, :].rearrange("(c p) d -> p c d", p=128), writes=[Bwo])

        with contextlib.ExitStack() as ar:
            tmpf = sbt(ar, "c_tmpf", [128, 512], F32)
            R4 = sbt(ar, "c_R4", [4, 128], F32)
            ld12 = sbt(ar, "c_ld12", [12, 128], F32)
            hg12 = sbt(ar, "c_hg12", [12, 128], F32)
            lgT = sbt(ar, "c_lgT", [128, 12], F32)
            Bt = Buf("c_tmp")
            BR4 = Buf("c_R4")
            Bl = Buf("c_ld")
            dma(SP, ld12[:], lbl.rearrange("l (h k) -> (l h) k", k=128), writes=[Bl])
            dma(SP, hg12[:], hng.rearrange("l (h k) -> (l h) k", k=128), writes=[Bl])
            for i in range(16):
                dma(SP, xres[:, i, :], xp[i * 128:(i + 1) * 128, :], writes=[Bx[i]])
            op(DVE, lambda: V.memset(xres[:, 16, :], 0.0), writes=[Bx[16]])
            for b in range(4):
                dma(SP, xres[32 * b:32 * b + 1, 16, :], xs[b:b + 1, :], writes=[Bx[16]])
            op(DVE, lambda: V.memset(yT[:, :, 2048:TOK], 0.0), writes=ByTs)
            op(DVE, lambda: V.memset(epsc[:], EPS), writes=[Bc])
            op(DVE, lambda: V.memset(ones_bf[:], 1.0), writes=[Bc])
            op(DVE, lambda: V.memset(ones_f[:], 1.0), writes=[Bc])
            op(POOL, lambda: G.memset(ident_f[:], 1.0), writes=[Bc])
            op(POOL, lambda: G.affine_select(out=ident_f[:], in_=ident_f[:], pattern=[[-1, 128]], compare_op=ALU.is_equal,
                                             fill=0.0, base=0, channel_multiplier=1), reads=[Bc], writes=[Bc])
            op(DVE, lambda: V.tensor_copy(out=ident_bf[:], in_=ident_f[:]), reads=[Bc], writes=[Bc])
            op(POOL, lambda: G.memset(tmpf[:, 0:256], 1.0), writes=[Bt])
            op(POOL, lambda: G.affine_select(out=tmpf[:, 0:128], in_=tmpf[:, 0:128], pattern=[[-1, 128]], compare_op=ALU.is_ge,
                                             fill=0.0, base=0, channel_multiplier=1), reads=[Bt], writes=[Bt])
            op(POOL, lambda: G.affine_select(out=tmpf[:, 128:256], in_=tmpf[:, 128:256], pattern=[[1, 128]], compare_op=ALU.is_ge,
                                             fill=0.0, base=0, channel_multiplier=-1), reads=[Bt], writes=[Bt])
            for kb in range(2):
                for h in range(2):
                    op(DVE, lambda kb=kb, h=h: V.tensor_copy(out=mask01[:, (h * 2 + kb) * 128:(h * 2 + kb + 1) * 128],
                                                            in_=tmpf[:, kb * 128:(kb + 1) * 128]), reads=[Bt], writes=[Bc])
            op(DVE, lambda: V.tensor_copy(out=maskcur[:], in_=tmpf[:, 128:256]), reads=[Bt], writes=[Bc])
            op(POOL, lambda: G.memset(R4[:], 1.0), writes=[BR4])
            op(POOL, lambda: G.affine_select(out=R4[:], in_=R4[:], pattern=[[1, 128]], compare_op=ALU.is_ge,
                                             fill=0.0, base=0, channel_multiplier=-32), reads=[BR4], writes=[BR4])
            op(POOL, lambda: G.affine_select(out=R4[:], in_=R4[:], pattern=[[-1, 128]], compare_op=ALU.is_ge,
                                             fill=0.0, base=31, channel_multiplier=32), reads=[BR4], writes=[BR4])
            op(PE, lambda: TE.matmul(PS[0][:, 0:128], lhsT=R4[:], rhs=R4[:], start=True, stop=True), reads=[BR4], writes=[BPS[0]])
            op(PE, lambda: TE.matmul(PS[0][:, 128:132], lhsT=R4[:], rhs=ident_f[0:4, 0:4], start=True, stop=True),
               reads=[BR4, Bc], writes=[BPS[0]])
            op(DVE, lambda: V.tensor_tensor(out=blockmask[:], in0=tmpf[:, 128:256], in1=PS[0][:, 0:128], op=ALU.mult),
               reads=[Bt, BPS[0]], writes=[Bc])
            op(DVE, lambda: V.tensor_copy(out=rowmask[:], in_=PS[0][:, 128:132]), reads=[BPS[0]], writes=[Bc])
            op(POOL, lambda: G.memset(rowmask_s[:], 1.0), writes=[Bc])
            op(POOL, lambda: G.affine_select(out=rowmask_s[:], in_=rowmask_s[:], pattern=[[-32, 4]], compare_op=ALU.is_equal,
                                             fill=0.0, base=0, channel_multiplier=1), reads=[Bc], writes=[Bc])
            op(DVE, lambda: V.tensor_scalar(out=rowmask_s3[:], in0=rowmask_s[:], scalar1=3.0, scalar2=None, op0=ALU.mult),
               reads=[Bc], writes=[Bc])
            op(PE, lambda: TE.transpose(PS[1][:, 0:12], ld12[:], ident_f[0:12, 0:12]), reads=[Bl, Bc], writes=[BPS[1]])
            op(PE, lambda: TE.transpose(PS[1][:, 16:28], hg12[:], ident_f[0:12, 0:12]), reads=[Bl, Bc], writes=[BPS[1]])
            op(DVE, lambda: V.tensor_copy(out=lgT[:], in_=PS[1][:, 0:12]), reads=[BPS[1]], writes=[Bt])
            op(DVE, lambda: V.tensor_copy(out=hgT[:], in_=PS[1][:, 16:28]), reads=[BPS[1]], writes=[Bc])
            op(DVE, lambda: V.memset(lbT[:], 0.0), writes=[Bc])
            op(DVE, lambda: V.tensor_tensor(out=lgT[:, 0:6], in0=lgT[:, 0:6], in1=lgT[:, 6:12], op=ALU.subtract), reads=[Bt], writes=[Bt])
            op(ACT, lambda: A.activation(out=lgT[:, 0:6], in_=lgT[:, 0:6], func=AF.Exp), reads=[Bt], writes=[Bt])
            op(DVE, lambda: V.tensor_scalar(out=lgT[:, 0:6], in0=lgT[:, 0:6], scalar1=1.0, scalar2=None, op0=ALU.add), reads=[Bt], writes=[Bt])
            op(DVE, lambda: V.reciprocal(out=lbT[:, 6:12], in_=lgT[:, 0:6]), reads=[Bt, Bc], writes=[Bc])
            op(DVE, lambda: V.tensor_scalar(out=omlbT[:], in0=lbT[:], scalar1=-1.0, scalar2=1.0, op0=ALU.mult, op1=ALU.add),
               reads=[Bc], writes=[Bc])
            load_unit(0)
            T.barrier()
            T.ck("const")

        def silu_from_psum(P, n, out_ap, tmp, Btmp, BP, wr):
            op(ACT, lambda: A.activation(out=tmp[:, 0:n], in_=P[:, 0:n], func=AF.Exp, scale=-1.0), reads=[BP], writes=[Btmp])
            op(DVE, lambda: V.tensor_scalar(out=tmp[:, 0:n], in0=tmp[:, 0:n], scalar1=1.0, scalar2=None, op0=ALU.add), reads=[Btmp], writes=[Btmp])
            op(DVE, lambda: V.reciprocal(out=tmp[:, 0:n], in_=tmp[:, 0:n]), reads=[Btmp], writes=[Btmp])
            op(DVE, lambda: V.tensor_tensor(out=out_ap, in0=P[:, 0:n], in1=tmp[:, 0:n], op=ALU.mult), reads=[Btmp, BP], writes=wr)

        def out_proj(l, nch):
            for i in range(NT):
                ybufs = (ByTp if i < 16 else ByTs)[0:nch]
                for half in range(2):
                    k = (2 * i + half) % 4
                    P = PS[k]
                    for cc in range(nch):
                        op(PE, lambda P=P, cc=cc, i=i, half=half: TE.matmul(
                            P[:], lhsT=yT[:, cc, tcols(i)], rhs=wo[:, cc, half * 512:(half + 1) * 512],
                            start=(cc == 0), stop=(cc == nch - 1)), reads=ybufs + [Bwo], writes=[BPS[k]])
                    xs_ap = xres[:, i, half * 512:(half + 1) * 512]
                    op(DVE, lambda P=P, xs_ap=xs_ap: V.tensor_tensor(out=xs_ap, in0=xs_ap, in1=P[:], op=ALU.add),
                       reads=[Bx[i], BPS[k]], writes=[Bx[i]])

        uidx = 0
        for l in range(2):
            if os.environ.get("KSKIP0", "") == "1":
                if l == 0:
                    T.stopped = True
                else:
                    T.stopped = False
                    ustate["loaded"] = 17
            with contextlib.ExitStack() as ar:
                gn = sbt(ar, "n_gn", [128, 1024], F32)
                sqj = sbt(ar, "n_sqj", [128, 1024], BF16)
                ss = sbt(ar, "n_ss", [128, NT], F32)
                rstd = sbt(ar, "n_rstd", [128, NT], F32)
                hb = [sbt(ar, f"n_hb{j}", [128, 1024], BF16) for j in range(2)]
                Bgn, Bsq, Bss, Brs = Buf("gn"), Buf("sqj"), Buf("ss"), Buf("rstd")
                Bhb = [Buf("hb0"), Buf("hb1")]
                dma(SP, gn[:], norm_g[l].partition_broadcast(128), writes=[Bgn])
                op(DVE, lambda: V.memset(ss[:], 0.0), writes=[Bss])
                for i in range(NT):
                    op(ACT, lambda i=i: A.activation(out=sqj[:], in_=xres[:, i, :], func=AF.Square, accum_out=ss[:, i:i + 1]),
                       reads=[Bx[i], Bss], writes=[Bsq, Bss])
                op(ACT, lambda: A.activation(out=ss[:], in_=ss[:], func=AF.Ln, scale=1.0 / 1024, bias=epsc[:, 0:1]),
                   reads=[Bss, Bc], writes=[Bss])
                op(ACT, lambda: A.activation(out=rstd[:], in_=ss[:], func=AF.Exp, scale=-0.5), reads=[Bss], writes=[Brs])
                for i in range(NT):
                    j = i % 2
                    op(DVE, lambda i=i, j=j: V.scalar_tensor_tensor(out=hb[j][:], in0=xres[:, i, :], scalar=rstd[:, i:i + 1], in1=gn[:],
                                                                    op0=ALU.mult, op1=ALU.mult),
                       reads=[Bx[i], Brs, Bgn], writes=[Bhb[j]])
                    for kc in range(8):
                        op(PE, lambda j=j, kc=kc: TE.transpose(PB[j][:, tcols(kc)], hb[j][:, tcols(kc)], ident_bf[:]),
                           reads=[Bhb[j], Bc], writes=[BPB[j]])
                    op(ACT, lambda i=i, j=j: A.activation(out=hT[:, :, tcols(i)], in_=PB[j][:].rearrange("p (k t) -> p k t", k=8), func=AF.Copy),
                       reads=[BPB[j]], writes=[BhT[i]])
                T.barrier()
                T.ck(f"norm{l}")

            with contextlib.ExitStack() as arA:
                qs_tok = sbt(arA, "a_qs", [128, 768], BF16)
                ks_tok = sbt(arA, "a_ks", [128, 768], BF16)
                vs_tok = sbt(arA, "a_vs", [128, 768], BF16)
                zsamp = sbt(arA, "a_zs", [128, 6, 4], F32)
                Bst = Buf("a_stash")
                load_wo(l, 0, 6)
                with contextlib.ExitStack() as ar:
                    qkT = sbt(ar, "a_qkT", [128, 2, TOK], BF16)
                    vnat = sbt(ar, "a_vnat", [128, NT, 128], BF16)
                    vord = sbt(ar, "a_vord", [128, 16, 128], BF16)
                    zT = sbt(ar, "a_zT", [128, TOK], BF16)
                    UD = sbt(ar, "a_UD", [128, 2, 2048], F32)
                    pp = [sbt(ar, f"a_p{j}", [128, 512], BF16) for j in range(2)]
                    kfin = [sbt(ar, f"a_kfin{j}", [128, 128], F32) for j in range(2)]
                    vfin = [sbt(ar, f"a_vfin{j}", [128, 128], F32) for j in range(2)]
                    qb = [sbt(ar, f"a_qb{j}", [128, 128], BF16) for j in range(2)]
                    kb_ = [sbt(ar, f"a_kb{j}", [128, 128], BF16) for j in range(2)]
                    ss4 = sbt(ar, "a_ss4", [128, 4], F32)
                    rs4 = sbt(ar, "a_rs4", [128, 4], F32)
                    qgc = sbt(ar, "a_qgc", [128, 128], F32)
                    kgc = sbt(ar, "a_kgc", [128, 128], F32)
                    BqkT, Bvn, Bvo, BzT, BUD = Buf("qkT"), Buf("vnat"), Buf("vord"), Buf("zT"), Buf("UD")
                    Bp = [Buf("p0"), Buf("p1")]
                    Bpm = [Buf("pm0"), Buf("pm1")]
                    Bkf = [Buf("kf0"), Buf("kf1")]
                    Bvf = [Buf("vf0"), Buf("vf1")]
                    Btq, Btk, Bsq4, Bss4, Brs4, Bg, Bez = Buf("tq"), Buf("tk"), Buf("sq"), Buf("ss4"), Buf("rs4"), Buf("g"), Buf("ezt")
                    Bqb = [Buf("qb0"), Buf("qb1")]
                    Bkb = [Buf("kb0"), Buf("kb1")]
                    blk_ctr = [0]
                    sq = UD[:, 1, 0:256]
                    ezt = UD[:, 0, 0:512]
                    Bsq4 = BUD
                    Bez = BUD

                    for c in range(6):
                        u = uidx
                        uidx += 1
                        load_unit(u)
                        load_unit(u + 1)
                        wt, Bw = WA[u % 2], BW[u % 2]
                        csl = slice(c * 128, (c + 1) * 128)
                        dma(SP, qgc[:], qng[l, csl].partition_broadcast(128), writes=[Bg])
                        dma(SP, kgc[:], kng[l, csl].partition_broadcast(128), writes=[Bg])
                        op(DVE, lambda: V.tensor_scalar(out=qgc[:], in0=qgc[:], scalar1=0.125, scalar2=None, op0=ALU.mult), reads=[Bg], writes=[Bg])
                        def a1_front(i):
                            j = i % 2
                            P, BP = PS[j], BPS[j]
                            for kc in range(8):
                                op(PE, lambda P=P, kc=kc, i=i: TE.matmul(P[:, 0:384], lhsT=hT[:, kc, tcols(i)], rhs=wt[:, kc, 0:384],
                                                                        start=(kc == 0), stop=(kc == 7)), reads=[BhT[i], Bw], writes=[BP], inc=(kc == 7))
                            op(ACT, lambda P=P: A.activation(out=sq, in_=P[:, 0:256], func=AF.Square), reads=[BP], writes=[Bsq4])
                            op(DVE, lambda: V.tensor_reduce(out=ss4[:], in_=sq.rearrange("p (h e) -> p h e", e=64), axis=AX.X, op=ALU.add),
                               reads=[Bsq4], writes=[Bss4])
                            op(ACT, lambda: A.activation(out=ss4[:], in_=ss4[:], func=AF.Ln, scale=1.0 / 64, bias=epsc[:, 0:1]),
                               reads=[Bss4, Bc], writes=[Bss4])
                            op(ACT, lambda: A.activation(out=rs4[:], in_=ss4[:], func=AF.Exp, scale=-0.5), reads=[Bss4], writes=[Brs4])
                            for h in range(2):
                                hs = slice(h * 64, (h + 1) * 64)
                                op(DVE, lambda P=P, j=j, h=h, hs=hs: V.scalar_tensor_tensor(out=qb[j][:, hs], in0=P[:, hs], scalar=rs4[:, h:h + 1], in1=qgc[:, hs],
                                                                                        op0=ALU.mult, op1=ALU.mult), reads=[BP, Brs4, Bg], writes=[Bqb[j]])
                            for h in range(2):
                                hs = slice(h * 64, (h + 1) * 64)
                                op(DVE, lambda P=P, j=j, h=h, hs=hs: V.scalar_tensor_tensor(out=kfin[j][:, hs], in0=P[:, 128 + h * 64:128 + (h + 1) * 64],
                                                                                        scalar=rs4[:, 2 + h:3 + h], in1=kgc[:, hs], op0=ALU.mult, op1=ALU.mult),
                                   reads=[BP, Brs4, Bg], writes=[Bkf[j]])
                            op(POOL, lambda j=j: G.tensor_copy(out=kb_[j][:], in_=kfin[j][:]), reads=[Bkf[j]], writes=[Bkb[j]])
                            op(ACT, lambda P=P, j=j: A.activation(out=vfin[j][:], in_=P[:, 256:384], func=AF.Copy), reads=[BP], writes=[Bvf[j]])
                            op(POOL, lambda i=i, j=j: G.tensor_copy(out=vnat[:, i, :], in_=vfin[j][:]), reads=[Bvf[j]], writes=[Bvn])
                            if i < 16:
                                dma(SP, okp[l, tcols(i), csl], kfin[j][:], reads=[Bkf[j]])
                                dma(SP, ovp[l, tcols(i), csl], vfin[j][:], reads=[Bvf[j]])
                            else:
                                for b in range(4):
                                    dma(SP, oks[l, b:b + 1, csl], kfin[j][32 * b:32 * b + 1, :], reads=[Bkf[j]])
                                    dma(SP, ovs[l, b:b + 1, csl], vfin[j][32 * b:32 * b + 1, :], reads=[Bvf[j]])
                                op(POOL, lambda j=j: G.tensor_copy(out=qs_tok[:, csl], in_=qb[j][:]), reads=[Bqb[j]], writes=[Bst])
                                op(POOL, lambda j=j: G.tensor_copy(out=ks_tok[:, csl], in_=kb_[j][:]), reads=[Bkb[j]], writes=[Bst])
                                op(POOL, lambda j=j: G.tensor_copy(out=vs_tok[:, csl], in_=vfin[j][:]), reads=[Bvf[j]], writes=[Bst])
                        def a1_back(i):
                            j = i % 2
                            op(PE, lambda j=j: TE.transpose(PB[j][:, 0:128], qb[j][:], ident_bf[:]), reads=[Bqb[j], Bc], writes=[BPB[j]])
                            op(PE, lambda j=j: TE.transpose(PB[j][:, 128:256], kb_[j][:], ident_bf[:]), reads=[Bkb[j], Bc], writes=[BPB[j]], inc=True)
                            op(DVE, lambda i=i, j=j: V.tensor_copy(out=qkT[:, :, tcols(i)], in_=PB[j][:, 0:256].rearrange("p (a t) -> p a t", a=2)),
                               reads=[BPB[j]], writes=[BqkT])
                        if os.environ.get("KPIPE", "0") == "1":
                            a1_front(0)
                            for i in range(1, NT):
                                a1_front(i)
                                a1_back(i - 1)
                            a1_back(NT - 1)
                        else:
                            for i in range(NT):
                                a1_front(i)
                                a1_back(i)
                        T.ck(f"A1_{l}_{c}")
                        for tg in range(5):
                            n = 512 if tg < 4 else 128
                            cols = slice(tg * 512, tg * 512 + n)
                            k = 2 + tg % 2
                            P, BP = PS[k], BPS[k]
                            hb_ = BhT[4 * tg:4 * tg + 4] if tg < 4 else [BhT[16]]
                            for kc in range(8):
                                op(PE, lambda P=P, kc=kc, cols=cols, n=n: TE.matmul(P[:, 0:n], lhsT=wt[:, kc, 384:512], rhs=hT[:, kc, cols],
                                                                                  start=(kc == 0), stop=(kc == 7)), reads=hb_ + [Bw], writes=[BP])
                            silu_from_psum(P, n, zT[:, cols], ezt, Bez, BP, [BzT])
                        op(POOL, lambda c=c: G.tensor_copy(out=zsamp[:, c, :], in_=zT[:, 2048:TOK:32]), reads=[BzT], writes=[Bst])

                        T.ck(f"A2_{l}_{c}")
                        def attn_front(qsl, kcur, kprev, vcur, vprev, first):
                            n_ = blk_ctr[0]
                            blk_ctr[0] += 1
                            jj = n_ % 2
                            Sb = [(PS[n_ % 2], BPS[n_ % 2]), (PS[2 + n_ % 2], BPS[2 + n_ % 2])]
                            kbs = [1] if kprev is None else [0, 1]
                            lo = 0 if kprev is not None else 128
                            for h in range(2):
                                S, BS = Sb[h]
                                hs = slice(h * 64, (h + 1) * 64)
                                for kb in kbs:
                                    ks = kprev if kb == 0 else kcur
                                    op(PE, lambda S=S, hs=hs, ks=ks, kb=kb: TE.matmul(S[:, kb * 128:(kb + 1) * 128], lhsT=qkT[hs, 1, ks], rhs=qkT[hs, 0, qsl],
                                                                                   start=True, stop=True), reads=[BqkT], writes=[BS], inc=(kb == 1))
                            for h in range(2):
                                S, BS = Sb[h]
                                op(ACT, lambda S=S, h=h: A.activation(out=pp[jj][:, h * 256 + lo:(h + 1) * 256], in_=S[:, lo:256], func=AF.Exp),
                                   reads=[BS], writes=[Bp[jj]])
                            pv = pp[jj][:].rearrange("p (h x) -> p h x", h=2)[:, :, lo:256]
                            mv_ = mask01[:].rearrange("p (h x) -> p h x", h=2)[:, :, lo:256]
                            op(POOL, lambda pv=pv, mv_=mv_: G.tensor_tensor(out=pv, in0=pv, in1=mv_, op=ALU.mult), reads=[Bp[jj], Bc], writes=[Bp[jj]])
                            return (n_, qsl, kbs, vcur, vprev, first)

                        def attn_back(ctx):
                            n_, qsl, kbs, vcur, vprev, first = ctx
                            jj = n_ % 2
                            U, BU = PS[4 + n_ % 2], BPS[4 + n_ % 2]
                            for part in range(2):
                                for h in range(2):
                                    hs = slice(h * 64, (h + 1) * 64)
                                    for idx, kb in enumerate(kbs):
                                        vb = vprev if kb == 0 else vcur
                                        lhsT = vb[:, hs] if part == 0 else ones_bf[:, 0:64]
                                        o0 = (h * 2 + kb) * 128
                                        op(PE, lambda U=U, hs=hs, part=part, lhsT=lhsT, o0=o0, idx=idx: TE.matmul(
                                            U[hs, part * 128:(part + 1) * 128], lhsT=lhsT, rhs=pp[jj][:, o0:o0 + 128],
                                            start=(idx == 0), stop=(idx == len(kbs) - 1)), reads=[Bp[jj], Bvn, Bvo, Bc], writes=[BU],
                                           inc=(part == 1 and h == 1 and idx == len(kbs) - 1))
                            uv = U[:, 0:256].rearrange("p (a q) -> p a q", a=2)
                            if first:
                                op(DVE, lambda: V.tensor_copy(out=UD[:, :, qsl], in_=uv), reads=[BU], writes=[BUD])
                            else:
                                op(DVE, lambda: V.tensor_tensor(out=UD[:, :, qsl], in0=UD[:, :, qsl], in1=uv, op=ALU.add), reads=[BU, BUD], writes=[BUD])

                        def run_blocks(specs):
                            prev = None
                            for sp_ in specs:
                                ctx = attn_front(*sp_)
                                if prev is not None:
                                    attn_back(prev)
                                prev = ctx
                            attn_back(prev)

                        run_blocks([(tcols(i), tcols(i), tcols(i - 1) if i > 0 else None, vnat[:, i, :], vnat[:, i - 1, :] if i > 0 else None, True)
                                    for i in range(16)])
                        T.ck(f"A4a_{l}_{c}")
                        for dil in (4, 16):
                            def tsl(blk):
                                if dil == 4:
                                    jb, r4 = blk // 4, blk % 4
                                    return slice(512 * jb + r4, 512 * (jb + 1), 4)
                                return slice(blk, 2048, 16)
                            for g4 in range(4):
                                k = 2 + g4 % 2
                                P, BP = PS[k], BPS[k]
                                for bi in range(4):
                                    blk = 4 * g4 + bi
                                    for kc in range(8):
                                        op(PE, lambda P=P, bi=bi, blk=blk, kc=kc: TE.matmul(P[:, tcols(bi)], lhsT=hT[:, kc, tsl(blk)], rhs=wt[:, kc, 256:384],
                                                                                           start=(kc == 0), stop=(kc == 7)), reads=BhT[0:16] + [Bw], writes=[BP])
                                op(ACT, lambda P=P, g4=g4: A.activation(out=vord[:, 4 * g4:4 * g4 + 4, :], in_=P[:].rearrange("p (a t) -> p a t", a=4), func=AF.Copy),
                                   reads=[BP], writes=[Bvo])
                            run_blocks([(tsl(blk), tsl(blk), tsl(blk - 4), vord[:, blk, :], vord[:, blk - 4, :], False) if (dil == 4 and blk >= 4)
                                        else (tsl(blk), tsl(blk), None, vord[:, blk, :], None, False) for blk in range(16)])
                        T.ck(f"A4b_{l}_{c}")
                        op(DVE, lambda: V.reciprocal(out=UD[:, 1, :], in_=UD[:, 1, :]), reads=[BUD], writes=[BUD])
                        op(DVE, lambda: V.tensor_tensor(out=UD[:, 0, :], in0=UD[:, 0, :], in1=UD[:, 1, :], op=ALU.mult), reads=[BUD], writes=[BUD])
                        op(POOL, lambda c=c: G.tensor_tensor(out=yT[:, c, 0:2048], in0=UD[:, 0, :], in1=zT[:, 0:2048], op=ALU.mult),
                           reads=[BUD, BzT], writes=[ByTp[c]])
                    T.barrier()

                T.ck(f"A4_{l}")
                with contextlib.ExitStack() as ar:
                    selb = sbt(ar, "s_selb", [128, 4, 128], BF16)
                    selbf = sbt(ar, "s_selbf", [128, 512], F32)
                    Kr = [sbt(ar, f"s_Kr{j}", [128, 768], BF16) for j in range(2)]
                    Vr = [sbt(ar, f"s_Vr{j}", [128, 3, 768], BF16) for j in range(2)]
                    prod = sbt(ar, "s_prod", [128, 768], F32)
                    sc = sbt(ar, "s_sc", [128, 12], F32)
                    pall = [sbt(ar, f"s_pall{j}", [128, 3, 12], BF16) for j in range(2)]
                    pnew = sbt(ar, "s_pnew", [128, 12], F32)
                    pnm = sbt(ar, "s_pnm", [128, 4, 12], BF16)
                    rd = sbt(ar, "s_rd", [128, 6], F32)
                    osb = sbt(ar, "s_osb", [128, 6], F32)
                    Bsel, Bprod, Bsc, Bpn, Bpnm, Brd, Bos = Buf("selb"), Buf("prod"), Buf("sc"), Buf("pnew"), Buf("pnm"), Buf("rd"), Buf("osb")
                    BKr = [Buf("Kr0"), Buf("Kr1")]
                    BVr = [Buf("Vr0"), Buf("Vr1")]
                    Bpa = [Buf("pa0"), Buf("pa1")]
                    op(POOL, lambda: G.memset(selbf[:], 1.0), writes=[Bsel])
                    op(POOL, lambda: G.affine_select(out=selbf[:].rearrange("p (b m) -> p b m", b=4), in_=selbf[:].rearrange("p (b m) -> p b m", b=4),
                                                     pattern=[[-32, 4], [0, 128]], compare_op=ALU.is_equal, fill=0.0, base=0, channel_multiplier=1),
                       reads=[Bsel], writes=[Bsel])
                    op(DVE, lambda: V.tensor_copy(out=selb[:].rearrange("p b m -> p (b m)"), in_=selbf[:]), reads=[Bsel], writes=[Bsel])
                    op(DVE, lambda: V.tensor_tensor(out=prod[:], in0=qs_tok[:], in1=ks_tok[:], op=ALU.mult), reads=[Bst], writes=[Bprod])
                    op(DVE, lambda: V.tensor_reduce(out=sc[:], in_=prod[:].rearrange("p (h e) -> p h e", e=64), axis=AX.X, op=ALU.add),
                       reads=[Bprod], writes=[Bsc])
                    op(ACT, lambda: A.activation(out=pnew[:], in_=sc[:], func=AF.Exp), reads=[Bsc], writes=[Bpn])
                    for b in range(4):
                        op(DVE, lambda b=b: V.tensor_scalar(out=pnm[:, b, :], in0=pnew[:], scalar1=rowmask_s3[:, b:b + 1], scalar2=None, op0=ALU.mult),
                           reads=[Bpn, Bc], writes=[Bpnm])
                    kctr = 0
                    for b in range(4):
                        bj = b % 2
                        for pi, (r0, step) in enumerate([(1920, 1), (1536, 4), (0, 16)]):
                            rows = slice(r0, 2048, step)
                            kj = kctr % 2
                            kctr += 1
                            dma(POOL, Kr[kj][:], ck[l, b, rows, :], writes=[BKr[kj]])
                            dma(POOL, Vr[bj][:, pi, :], cv[l, b, rows, :], writes=[BVr[bj]])
                            for hf in range(2):
                                op(PE, lambda b=b, hf=hf: TE.matmul(PS[hf][:, 0:384], lhsT=selb[:, b, :], rhs=qs_tok[:, hf * 384:(hf + 1) * 384],
                                                                  start=True, stop=True), reads=[Bsel, Bst], writes=[BPS[hf]])
                            for hf in range(2):
                                op(DVE, lambda kj=kj, hf=hf: V.tensor_tensor(out=prod[:, hf * 384:(hf + 1) * 384], in0=Kr[kj][:, hf * 384:(hf + 1) * 384],
                                                                           in1=PS[hf][:, 0:384], op=ALU.mult), reads=[BKr[kj], BPS[hf]], writes=[Bprod])
                            op(DVE, lambda: V.tensor_reduce(out=sc[:], in_=prod[:].rearrange("p (h e) -> p h e", e=64), axis=AX.X, op=ALU.add),
                               reads=[Bprod], writes=[Bsc])
                            op(ACT, lambda bj=bj, pi=pi: A.activation(out=pall[bj][:, pi, :], in_=sc[:], func=AF.Exp), reads=[Bsc], writes=[Bpa[bj]])
                        PO, BPO = PS[2 + bj], BPS[2 + bj]
                        for part in range(2):
                            for c in range(6):
                                o0 = part * 16 + 2 * c
                                for pi in range(3):
                                    lhsT = Vr[bj][:, pi, c * 128:(c + 1) * 128] if part == 0 else ones_bf[:]
                                    op(PE, lambda PO=PO, o0=o0, lhsT=lhsT, pi=pi, c=c: TE.matmul(PO[:, o0:o0 + 2], lhsT=lhsT, rhs=pall[bj][:, pi, 2 * c:2 * c + 2],
                                                                                              start=(pi == 0), stop=False), reads=[BVr[bj], Bpa[bj], Bc], writes=[BPO])
                                lhsT = vs_tok[:, c * 128:(c + 1) * 128] if part == 0 else ones_bf[:]
                                op(PE, lambda PO=PO, o0=o0, lhsT=lhsT, b=b, c=c: TE.matmul(PO[:, o0:o0 + 2], lhsT=lhsT, rhs=pnm[:, b, 2 * c:2 * c + 2],
                                                                                        start=False, stop=True), reads=[Bst, Bpnm, Bc], writes=[BPO])
                        for hh in range(2):
                            rws = slice(hh * 64, hh * 64 + 64)
                            op(DVE, lambda PO=PO, rws=rws, hh=hh: V.reciprocal(out=rd[rws, :], in_=PO[rws, 16 + hh:28:2]), reads=[BPO], writes=[Brd])
                            op(DVE, lambda PO=PO, rws=rws, hh=hh: V.tensor_tensor(out=osb[rws, :], in0=PO[rws, hh:12:2], in1=rd[rws, :], op=ALU.mult),
                               reads=[BPO, Brd], writes=[Bos])
                        op(DVE, lambda b=b: V.tensor_tensor(out=yT[:, :, 2048 + 32 * b:2048 + 32 * b + 1], in0=osb[:].unsqueeze(2),
                                                            in1=zsamp[:, :, b:b + 1], op=ALU.mult), reads=[Bos, Bst], writes=ByTs)
                    T.barrier()
                T.ck(f"A5_{l}")
                out_proj(l, 6)
                T.barrier()
                T.ck(f"A_{l}")

            with contextlib.ExitStack() as ar:
                vn = sbt(ar, "b_vn", [128, NT, 512], BF16)
                lng_t = sbt(ar, "b_lng", [128, 512], F32)
                lnb_t = sbt(ar, "b_lnb", [128, 512], F32)
                st6 = sbt(ar, "b_st6", [128, 6], F32)
                mv = sbt(ar, "b_mv", [128, 2], F32)
                lnv = sbt(ar, "b_lnv", [128, 1], F32)
                rsb = sbt(ar, "b_rsb", [128, 1], F32)
                vnf = sbt(ar, "b_vnf", [128, 512], F32)
                vnf2 = [sbt(ar, f"b_vnf2{j}", [128, 512], F32) for j in range(2)]
                wl = sbt(ar, "b_wl", [128, 8, 128], F32)
                wlb = sbt(ar, "b_wlb", [128, 8, 128], BF16)
                wT = sbt(ar, "b_wT", [128, 8, 128], BF16)
                wTs = sbt(ar, "b_wTs", [128, 8, 128], BF16)
                w00 = sbt(ar, "b_w00", [128, 8], F32)
                bsf = sbt(ar, "b_bsf", [8, 128], F32)
                bsb = sbt(ar, "b_bsb", [8, 128], BF16)
                bs0 = sbt(ar, "b_bs0", [8, 128], BF16)
                ezb = sbt(ar, "b_ez", [128, 512], F32)
                t1 = sbt(ar, "b_t1", [128, 512], F32)
                Bvn_, Blg, Bst6, Bmv, Blnv, Brsb, Bvnf = Buf("vn"), Buf("lng"), Buf("st6"), Buf("mv"), Buf("lnv"), Buf("rsb"), Buf("vnf")
                Bvnf2 = [Buf("vnf20"), Buf("vnf21")]
                Bwl, Bwlb, BwT, Bbs, Bezb, Bt1 = Buf("wl"), Buf("wlb"), Buf("wT"), Buf("bs"), Buf("ezb"), Buf("t1")
                load_wo(l, 768, 4)
                bsel = sbt(ar, "b_bsel", [8, 512], BF16)
                bself = sbt(ar, "b_bself", [8, 512], F32)
                Bbsl = Buf("bsel")
                op(POOL, lambda: G.memset(bself[:], 1.0), writes=[Bbsl])
                op(POOL, lambda: G.affine_select(out=bself[:], in_=bself[:], pattern=[[1, 512]], compare_op=ALU.is_ge,
                                                 fill=0.0, base=0, channel_multiplier=-64), reads=[Bbsl], writes=[Bbsl])
                op(POOL, lambda: G.affine_select(out=bself[:], in_=bself[:], pattern=[[-1, 512]], compare_op=ALU.is_ge,
                                                 fill=0.0, base=63, channel_multiplier=64), reads=[Bbsl], writes=[Bbsl])
                op(DVE, lambda: V.tensor_copy(out=bsel[:], in_=bself[:]), reads=[Bbsl], writes=[Bbsl])
                dma(SP, lng_t[:], lng[l].partition_broadcast(128), writes=[Blg])
                dma(SP, lnb_t[:], lnb[l].partition_broadcast(128), writes=[Blg])
                dma(SP, wl[:], sgw[l].rearrange("g t s -> t g s"), writes=[Bwl])
                dma(SP, bsf[:], sgb[l], writes=[Bbs])
                u = uidx
                uidx += 1
                load_unit(u)
                load_unit(u + 1)
                wt, Bw = WA[u % 2], BW[u % 2]
                for i in range(NT):
                    j = i % 2
                    P, BP = PS[j], BPS[j]
                    for kc in range(8):
                        op(PE, lambda P=P, kc=kc, i=i: TE.matmul(P[:], lhsT=hT[:, kc, tcols(i)], rhs=wt[:, kc, :], start=(kc == 0), stop=(kc == 7)),
                           reads=[BhT[i], Bw], writes=[BP])
                    op(DVE, lambda P=P: V.bn_stats(out=st6[:], in_=P[:]), reads=[BP], writes=[Bst6])
                    op(DVE, lambda: V.bn_aggr(out=mv[:], in_=st6[:]), reads=[Bst6], writes=[Bmv])
                    op(ACT, lambda: A.activation(out=lnv[:], in_=mv[:, 1:2], func=AF.Ln, scale=1.0, bias=epsc[:, 0:1]), reads=[Bmv, Bc], writes=[Blnv])
                    op(ACT, lambda: A.activation(out=rsb[:], in_=lnv[:], func=AF.Exp, scale=-0.5), reads=[Blnv], writes=[Brsb])
                    op(DVE, lambda P=P: V.tensor_scalar(out=vnf[:], in0=P[:], scalar1=mv[:, 0:1], scalar2=rsb[:, 0:1], op0=ALU.subtract, op1=ALU.mult),
                       reads=[BP, Bmv, Brsb], writes=[Bvnf])
                    op(POOL, lambda: G.tensor_tensor(out=vnf[:], in0=vnf[:], in1=lng_t[:], op=ALU.mult), reads=[Bvnf, Blg], writes=[Bvnf])
                    op(POOL, lambda j=j: G.tensor_tensor(out=vnf2[j][:], in0=vnf[:], in1=lnb_t[:], op=ALU.add), reads=[Bvnf, Blg], writes=[Bvnf2[j]])
                    op(ACT, lambda i=i, j=j: A.activation(out=vn[:, i, :], in_=vnf2[j][:], func=AF.Copy), reads=[Bvnf2[j]], writes=[Bvn_])
                    if i == 16:
                        for b in range(4):
                            dma(SP, osg[l, b:b + 1, :], vnf2[j][32 * b:32 * b + 1, :], reads=[Bvnf2[j]])
                T.ck(f"B1_{l}")
                op(DVE, lambda: V.tensor_copy(out=wlb[:], in_=wl[:]), reads=[Bwl], writes=[Bwlb])
                for g in range(8):
                    op(PE, lambda g=g: TE.transpose(PB[0][:, tcols(g)], wlb[:, g, :], ident_bf[:]), reads=[Bwlb, Bc], writes=[BPB[0]])
                op(DVE, lambda: V.tensor_tensor(out=wT[:], in0=PB[0][:].rearrange("p (g t) -> p g t", g=8),
                                                in1=maskcur[:].unsqueeze(1).to_broadcast([128, 8, 128]), op=ALU.mult), reads=[BPB[0], Bc], writes=[BwT])
                op(PE, lambda: TE.matmul(PS[2][:, 0:8], lhsT=ones_f[0:1, :], rhs=wl[0:1, :, 0], start=True, stop=True), reads=[Bwl, Bc], writes=[BPS[2]])
                op(DVE, lambda: V.tensor_copy(out=w00[:], in_=PS[2][:, 0:8]), reads=[BPS[2]], writes=[BwT])
                op(DVE, lambda: V.tensor_tensor(out=wTs[:], in0=ident_bf[:].unsqueeze(1).to_broadcast([128, 8, 128]),
                                                in1=w00[:].unsqueeze(2).to_broadcast([128, 8, 128]), op=ALU.mult), reads=[BwT, Bc], writes=[BwT])
                op(DVE, lambda: V.tensor_copy(out=bsb[:], in_=bsf[:]), reads=[Bbs], writes=[Bbs])
                op(DVE, lambda: V.tensor_copy(out=bs0[:], in_=bsf[:, 0:1].to_broadcast([8, 128])), reads=[Bbs], writes=[Bbs])
                T.ck(f"B2_{l}")
                for cb in range(4):
                    u = uidx
                    uidx += 1
                    load_unit(u)
                    load_unit(u + 1)
                    wt, Bw = WA[u % 2], BW[u % 2]
                    for tg in range(5):
                        n = 512 if tg < 4 else 128
                        cols = slice(tg * 512, tg * 512 + n)
                        hb_ = BhT[4 * tg:4 * tg + 4] if tg < 4 else [BhT[16]]
                        Pu, BPu = PS[0 + tg % 2], BPS[0 + tg % 2]
                        Pz, BPz = PS[2 + tg % 2], BPS[2 + tg % 2]
                        Pm, BPm = PS[4 + tg % 2], BPS[4 + tg % 2]
                        for kc in range(8):
                            op(PE, lambda Pu=Pu, kc=kc, cols=cols, n=n: TE.matmul(Pu[:, 0:n], lhsT=wt[:, kc, 0:128], rhs=hT[:, kc, cols],
                                                                                start=(kc == 0), stop=(kc == 7)), reads=hb_ + [Bw], writes=[BPu])
                        for kc in range(8):
                            op(PE, lambda Pz=Pz, kc=kc, cols=cols, n=n: TE.matmul(Pz[:, 0:n], lhsT=wt[:, kc, 128:256], rhs=hT[:, kc, cols],
                                                                                start=(kc == 0), stop=(kc == 7)), reads=hb_ + [Bw], writes=[BPz])
                        for ti in range(n // 128):
                            i = 4 * tg + ti
                            for gg in range(2):
                                g = 2 * cb + gg
                                rws = slice(gg * 64, gg * 64 + 64)
                                wmat = wT if i < 16 else wTs
                                bmat = bsb if i < 16 else bs0
                                op(PE, lambda Pm=Pm, rws=rws, ti=ti, i=i, g=g, wmat=wmat: TE.matmul(
                                    Pm[rws, tcols(ti)], lhsT=vn[:, i, g * 64:(g + 1) * 64], rhs=wmat[:, g, :], start=True, stop=False),
                                   reads=[Bvn_, BwT], writes=[BPm])
                                op(PE, lambda Pm=Pm, rws=rws, ti=ti, g=g, bmat=bmat: TE.matmul(
                                    Pm[rws, tcols(ti)], lhsT=bsel[0:8, g * 64:(g + 1) * 64], rhs=bmat[0:8, :], start=False, stop=True),
                                   reads=[Bbs, Bbsl], writes=[BPm])
                        silu_from_psum(Pz, n, t1[:, 0:n], ezb, Bezb, BPz, [Bt1])
                        op(DVE, lambda Pu=Pu, n=n: V.tensor_tensor(out=t1[:, 0:n], in0=t1[:, 0:n], in1=Pu[:, 0:n], op=ALU.mult), reads=[Bt1, BPu], writes=[Bt1])
                        op(DVE, lambda Pm=Pm, n=n, cols=cols, cb=cb: V.tensor_tensor(out=yT[:, cb, cols], in0=t1[:, 0:n], in1=Pm[:, 0:n], op=ALU.mult),
                           reads=[Bt1, BPm], writes=[ByTp[cb] if tg < 4 else ByTs[cb]])
                T.barrier()
                T.ck(f"B3_{l}")
                out_proj(l, 4)
                T.barrier()
                T.ck(f"B_{l}")

            with contextlib.ExitStack() as ar:
                scanmask = sbt(ar, "c_scanm", [128, 512], F32)
                names = ["ef", "f", "g", "G", "eG", "kk", "kt", "khT", "eq", "q", "qt", "ez", "zs", "ln"]
                alias = {"eNG": "g", "rs": "ln", "sq": "eq", "o": "ez"}
                t = {nm: sbt(ar, "c_" + nm, [128, 512], F32) for nm in names}
                Bt_ = {nm: Buf("c_" + nm) for nm in names}
                for a_, b_ in alias.items():
                    t[a_] = t[b_]
                    Bt_[a_] = Bt_[b_]
                tv = sbt(ar, "c_tv", [128, 4, 128], F32)
                khtok = sbt(ar, "c_khtok", [128, 4, 4, 128], F32)
                ATm = sbt(ar, "c_ATm", [128, 4, 128], F32)
                Sab = [sbt(ar, f"c_S{j}", [128, 128], F32) for j in range(2)]
                S0 = [sbt(ar, f"c_S0{j}", [128, 128], F32) for j in range(2)]
                Sn = [sbt(ar, f"c_Sn{j}", [128, 128], F32) for j in range(2)]
                Btv, Bkh, BAT, Bsm = Buf("tv"), Buf("khtok"), Buf("ATm"), Buf("scanm")
                BS_ = [Buf("S0"), Buf("S1")]
                BS0 = [Buf("S00"), Buf("S01")]
                BSn = [Buf("Sn0"), Buf("Sn1")]
                load_wo(l, 1280, 6)
                op(DVE, lambda: V.memset(scanmask[:], 1.0), writes=[Bsm])
                op(DVE, lambda: V.memset(scanmask[:].rearrange("p (c j) -> p c j", j=32)[:, :, 0:1], 0.0), reads=[Bsm], writes=[Bsm])
                for hd in range(6):
                    u = uidx
                    uidx += 1
                    load_unit(u)
                    load_unit(u + 1)
                    wt, Bw = WA[u % 2], BW[u % 2]
                    lbc = lbT[:, l * 6 + hd:l * 6 + hd + 1]
                    omc = omlbT[:, l * 6 + hd:l * 6 + hd + 1]
                    hgc = hgT[:, l * 6 + hd:l * 6 + hd + 1]
                    op(DVE, lambda: V.memset(Sab[0][:], 0.0), writes=[BS_[0]])
                    sidx = 0
                    for tg in range(5):
                        n = 512 if tg < 4 else 128
                        nti = n // 128
                        cols = slice(tg * 512, tg * 512 + n)
                        hb_ = BhT[4 * tg:4 * tg + 4] if tg < 4 else [BhT[16]]
                        for pi_, (P, BP, w0) in enumerate([(PS[0], BPS[0], 0), (PS[1], BPS[1], 128), (PS[2], BPS[2], 384)]):
                            for kc in range(8):
                                op(PE, lambda P=P, kc=kc, w0=w0, cols=cols, n=n: TE.matmul(P[:, 0:n], lhsT=wt[:, kc, w0:w0 + 128], rhs=hT[:, kc, cols],
                                                                                         start=(kc == 0), stop=(kc == 7)), reads=hb_ + [Bw], writes=[BP])
                        for ti in range(nti):
                            i = 4 * tg + ti
                            for kc in range(8):
                                op(PE, lambda ti=ti, i=i, kc=kc: TE.matmul(PS[3][:, tcols(ti)], lhsT=hT[:, kc, tcols(i)], rhs=wt[:, kc, 256:384],
                                                                         start=(kc == 0), stop=(kc == 7)), reads=[BhT[i], Bw], writes=[BPS[3]])
                        sl = slice(0, n)
                        op(ACT, lambda: A.activation(out=t["ef"][:, sl], in_=PS[1][:, sl], func=AF.Exp, scale=-1.0), reads=[BPS[1]], writes=[Bt_["ef"]])
                        op(DVE, lambda: V.tensor_scalar(out=t["ef"][:, sl], in0=t["ef"][:, sl], scalar1=1.0, scalar2=None, op0=ALU.add), reads=[Bt_["ef"]], writes=[Bt_["ef"]])
                        op(DVE, lambda: V.reciprocal(out=t["ef"][:, sl], in_=t["ef"][:, sl]), reads=[Bt_["ef"]], writes=[Bt_["ef"]])
                        op(DVE, lambda: V.tensor_scalar(out=t["f"][:, sl], in0=t["ef"][:, sl], scalar1=omc, scalar2=lbc, op0=ALU.mult, op1=ALU.add),
                           reads=[Bt_["ef"], Bc], writes=[Bt_["f"]])
                        op(POOL, lambda: G.tensor_scalar(out=t["kk"][:, sl], in0=t["f"][:, sl], scalar1=-1.0, scalar2=1.0, op0=ALU.mult, op1=ALU.add),
                           reads=[Bt_["f"]], writes=[Bt_["kk"]])
                        op(ACT, lambda: A.activation(out=t["eq"][:, sl], in_=PS[0][:, sl], func=AF.Exp, scale=-1.0), reads=[BPS[0]], writes=[Bt_["eq"]])
                        op(DVE, lambda: V.tensor_scalar(out=t["eq"][:, sl], in0=t["eq"][:, sl], scalar1=1.0, scalar2=None, op0=ALU.add), reads=[Bt_["eq"]], writes=[Bt_["eq"]])
                        op(DVE, lambda: V.reciprocal(out=t["eq"][:, sl], in_=t["eq"][:, sl]), reads=[Bt_["eq"]], writes=[Bt_["eq"]])
                        op(DVE, lambda: V.tensor_tensor(out=t["q"][:, sl], in0=PS[0][:, sl], in1=t["eq"][:, sl], op=ALU.mult), reads=[BPS[0], Bt_["eq"]], writes=[Bt_["q"]])
                        silu_from_psum(PS[2], n, t["zs"][:, sl], t["ez"], Bt_["ez"], BPS[2], [Bt_["zs"]])
                        op(ACT, lambda: A.activation(out=tv[:, 0:nti, :], in_=PS[3][:, sl].rearrange("p (a v) -> p a v", v=128), func=AF.Copy),
                           reads=[BPS[3]], writes=[Btv])
                        if tg < 4:
                            op(ACT, lambda: A.activation(out=t["g"][:], in_=t["f"][:], func=AF.Ln), reads=[Bt_["f"]], writes=[Bt_["g"]])
                            op(DVE, lambda: V.tensor_tensor_scan(out=t["G"][:], data0=scanmask[:], data1=t["g"][:], initial=0.0, op0=ALU.mult, op1=ALU.add),
                               reads=[Bt_["g"], Bsm], writes=[Bt_["G"]])
                            op(ACT, lambda: A.activation(out=t["eG"][:], in_=t["G"][:], func=AF.Exp), reads=[Bt_["G"]], writes=[Bt_["eG"]])
                            op(ACT, lambda: A.activation(out=t["eNG"][:], in_=t["G"][:], func=AF.Exp, scale=-1.0), reads=[Bt_["G"]], writes=[Bt_["eNG"]])
                            op(POOL, lambda: G.tensor_tensor(out=t["kt"][:], in0=t["kk"][:], in1=t["eNG"][:], op=ALU.mult), reads=[Bt_["kk"], Bt_["eNG"]], writes=[Bt_["kt"]])
                            op(POOL, lambda: G.tensor_tensor(out=t["khT"][:].rearrange("p (c j) -> p c j", j=32), in0=t["kt"][:].rearrange("p (c j) -> p c j", j=32),
                                                             in1=t["eG"][:, 31:512:32].unsqueeze(2).to_broadcast([128, 16, 32]), op=ALU.mult),
                               reads=[Bt_["kt"], Bt_["eG"]], writes=[Bt_["khT"]])
                            op(POOL, lambda: G.tensor_tensor(out=t["qt"][:], in0=t["q"][:], in1=t["eG"][:], op=ALU.mult), reads=[Bt_["q"], Bt_["eG"]], writes=[Bt_["qt"]])
                            T.ck(f"C1_{l}_{hd}_{tg}")
                            for ti in range(4):
                                op(PE, lambda ti=ti: TE.transpose(PS[0][:, tcols(ti)], t["khT"][:, tcols(ti)], ident_f[:]), reads=[Bt_["khT"], Bc], writes=[BPS[0]])
                            for ch in range(4):
                                if ch % 2 == 0:
                                    op(ACT, lambda ch=ch: A.activation(out=khtok[:, :, ch, :], in_=PS[0][:].rearrange("p (a k) -> p a k", a=4), func=AF.Copy,
                                                                       scale=rowmask[:, ch:ch + 1]), reads=[BPS[0], Bc], writes=[Bkh])
                                else:
                                    op(DVE, lambda ch=ch: V.tensor_scalar(out=khtok[:, :, ch, :], in0=PS[0][:].rearrange("p (a k) -> p a k", a=4),
                                                                          scalar1=rowmask[:, ch:ch + 1], scalar2=None, op0=ALU.mult), reads=[BPS[0], Bc], writes=[Bkh])
                            for ti in range(4):
                                op(PE, lambda ti=ti: TE.matmul(PS[1][:, tcols(ti)], lhsT=t["kt"][:, tcols(ti)], rhs=t["qt"][:, tcols(ti)], start=True, stop=True),
                                   reads=[Bt_["kt"], Bt_["qt"]], writes=[BPS[1]])
                            op(DVE, lambda: V.tensor_tensor(out=ATm[:], in0=PS[1][:].rearrange("p (a t) -> p a t", a=4),
                                                            in1=blockmask[:].unsqueeze(1).to_broadcast([128, 4, 128]), op=ALU.mult), reads=[BPS[1], Bc], writes=[BAT])
                            T.ck(f"C2_{l}_{hd}_{tg}")
                            for nchk in range(16):
                                ti, ch = nchk // 4, nchk % 4
                                cc = slice(nchk * 32, nchk * 32 + 32)
                                cur, nxt = sidx % 2, (sidx + 1) % 2
                                sidx += 1
                                op(PE, lambda ti=ti, ch=ch, cc=cc: TE.matmul(PS[2][:, cc], lhsT=tv[:, ti, :], rhs=ATm[:, ti, ch * 32:(ch + 1) * 32], start=True, stop=False),
                                   reads=[Btv, BAT], writes=[BPS[2]])
                                op(PE, lambda cur=cur, cc=cc: TE.matmul(PS[2][:, cc], lhsT=Sab[cur][:], rhs=t["qt"][:, cc], start=False, stop=True),
                                   reads=[BS_[cur], Bt_["qt"]], writes=[BPS[2]])
                                ku = 3 + nchk % 2
                                op(PE, lambda ku=ku, ti=ti, ch=ch: TE.matmul(PS[ku][:, 0:128], lhsT=khtok[:, ti, ch, :], rhs=tv[:, ti, :], start=True, stop=True),
                                   reads=[Bkh, Btv], writes=[BPS[ku]])
                                op(DVE, lambda cur=cur, nxt=nxt, ku=ku, nchk=nchk: V.scalar_tensor_tensor(
                                    out=Sab[nxt][:], in0=Sab[cur][:], scalar=t["eG"][:, nchk * 32 + 31:nchk * 32 + 32], in1=PS[ku][:, 0:128],
                                    op0=ALU.mult, op1=ALU.add), reads=[BS_[cur], Bt_["eG"], BPS[ku]], writes=[BS_[nxt]])
                            no = 512
                        else:
                            op(PE, lambda: TE.transpose(PS[0][:, 0:128], t["kk"][:, 0:128], ident_f[:]), reads=[Bt_["kk"], Bc], writes=[BPS[0]])
                            for b in range(4):
                                op(DVE, lambda b=b: V.tensor_scalar(out=khtok[:, 0, b, :], in0=PS[0][:, 0:128], scalar1=rowmask_s[:, b:b + 1], scalar2=None, op0=ALU.mult),
                                   reads=[BPS[0], Bc], writes=[Bkh])
                            for b in range(4):
                                bj = b % 2
                                dma(SP, S0[bj][:], st[l, b, hd], writes=[BS0[bj]])
                                ku = 3 + bj
                                op(PE, lambda ku=ku, b=b: TE.matmul(PS[ku][:, 0:128], lhsT=khtok[:, 0, b, :], rhs=tv[:, 0, :], start=True, stop=True),
                                   reads=[Bkh, Btv], writes=[BPS[ku]])
                                op(DVE, lambda bj=bj, ku=ku, b=b: V.scalar_tensor_tensor(out=Sn[bj][:], in0=S0[bj][:], scalar=t["f"][:, 32 * b:32 * b + 1],
                                                                                       in1=PS[ku][:, 0:128], op0=ALU.mult, op1=ALU.add),
                                   reads=[BS0[bj], Bt_["f"], BPS[ku]], writes=[BSn[bj]])
                                dma(SP, ohs[l, b, hd], Sn[bj][:], reads=[BSn[bj]])
                                op(PE, lambda bj=bj, b=b: TE.matmul(PS[2][:, b:b + 1], lhsT=Sn[bj][:], rhs=t["q"][:, 32 * b:32 * b + 1], start=True, stop=True),
                                   reads=[BSn[bj], Bt_["q"]], writes=[BPS[2]])
                            no = 4
                        T.ck(f"C3_{l}_{hd}_{tg}")
                        so = slice(0, no)
                        op(ACT, lambda: A.activation(out=t["o"][:, so], in_=PS[2][:, so], func=AF.Copy), reads=[BPS[2]], writes=[Bt_["o"]])
                        op(ACT, lambda: A.activation(out=t["sq"][:, so], in_=PS[2][:, so], func=AF.Square), reads=[BPS[2]], writes=[Bt_["sq"]])
                        op(PE, lambda: TE.matmul(PS[5][:, so], lhsT=ones_f[:], rhs=t["sq"][:, so], start=True, stop=True), reads=[Bt_["sq"], Bc], writes=[BPS[5]])
                        op(ACT, lambda: A.activation(out=t["ln"][:, so], in_=PS[5][:, so], func=AF.Ln, scale=1.0 / 128, bias=epsc[:, 0:1]), reads=[BPS[5], Bc], writes=[Bt_["ln"]])
                        op(ACT, lambda: A.activation(out=t["rs"][:, so], in_=t["ln"][:, so], func=AF.Exp, scale=-0.5), reads=[Bt_["ln"]], writes=[Bt_["rs"]])
                        op(DVE, lambda: V.tensor_tensor(out=t["o"][:, so], in0=t["o"][:, so], in1=t["rs"][:, so], op=ALU.mult), reads=[Bt_["o"], Bt_["rs"]], writes=[Bt_["o"]])
                        if tg < 4:
                            op(DVE, lambda cols=cols, hd=hd: V.scalar_tensor_tensor(out=yT[:, hd, cols], in0=t["o"][:], scalar=hgc, in1=t["zs"][:], op0=ALU.mult, op1=ALU.mult),
                               reads=[Bt_["o"], Bt_["zs"], Bc], writes=[ByTp[hd]])
                        else:
                            op(DVE, lambda hd=hd: V.scalar_tensor_tensor(out=yT[:, hd, 2048:TOK:32], in0=t["o"][:, 0:4], scalar=hgc, in1=t["zs"][:, 0:128:32],
                                                                         op0=ALU.mult, op1=ALU.mult), reads=[Bt_["o"], Bt_["zs"], Bc], writes=[ByTs[hd]])
                        T.ck(f"C4_{l}_{hd}_{tg}")
                        if tg == 3:
                            dma(SP, ohp[l, hd], Sab[sidx % 2][:], reads=[BS_[sidx % 2]])
                T.barrier()
                T.ck(f"C5_{l}")
                out_proj(l, 6)
                T.barrier()
                T.ck(f"L_{l}")

        T.force = True
        for i in range(16):
            dma(SP, yp[i * 128:(i + 1) * 128, :], xres[:, i, :], reads=[Bx[i]])
        for b in range(4):
            dma(SP, ys[b:b + 1, :], xres[32 * b:32 * b + 1, 16, :], reads=[Bx[16]])
        for Q in (SP, POOL):
            for s in Q.slots:
                if s.val > 0:
                    T._wait(SP, s.sem, s.val)
    return nc


_NC_CACHE = {}


def kernel(x_prompt, x_sample, cache_k, cache_v, state_hgrn, norm_g, w_in, q_norm_g, k_norm_g,
           sgu_ln_g, sgu_ln_b, sgu_w, sgu_b, hgrn_lb_logits, hgrn_norm_g, w_out):
    f = lambda a: np.ascontiguousarray(np.asarray(a, dtype=np.float32))
    x_prompt, x_sample, cache_k, cache_v, state_hgrn = map(f, (x_prompt, x_sample, cache_k, cache_v, state_hgrn))
    shared = {
        "norm_g": f(norm_g), "w_in": f(w_in), "q_norm_g": f(q_norm_g), "k_norm_g": f(k_norm_g),
        "sgu_ln_g": f(sgu_ln_g), "sgu_ln_b": f(sgu_ln_b), "sgu_w": f(sgu_w), "sgu_b": f(sgu_b),
        "hgrn_lb_logits": f(hgrn_lb_logits), "hgrn_norm_g": f(hgrn_norm_g), "w_out": f(w_out),
    }
    in_maps = []
    for c in range(NCORES):
        sb = slice(4 * c, 4 * c + 4)
        m = dict(shared)
        m["xp"] = np.ascontiguousarray(x_prompt[c])
        m["xs"] = np.ascontiguousarray(x_sample[sb, 0, :])
        m["ck"] = np.ascontiguousarray(cache_k[:, sb].reshape(2, 4, 2048, 768))
        m["cv"] = np.ascontiguousarray(cache_v[:, sb].reshape(2, 4, 2048, 768))
        m["st"] = np.ascontiguousarray(state_hgrn[:, sb])
        in_maps.append(m)
    if "nc" not in _NC_CACHE:
        _NC_CACHE["nc"] = build_nc()
    res = run_bass_kernel_spmd(_NC_CACHE["nc"], in_maps, core_ids=list(range(NCORES)))
    R = res.results
    y_prompt = np.stack([R[c]["yp"] for c in range(NCORES)]).astype(np.float32)
    y_sample = np.concatenate([R[c]["ys"] for c in range(NCORES)])[:, None, :].astype(np.float32)
    nkp = np.stack([R[c]["okp"] for c in range(NCORES)], axis=1).reshape(2, 8, 2048, 12, 64).astype(np.float32)
    nvp = np.stack([R[c]["ovp"] for c in range(NCORES)], axis=1).reshape(2, 8, 2048, 12, 64).astype(np.float32)
    nks = np.concatenate([R[c]["oks"] for c in range(NCORES)], axis=1).reshape(2, 32, 1, 12, 64).astype(np.float32)
    nvs = np.concatenate([R[c]["ovs"] for c in range(NCORES)], axis=1).reshape(2, 32, 1, 12, 64).astype(np.float32)
    nsg = np.concatenate([R[c]["osg"] for c in range(NCORES)], axis=1).reshape(2, 32, 1, 512).astype(np.float32)
    nhp = np.stack([R[c]["ohp"] for c in range(NCORES)], axis=1).astype(np.float32)
    nhs = np.concatenate([R[c]["ohs"] for c in range(NCORES)], axis=1).astype(np.float32)
    return (y_prompt, y_sample, nkp, nvp, nks, nvs, nsg, nhp, nhs)
```

```python
import bisect
import contextlib
import os

import numpy as np

import concourse.bass as bass
import concourse.mybir as mybir
from concourse.bass_utils import run_bass_kernel_spmd

F32 = mybir.dt.float32
BF16 = mybir.dt.bfloat16
AF = mybir.ActivationFunctionType
ALU = mybir.AluOpType
AX = mybir.AxisListType

NCORES = 8
NT = 17
TOK = NT * 128
EPS = 1e-6


class Eng:
    def __init__(self, name, eng, sem):
        self.name, self.eng, self.sem = name, eng, sem
        self.nseq = 0
        self.cnt = 0
        self.last = None
        self.inc_seq = []
        self.waited = {}
        self.slots = []
        self.rr = 0


class Slot:
    def __init__(self, sem):
        self.sem = sem
        self.val = 0


class Buf:
    __slots__ = ("name", "w", "r")

    def __init__(self, name):
        self.name = name
        self.w = None
        self.r = {}


class Tracker:
    def __init__(self):
        self.engs = []
        self.stopped = False
        self.force = False
        self.stop_at = os.environ.get("KSTOP", "")

    def ck(self, name):
        if self.stop_at and name == self.stop_at:
            self.stopped = True

    def resolve(self, ev):
        if ev[0] == "s":
            return ev[1], ev[2]
        E, seq = ev[1], ev[2]
        i = bisect.bisect_left(E.inc_seq, seq)
        if i < len(E.inc_seq):
            return E.sem, i + 1
        E.last.then_inc(E.sem, 1)
        E.cnt += 1
        E.inc_seq.append(E.nseq)
        return E.sem, E.cnt

    def _wait(self, E, sem, val):
        if E.waited.get(sem.num, 0) < val:
            E.eng.wait_ge(sem, val)
            E.waited[sem.num] = val

    def deps(self, E, reads, writes):
        evs = []
        for b in reads:
            if b.w is not None:
                evs.append(b.w)
        for b in writes:
            if b.w is not None:
                evs.append(b.w)
            for k, ev in b.r.items():
                if ev[0] == "e" and ev[1] is E:
                    continue
                evs.append(ev)
        need = {}
        for ev in evs:
            if ev[0] == "e" and ev[1] is E and E.name == "pe":
                continue
            sem, val = self.resolve(ev)
            if need.get(sem.num, (None, 0))[1] < val:
                need[sem.num] = (sem, val)
        for num, (sem, val) in need.items():
            self._wait(E, sem, val)

    def op(self, E, fn, reads=(), writes=(), inc=None):
        if self.stopped and not self.force:
            return None
        self.deps(E, reads, writes)
        ins = fn()
        E.nseq += 1
        E.last = ins
        if inc and os.environ.get("KINC", "0") == "1":
            ins.then_inc(E.sem, 1)
            E.cnt += 1
            E.inc_seq.append(E.nseq)
        ev = ("e", E, E.nseq)
        for b in writes:
            b.w = ev
            b.r = {}
        for b in reads:
            b.r[E.name] = ev
        return ins

    def dma(self, Q, out, in_, reads=(), writes=()):
        if self.stopped and not self.force:
            return
        self.deps(Q, reads, writes)
        slot = Q.slots[Q.rr % len(Q.slots)]
        Q.rr += 1
        if slot.val > 0:
            self._wait(Q, slot.sem, slot.val)
        Q.eng.dma_start(out=out, in_=in_).then_inc(slot.sem, 16)
        slot.val += 16
        ev = ("s", slot.sem, slot.val)
        for b in writes:
            b.w = ev
            b.r = {}
        for b in reads:
            b.r[("d", slot.sem.num)] = ev

    def barrier(self):
        if self.stopped and not self.force:
            return
        pts = []
        for F in self.engs:
            if F.nseq > 0:
                pts.append(self.resolve(("e", F, F.nseq)))
            for s in F.slots:
                if s.val > 0:
                    pts.append((s.sem, s.val))
        for E in self.engs:
            for sem, val in pts:
                if sem is E.sem and E.name == "pe":
                    continue
                self._wait(E, sem, val)


def build_nc():
    nc = bass.Bass("TRN2", target_bir_lowering=False)

    def din(name, shape):
        return nc.dram_tensor(name, shape, F32, kind="ExternalInput").ap()

    def dout(name, shape):
        return nc.dram_tensor(name, shape, F32, kind="ExternalOutput").ap()

    xp = din("xp", [2048, 1024])
    xs = din("xs", [4, 1024])
    ck = din("ck", [2, 4, 2048, 768])
    cv = din("cv", [2, 4, 2048, 768])
    st = din("st", [2, 4, 6, 128, 128])
    norm_g = din("norm_g", [2, 1024])
    w_in = din("w_in", [2, 1024, 7680])
    qng = din("q_norm_g", [2, 768])
    kng = din("k_norm_g", [2, 768])
    lng = din("sgu_ln_g", [2, 512])
    lnb = din("sgu_ln_b", [2, 512])
    sgw = din("sgu_w", [2, 8, 128, 128])
    sgb = din("sgu_b", [2, 8, 128])
    lbl = din("hgrn_lb_logits", [2, 768])
    hng = din("hgrn_norm_g", [2, 768])
    w_out = din("w_out", [2, 2048, 1024])
    yp = dout("yp", [2048, 1024])
    ys = dout("ys", [4, 1024])
    okp = dout("okp", [2, 2048, 768])
    ovp = dout("ovp", [2, 2048, 768])
    oks = dout("oks", [2, 4, 768])
    ovs = dout("ovs", [2, 4, 768])
    osg = dout("osg", [2, 4, 512])
    ohp = dout("ohp", [2, 6, 128, 128])
    ohs = dout("ohs", [2, 4, 6, 128, 128])

    T = Tracker()
    es = contextlib.ExitStack()
    with es:
        nmctr = [0]

        def sbt(stack, name, shape, dt):
            nmctr[0] += 1
            return stack.enter_context(nc.sbuf_tensor(f"{name}_{nmctr[0]}", shape, dt))

        sems = [es.enter_context(nc.semaphore(f"sem{i}")) for i in range(20)]
        PE = Eng("pe", nc.tensor, sems[0])
        ACT = Eng("act", nc.scalar, sems[1])
        DVE = Eng("dve", nc.vector, sems[2])
        POOL = Eng("pool", nc.gpsimd, sems[3])
        SP = Eng("sp", nc.sync, None)
        SP.slots = [Slot(s) for s in sems[4:12]]
        POOL.slots = [Slot(s) for s in sems[12:20]]
        T.engs = [PE, ACT, DVE, POOL, SP]
        op, dma = T.op, T.dma
        V, A, G, TE = nc.vector, nc.scalar, nc.gpsimd, nc.tensor

        PS = [es.enter_context(nc.psum_tensor(f"ps{i}", [128, 512], F32)) for i in range(6)]
        PB = [es.enter_context(nc.psum_tensor(f"pb{i}", [128, 1024], BF16)) for i in range(2)]
        BPS = [Buf(f"ps{i}") for i in range(6)]
        BPB = [Buf(f"pb{i}") for i in range(2)]

        xres = sbt(es, "xres", [128, NT, 1024], F32)
        hT = sbt(es, "hT", [128, 8, TOK], BF16)
        yT = sbt(es, "yT", [128, 6, TOK], BF16)
        WA = [sbt(es, f"wA{i}", [128, 8, 512], BF16) for i in range(2)]
        wo = sbt(es, "wo", [128, 6, 1024], BF16)
        ident_bf = sbt(es, "ident_bf", [128, 128], BF16)
        ident_f = sbt(es, "ident_f", [128, 128], F32)
        ones_bf = sbt(es, "ones_bf", [128, 128], BF16)
        ones_f = sbt(es, "ones_f", [128, 128], F32)
        mask01 = sbt(es, "mask01", [128, 512], BF16)
        maskcur = sbt(es, "maskcur", [128, 128], BF16)
        blockmask = sbt(es, "blockmask", [128, 128], F32)
        rowmask = sbt(es, "rowmask", [128, 4], F32)
        rowmask_s = sbt(es, "rowmask_s", [128, 4], F32)
        rowmask_s3 = sbt(es, "rowmask_s3", [128, 4], F32)
        epsc = sbt(es, "epsc", [128, 1], F32)
        lbT = sbt(es, "lbT", [128, 12], F32)
        omlbT = sbt(es, "omlbT", [128, 12], F32)
        hgT = sbt(es, "hgT", [128, 12], F32)

        Bx = [Buf(f"x{i}") for i in range(NT)]
        BhT = [Buf(f"hT{i}") for i in range(NT)]
        ByTp = [Buf(f"yTp{i}") for i in range(6)]
        ByTs = [Buf(f"yTs{i}") for i in range(6)]
        BW = [Buf("wA0"), Buf("wA1")]
        Bwo = Buf("wo")
        Bc = Buf("consts")

        def tcols(i):
            return slice(i * 128, (i + 1) * 128)

        units = []
        for l in range(2):
            for c in range(6):
                units.append((l, [(0, c * 128), (128, 768 + c * 128), (256, 1536 + c * 128), (384, 2304 + c * 128)], 128))
            units.append((l, [(0, 3584)], 512))
            for cb in range(4):
                units.append((l, [(0, 3072 + cb * 128), (128, 4096 + cb * 128)], 128))
            for hd in range(6):
                units.append((l, [(0, 4608 + hd * 128), (128, 5376 + hd * 128), (256, 6144 + hd * 128), (384, 6912 + hd * 128)], 128))
        ustate = {"loaded": 0}

        def load_unit(u):
            if u >= len(units) or u < ustate["loaded"]:
                return
            assert u == ustate["loaded"]
            ustate["loaded"] = u + 1
            l, parts, wdt = units[u]
            wt = WA[u % 2]
            for (dst, src, ) in [(p[0], p[1]) for p in parts]:
                dma(POOL, wt[:, :, dst:dst + wdt],
                    w_in[l, :, src:src + wdt].rearrange("(kc p) n -> p kc n", p=128), writes=[BW[u % 2]])

        def load_wo(l, r0, nch):
            dma(POOL, wo[:, 0:nch, :], w_out[l, r0:r0 + nch * 128, :].rearrange("(c p) d -> p c d", p=128), writes=[Bwo])

        with contextlib.ExitStack() as ar:
            tmpf = sbt(ar, "c_tmpf", [128, 512], F32)
            R4 = sbt(ar, "c_R4", [4, 128], F32)
            ld12 = sbt(ar, "c_ld12", [12, 128], F32)
            hg12 = sbt(ar, "c_hg12", [12, 128], F32)
            lgT = sbt(ar, "c_lgT", [128, 12], F32)
            Bt = Buf("c_tmp")
            BR4 = Buf("c_R4")
            Bl = Buf("c_ld")
            dma(SP, ld12[:], lbl.rearrange("l (h k) -> (l h) k", k=128), writes=[Bl])
            dma(SP, hg12[:], hng.rearrange("l (h k) -> (l h) k", k=128), writes=[Bl])
            for i in range(16):
                dma(SP, xres[:, i, :], xp[i * 128:(i + 1) * 128, :], writes=[Bx[i]])
            op(DVE, lambda: V.memset(xres[:, 16, :], 0.0), writes=[Bx[16]])
            for b in range(4):
                dma(SP, xres[32 * b:32 * b + 1, 16, :], xs[b:b + 1, :], writes=[Bx[16]])
            op(DVE, lambda: V.memset(yT[:, :, 2048:TOK], 0.0), writes=ByTs)
            op(DVE, lambda: V.memset(epsc[:], EPS), writes=[Bc])
            op(DVE, lambda: V.memset(ones_bf[:], 1.0), writes=[Bc])
            op(DVE, lambda: V.memset(ones_f[:], 1.0), writes=[Bc])
            op(POOL, lambda: G.memset(ident_f[:], 1.0), writes=[Bc])
            op(POOL, lambda: G.affine_select(out=ident_f[:], in_=ident_f[:], pattern=[[-1, 128]], compare_op=ALU.is_equal,
                                             fill=0.0, base=0, channel_multiplier=1), reads=[Bc], writes=[Bc])
            op(DVE, lambda: V.tensor_copy(out=ident_bf[:], in_=ident_f[:]), reads=[Bc], writes=[Bc])
            op(POOL, lambda: G.memset(tmpf[:, 0:256], 1.0), writes=[Bt])
            op(POOL, lambda: G.affine_select(out=tmpf[:, 0:128], in_=tmpf[:, 0:128], pattern=[[-1, 128]], compare_op=ALU.is_ge,
                                             fill=0.0, base=0, channel_multiplier=1), reads=[Bt], writes=[Bt])
            op(POOL, lambda: G.affine_select(out=tmpf[:, 128:256], in_=tmpf[:, 128:256], pattern=[[1, 128]], compare_op=ALU.is_ge,
                                             fill=0.0, base=0, channel_multiplier=-1), reads=[Bt], writes=[Bt])
            for kb in range(2):
                for h in range(2):
                    op(DVE, lambda kb=kb, h=h: V.tensor_copy(out=mask01[:, (h * 2 + kb) * 128:(h * 2 + kb + 1) * 128],
                                                            in_=tmpf[:, kb * 128:(kb + 1) * 128]), reads=[Bt], writes=[Bc])
            op(DVE, lambda: V.tensor_copy(out=maskcur[:], in_=tmpf[:, 128:256]), reads=[Bt], writes=[Bc])
            op(POOL, lambda: G.memset(R4[:], 1.0), writes=[BR4])
            op(POOL, lambda: G.affine_select(out=R4[:], in_=R4[:], pattern=[[1, 128]], compare_op=ALU.is_ge,
                                             fill=0.0, base=0, channel_multiplier=-32), reads=[BR4], writes=[BR4])
            op(POOL, lambda: G.affine_select(out=R4[:], in_=R4[:], pattern=[[-1, 128]], compare_op=ALU.is_ge,
                                             fill=0.0, base=31, channel_multiplier=32), reads=[BR4], writes=[BR4])
            op(PE, lambda: TE.matmul(PS[0][:, 0:128], lhsT=R4[:], rhs=R4[:], start=True, stop=True), reads=[BR4], writes=[BPS[0]])
            op(PE, lambda: TE.matmul(PS[0][:, 128:132], lhsT=R4[:], rhs=ident_f[0:4, 0:4], start=True, stop=True),
               reads=[BR4, Bc], writes=[BPS[0]])
            op(DVE, lambda: V.tensor_tensor(out=blockmask[:], in0=tmpf[:, 128:256], in1=PS[0][:, 0:128], op=ALU.mult),
               reads=[Bt, BPS[0]], writes=[Bc])
            op(DVE, lambda: V.tensor_copy(out=rowmask[:], in_=PS[0][:, 128:132]), reads=[BPS[0]], writes=[Bc])
            op(POOL, lambda: G.memset(rowmask_s[:], 1.0), writes=[Bc])
            op(POOL, lambda: G.affine_select(out=rowmask_s[:], in_=rowmask_s[:], pattern=[[-32, 4]], compare_op=ALU.is_equal,
                                             fill=0.0, base=0, channel_multiplier=1), reads=[Bc], writes=[Bc])
            op(DVE, lambda: V.tensor_scalar(out=rowmask_s3[:], in0=rowmask_s[:], scalar1=3.0, scalar2=None, op0=ALU.mult),
               reads=[Bc], writes=[Bc])
            op(PE, lambda: TE.transpose(PS[1][:, 0:12], ld12[:], ident_f[0:12, 0:12]), reads=[Bl, Bc], writes=[BPS[1]])
            op(PE, lambda: TE.transpose(PS[1][:, 16:28], hg12[:], ident_f[0:12, 0:12]), reads=[Bl, Bc], writes=[BPS[1]])
            op(DVE, lambda: V.tensor_copy(out=lgT[:], in_=PS[1][:, 0:12]), reads=[BPS[1]], writes=[Bt])
            op(DVE, lambda: V.tensor_copy(out=hgT[:], in_=PS[1][:, 16:28]), reads=[BPS[1]], writes=[Bc])
            op(DVE, lambda: V.memset(lbT[:], 0.0), writes=[Bc])
            op(DVE, lambda: V.tensor_tensor(out=lgT[:, 0:6], in0=lgT[:, 0:6], in1=lgT[:, 6:12], op=ALU.subtract), reads=[Bt], writes=[Bt])
            op(ACT, lambda: A.activation(out=lgT[:, 0:6], in_=lgT[:, 0:6], func=AF.Exp), reads=[Bt], writes=[Bt])
            op(DVE, lambda: V.tensor_scalar(out=lgT[:, 0:6], in0=lgT[:, 0:6], scalar1=1.0, scalar2=None, op0=ALU.add), reads=[Bt], writes=[Bt])
            op(DVE, lambda: V.reciprocal(out=lbT[:, 6:12], in_=lgT[:, 0:6]), reads=[Bt, Bc], writes=[Bc])
            op(DVE, lambda: V.tensor_scalar(out=omlbT[:], in0=lbT[:], scalar1=-1.0, scalar2=1.0, op0=ALU.mult, op1=ALU.add),
               reads=[Bc], writes=[Bc])
            load_unit(0)
            T.barrier()
            T.ck("const")

        def silu_from_psum(P, n, out_ap, tmp, Btmp, BP, wr):
            op(ACT, lambda: A.activation(out=tmp[:, 0:n], in_=P[:, 0:n], func=AF.Exp, scale=-1.0), reads=[BP], writes=[Btmp])
            op(DVE, lambda: V.tensor_scalar(out=tmp[:, 0:n], in0=tmp[:, 0:n], scalar1=1.0, scalar2=None, op0=ALU.add), reads=[Btmp], writes=[Btmp])
            op(DVE, lambda: V.reciprocal(out=tmp[:, 0:n], in_=tmp[:, 0:n]), reads=[Btmp], writes=[Btmp])
            op(DVE, lambda: V.tensor_tensor(out=out_ap, in0=P[:, 0:n], in1=tmp[:, 0:n], op=ALU.mult), reads=[Btmp, BP], writes=wr)

        def out_proj(l, nch):
            for i in range(NT):
                ybufs = (ByTp if i < 16 else ByTs)[0:nch]
                for half in range(2):
                    k = (2 * i + half) % 4
                    P = PS[k]
                    for cc in range(nch):
                        op(PE, lambda P=P, cc=cc, i=i, half=half: TE.matmul(
                            P[:], lhsT=yT[:, cc, tcols(i)], rhs=wo[:, cc, half * 512:(half + 1) * 512],
                            start=(cc == 0), stop=(cc == nch - 1)), reads=ybufs + [Bwo], writes=[BPS[k]])
                    xs_ap = xres[:, i, half * 512:(half + 1) * 512]
                    op(DVE, lambda P=P, xs_ap=xs_ap: V.tensor_tensor(out=xs_ap, in0=xs_ap, in1=P[:], op=ALU.add),
                       reads=[Bx[i], BPS[k]], writes=[Bx[i]])

        uidx = 0
        for l in range(2):
            if os.environ.get("KSKIP0", "") == "1":
                if l == 0:
                    T.stopped = True
                else:
                    T.stopped = False
                    ustate["loaded"] = 17
            with contextlib.ExitStack() as ar:
                gn = sbt(ar, "n_gn", [128, 1024], F32)
                sqj = sbt(ar, "n_sqj", [128, 1024], BF16)
                ss = sbt(ar, "n_ss", [128, NT], F32)
                rstd = sbt(ar, "n_rstd", [128, NT], F32)
                hb = [sbt(ar, f"n_hb{j}", [128, 1024], BF16) for j in range(2)]
                Bgn, Bsq, Bss, Brs = Buf("gn"), Buf("sqj"), Buf("ss"), Buf("rstd")
                Bhb = [Buf("hb0"), Buf("hb1")]
                dma(SP, gn[:], norm_g[l].partition_broadcast(128), writes=[Bgn])
                op(DVE, lambda: V.memset(ss[:], 0.0), writes=[Bss])
                for i in range(NT):
                    op(ACT, lambda i=i: A.activation(out=sqj[:], in_=xres[:, i, :], func=AF.Square, accum_out=ss[:, i:i + 1]),
                       reads=[Bx[i], Bss], writes=[Bsq, Bss])
                op(ACT, lambda: A.activation(out=ss[:], in_=ss[:], func=AF.Ln, scale=1.0 / 1024, bias=epsc[:, 0:1]),
                   reads=[Bss, Bc], writes=[Bss])
                op(ACT, lambda: A.activation(out=rstd[:], in_=ss[:], func=AF.Exp, scale=-0.5), reads=[Bss], writes=[Brs])
                for i in range(NT):
                    j = i % 2
                    op(DVE, lambda i=i, j=j: V.scalar_tensor_tensor(out=hb[j][:], in0=xres[:, i, :], scalar=rstd[:, i:i + 1], in1=gn[:],
                                                                    op0=ALU.mult, op1=ALU.mult),
                       reads=[Bx[i], Brs, Bgn], writes=[Bhb[j]])
                    for kc in range(8):
                        op(PE, lambda j=j, kc=kc: TE.transpose(PB[j][:, tcols(kc)], hb[j][:, tcols(kc)], ident_bf[:]),
                           reads=[Bhb[j], Bc], writes=[BPB[j]])
                    op(ACT, lambda i=i, j=j: A.activation(out=hT[:, :, tcols(i)], in_=PB[j][:].rearrange("p (k t) -> p k t", k=8), func=AF.Copy),
                       reads=[BPB[j]], writes=[BhT[i]])
                T.barrier()
                T.ck(f"norm{l}")

            with contextlib.ExitStack() as arA:
                qs_tok = sbt(arA, "a_qs", [128, 768], BF16)
                ks_tok = sbt(arA, "a_ks", [128, 768], BF16)
                vs_tok = sbt(arA, "a_vs", [128, 768], BF16)
                zsamp = sbt(arA, "a_zs", [128, 6, 4], F32)
                Bst = Buf("a_stash")
                load_wo(l, 0, 6)
                with contextlib.ExitStack() as ar:
                    qkT = sbt(ar, "a_qkT", [128, 2, TOK], BF16)
                    vnat = sbt(ar, "a_vnat", [128, NT, 128], BF16)
                    vord = sbt(ar, "a_vord", [128, 16, 128], BF16)
                    zT = sbt(ar, "a_zT", [128, TOK], BF16)
                    UD = sbt(ar, "a_UD", [128, 2, 2048], F32)
                    pp = [sbt(ar, f"a_p{j}", [128, 512], BF16) for j in range(2)]
                    kfin = [sbt(ar, f"a_kfin{j}", [128, 128], F32) for j in range(2)]
                    vfin = [sbt(ar, f"a_vfin{j}", [128, 128], F32) for j in range(2)]
                    qb = [sbt(ar, f"a_qb{j}", [128, 128], BF16) for j in range(2)]
                    kb_ = [sbt(ar, f"a_kb{j}", [128, 128], BF16) for j in range(2)]
                    ss4 = sbt(ar, "a_ss4", [128, 4], F32)
                    rs4 = sbt(ar, "a_rs4", [128, 4], F32)
                    qgc = sbt(ar, "a_qgc", [128, 128], F32)
                    kgc = sbt(ar, "a_kgc", [128, 128], F32)
                    BqkT, Bvn, Bvo, BzT, BUD = Buf("qkT"), Buf("vnat"), Buf("vord"), Buf("zT"), Buf("UD")
                    Bp = [Buf("p0"), Buf("p1")]
                    Bpm = [Buf("pm0"), Buf("pm1")]
                    Bkf = [Buf("kf0"), Buf("kf1")]
                    Bvf = [Buf("vf0"), Buf("vf1")]
                    Btq, Btk, Bsq4, Bss4, Brs4, Bg, Bez = Buf("tq"), Buf("tk"), Buf("sq"), Buf("ss4"), Buf("rs4"), Buf("g"), Buf("ezt")
                    Bqb = [Buf("qb0"), Buf("qb1")]
                    Bkb = [Buf("kb0"), Buf("kb1")]
                    blk_ctr = [0]
                    sq = UD[:, 1, 0:256]
                    ezt = UD[:, 0, 0:512]
                    Bsq4 = BUD
                    Bez = BUD

                    for c in range(6):
                        u = uidx
                        uidx += 1
                        load_unit(u)
                        load_unit(u + 1)
                        wt, Bw = WA[u % 2], BW[u % 2]
                        csl = slice(c * 128, (c + 1) * 128)
                        dma(SP, qgc[:], qng[l, csl].partition_broadcast(128), writes=[Bg])
                        dma(SP, kgc[:], kng[l, csl].partition_broadcast(128), writes=[Bg])
                        op(DVE, lambda: V.tensor_scalar(out=qgc[:], in0=qgc[:], scalar1=0.125, scalar2=None, op0=ALU.mult), reads=[Bg], writes=[Bg])
                        def a1_front(i):
                            j = i % 2
                            P, BP = PS[j], BPS[j]
                            for kc in range(8):
                                op(PE, lambda P=P, kc=kc, i=i: TE.matmul(P[:, 0:384], lhsT=hT[:, kc, tcols(i)], rhs=wt[:, kc, 0:384],
                                                                        start=(kc == 0), stop=(kc == 7)), reads=[BhT[i], Bw], writes=[BP], inc=(kc == 7))
                            op(ACT, lambda P=P: A.activation(out=sq, in_=P[:, 0:256], func=AF.Square), reads=[BP], writes=[Bsq4])
                            op(DVE, lambda: V.tensor_reduce(out=ss4[:], in_=sq.rearrange("p (h e) -> p h e", e=64), axis=AX.X, op=ALU.add),
                               reads=[Bsq4], writes=[Bss4])
                            op(ACT, lambda: A.activation(out=ss4[:], in_=ss4[:], func=AF.Ln, scale=1.0 / 64, bias=epsc[:, 0:1]),
                               reads=[Bss4, Bc], writes=[Bss4])
                            op(ACT, lambda: A.activation(out=rs4[:], in_=ss4[:], func=AF.Exp, scale=-0.5), reads=[Bss4], writes=[Brs4])
                            for h in range(2):
                                hs = slice(h * 64, (h + 1) * 64)
                                op(DVE, lambda P=P, j=j, h=h, hs=hs: V.scalar_tensor_tensor(out=qb[j][:, hs], in0=P[:, hs], scalar=rs4[:, h:h + 1], in1=qgc[:, hs],
                                                                                        op0=ALU.mult, op1=ALU.mult), reads=[BP, Brs4, Bg], writes=[Bqb[j]])
                            for h in range(2):
                                hs = slice(h * 64, (h + 1) * 64)
                                op(DVE, lambda P=P, j=j, h=h, hs=hs: V.scalar_tensor_tensor(out=kfin[j][:, hs], in0=P[:, 128 + h * 64:128 + (h + 1) * 64],
                                                                                        scalar=rs4[:, 2 + h:3 + h], in1=kgc[:, hs], op0=ALU.mult, op1=ALU.mult),
                                   reads=[BP, Brs4, Bg], writes=[Bkf[j]])
                            op(POOL, lambda j=j: G.tensor_copy(out=kb_[j][:], in_=kfin[j][:]), reads=[Bkf[j]], writes=[Bkb[j]])
                            op(ACT, lambda P=P, j=j: A.activation(out=vfin[j][:], in_=P[:, 256:384], func=AF.Copy), reads=[BP], writes=[Bvf[j]])
                            op(POOL, lambda i=i, j=j: G.tensor_copy(out=vnat[:, i, :], in_=vfin[j][:]), reads=[Bvf[j]], writes=[Bvn])
                            if i < 16:
                                dma(SP, okp[l, tcols(i), csl], kfin[j][:], reads=[Bkf[j]])
                                dma(SP, ovp[l, tcols(i), csl], vfin[j][:], reads=[Bvf[j]])
                            else:
                                for b in range(4):
                                    dma(SP, oks[l, b:b + 1, csl], kfin[j][32 * b:32 * b + 1, :], reads=[Bkf[j]])
                                    dma(SP, ovs[l, b:b + 1, csl], vfin[j][32 * b:32 * b + 1, :], reads=[Bvf[j]])
                                op(POOL, lambda j=j: G.tensor_copy(out=qs_tok[:, csl], in_=qb[j][:]), reads=[Bqb[j]], writes=[Bst])
                                op(POOL, lambda j=j: G.tensor_copy(out=ks_tok[:, csl], in_=kb_[j][:]), reads=[Bkb[j]], writes=[Bst])
                                op(POOL, lambda j=j: G.tensor_copy(out=vs_tok[:, csl], in_=vfin[j][:]), reads=[Bvf[j]], writes=[Bst])
                        def a1_back(i):
                            j = i % 2
                            op(PE, lambda j=j: TE.transpose(PB[j][:, 0:128], qb[j][:], ident_bf[:]), reads=[Bqb[j], Bc], writes=[BPB[j]])
                            op(PE, lambda j=j: TE.transpose(PB[j][:, 128:256], kb_[j][:], ident_bf[:]), reads=[Bkb[j], Bc], writes=[BPB[j]], inc=True)
                            op(DVE, lambda i=i, j=j: V.tensor_copy(out=qkT[:, :, tcols(i)], in_=PB[j][:, 0:256].rearrange("p (a t) -> p a t", a=2)),
                               reads=[BPB[j]], writes=[BqkT])
                        if os.environ.get("KPIPE", "0") == "1":
                            a1_front(0)
                            for i in range(1, NT):
                                a1_front(i)
                                a1_back(i - 1)
                            a1_back(NT - 1)
                        else:
                            for i in range(NT):
                                a1_front(i)
                                a1_back(i)
                        T.ck(f"A1_{l}_{c}")
                        for tg in range(5):
                            n = 512 if tg < 4 else 128
                            cols = slice(tg * 512, tg * 512 + n)
                            k = 2 + tg % 2
                            P, BP = PS[k], BPS[k]
                            hb_ = BhT[4 * tg:4 * tg + 4] if tg < 4 else [BhT[16]]
                            for kc in range(8):
                                op(PE, lambda P=P, kc=kc, cols=cols, n=n: TE.matmul(P[:, 0:n], lhsT=wt[:, kc, 384:512], rhs=hT[:, kc, cols],
                                                                                  start=(kc == 0), stop=(kc == 7)), reads=hb_ + [Bw], writes=[BP])
                            silu_from_psum(P, n, zT[:, cols], ezt, Bez, BP, [BzT])
                        op(POOL, lambda c=c: G.tensor_copy(out=zsamp[:, c, :], in_=zT[:, 2048:TOK:32]), reads=[BzT], writes=[Bst])

                        T.ck(f"A2_{l}_{c}")
                        def attn_front(qsl, kcur, kprev, vcur, vprev, first):
                            n_ = blk_ctr[0]
                            blk_ctr[0] += 1
                            jj = n_ % 2
                            Sb = [(PS[n_ % 2], BPS[n_ % 2]), (PS[2 + n_ % 2], BPS[2 + n_ % 2])]
                            kbs = [1] if kprev is None else [0, 1]
                            lo = 0 if kprev is not None else 128
                            for h in range(2):
                                S, BS = Sb[h]
                                hs = slice(h * 64, (h + 1) * 64)
                                for kb in kbs:
                                    ks = kprev if kb == 0 else kcur
                                    op(PE, lambda S=S, hs=hs, ks=ks, kb=kb: TE.matmul(S[:, kb * 128:(kb + 1) * 128], lhsT=qkT[hs, 1, ks], rhs=qkT[hs, 0, qsl],
                                                                                   start=True, stop=True), reads=[BqkT], writes=[BS], inc=(kb == 1))
                            for h in range(2):
                                S, BS = Sb[h]
                                op(ACT, lambda S=S, h=h: A.activation(out=pp[jj][:, h * 256 + lo:(h + 1) * 256], in_=S[:, lo:256], func=AF.Exp),
                                   reads=[BS], writes=[Bp[jj]])
                            pv = pp[jj][:].rearrange("p (h x) -> p h x", h=2)[:, :, lo:256]
                            mv_ = mask01[:].rearrange("p (h x) -> p h x", h=2)[:, :, lo:256]
                            op(POOL, lambda pv=pv, mv_=mv_: G.tensor_tensor(out=pv, in0=pv, in1=mv_, op=ALU.mult), reads=[Bp[jj], Bc], writes=[Bp[jj]])
                            return (n_, qsl, kbs, vcur, vprev, first)

                        def attn_back(ctx):
                            n_, qsl, kbs, vcur, vprev, first = ctx
                            jj = n_ % 2
                            U, BU = PS[4 + n_ % 2], BPS[4 + n_ % 2]
                            for part in range(2):
                                for h in range(2):
                                    hs = slice(h * 64, (h + 1) * 64)
                                    for idx, kb in enumerate(kbs):
                                        vb = vprev if kb == 0 else vcur
                                        lhsT = vb[:, hs] if part == 0 else ones_bf[:, 0:64]
                                        o0 = (h * 2 + kb) * 128
                                        op(PE, lambda U=U, hs=hs, part=part, lhsT=lhsT, o0=o0, idx=idx: TE.matmul(
                                            U[hs, part * 128:(part + 1) * 128], lhsT=lhsT, rhs=pp[jj][:, o0:o0 + 128],
                                            start=(idx == 0), stop=(idx == len(kbs) - 1)), reads=[Bp[jj], Bvn, Bvo, Bc], writes=[BU],
                                           inc=(part == 1 and h == 1 and idx == len(kbs) - 1))
                            uv = U[:, 0:256].rearrange("p (a q) -> p a q", a=2)
                            if first:
                                op(DVE, lambda: V.tensor_copy(out=UD[:, :, qsl], in_=uv), reads=[BU], writes=[BUD])
                            else:
                                op(DVE, lambda: V.tensor_tensor(out=UD[:, :, qsl], in0=UD[:, :, qsl], in1=uv, op=ALU.add), reads=[BU, BUD], writes=[BUD])

                        def run_blocks(specs):
                            prev = None
                            for sp_ in specs:
                                ctx = attn_front(*sp_)
                                if prev is not None:
                                    attn_back(prev)
                                prev = ctx
                            attn_back(prev)

                        run_blocks([(tcols(i), tcols(i), tcols(i - 1) if i > 0 else None, vnat[:, i, :], vnat[:, i - 1, :] if i > 0 else None, True)
                                    for i in range(16)])
                        T.ck(f"A4a_{l}_{c}")
                        for dil in (4, 16):
                            def tsl(blk):
                                if dil == 4:
                                    jb, r4 = blk // 4, blk % 4
                                    return slice(512 * jb + r4, 512 * (jb + 1), 4)
                                return slice(blk, 2048, 16)
                            for g4 in range(4):
                                k = 2 + g4 % 2
                                P, BP = PS[k], BPS[k]
                                for bi in range(4):
                                    blk = 4 * g4 + bi
                                    for kc in range(8):
                                        op(PE, lambda P=P, bi=bi, blk=blk, kc=kc: TE.matmul(P[:, tcols(bi)], lhsT=hT[:, kc, tsl(blk)], rhs=wt[:, kc, 256:384],
                                                                                           start=(kc == 0), stop=(kc == 7)), reads=BhT[0:16] + [Bw], writes=[BP])
                                op(ACT, lambda P=P, g4=g4: A.activation(out=vord[:, 4 * g4:4 * g4 + 4, :], in_=P[:].rearrange("p (a t) -> p a t", a=4), func=AF.Copy),
                                   reads=[BP], writes=[Bvo])
                            run_blocks([(tsl(blk), tsl(blk), tsl(blk - 4), vord[:, blk, :], vord[:, blk - 4, :], False) if (dil == 4 and blk >= 4)
                                        else (tsl(blk), tsl(blk), None, vord[:, blk, :], None, False) for blk in range(16)])
                        T.ck(f"A4b_{l}_{c}")
                        op(DVE, lambda: V.reciprocal(out=UD[:, 1, :], in_=UD[:, 1, :]), reads=[BUD], writes=[BUD])
                        op(DVE, lambda: V.tensor_tensor(out=UD[:, 0, :], in0=UD[:, 0, :], in1=UD[:, 1, :], op=ALU.mult), reads=[BUD], writes=[BUD])
                        op(POOL, lambda c=c: G.tensor_tensor(out=yT[:, c, 0:2048], in0=UD[:, 0, :], in1=zT[:, 0:2048], op=ALU.mult),
                           reads=[BUD, BzT], writes=[ByTp[c]])
                    T.barrier()

                T.ck(f"A4_{l}")
                with contextlib.ExitStack() as ar:
                    selb = sbt(ar, "s_selb", [128, 4, 128], BF16)
                    selbf = sbt(ar, "s_selbf", [128, 512], F32)
                    Kr = [sbt(ar, f"s_Kr{j}", [128, 768], BF16) for j in range(2)]
                    Vr = [sbt(ar, f"s_Vr{j}", [128, 3, 768], BF16) for j in range(2)]
                    prod = sbt(ar, "s_prod", [128, 768], F32)
                    sc = sbt(ar, "s_sc", [128, 12], F32)
                    pall = [sbt(ar, f"s_pall{j}", [128, 3, 12], BF16) for j in range(2)]
                    pnew = sbt(ar, "s_pnew", [128, 12], F32)
                    pnm = sbt(ar, "s_pnm", [128, 4, 12], BF16)
                    rd = sbt(ar, "s_rd", [128, 6], F32)
                    osb = sbt(ar, "s_osb", [128, 6], F32)
                    Bsel, Bprod, Bsc, Bpn, Bpnm, Brd, Bos = Buf("selb"), Buf("prod"), Buf("sc"), Buf("pnew"), Buf("pnm"), Buf("rd"), Buf("osb")
                    BKr = [Buf("Kr0"), Buf("Kr1")]
                    BVr = [Buf("Vr0"), Buf("Vr1")]
                    Bpa = [Buf("pa0"), Buf("pa1")]
                    op(POOL, lambda: G.memset(selbf[:], 1.0), writes=[Bsel])
                    op(POOL, lambda: G.affine_select(out=selbf[:].rearrange("p (b m) -> p b m", b=4), in_=selbf[:].rearrange("p (b m) -> p b m", b=4),
                                                     pattern=[[-32, 4], [0, 128]], compare_op=ALU.is_equal, fill=0.0, base=0, channel_multiplier=1),
                       reads=[Bsel], writes=[Bsel])
                    op(DVE, lambda: V.tensor_copy(out=selb[:].rearrange("p b m -> p (b m)"), in_=selbf[:]), reads=[Bsel], writes=[Bsel])
                    op(DVE, lambda: V.tensor_tensor(out=prod[:], in0=qs_tok[:], in1=ks_tok[:], op=ALU.mult), reads=[Bst], writes=[Bprod])
                    op(DVE, lambda: V.tensor_reduce(out=sc[:], in_=prod[:].rearrange("p (h e) -> p h e", e=64), axis=AX.X, op=ALU.add),
                       reads=[Bprod], writes=[Bsc])
                    op(ACT, lambda: A.activation(out=pnew[:], in_=sc[:], func=AF.Exp), reads=[Bsc], writes=[Bpn])
                    for b in range(4):
                        op(DVE, lambda b=b: V.tensor_scalar(out=pnm[:, b, :], in0=pnew[:], scalar1=rowmask_s3[:, b:b + 1], scalar2=None, op0=ALU.mult),
                           reads=[Bpn, Bc], writes=[Bpnm])
                    kctr = 0
                    for b in range(4):
                        bj = b % 2
                        for pi, (r0, step) in enumerate([(1920, 1), (1536, 4), (0, 16)]):
                            rows = slice(r0, 2048, step)
                            kj = kctr % 2
                            kctr += 1
                            dma(POOL, Kr[kj][:], ck[l, b, rows, :], writes=[BKr[kj]])
                            dma(POOL, Vr[bj][:, pi, :], cv[l, b, rows, :], writes=[BVr[bj]])
                            for hf in range(2):
                                op(PE, lambda b=b, hf=hf: TE.matmul(PS[hf][:, 0:384], lhsT=selb[:, b, :], rhs=qs_tok[:, hf * 384:(hf + 1) * 384],
                                                                  start=True, stop=True), reads=[Bsel, Bst], writes=[BPS[hf]])
                            for hf in range(2):
                                op(DVE, lambda kj=kj, hf=hf: V.tensor_tensor(out=prod[:, hf * 384:(hf + 1) * 384], in0=Kr[kj][:, hf * 384:(hf + 1) * 384],
                                                                           in1=PS[hf][:, 0:384], op=ALU.mult), reads=[BKr[kj], BPS[hf]], writes=[Bprod])
                            op(DVE, lambda: V.tensor_reduce(out=sc[:], in_=prod[:].rearrange("p (h e) -> p h e", e=64), axis=AX.X, op=ALU.add),
                               reads=[Bprod], writes=[Bsc])
                            op(ACT, lambda bj=bj, pi=pi: A.activation(out=pall[bj][:, pi, :], in_=sc[:], func=AF.Exp), reads=[Bsc], writes=[Bpa[bj]])
                        PO, BPO = PS[2 + bj], BPS[2 + bj]
                        for part in range(2):
                            for c in range(6):
                                o0 = part * 16 + 2 * c
                                for pi in range(3):
                                    lhsT = Vr[bj][:, pi, c * 128:(c + 1) * 128] if part == 0 else ones_bf[:]
                                    op(PE, lambda PO=PO, o0=o0, lhsT=lhsT, pi=pi, c=c: TE.matmul(PO[:, o0:o0 + 2], lhsT=lhsT, rhs=pall[bj][:, pi, 2 * c:2 * c + 2],
                                                                                              start=(pi == 0), stop=False), reads=[BVr[bj], Bpa[bj], Bc], writes=[BPO])
                                lhsT = vs_tok[:, c * 128:(c + 1) * 128] if part == 0 else ones_bf[:]
                                op(PE, lambda PO=PO, o0=o0, lhsT=lhsT, b=b, c=c: TE.matmul(PO[:, o0:o0 + 2], lhsT=lhsT, rhs=pnm[:, b, 2 * c:2 * c + 2],
                                                                                        start=False, stop=True), reads=[Bst, Bpnm, Bc], writes=[BPO])
                        for hh in range(2):
                            rws = slice(hh * 64, hh * 64 + 64)
                            op(DVE, lambda PO=PO, rws=rws, hh=hh: V.reciprocal(out=rd[rws, :], in_=PO[rws, 16 + hh:28:2]), reads=[BPO], writes=[Brd])
                            op(DVE, lambda PO=PO, rws=rws, hh=hh: V.tensor_tensor(out=osb[rws, :], in0=PO[rws, hh:12:2], in1=rd[rws, :], op=ALU.mult),
                               reads=[BPO, Brd], writes=[Bos])
                        op(DVE, lambda b=b: V.tensor_tensor(out=yT[:, :, 2048 + 32 * b:2048 + 32 * b + 1], in0=osb[:].unsqueeze(2),
                                                            in1=zsamp[:, :, b:b + 1], op=ALU.mult), reads=[Bos, Bst], writes=ByTs)
                    T.barrier()
                T.ck(f"A5_{l}")
                out_proj(l, 6)
                T.barrier()
                T.ck(f"A_{l}")

            with contextlib.ExitStack() as ar:
                vn = sbt(ar, "b_vn", [128, NT, 512], BF16)
                lng_t = sbt(ar, "b_lng", [128, 512], F32)
                lnb_t = sbt(ar, "b_lnb", [128, 512], F32)
                st6 = sbt(ar, "b_st6", [128, 6], F32)
                mv = sbt(ar, "b_mv", [128, 2], F32)
                lnv = sbt(ar, "b_lnv", [128, 1], F32)
                rsb = sbt(ar, "b_rsb", [128, 1], F32)
                vnf = sbt(ar, "b_vnf", [128, 512], F32)
                vnf2 = [sbt(ar, f"b_vnf2{j}", [128, 512], F32) for j in range(2)]
                wl = sbt(ar, "b_wl", [128, 8, 128], F32)
                wlb = sbt(ar, "b_wlb", [128, 8, 128], BF16)
                wT = sbt(ar, "b_wT", [128, 8, 128], BF16)
                wTs = sbt(ar, "b_wTs", [128, 8, 128], BF16)
                w00 = sbt(ar, "b_w00", [128, 8], F32)
                bsf = sbt(ar, "b_bsf", [8, 128], F32)
                bsb = sbt(ar, "b_bsb", [8, 128], BF16)
                bs0 = sbt(ar, "b_bs0", [8, 128], BF16)
                ezb = sbt(ar, "b_ez", [128, 512], F32)
                t1 = sbt(ar, "b_t1", [128, 512], F32)
                Bvn_, Blg, Bst6, Bmv, Blnv, Brsb, Bvnf = Buf("vn"), Buf("lng"), Buf("st6"), Buf("mv"), Buf("lnv"), Buf("rsb"), Buf("vnf")
                Bvnf2 = [Buf("vnf20"), Buf("vnf21")]
                Bwl, Bwlb, BwT, Bbs, Bezb, Bt1 = Buf("wl"), Buf("wlb"), Buf("wT"), Buf("bs"), Buf("ezb"), Buf("t1")
                load_wo(l, 768, 4)
                bsel = sbt(ar, "b_bsel", [8, 512], BF16)
                bself = sbt(ar, "b_bself", [8, 512], F32)
                Bbsl = Buf("bsel")
                op(POOL, lambda: G.memset(bself[:], 1.0), writes=[Bbsl])
                op(POOL, lambda: G.affine_select(out=bself[:], in_=bself[:], pattern=[[1, 512]], compare_op=ALU.is_ge,
                                                 fill=0.0, base=0, channel_multiplier=-64), reads=[Bbsl], writes=[Bbsl])
                op(POOL, lambda: G.affine_select(out=bself[:], in_=bself[:], pattern=[[-1, 512]], compare_op=ALU.is_ge,
                                                 fill=0.0, base=63, channel_multiplier=64), reads=[Bbsl], writes=[Bbsl])
                op(DVE, lambda: V.tensor_copy(out=bsel[:], in_=bself[:]), reads=[Bbsl], writes=[Bbsl])
                dma(SP, lng_t[:], lng[l].partition_broadcast(128), writes=[Blg])
                dma(SP, lnb_t[:], lnb[l].partition_broadcast(128), writes=[Blg])
                dma(SP, wl[:], sgw[l].rearrange("g t s -> t g s"), writes=[Bwl])
                dma(SP, bsf[:], sgb[l], writes=[Bbs])
                u = uidx
                uidx += 1
                load_unit(u)
                load_unit(u + 1)
                wt, Bw = WA[u % 2], BW[u % 2]
                for i in range(NT):
                    j = i % 2
                    P, BP = PS[j], BPS[j]
                    for kc in range(8):
                        op(PE, lambda P=P, kc=kc, i=i: TE.matmul(P[:], lhsT=hT[:, kc, tcols(i)], rhs=wt[:, kc, :], start=(kc == 0), stop=(kc == 7)),
                           reads=[BhT[i], Bw], writes=[BP])
                    op(DVE, lambda P=P: V.bn_stats(out=st6[:], in_=P[:]), reads=[BP], writes=[Bst6])
                    op(DVE, lambda: V.bn_aggr(out=mv[:], in_=st6[:]), reads=[Bst6], writes=[Bmv])
                    op(ACT, lambda: A.activation(out=lnv[:], in_=mv[:, 1:2], func=AF.Ln, scale=1.0, bias=epsc[:, 0:1]), reads=[Bmv, Bc], writes=[Blnv])
                    op(ACT, lambda: A.activation(out=rsb[:], in_=lnv[:], func=AF.Exp, scale=-0.5), reads=[Blnv], writes=[Brsb])
                    op(DVE, lambda P=P: V.tensor_scalar(out=vnf[:], in0=P[:], scalar1=mv[:, 0:1], scalar2=rsb[:, 0:1], op0=ALU.subtract, op1=ALU.mult),
                       reads=[BP, Bmv, Brsb], writes=[Bvnf])
                    op(POOL, lambda: G.tensor_tensor(out=vnf[:], in0=vnf[:], in1=lng_t[:], op=ALU.mult), reads=[Bvnf, Blg], writes=[Bvnf])
                    op(POOL, lambda j=j: G.tensor_tensor(out=vnf2[j][:], in0=vnf[:], in1=lnb_t[:], op=ALU.add), reads=[Bvnf, Blg], writes=[Bvnf2[j]])
                    op(ACT, lambda i=i, j=j: A.activation(out=vn[:, i, :], in_=vnf2[j][:], func=AF.Copy), reads=[Bvnf2[j]], writes=[Bvn_])
                    if i == 16:
                        for b in range(4):
                            dma(SP, osg[l, b:b + 1, :], vnf2[j][32 * b:32 * b + 1, :], reads=[Bvnf2[j]])
                T.ck(f"B1_{l}")
                op(DVE, lambda: V.tensor_copy(out=wlb[:], in_=wl[:]), reads=[Bwl], writes=[Bwlb])
                for g in range(8):
                    op(PE, lambda g=g: TE.transpose(PB[0][:, tcols(g)], wlb[:, g, :], ident_bf[:]), reads=[Bwlb, Bc], writes=[BPB[0]])
                op(DVE, lambda: V.tensor_tensor(out=wT[:], in0=PB[0][:].rearrange("p (g t) -> p g t", g=8),
                                                in1=maskcur[:].unsqueeze(1).to_broadcast([128, 8, 128]), op=ALU.mult), reads=[BPB[0], Bc], writes=[BwT])
                op(PE, lambda: TE.matmul(PS[2][:, 0:8], lhsT=ones_f[0:1, :], rhs=wl[0:1, :, 0], start=True, stop=True), reads=[Bwl, Bc], writes=[BPS[2]])
                op(DVE, lambda: V.tensor_copy(out=w00[:], in_=PS[2][:, 0:8]), reads=[BPS[2]], writes=[BwT])
                op(DVE, lambda: V.tensor_tensor(out=wTs[:], in0=ident_bf[:].unsqueeze(1).to_broadcast([128, 8, 128]),
                                                in1=w00[:].unsqueeze(2).to_broadcast([128, 8, 128]), op=ALU.mult), reads=[BwT, Bc], writes=[BwT])
                op(DVE, lambda: V.tensor_copy(out=bsb[:], in_=bsf[:]), reads=[Bbs], writes=[Bbs])
                op(DVE, lambda: V.tensor_copy(out=bs0[:], in_=bsf[:, 0:1].to_broadcast([8, 128])), reads=[Bbs], writes=[Bbs])
                T.ck(f"B2_{l}")
                for cb in range(4):
                    u = uidx
                    uidx += 1
                    load_unit(u)
                    load_unit(u + 1)
                    wt, Bw = WA[u % 2], BW[u % 2]
                    for tg in range(5):
                        n = 512 if tg < 4 else 128
                        cols = slice(tg * 512, tg * 512 + n)
                        hb_ = BhT[4 * tg:4 * tg + 4] if tg < 4 else [BhT[16]]
                        Pu, BPu = PS[0 + tg % 2], BPS[0 + tg % 2]
                        Pz, BPz = PS[2 + tg % 2], BPS[2 + tg % 2]
                        Pm, BPm = PS[4 + tg % 2], BPS[4 + tg % 2]
                        for kc in range(8):
                            op(PE, lambda Pu=Pu, kc=kc, cols=cols, n=n: TE.matmul(Pu[:, 0:n], lhsT=wt[:, kc, 0:128], rhs=hT[:, kc, cols],
                                                                                start=(kc == 0), stop=(kc == 7)), reads=hb_ + [Bw], writes=[BPu])
                        for kc in range(8):
                            op(PE, lambda Pz=Pz, kc=kc, cols=cols, n=n: TE.matmul(Pz[:, 0:n], lhsT=wt[:, kc, 128:256], rhs=hT[:, kc, cols],
                                                                                start=(kc == 0), stop=(kc == 7)), reads=hb_ + [Bw], writes=[BPz])
                        for ti in range(n // 128):
                            i = 4 * tg + ti
                            for gg in range(2):
                                g = 2 * cb + gg
                                rws = slice(gg * 64, gg * 64 + 64)
                                wmat = wT if i < 16 else wTs
                                bmat = bsb if i < 16 else bs0
                                op(PE, lambda Pm=Pm, rws=rws, ti=ti, i=i, g=g, wmat=wmat: TE.matmul(
                                    Pm[rws, tcols(ti)], lhsT=vn[:, i, g * 64:(g + 1) * 64], rhs=wmat[:, g, :], start=True, stop=False),
                                   reads=[Bvn_, BwT], writes=[BPm])
                                op(PE, lambda Pm=Pm, rws=rws, ti=ti, g=g, bmat=bmat: TE.matmul(
                                    Pm[rws, tcols(ti)], lhsT=bsel[0:8, g * 64:(g + 1) * 64], rhs=bmat[0:8, :], start=False, stop=True),
                                   reads=[Bbs, Bbsl], writes=[BPm])
                        silu_from_psum(Pz, n, t1[:, 0:n], ezb, Bezb, BPz, [Bt1])
                        op(DVE, lambda Pu=Pu, n=n: V.tensor_tensor(out=t1[:, 0:n], in0=t1[:, 0:n], in1=Pu[:, 0:n], op=ALU.mult), reads=[Bt1, BPu], writes=[Bt1])
                        op(DVE, lambda Pm=Pm, n=n, cols=cols, cb=cb: V.tensor_tensor(out=yT[:, cb, cols], in0=t1[:, 0:n], in1=Pm[:, 0:n], op=ALU.mult),
                           reads=[Bt1, BPm], writes=[ByTp[cb] if tg < 4 else ByTs[cb]])
                T.barrier()
                T.ck(f"B3_{l}")
                out_proj(l, 4)
                T.barrier()
                T.ck(f"B_{l}")

            with contextlib.ExitStack() as ar:
                scanmask = sbt(ar, "c_scanm", [128, 512], F32)
                names = ["ef", "f", "g", "G", "eG", "kk", "kt", "khT", "eq", "q", "qt", "ez", "zs", "ln"]
                alias = {"eNG": "g", "rs": "ln", "sq": "eq", "o": "ez"}
                t = {nm: sbt(ar, "c_" + nm, [128, 512], F32) for nm in names}
                Bt_ = {nm: Buf("c_" + nm) for nm in names}
                for a_, b_ in alias.items():
                    t[a_] = t[b_]
                    Bt_[a_] = Bt_[b_]
                tv = sbt(ar, "c_tv", [128, 4, 128], F32)
                khtok = sbt(ar, "c_khtok", [128, 4, 4, 128], F32)
                ATm = sbt(ar, "c_ATm", [128, 4, 128], F32)
                Sring = sbt(ar, "c_Sring", [128, 9, 128], F32)
                BSr = [Buf(f"Sr{j}") for j in range(9)]
                S0 = [Sring[:, 1, :], Sring[:, 2, :]]
                Sn = [Sring[:, 3, :], Sring[:, 4, :]]
                BS0 = [BSr[1], BSr[2]]
                BSn = [BSr[3], BSr[4]]
                Btv, Bkh, BAT, Bsm = Buf("tv"), Buf("khtok"), Buf("ATm"), Buf("scanm")
                load_wo(l, 1280, 6)
                op(DVE, lambda: V.memset(scanmask[:], 1.0), writes=[Bsm])
                op(DVE, lambda: V.memset(scanmask[:].rearrange("p (c j) -> p c j", j=32)[:, :, 0:1], 0.0), reads=[Bsm], writes=[Bsm])
                for hd in range(6):
                    u = uidx
                    uidx += 1
                    load_unit(u)
                    load_unit(u + 1)
                    wt, Bw = WA[u % 2], BW[u % 2]
                    lbc = lbT[:, l * 6 + hd:l * 6 + hd + 1]
                    omc = omlbT[:, l * 6 + hd:l * 6 + hd + 1]
                    hgc = hgT[:, l * 6 + hd:l * 6 + hd + 1]
                    op(DVE, lambda: V.memset(Sring[:, 0, :], 0.0), writes=[BSr[0]])
                    for tg in range(5):
                        n = 512 if tg < 4 else 128
                        nti = n // 128
                        cols = slice(tg * 512, tg * 512 + n)
                        hb_ = BhT[4 * tg:4 * tg + 4] if tg < 4 else [BhT[16]]
                        for pi_, (P, BP, w0) in enumerate([(PS[0], BPS[0], 0), (PS[1], BPS[1], 128), (PS[2], BPS[2], 384)]):
                            for kc in range(8):
                                op(PE, lambda P=P, kc=kc, w0=w0, cols=cols, n=n: TE.matmul(P[:, 0:n], lhsT=wt[:, kc, w0:w0 + 128], rhs=hT[:, kc, cols],
                                                                                         start=(kc == 0), stop=(kc == 7)), reads=hb_ + [Bw], writes=[BP])
                        for ti in range(nti):
                            i = 4 * tg + ti
                            for kc in range(8):
                                op(PE, lambda ti=ti, i=i, kc=kc: TE.matmul(PS[3][:, tcols(ti)], lhsT=hT[:, kc, tcols(i)], rhs=wt[:, kc, 256:384],
                                                                         start=(kc == 0), stop=(kc == 7)), reads=[BhT[i], Bw], writes=[BPS[3]])
                        sl = slice(0, n)
                        op(ACT, lambda: A.activation(out=t["ef"][:, sl], in_=PS[1][:, sl], func=AF.Exp, scale=-1.0), reads=[BPS[1]], writes=[Bt_["ef"]])
                        op(DVE, lambda: V.tensor_scalar(out=t["ef"][:, sl], in0=t["ef"][:, sl], scalar1=1.0, scalar2=None, op0=ALU.add), reads=[Bt_["ef"]], writes=[Bt_["ef"]])
                        op(DVE, lambda: V.reciprocal(out=t["ef"][:, sl], in_=t["ef"][:, sl]), reads=[Bt_["ef"]], writes=[Bt_["ef"]])
                        op(DVE, lambda: V.tensor_scalar(out=t["f"][:, sl], in0=t["ef"][:, sl], scalar1=omc, scalar2=lbc, op0=ALU.mult, op1=ALU.add),
                           reads=[Bt_["ef"], Bc], writes=[Bt_["f"]])
                        op(POOL, lambda: G.tensor_scalar(out=t["kk"][:, sl], in0=t["f"][:, sl], scalar1=-1.0, scalar2=1.0, op0=ALU.mult, op1=ALU.add),
                           reads=[Bt_["f"]], writes=[Bt_["kk"]])
                        op(ACT, lambda: A.activation(out=t["eq"][:, sl], in_=PS[0][:, sl], func=AF.Exp, scale=-1.0), reads=[BPS[0]], writes=[Bt_["eq"]])
                        op(DVE, lambda: V.tensor_scalar(out=t["eq"][:, sl], in0=t["eq"][:, sl], scalar1=1.0, scalar2=None, op0=ALU.add), reads=[Bt_["eq"]], writes=[Bt_["eq"]])
                        op(DVE, lambda: V.reciprocal(out=t["eq"][:, sl], in_=t["eq"][:, sl]), reads=[Bt_["eq"]], writes=[Bt_["eq"]])
                        op(DVE, lambda: V.tensor_tensor(out=t["q"][:, sl], in0=PS[0][:, sl], in1=t["eq"][:, sl], op=ALU.mult), reads=[BPS[0], Bt_["eq"]], writes=[Bt_["q"]])
                        silu_from_psum(PS[2], n, t["zs"][:, sl], t["ez"], Bt_["ez"], BPS[2], [Bt_["zs"]])
                        op(ACT, lambda: A.activation(out=tv[:, 0:nti, :], in_=PS[3][:, sl].rearrange("p (a v) -> p a v", v=128), func=AF.Copy),
                           reads=[BPS[3]], writes=[Btv])
                        if tg < 4:
                            op(ACT, lambda: A.activation(out=t["g"][:], in_=t["f"][:], func=AF.Ln), reads=[Bt_["f"]], writes=[Bt_["g"]])
                            op(DVE, lambda: V.tensor_tensor_scan(out=t["G"][:], data0=scanmask[:], data1=t["g"][:], initial=0.0, op0=ALU.mult, op1=ALU.add),
                               reads=[Bt_["g"], Bsm], writes=[Bt_["G"]])
                            op(ACT, lambda: A.activation(out=t["eG"][:], in_=t["G"][:], func=AF.Exp), reads=[Bt_["G"]], writes=[Bt_["eG"]])
                            op(ACT, lambda: A.activation(out=t["eNG"][:], in_=t["G"][:], func=AF.Exp, scale=-1.0), reads=[Bt_["G"]], writes=[Bt_["eNG"]])
                            op(POOL, lambda: G.tensor_tensor(out=t["kt"][:], in0=t["kk"][:], in1=t["eNG"][:], op=ALU.mult), reads=[Bt_["kk"], Bt_["eNG"]], writes=[Bt_["kt"]])
                            op(POOL, lambda: G.tensor_tensor(out=t["khT"][:].rearrange("p (c j) -> p c j", j=32), in0=t["kt"][:].rearrange("p (c j) -> p c j", j=32),
                                                             in1=t["eG"][:, 31:512:32].unsqueeze(2).to_broadcast([128, 16, 32]), op=ALU.mult),
                               reads=[Bt_["kt"], Bt_["eG"]], writes=[Bt_["khT"]])
                            op(POOL, lambda: G.tensor_tensor(out=t["qt"][:], in0=t["q"][:], in1=t["eG"][:], op=ALU.mult), reads=[Bt_["q"], Bt_["eG"]], writes=[Bt_["qt"]])
                            T.ck(f"C1_{l}_{hd}_{tg}")
                            for ti in range(4):
                                op(PE, lambda ti=ti: TE.transpose(PS[0][:, tcols(ti)], t["khT"][:, tcols(ti)], ident_f[:]), reads=[Bt_["khT"], Bc], writes=[BPS[0]])
                            for ch in range(4):
                                if ch % 2 == 0:
                                    op(ACT, lambda ch=ch: A.activation(out=khtok[:, :, ch, :], in_=PS[0][:].rearrange("p (a k) -> p a k", a=4), func=AF.Copy,
                                                                       scale=rowmask[:, ch:ch + 1]), reads=[BPS[0], Bc], writes=[Bkh])
                                else:
                                    op(DVE, lambda ch=ch: V.tensor_scalar(out=khtok[:, :, ch, :], in0=PS[0][:].rearrange("p (a k) -> p a k", a=4),
                                                                          scalar1=rowmask[:, ch:ch + 1], scalar2=None, op0=ALU.mult), reads=[BPS[0], Bc], writes=[Bkh])
                            for ti in range(4):
                                op(PE, lambda ti=ti: TE.matmul(PS[1][:, tcols(ti)], lhsT=t["kt"][:, tcols(ti)], rhs=t["qt"][:, tcols(ti)], start=True, stop=True),
                                   reads=[Bt_["kt"], Bt_["qt"]], writes=[BPS[1]])
                            op(DVE, lambda: V.tensor_tensor(out=ATm[:], in0=PS[1][:].rearrange("p (a t) -> p a t", a=4),
                                                            in1=blockmask[:].unsqueeze(1).to_broadcast([128, 4, 128]), op=ALU.mult), reads=[BPS[1], Bc], writes=[BAT])
                            T.ck(f"C2_{l}_{hd}_{tg}")
                            banks_ = [0, 1, 3, 4]
                            for nchk in range(16):
                                ti, ch = nchk // 4, nchk % 4
                                bk, col = banks_[nchk // 4], (nchk % 4) * 128
                                op(PE, lambda bk=bk, col=col, ti=ti, ch=ch: TE.matmul(PS[bk][:, col:col + 128], lhsT=khtok[:, ti, ch, :], rhs=tv[:, ti, :], start=True, stop=True),
                                   reads=[Bkh, Btv], writes=[BPS[bk]], inc=(nchk % 4 == 3))
                            for half in range(2):
                                for r_ in range(8):
                                    nchk = half * 8 + r_
                                    bk, col = banks_[nchk // 4], (nchk % 4) * 128
                                    op(DVE, lambda bk=bk, col=col, nchk=nchk, r_=r_: V.scalar_tensor_tensor(
                                        out=Sring[:, r_ + 1, :], in0=Sring[:, r_, :], scalar=t["eG"][:, nchk * 32 + 31:nchk * 32 + 32], in1=PS[bk][:, col:col + 128],
                                        op0=ALU.mult, op1=ALU.add), reads=[BSr[r_], Bt_["eG"], BPS[bk]], writes=[BSr[r_ + 1]])
                                for r_ in range(8):
                                    nchk = half * 8 + r_
                                    ti, ch = nchk // 4, nchk % 4
                                    cc = slice(nchk * 32, nchk * 32 + 32)
                                    op(PE, lambda ti=ti, ch=ch, cc=cc: TE.matmul(PS[2][:, cc], lhsT=tv[:, ti, :], rhs=ATm[:, ti, ch * 32:(ch + 1) * 32], start=True, stop=False),
                                       reads=[Btv, BAT], writes=[BPS[2]])
                                    op(PE, lambda r_=r_, cc=cc: TE.matmul(PS[2][:, cc], lhsT=Sring[:, r_, :], rhs=t["qt"][:, cc], start=False, stop=True),
                                       reads=[BSr[r_], Bt_["qt"]], writes=[BPS[2]], inc=(r_ == 7))
                                op(DVE, lambda: V.tensor_copy(out=Sring[:, 0, :], in_=Sring[:, 8, :]), reads=[BSr[8]], writes=[BSr[0]])
                            no = 512
                        else:
                            op(PE, lambda: TE.transpose(PS[0][:, 0:128], t["kk"][:, 0:128], ident_f[:]), reads=[Bt_["kk"], Bc], writes=[BPS[0]])
                            for b in range(4):
                                op(DVE, lambda b=b: V.tensor_scalar(out=khtok[:, 0, b, :], in0=PS[0][:, 0:128], scalar1=rowmask_s[:, b:b + 1], scalar2=None, op0=ALU.mult),
                                   reads=[BPS[0], Bc], writes=[Bkh])
                            for b in range(4):
                                bj = b % 2
                                dma(SP, S0[bj], st[l, b, hd], writes=[BS0[bj]])
                                ku = 3 + bj
                                op(PE, lambda ku=ku, b=b: TE.matmul(PS[ku][:, 0:128], lhsT=khtok[:, 0, b, :], rhs=tv[:, 0, :], start=True, stop=True),
                                   reads=[Bkh, Btv], writes=[BPS[ku]])
                                op(DVE, lambda bj=bj, ku=ku, b=b: V.scalar_tensor_tensor(out=Sn[bj], in0=S0[bj], scalar=t["f"][:, 32 * b:32 * b + 1],
                                                                                       in1=PS[ku][:, 0:128], op0=ALU.mult, op1=ALU.add),
                                   reads=[BS0[bj], Bt_["f"], BPS[ku]], writes=[BSn[bj]])
                                dma(SP, ohs[l, b, hd], Sn[bj], reads=[BSn[bj]])
                                op(PE, lambda bj=bj, b=b: TE.matmul(PS[2][:, b:b + 1], lhsT=Sn[bj], rhs=t["q"][:, 32 * b:32 * b + 1], start=True, stop=True),
                                   reads=[BSn[bj], Bt_["q"]], writes=[BPS[2]])
                            no = 4
                        T.ck(f"C3_{l}_{hd}_{tg}")
                        so = slice(0, no)
                        op(ACT, lambda: A.activation(out=t["o"][:, so], in_=PS[2][:, so], func=AF.Copy), reads=[BPS[2]], writes=[Bt_["o"]])
                        op(ACT, lambda: A.activation(out=t["sq"][:, so], in_=PS[2][:, so], func=AF.Square), reads=[BPS[2]], writes=[Bt_["sq"]])
                        op(PE, lambda: TE.matmul(PS[5][:, so], lhsT=ones_f[:], rhs=t["sq"][:, so], start=True, stop=True), reads=[Bt_["sq"], Bc], writes=[BPS[5]])
                        op(ACT, lambda: A.activation(out=t["ln"][:, so], in_=PS[5][:, so], func=AF.Ln, scale=1.0 / 128, bias=epsc[:, 0:1]), reads=[BPS[5], Bc], writes=[Bt_["ln"]])
                        op(ACT, lambda: A.activation(out=t["rs"][:, so], in_=t["ln"][:, so], func=AF.Exp, scale=-0.5), reads=[Bt_["ln"]], writes=[Bt_["rs"]])
                        op(DVE, lambda: V.tensor_tensor(out=t["o"][:, so], in0=t["o"][:, so], in1=t["rs"][:, so], op=ALU.mult), reads=[Bt_["o"], Bt_["rs"]], writes=[Bt_["o"]])
                        if tg < 4:
                            op(DVE, lambda cols=cols, hd=hd: V.scalar_tensor_tensor(out=yT[:, hd, cols], in0=t["o"][:], scalar=hgc, in1=t["zs"][:], op0=ALU.mult, op1=ALU.mult),
                               reads=[Bt_["o"], Bt_["zs"], Bc], writes=[ByTp[hd]])
                        else:
                            op(DVE, lambda hd=hd: V.scalar_tensor_tensor(out=yT[:, hd, 2048:TOK:32], in0=t["o"][:, 0:4], scalar=hgc, in1=t["zs"][:, 0:128:32],
                                                                         op0=ALU.mult, op1=ALU.mult), reads=[Bt_["o"], Bt_["zs"], Bc], writes=[ByTs[hd]])
                        T.ck(f"C4_{l}_{hd}_{tg}")
                        if tg == 3:
                            dma(SP, ohp[l, hd], Sring[:, 0, :], reads=[BSr[0]])
                T.barrier()
                T.ck(f"C5_{l}")
                out_proj(l, 6)
                T.barrier()
                T.ck(f"L_{l}")

        T.force = True
        for i in range(16):
            dma(SP, yp[i * 128:(i + 1) * 128, :], xres[:, i, :], reads=[Bx[i]])
        for b in range(4):
            dma(SP, ys[b:b + 1, :], xres[32 * b:32 * b + 1, 16, :], reads=[Bx[16]])
        for Q in (SP, POOL):
            for s in Q.slots:
                if s.val > 0:
                    T._wait(SP, s.sem, s.val)
    return nc


_NC_CACHE = {}


def kernel(x_prompt, x_sample, cache_k, cache_v, state_hgrn, norm_g, w_in, q_norm_g, k_norm_g,
           sgu_ln_g, sgu_ln_b, sgu_w, sgu_b, hgrn_lb_logits, hgrn_norm_g, w_out):
    f = lambda a: np.ascontiguousarray(np.asarray(a, dtype=np.float32))
    x_prompt, x_sample, cache_k, cache_v, state_hgrn = map(f, (x_prompt, x_sample, cache_k, cache_v, state_hgrn))
    shared = {
        "norm_g": f(norm_g), "w_in": f(w_in), "q_norm_g": f(q_norm_g), "k_norm_g": f(k_norm_g),
        "sgu_ln_g": f(sgu_ln_g), "sgu_ln_b": f(sgu_ln_b), "sgu_w": f(sgu_w), "sgu_b": f(sgu_b),
        "hgrn_lb_logits": f(hgrn_lb_logits), "hgrn_norm_g": f(hgrn_norm_g), "w_out": f(w_out),
    }
    in_maps = []
    for c in range(NCORES):
        sb = slice(4 * c, 4 * c + 4)
        m = dict(shared)
        m["xp"] = np.ascontiguousarray(x_prompt[c])
        m["xs"] = np.ascontiguousarray(x_sample[sb, 0, :])
        m["ck"] = np.ascontiguousarray(cache_k[:, sb].reshape(2, 4, 2048, 768))
        m["cv"] = np.ascontiguousarray(cache_v[:, sb].reshape(2, 4, 2048, 768))
        m["st"] = np.ascontiguousarray(state_hgrn[:, sb])
        in_maps.append(m)
    if "nc" not in _NC_CACHE:
        _NC_CACHE["nc"] = build_nc()
    res = run_bass_kernel_spmd(_NC_CACHE["nc"], in_maps, core_ids=list(range(NCORES)))
    R = res.results
    y_prompt = np.stack([R[c]["yp"] for c in range(NCORES)]).astype(np.float32)
    y_sample = np.concatenate([R[c]["ys"] for c in range(NCORES)])[:, None, :].astype(np.float32)
    nkp = np.stack([R[c]["okp"] for c in range(NCORES)], axis=1).reshape(2, 8, 2048, 12, 64).astype(np.float32)
    nvp = np.stack([R[c]["ovp"] for c in range(NCORES)], axis=1).reshape(2, 8, 2048, 12, 64).astype(np.float32)
    nks = np.concatenate([R[c]["oks"] for c in range(NCORES)], axis=1).reshape(2, 32, 1, 12, 64).astype(np.float32)
    nvs = np.concatenate([R[c]["ovs"] for c in range(NCORES)], axis=1).reshape(2, 32, 1, 12, 64).astype(np.float32)
    nsg = np.concatenate([R[c]["osg"] for c in range(NCORES)], axis=1).reshape(2, 32, 1, 512).astype(np.float32)
    nhp = np.stack([R[c]["ohp"] for c in range(NCORES)], axis=1).astype(np.float32)
    nhs = np.concatenate([R[c]["ohs"] for c in range(NCORES)], axis=1).astype(np.float32)
    return (y_prompt, y_sample, nkp, nvp, nks, nvs, nsg, nhp, nhs)
```

```python
import bisect
import contextlib
import os

import numpy as np

import concourse.bass as bass
import concourse.mybir as mybir
from concourse.bass_utils import run_bass_kernel_spmd

F32 = mybir.dt.float32
BF16 = mybir.dt.bfloat16
AF = mybir.ActivationFunctionType
ALU = mybir.AluOpType
AX = mybir.AxisListType

NCORES = 8
NT = 17
TOK = NT * 128
EPS = 1e-6


class Eng:
    def __init__(self, name, eng, sem):
        self.name, self.eng, self.sem = name, eng, sem
        self.nseq = 0
        self.cnt = 0
        self.last = None
        self.inc_seq = []
        self.waited = {}
        self.slots = []
        self.rr = 0


class Slot:
    def __init__(self, sem):
        self.sem = sem
        self.val = 0


class Buf:
    __slots__ = ("name", "w", "r")

    def __init__(self, name):
        self.name = name
        self.w = None
        self.r = {}


class Tracker:
    def __init__(self):
        self.engs = []
        self.stopped = False
        self.force = False
        self.stop_at = os.environ.get("KSTOP", "")

    def ck(self, name):
        if self.stop_at and name == self.stop_at:
            self.stopped = True

    def resolve(self, ev):
        if ev[0] == "s":
            return ev[1], ev[2]
        E, seq = ev[1], ev[2]
        i = bisect.bisect_left(E.inc_seq, seq)
        if i < len(E.inc_seq):
            return E.sem, i + 1
        E.last.then_inc(E.sem, 1)
        E.cnt += 1
        E.inc_seq.append(E.nseq)
        return E.sem, E.cnt

    def _wait(self, E, sem, val):
        if E.waited.get(sem.num, 0) < val:
            E.eng.wait_ge(sem, val)
            E.waited[sem.num] = val

    def deps(self, E, reads, writes):
        evs = []
        for b in reads:
            if b.w is not None:
                evs.append(b.w)
        for b in writes:
            if b.w is not None:
                evs.append(b.w)
            for k, ev in b.r.items():
                if ev[0] == "e" and ev[1] is E:
                    continue
                evs.append(ev)
        need = {}
        for ev in evs:
            if ev[0] == "e" and ev[1] is E and E.name == "pe":
                continue
            sem, val = self.resolve(ev)
            if need.get(sem.num, (None, 0))[1] < val:
                need[sem.num] = (sem, val)
        for num, (sem, val) in need.items():
            self._wait(E, sem, val)

    def op(self, E, fn, reads=(), writes=(), inc=None):
        if self.stopped and not self.force:
            return None
        self.deps(E, reads, writes)
        ins = fn()
        E.nseq += 1
        E.last = ins
        if inc and os.environ.get("KINC", "0") == "1":
            ins.then_inc(E.sem, 1)
            E.cnt += 1
            E.inc_seq.append(E.nseq)
        ev = ("e", E, E.nseq)
        for b in writes:
            b.w = ev
            b.r = {}
        for b in reads:
            b.r[E.name] = ev
        return ins

    def dma(self, Q, out, in_, reads=(), writes=()):
        if self.stopped and not self.force:
            return
        self.deps(Q, reads, writes)
        slot = Q.slots[Q.rr % len(Q.slots)]
        Q.rr += 1
        if slot.val > 0:
            self._wait(Q, slot.sem, slot.val)
        Q.eng.dma_start(out=out, in_=in_).then_inc(slot.sem, 16)
        slot.val += 16
        ev = ("s", slot.sem, slot.val)
        for b in writes:
            b.w = ev
            b.r = {}
        for b in reads:
            b.r[("d", slot.sem.num)] = ev

    def barrier(self):
        if self.stopped and not self.force:
            return
        pts = []
        for F in self.engs:
            if F.nseq > 0:
                pts.append(self.resolve(("e", F, F.nseq)))
            for s in F.slots:
                if s.val > 0:
                    pts.append((s.sem, s.val))
        for E in self.engs:
            for sem, val in pts:
                if sem is E.sem and E.name == "pe":
                    continue
                self._wait(E, sem, val)


def build_nc():
    nc = bass.Bass("TRN2", target_bir_lowering=False)

    def din(name, shape):
        return nc.dram_tensor(name, shape, F32, kind="ExternalInput").ap()

    def dout(name, shape):
        return nc.dram_tensor(name, shape, F32, kind="ExternalOutput").ap()

    xp = din("xp", [2048, 1024])
    xs = din("xs", [4, 1024])
    ck = din("ck", [2, 4, 2048, 768])
    cv = din("cv", [2, 4, 2048, 768])
    st = din("st", [2, 4, 6, 128, 128])
    norm_g = din("norm_g", [2, 1024])
    w_in = din("w_in", [2, 1024, 7680])
    qng = din("q_norm_g", [2, 768])
    kng = din("k_norm_g", [2, 768])
    lng = din("sgu_ln_g", [2, 512])
    lnb = din("sgu_ln_b", [2, 512])
    sgw = din("sgu_w", [2, 8, 128, 128])
    sgb = din("sgu_b", [2, 8, 128])
    lbl = din("hgrn_lb_logits", [2, 768])
    hng = din("hgrn_norm_g", [2, 768])
    w_out = din("w_out", [2, 2048, 1024])
    yp = dout("yp", [2048, 1024])
    ys = dout("ys", [4, 1024])
    okp = dout("okp", [2, 2048, 768])
    ovp = dout("ovp", [2, 2048, 768])
    oks = dout("oks", [2, 4, 768])
    ovs = dout("ovs", [2, 4, 768])
    osg = dout("osg", [2, 4, 512])
    ohp = dout("ohp", [2, 6, 128, 128])
    ohs = dout("ohs", [2, 4, 6, 128, 128])

    T = Tracker()
    es = contextlib.ExitStack()
    with es:
        nmctr = [0]

        def sbt(stack, name, shape, dt):
            nmctr[0] += 1
            return stack.enter_context(nc.sbuf_tensor(f"{name}_{nmctr[0]}", shape, dt))

        sems = [es.enter_context(nc.semaphore(f"sem{i}")) for i in range(20)]
        PE = Eng("pe", nc.tensor, sems[0])
        ACT = Eng("act", nc.scalar, sems[1])
        DVE = Eng("dve", nc.vector, sems[2])
        POOL = Eng("pool", nc.gpsimd, sems[3])
        SP = Eng("sp", nc.sync, None)
        SP.slots = [Slot(s) for s in sems[4:12]]
        POOL.slots = [Slot(s) for s in sems[12:20]]
        T.engs = [PE, ACT, DVE, POOL, SP]
        op, dma = T.op, T.dma
        V, A, G, TE = nc.vector, nc.scalar, nc.gpsimd, nc.tensor

        PS = [es.enter_context(nc.psum_tensor(f"ps{i}", [128, 512], F32)) for i in range(6)]
        PB = [es.enter_context(nc.psum_tensor(f"pb{i}", [128, 1024], BF16)) for i in range(2)]
        BPS = [Buf(f"ps{i}") for i in range(6)]
        BPB = [Buf(f"pb{i}") for i in range(2)]

        xres = sbt(es, "xres", [128, NT, 1024], F32)
        hT = sbt(es, "hT", [128, 8, TOK], BF16)
        yT = sbt(es, "yT", [128, 6, TOK], BF16)
        WA = [sbt(es, f"wA{i}", [128, 8, 512], BF16) for i in range(2)]
        wo = sbt(es, "wo", [128, 6, 1024], BF16)
        ident_bf = sbt(es, "ident_bf", [128, 128], BF16)
        ident_f = sbt(es, "ident_f", [128, 128], F32)
        ones_bf = sbt(es, "ones_bf", [128, 128], BF16)
        ones_f = sbt(es, "ones_f", [128, 128], F32)
        mask01 = sbt(es, "mask01", [128, 512], BF16)
        maskcur = sbt(es, "maskcur", [128, 128], BF16)
        blockmask = sbt(es, "blockmask", [128, 128], F32)
        rowmask = sbt(es, "rowmask", [128, 4], F32)
        rowmask_s = sbt(es, "rowmask_s", [128, 4], F32)
        rowmask_s3 = sbt(es, "rowmask_s3", [128, 4], F32)
        epsc = sbt(es, "epsc", [128, 1], F32)
        lbT = sbt(es, "lbT", [128, 12], F32)
        omlbT = sbt(es, "omlbT", [128, 12], F32)
        hgT = sbt(es, "hgT", [128, 12], F32)

        Bx = [Buf(f"x{i}") for i in range(NT)]
        BhT = [Buf(f"hT{i}") for i in range(NT)]
        ByTp = [Buf(f"yTp{i}") for i in range(6)]
        ByTs = [Buf(f"yTs{i}") for i in range(6)]
        BW = [Buf("wA0"), Buf("wA1")]
        Bwo = Buf("wo")
        Bc = Buf("consts")

        def tcols(i):
            return slice(i * 128, (i + 1) * 128)

        units = []
        for l in range(2):
            for c in range(6):
                units.append((l, [(0, c * 128), (128, 768 + c * 128), (256, 1536 + c * 128), (384, 2304 + c * 128)], 128))
            units.append((l, [(0, 3584)], 512))
            for cb in range(4):
                units.append((l, [(0, 3072 + cb * 128), (128, 4096 + cb * 128)], 128))
            for hd in range(6):
                units.append((l, [(0, 4608 + hd * 128), (128, 5376 + hd * 128), (256, 6144 + hd * 128), (384, 6912 + hd * 128)], 128))
        ustate = {"loaded": 0}

        def load_unit(u):
            if u >= len(units) or u < ustate["loaded"]:
                return
            assert u == ustate["loaded"]
            ustate["loaded"] = u + 1
            l, parts, wdt = units[u]
            wt = WA[u % 2]
            for (dst, src, ) in [(p[0], p[1]) for p in parts]:
                dma(POOL, wt[:, :, dst:dst + wdt],
                    w_in[l, :, src:src + wdt].rearrange("(kc p) n -> p kc n", p=128), writes=[BW[u % 2]])

        def load_wo(l, r0, nch):
            dma(POOL, wo[:, 0:nch, :], w_out[l, r0:r0 + nch * 128, :].rearrange("(c p) d -> p c d", p=128), writes=[Bwo])

        with contextlib.ExitStack() as ar:
            tmpf = sbt(ar, "c_tmpf", [128, 512], F32)
            R4 = sbt(ar, "c_R4", [4, 128], F32)
            ld12 = sbt(ar, "c_ld12", [12, 128], F32)
            hg12 = sbt(ar, "c_hg12", [12, 128], F32)
            lgT = sbt(ar, "c_lgT", [128, 12], F32)
            Bt = Buf("c_tmp")
            BR4 = Buf("c_R4")
            Bl = Buf("c_ld")
            dma(SP, ld12[:], lbl.rearrange("l (h k) -> (l h) k", k=128), writes=[Bl])
            dma(SP, hg12[:], hng.rearrange("l (h k) -> (l h) k", k=128), writes=[Bl])
            for i in range(16):
                dma(SP, xres[:, i, :], xp[i * 128:(i + 1) * 128, :], writes=[Bx[i]])
            op(DVE, lambda: V.memset(xres[:, 16, :], 0.0), writes=[Bx[16]])
            for b in range(4):
                dma(SP, xres[32 * b:32 * b + 1, 16, :], xs[b:b + 1, :], writes=[Bx[16]])
            op(DVE, lambda: V.memset(yT[:, :, 2048:TOK], 0.0), writes=ByTs)
            op(DVE, lambda: V.memset(epsc[:], EPS), writes=[Bc])
            op(DVE, lambda: V.memset(ones_bf[:], 1.0), writes=[Bc])
            op(DVE, lambda: V.memset(ones_f[:], 1.0), writes=[Bc])
            op(POOL, lambda: G.memset(ident_f[:], 1.0), writes=[Bc])
            op(POOL, lambda: G.affine_select(out=ident_f[:], in_=ident_f[:], pattern=[[-1, 128]], compare_op=ALU.is_equal,
                                             fill=0.0, base=0, channel_multiplier=1), reads=[Bc], writes=[Bc])
            op(DVE, lambda: V.tensor_copy(out=ident_bf[:], in_=ident_f[:]), reads=[Bc], writes=[Bc])
            op(POOL, lambda: G.memset(tmpf[:, 0:256], 1.0), writes=[Bt])
            op(POOL, lambda: G.affine_select(out=tmpf[:, 0:128], in_=tmpf[:, 0:128], pattern=[[-1, 128]], compare_op=ALU.is_ge,
                                             fill=0.0, base=0, channel_multiplier=1), reads=[Bt], writes=[Bt])
            op(POOL, lambda: G.affine_select(out=tmpf[:, 128:256], in_=tmpf[:, 128:256], pattern=[[1, 128]], compare_op=ALU.is_ge,
                                             fill=0.0, base=0, channel_multiplier=-1), reads=[Bt], writes=[Bt])
            for kb in range(2):
                for h in range(2):
                    op(DVE, lambda kb=kb, h=h: V.tensor_copy(out=mask01[:, (h * 2 + kb) * 128:(h * 2 + kb + 1) * 128],
                                                            in_=tmpf[:, kb * 128:(kb + 1) * 128]), reads=[Bt], writes=[Bc])
            op(DVE, lambda: V.tensor_copy(out=maskcur[:], in_=tmpf[:, 128:256]), reads=[Bt], writes=[Bc])
            op(POOL, lambda: G.memset(R4[:], 1.0), writes=[BR4])
            op(POOL, lambda: G.affine_select(out=R4[:], in_=R4[:], pattern=[[1, 128]], compare_op=ALU.is_ge,
                                             fill=0.0, base=0, channel_multiplier=-32), reads=[BR4], writes=[BR4])
            op(POOL, lambda: G.affine_select(out=R4[:], in_=R4[:], pattern=[[-1, 128]], compare_op=ALU.is_ge,
                                             fill=0.0, base=31, channel_multiplier=32), reads=[BR4], writes=[BR4])
            op(PE, lambda: TE.matmul(PS[0][:, 0:128], lhsT=R4[:], rhs=R4[:], start=True, stop=True), reads=[BR4], writes=[BPS[0]])
            op(PE, lambda: TE.matmul(PS[0][:, 128:132], lhsT=R4[:], rhs=ident_f[0:4, 0:4], start=True, stop=True),
               reads=[BR4, Bc], writes=[BPS[0]])
            op(DVE, lambda: V.tensor_tensor(out=blockmask[:], in0=tmpf[:, 128:256], in1=PS[0][:, 0:128], op=ALU.mult),
               reads=[Bt, BPS[0]], writes=[Bc])
            op(DVE, lambda: V.tensor_copy(out=rowmask[:], in_=PS[0][:, 128:132]), reads=[BPS[0]], writes=[Bc])
            op(POOL, lambda: G.memset(rowmask_s[:], 1.0), writes=[Bc])
            op(POOL, lambda: G.affine_select(out=rowmask_s[:], in_=rowmask_s[:], pattern=[[-32, 4]], compare_op=ALU.is_equal,
                                             fill=0.0, base=0, channel_multiplier=1), reads=[Bc], writes=[Bc])
            op(DVE, lambda: V.tensor_scalar(out=rowmask_s3[:], in0=rowmask_s[:], scalar1=3.0, scalar2=None, op0=ALU.mult),
               reads=[Bc], writes=[Bc])
            op(PE, lambda: TE.transpose(PS[1][:, 0:12], ld12[:], ident_f[0:12, 0:12]), reads=[Bl, Bc], writes=[BPS[1]])
            op(PE, lambda: TE.transpose(PS[1][:, 16:28], hg12[:], ident_f[0:12, 0:12]), reads=[Bl, Bc], writes=[BPS[1]])
            op(DVE, lambda: V.tensor_copy(out=lgT[:], in_=PS[1][:, 0:12]), reads=[BPS[1]], writes=[Bt])
            op(DVE, lambda: V.tensor_copy(out=hgT[:], in_=PS[1][:, 16:28]), reads=[BPS[1]], writes=[Bc])
            op(DVE, lambda: V.memset(lbT[:], 0.0), writes=[Bc])
            op(DVE, lambda: V.tensor_tensor(out=lgT[:, 0:6], in0=lgT[:, 0:6], in1=lgT[:, 6:12], op=ALU.subtract), reads=[Bt], writes=[Bt])
            op(ACT, lambda: A.activation(out=lgT[:, 0:6], in_=lgT[:, 0:6], func=AF.Exp), reads=[Bt], writes=[Bt])
            op(DVE, lambda: V.tensor_scalar(out=lgT[:, 0:6], in0=lgT[:, 0:6], scalar1=1.0, scalar2=None, op0=ALU.add), reads=[Bt], writes=[Bt])
            op(DVE, lambda: V.reciprocal(out=lbT[:, 6:12], in_=lgT[:, 0:6]), reads=[Bt, Bc], writes=[Bc])
            op(DVE, lambda: V.tensor_scalar(out=omlbT[:], in0=lbT[:], scalar1=-1.0, scalar2=1.0, op0=ALU.mult, op1=ALU.add),
               reads=[Bc], writes=[Bc])
            load_unit(0)
            T.barrier()
            T.ck("const")

        def sigmoid_act(src_ap, dst_ap, rd, Bdst):
            op(ACT, lambda: A.activation(out=dst_ap, in_=src_ap, func=AF.Exp, scale=-1.0), reads=rd, writes=[Bdst])
            op(ACT, lambda: A.activation(out=dst_ap, in_=dst_ap, func=AF.Ln, scale=1.0, bias=1.0), reads=[Bdst], writes=[Bdst])
            op(ACT, lambda: A.activation(out=dst_ap, in_=dst_ap, func=AF.Exp, scale=-1.0), reads=[Bdst], writes=[Bdst])

        def silu_from_psum(P, n, out_ap, tmp, Btmp, BP, wr):
            sigmoid_act(P[:, 0:n], tmp[:, 0:n], [BP], Btmp)
            op(DVE, lambda: V.tensor_tensor(out=out_ap, in0=P[:, 0:n], in1=tmp[:, 0:n], op=ALU.mult), reads=[Btmp, BP], writes=wr)

        def out_proj(l, nch):
            for i in range(NT):
                ybufs = (ByTp if i < 16 else ByTs)[0:nch]
                for half in range(2):
                    k = (2 * i + half) % 4
                    P = PS[k]
                    for cc in range(nch):
                        op(PE, lambda P=P, cc=cc, i=i, half=half: TE.matmul(
                            P[:], lhsT=yT[:, cc, tcols(i)], rhs=wo[:, cc, half * 512:(half + 1) * 512],
                            start=(cc == 0), stop=(cc == nch - 1)), reads=ybufs + [Bwo], writes=[BPS[k]])
                    xs_ap = xres[:, i, half * 512:(half + 1) * 512]
                    op(DVE, lambda P=P, xs_ap=xs_ap: V.tensor_tensor(out=xs_ap, in0=xs_ap, in1=P[:], op=ALU.add),
                       reads=[Bx[i], BPS[k]], writes=[Bx[i]])

        uidx = 0
        for l in range(2):
            if os.environ.get("KSKIP0", "") == "1":
                if l == 0:
                    T.stopped = True
                else:
                    T.stopped = False
                    ustate["loaded"] = 17
            with contextlib.ExitStack() as ar:
                gn = sbt(ar, "n_gn", [128, 1024], F32)
                sqj = sbt(ar, "n_sqj", [128, 1024], BF16)
                ss = sbt(ar, "n_ss", [128, NT], F32)
                rstd = sbt(ar, "n_rstd", [128, NT], F32)
                hb = [sbt(ar, f"n_hb{j}", [128, 1024], BF16) for j in range(2)]
                Bgn, Bsq, Bss, Brs = Buf("gn"), Buf("sqj"), Buf("ss"), Buf("rstd")
                Bhb = [Buf("hb0"), Buf("hb1")]
                dma(SP, gn[:], norm_g[l].partition_broadcast(128), writes=[Bgn])
                op(DVE, lambda: V.memset(ss[:], 0.0), writes=[Bss])
                for i in range(NT):
                    op(ACT, lambda i=i: A.activation(out=sqj[:], in_=xres[:, i, :], func=AF.Square, accum_out=ss[:, i:i + 1]),
                       reads=[Bx[i], Bss], writes=[Bsq, Bss])
                op(ACT, lambda: A.activation(out=ss[:], in_=ss[:], func=AF.Ln, scale=1.0 / 1024, bias=epsc[:, 0:1]),
                   reads=[Bss, Bc], writes=[Bss])
                op(ACT, lambda: A.activation(out=rstd[:], in_=ss[:], func=AF.Exp, scale=-0.5), reads=[Bss], writes=[Brs])
                for i in range(NT):
                    j = i % 2
                    op(DVE, lambda i=i, j=j: V.scalar_tensor_tensor(out=hb[j][:], in0=xres[:, i, :], scalar=rstd[:, i:i + 1], in1=gn[:],
                                                                    op0=ALU.mult, op1=ALU.mult),
                       reads=[Bx[i], Brs, Bgn], writes=[Bhb[j]])
                    for kc in range(8):
                        op(PE, lambda j=j, kc=kc: TE.transpose(PB[j][:, tcols(kc)], hb[j][:, tcols(kc)], ident_bf[:]),
                           reads=[Bhb[j], Bc], writes=[BPB[j]])
                    op(ACT, lambda i=i, j=j: A.activation(out=hT[:, :, tcols(i)], in_=PB[j][:].rearrange("p (k t) -> p k t", k=8), func=AF.Copy),
                       reads=[BPB[j]], writes=[BhT[i]])
                T.barrier()
                T.ck(f"norm{l}")

            with contextlib.ExitStack() as arA:
                qs_tok = sbt(arA, "a_qs", [128, 768], BF16)
                ks_tok = sbt(arA, "a_ks", [128, 768], BF16)
                vs_tok = sbt(arA, "a_vs", [128, 768], BF16)
                zsamp = sbt(arA, "a_zs", [128, 6, 4], F32)
                Bst = Buf("a_stash")
                load_wo(l, 0, 6)
                with contextlib.ExitStack() as ar:
                    qkT = sbt(ar, "a_qkT", [128, 2, TOK], BF16)
                    vnat = sbt(ar, "a_vnat", [128, NT, 128], BF16)
                    vord = sbt(ar, "a_vord", [128, 16, 128], BF16)
                    zT = sbt(ar, "a_zT", [128, TOK], BF16)
                    UD = sbt(ar, "a_UD", [128, 2, 2048], F32)
                    pp = [sbt(ar, f"a_p{j}", [128, 512], BF16) for j in range(2)]
                    kfin = [sbt(ar, f"a_kfin{j}", [128, 128], F32) for j in range(2)]
                    vfin = [sbt(ar, f"a_vfin{j}", [128, 128], F32) for j in range(2)]
                    qb = [sbt(ar, f"a_qb{j}", [128, 128], BF16) for j in range(2)]
                    kb_ = [sbt(ar, f"a_kb{j}", [128, 128], BF16) for j in range(2)]
                    ss4 = sbt(ar, "a_ss4", [128, 4], F32)
                    rs4 = sbt(ar, "a_rs4", [128, 4], F32)
                    qgc = sbt(ar, "a_qgc", [128, 128], F32)
                    kgc = sbt(ar, "a_kgc", [128, 128], F32)
                    BqkT, Bvn, Bvo, BzT, BUD = Buf("qkT"), Buf("vnat"), Buf("vord"), Buf("zT"), Buf("UD")
                    Bp = [Buf("p0"), Buf("p1")]
                    Bpm = [Buf("pm0"), Buf("pm1")]
                    Bkf = [Buf("kf0"), Buf("kf1")]
                    Bvf = [Buf("vf0"), Buf("vf1")]
                    Btq, Btk, Bsq4, Bss4, Brs4, Bg, Bez = Buf("tq"), Buf("tk"), Buf("sq"), Buf("ss4"), Buf("rs4"), Buf("g"), Buf("ezt")
                    Bqb = [Buf("qb0"), Buf("qb1")]
                    Bkb = [Buf("kb0"), Buf("kb1")]
                    blk_ctr = [0]
                    sq = UD[:, 1, 0:256]
                    ezt = UD[:, 0, 0:512]
                    Bsq4 = BUD
                    Bez = BUD

                    for c in range(6):
                        u = uidx
                        uidx += 1
                        load_unit(u)
                        load_unit(u + 1)
                        wt, Bw = WA[u % 2], BW[u % 2]
                        csl = slice(c * 128, (c + 1) * 128)
                        dma(SP, qgc[:], qng[l, csl].partition_broadcast(128), writes=[Bg])
                        dma(SP, kgc[:], kng[l, csl].partition_broadcast(128), writes=[Bg])
                        op(DVE, lambda: V.tensor_scalar(out=qgc[:], in0=qgc[:], scalar1=0.125, scalar2=None, op0=ALU.mult), reads=[Bg], writes=[Bg])
                        def a1_front(i):
                            j = i % 2
                            P, BP = PS[j], BPS[j]
                            for kc in range(8):
                                op(PE, lambda P=P, kc=kc, i=i: TE.matmul(P[:, 0:384], lhsT=hT[:, kc, tcols(i)], rhs=wt[:, kc, 0:384],
                                                                        start=(kc == 0), stop=(kc == 7)), reads=[BhT[i], Bw], writes=[BP], inc=(kc == 7))
                            op(ACT, lambda P=P: A.activation(out=sq, in_=P[:, 0:256], func=AF.Square), reads=[BP], writes=[Bsq4])
                            op(DVE, lambda: V.tensor_reduce(out=ss4[:], in_=sq.rearrange("p (h e) -> p h e", e=64), axis=AX.X, op=ALU.add),
                               reads=[Bsq4], writes=[Bss4])
                            op(ACT, lambda: A.activation(out=ss4[:], in_=ss4[:], func=AF.Ln, scale=1.0 / 64, bias=epsc[:, 0:1]),
                               reads=[Bss4, Bc], writes=[Bss4])
                            op(ACT, lambda: A.activation(out=rs4[:], in_=ss4[:], func=AF.Exp, scale=-0.5), reads=[Bss4], writes=[Brs4])
                            for h in range(2):
                                hs = slice(h * 64, (h + 1) * 64)
                                op(DVE, lambda P=P, j=j, h=h, hs=hs: V.scalar_tensor_tensor(out=qb[j][:, hs], in0=P[:, hs], scalar=rs4[:, h:h + 1], in1=qgc[:, hs],
                                                                                        op0=ALU.mult, op1=ALU.mult), reads=[BP, Brs4, Bg], writes=[Bqb[j]])
                            for h in range(2):
                                hs = slice(h * 64, (h + 1) * 64)
                                op(DVE, lambda P=P, j=j, h=h, hs=hs: V.scalar_tensor_tensor(out=kfin[j][:, hs], in0=P[:, 128 + h * 64:128 + (h + 1) * 64],
                                                                                        scalar=rs4[:, 2 + h:3 + h], in1=kgc[:, hs], op0=ALU.mult, op1=ALU.mult),
                                   reads=[BP, Brs4, Bg], writes=[Bkf[j]])
                            op(POOL, lambda j=j: G.tensor_copy(out=kb_[j][:], in_=kfin[j][:]), reads=[Bkf[j]], writes=[Bkb[j]])
                            op(ACT, lambda P=P, j=j: A.activation(out=vfin[j][:], in_=P[:, 256:384], func=AF.Copy), reads=[BP], writes=[Bvf[j]])
                            op(POOL, lambda i=i, j=j: G.tensor_copy(out=vnat[:, i, :], in_=vfin[j][:]), reads=[Bvf[j]], writes=[Bvn])
                            if i < 16:
                                dma(SP, okp[l, tcols(i), csl], kfin[j][:], reads=[Bkf[j]])
                                dma(SP, ovp[l, tcols(i), csl], vfin[j][:], reads=[Bvf[j]])
                            else:
                                for b in range(4):
                                    dma(SP, oks[l, b:b + 1, csl], kfin[j][32 * b:32 * b + 1, :], reads=[Bkf[j]])
                                    dma(SP, ovs[l, b:b + 1, csl], vfin[j][32 * b:32 * b + 1, :], reads=[Bvf[j]])
                                op(POOL, lambda j=j: G.tensor_copy(out=qs_tok[:, csl], in_=qb[j][:]), reads=[Bqb[j]], writes=[Bst])
                                op(POOL, lambda j=j: G.tensor_copy(out=ks_tok[:, csl], in_=kb_[j][:]), reads=[Bkb[j]], writes=[Bst])
                                op(POOL, lambda j=j: G.tensor_copy(out=vs_tok[:, csl], in_=vfin[j][:]), reads=[Bvf[j]], writes=[Bst])
                        def a1_back(i):
                            j = i % 2
                            op(PE, lambda j=j: TE.transpose(PB[j][:, 0:128], qb[j][:], ident_bf[:]), reads=[Bqb[j], Bc], writes=[BPB[j]])
                            op(PE, lambda j=j: TE.transpose(PB[j][:, 128:256], kb_[j][:], ident_bf[:]), reads=[Bkb[j], Bc], writes=[BPB[j]], inc=True)
                            op(DVE, lambda i=i, j=j: V.tensor_copy(out=qkT[:, :, tcols(i)], in_=PB[j][:, 0:256].rearrange("p (a t) -> p a t", a=2)),
                               reads=[BPB[j]], writes=[BqkT])
                        if os.environ.get("KPIPE", "0") == "1":
                            a1_front(0)
                            for i in range(1, NT):
                                a1_front(i)
                                a1_back(i - 1)
                            a1_back(NT - 1)
                        else:
                            for i in range(NT):
                                a1_front(i)
                                a1_back(i)
                        T.ck(f"A1_{l}_{c}")
                        for tg in range(5):
                            n = 512 if tg < 4 else 128
                            cols = slice(tg * 512, tg * 512 + n)
                            k = 2 + tg % 2
                            P, BP = PS[k], BPS[k]
                            hb_ = BhT[4 * tg:4 * tg + 4] if tg < 4 else [BhT[16]]
                            for kc in range(8):
                                op(PE, lambda P=P, kc=kc, cols=cols, n=n: TE.matmul(P[:, 0:n], lhsT=wt[:, kc, 384:512], rhs=hT[:, kc, cols],
                                                                                  start=(kc == 0), stop=(kc == 7)), reads=hb_ + [Bw], writes=[BP])
                            silu_from_psum(P, n, zT[:, cols], ezt, Bez, BP, [BzT])
                        op(POOL, lambda c=c: G.tensor_copy(out=zsamp[:, c, :], in_=zT[:, 2048:TOK:32]), reads=[BzT], writes=[Bst])

                        T.ck(f"A2_{l}_{c}")
                        def attn_front(qsl, kcur, kprev, vcur, vprev, first):
                            n_ = blk_ctr[0]
                            blk_ctr[0] += 1
                            jj = n_ % 2
                            Sb = [(PS[n_ % 2], BPS[n_ % 2]), (PS[2 + n_ % 2], BPS[2 + n_ % 2])]
                            kbs = [1] if kprev is None else [0, 1]
                            lo = 0 if kprev is not None else 128
                            for h in range(2):
                                S, BS = Sb[h]
                                hs = slice(h * 64, (h + 1) * 64)
                                for kb in kbs:
                                    ks = kprev if kb == 0 else kcur
                                    op(PE, lambda S=S, hs=hs, ks=ks, kb=kb: TE.matmul(S[:, kb * 128:(kb + 1) * 128], lhsT=qkT[hs, 1, ks], rhs=qkT[hs, 0, qsl],
                                                                                   start=True, stop=True), reads=[BqkT], writes=[BS], inc=(kb == 1))
                            for h in range(2):
                                S, BS = Sb[h]
                                op(ACT, lambda S=S, h=h: A.activation(out=pp[jj][:, h * 256 + lo:(h + 1) * 256], in_=S[:, lo:256], func=AF.Exp),
                                   reads=[BS], writes=[Bp[jj]])
                            pv = pp[jj][:].rearrange("p (h x) -> p h x", h=2)[:, :, lo:256]
                            mv_ = mask01[:].rearrange("p (h x) -> p h x", h=2)[:, :, lo:256]
                            op(POOL, lambda pv=pv, mv_=mv_: G.tensor_tensor(out=pv, in0=pv, in1=mv_, op=ALU.mult), reads=[Bp[jj], Bc], writes=[Bp[jj]])
                            return (n_, qsl, kbs, vcur, vprev, first)

                        def attn_back(ctx):
                            n_, qsl, kbs, vcur, vprev, first = ctx
                            jj = n_ % 2
                            U, BU = PS[4 + n_ % 2], BPS[4 + n_ % 2]
                            for part in range(2):
                                for h in range(2):
                                    hs = slice(h * 64, (h + 1) * 64)
                                    for idx, kb in enumerate(kbs):
                                        vb = vprev if kb == 0 else vcur
                                        lhsT = vb[:, hs] if part == 0 else ones_bf[:, 0:64]
                                        o0 = (h * 2 + kb) * 128
                                        op(PE, lambda U=U, hs=hs, part=part, lhsT=lhsT, o0=o0, idx=idx: TE.matmul(
                                            U[hs, part * 128:(part + 1) * 128], lhsT=lhsT, rhs=pp[jj][:, o0:o0 + 128],
                                            start=(idx == 0), stop=(idx == len(kbs) - 1)), reads=[Bp[jj], Bvn, Bvo, Bc], writes=[BU],
                                           inc=(part == 1 and h == 1 and idx == len(kbs) - 1))
                            uv = U[:, 0:256].rearrange("p (a q) -> p a q", a=2)
                            if first:
                                op(DVE, lambda: V.tensor_copy(out=UD[:, :, qsl], in_=uv), reads=[BU], writes=[BUD])
                            else:
                                op(DVE, lambda: V.tensor_tensor(out=UD[:, :, qsl], in0=UD[:, :, qsl], in1=uv, op=ALU.add), reads=[BU, BUD], writes=[BUD])

                        def run_blocks(specs):
                            prev = None
                            for sp_ in specs:
                                ctx = attn_front(*sp_)
                                if prev is not None:
                                    attn_back(prev)
                                prev = ctx
                            attn_back(prev)

                        run_blocks([(tcols(i), tcols(i), tcols(i - 1) if i > 0 else None, vnat[:, i, :], vnat[:, i - 1, :] if i > 0 else None, True)
                                    for i in range(16)])
                        T.ck(f"A4a_{l}_{c}")
                        for dil in (4, 16):
                            def tsl(blk):
                                if dil == 4:
                                    jb, r4 = blk // 4, blk % 4
                                    return slice(512 * jb + r4, 512 * (jb + 1), 4)
                                return slice(blk, 2048, 16)
                            for g4 in range(4):
                                k = 2 + g4 % 2
                                P, BP = PS[k], BPS[k]
                                for bi in range(4):
                                    blk = 4 * g4 + bi
                                    for kc in range(8):
                                        op(PE, lambda P=P, bi=bi, blk=blk, kc=kc: TE.matmul(P[:, tcols(bi)], lhsT=hT[:, kc, tsl(blk)], rhs=wt[:, kc, 256:384],
                                                                                           start=(kc == 0), stop=(kc == 7)), reads=BhT[0:16] + [Bw], writes=[BP])
                                op(ACT, lambda P=P, g4=g4: A.activation(out=vord[:, 4 * g4:4 * g4 + 4, :], in_=P[:].rearrange("p (a t) -> p a t", a=4), func=AF.Copy),
                                   reads=[BP], writes=[Bvo])
                            run_blocks([(tsl(blk), tsl(blk), tsl(blk - 4), vord[:, blk, :], vord[:, blk - 4, :], False) if (dil == 4 and blk >= 4)
                                        else (tsl(blk), tsl(blk), None, vord[:, blk, :], None, False) for blk in range(16)])
                        T.ck(f"A4b_{l}_{c}")
                        op(ACT, lambda: A.activation(out=UD[:, 1, :], in_=UD[:, 1, :], func=AF.Ln), reads=[BUD], writes=[BUD])
                        op(ACT, lambda: A.activation(out=UD[:, 1, :], in_=UD[:, 1, :], func=AF.Exp, scale=-1.0), reads=[BUD], writes=[BUD])
                        op(DVE, lambda: V.tensor_tensor(out=UD[:, 0, :], in0=UD[:, 0, :], in1=UD[:, 1, :], op=ALU.mult), reads=[BUD], writes=[BUD])
                        op(POOL, lambda c=c: G.tensor_tensor(out=yT[:, c, 0:2048], in0=UD[:, 0, :], in1=zT[:, 0:2048], op=ALU.mult),
                           reads=[BUD, BzT], writes=[ByTp[c]])
                    T.barrier()

                T.ck(f"A4_{l}")
                with contextlib.ExitStack() as ar:
                    selb = sbt(ar, "s_selb", [128, 4, 128], BF16)
                    selbf = sbt(ar, "s_selbf", [128, 512], F32)
                    Kr = [sbt(ar, f"s_Kr{j}", [128, 768], BF16) for j in range(2)]
                    Vr = [sbt(ar, f"s_Vr{j}", [128, 3, 768], BF16) for j in range(2)]
                    prod = sbt(ar, "s_prod", [128, 768], F32)
                    sc = sbt(ar, "s_sc", [128, 12], F32)
                    pall = [sbt(ar, f"s_pall{j}", [128, 3, 12], BF16) for j in range(2)]
                    pnew = sbt(ar, "s_pnew", [128, 12], F32)
                    pnm = sbt(ar, "s_pnm", [128, 4, 12], BF16)
                    rd = sbt(ar, "s_rd", [128, 6], F32)
                    osb = sbt(ar, "s_osb", [128, 6], F32)
                    Bsel, Bprod, Bsc, Bpn, Bpnm, Brd, Bos = Buf("selb"), Buf("prod"), Buf("sc"), Buf("pnew"), Buf("pnm"), Buf("rd"), Buf("osb")
                    BKr = [Buf("Kr0"), Buf("Kr1")]
                    BVr = [Buf("Vr0"), Buf("Vr1")]
                    Bpa = [Buf("pa0"), Buf("pa1")]
                    op(POOL, lambda: G.memset(selbf[:], 1.0), writes=[Bsel])
                    op(POOL, lambda: G.affine_select(out=selbf[:].rearrange("p (b m) -> p b m", b=4), in_=selbf[:].rearrange("p (b m) -> p b m", b=4),
                                                     pattern=[[-32, 4], [0, 128]], compare_op=ALU.is_equal, fill=0.0, base=0, channel_multiplier=1),
                       reads=[Bsel], writes=[Bsel])
                    op(DVE, lambda: V.tensor_copy(out=selb[:].rearrange("p b m -> p (b m)"), in_=selbf[:]), reads=[Bsel], writes=[Bsel])
                    op(DVE, lambda: V.tensor_tensor(out=prod[:], in0=qs_tok[:], in1=ks_tok[:], op=ALU.mult), reads=[Bst], writes=[Bprod])
                    op(DVE, lambda: V.tensor_reduce(out=sc[:], in_=prod[:].rearrange("p (h e) -> p h e", e=64), axis=AX.X, op=ALU.add),
                       reads=[Bprod], writes=[Bsc])
                    op(ACT, lambda: A.activation(out=pnew[:], in_=sc[:], func=AF.Exp), reads=[Bsc], writes=[Bpn])
                    for b in range(4):
                        op(DVE, lambda b=b: V.tensor_scalar(out=pnm[:, b, :], in0=pnew[:], scalar1=rowmask_s3[:, b:b + 1], scalar2=None, op0=ALU.mult),
                           reads=[Bpn, Bc], writes=[Bpnm])
                    kctr = 0
                    for b in range(4):
                        bj = b % 2
                        for pi, (r0, step) in enumerate([(1920, 1), (1536, 4), (0, 16)]):
                            rows = slice(r0, 2048, step)
                            kj = kctr % 2
                            kctr += 1
                            dma(POOL, Kr[kj][:], ck[l, b, rows, :], writes=[BKr[kj]])
                            dma(POOL, Vr[bj][:, pi, :], cv[l, b, rows, :], writes=[BVr[bj]])
                            for hf in range(2):
                                op(PE, lambda b=b, hf=hf: TE.matmul(PS[hf][:, 0:384], lhsT=selb[:, b, :], rhs=qs_tok[:, hf * 384:(hf + 1) * 384],
                                                                  start=True, stop=True), reads=[Bsel, Bst], writes=[BPS[hf]])
                            for hf in range(2):
                                op(DVE, lambda kj=kj, hf=hf: V.tensor_tensor(out=prod[:, hf * 384:(hf + 1) * 384], in0=Kr[kj][:, hf * 384:(hf + 1) * 384],
                                                                           in1=PS[hf][:, 0:384], op=ALU.mult), reads=[BKr[kj], BPS[hf]], writes=[Bprod])
                            op(DVE, lambda: V.tensor_reduce(out=sc[:], in_=prod[:].rearrange("p (h e) -> p h e", e=64), axis=AX.X, op=ALU.add),
                               reads=[Bprod], writes=[Bsc])
                            op(ACT, lambda bj=bj, pi=pi: A.activation(out=pall[bj][:, pi, :], in_=sc[:], func=AF.Exp), reads=[Bsc], writes=[Bpa[bj]])
                        PO, BPO = PS[2 + bj], BPS[2 + bj]
                        for part in range(2):
                            for c in range(6):
                                o0 = part * 16 + 2 * c
                                for pi in range(3):
                                    lhsT = Vr[bj][:, pi, c * 128:(c + 1) * 128] if part == 0 else ones_bf[:]
                                    op(PE, lambda PO=PO, o0=o0, lhsT=lhsT, pi=pi, c=c: TE.matmul(PO[:, o0:o0 + 2], lhsT=lhsT, rhs=pall[bj][:, pi, 2 * c:2 * c + 2],
                                                                                              start=(pi == 0), stop=False), reads=[BVr[bj], Bpa[bj], Bc], writes=[BPO])
                                lhsT = vs_tok[:, c * 128:(c + 1) * 128] if part == 0 else ones_bf[:]
                                op(PE, lambda PO=PO, o0=o0, lhsT=lhsT, b=b, c=c: TE.matmul(PO[:, o0:o0 + 2], lhsT=lhsT, rhs=pnm[:, b, 2 * c:2 * c + 2],
                                                                                        start=False, stop=True), reads=[Bst, Bpnm, Bc], writes=[BPO])
                        for hh in range(2):
                            rws = slice(hh * 64, hh * 64 + 64)
                            op(DVE, lambda PO=PO, rws=rws, hh=hh: V.reciprocal(out=rd[rws, :], in_=PO[rws, 16 + hh:28:2]), reads=[BPO], writes=[Brd])
                            op(DVE, lambda PO=PO, rws=rws, hh=hh: V.tensor_tensor(out=osb[rws, :], in0=PO[rws, hh:12:2], in1=rd[rws, :], op=ALU.mult),
                               reads=[BPO, Brd], writes=[Bos])
                        op(DVE, lambda b=b: V.tensor_tensor(out=yT[:, :, 2048 + 32 * b:2048 + 32 * b + 1], in0=osb[:].unsqueeze(2),
                                                            in1=zsamp[:, :, b:b + 1], op=ALU.mult), reads=[Bos, Bst], writes=ByTs)
                    T.barrier()
                T.ck(f"A5_{l}")
                out_proj(l, 6)
                T.barrier()
                T.ck(f"A_{l}")

            with contextlib.ExitStack() as ar:
                vn = sbt(ar, "b_vn", [128, NT, 512], BF16)
                lng_t = sbt(ar, "b_lng", [128, 512], F32)
                lnb_t = sbt(ar, "b_lnb", [128, 512], F32)
                st6 = sbt(ar, "b_st6", [128, 6], F32)
                mv = sbt(ar, "b_mv", [128, 2], F32)
                lnv = sbt(ar, "b_lnv", [128, 1], F32)
                rsb = sbt(ar, "b_rsb", [128, 1], F32)
                vnf = sbt(ar, "b_vnf", [128, 512], F32)
                vnf2 = [sbt(ar, f"b_vnf2{j}", [128, 512], F32) for j in range(2)]
                wl = sbt(ar, "b_wl", [128, 8, 128], F32)
                wlb = sbt(ar, "b_wlb", [128, 8, 128], BF16)
                wT = sbt(ar, "b_wT", [128, 8, 128], BF16)
                wTs = sbt(ar, "b_wTs", [128, 8, 128], BF16)
                w00 = sbt(ar, "b_w00", [128, 8], F32)
                bsf = sbt(ar, "b_bsf", [8, 128], F32)
                bsb = sbt(ar, "b_bsb", [8, 128], BF16)
                bs0 = sbt(ar, "b_bs0", [8, 128], BF16)
                ezb = sbt(ar, "b_ez", [128, 512], F32)
                t1 = sbt(ar, "b_t1", [128, 512], F32)
                Bvn_, Blg, Bst6, Bmv, Blnv, Brsb, Bvnf = Buf("vn"), Buf("lng"), Buf("st6"), Buf("mv"), Buf("lnv"), Buf("rsb"), Buf("vnf")
                Bvnf2 = [Buf("vnf20"), Buf("vnf21")]
                Bwl, Bwlb, BwT, Bbs, Bezb, Bt1 = Buf("wl"), Buf("wlb"), Buf("wT"), Buf("bs"), Buf("ezb"), Buf("t1")
                load_wo(l, 768, 4)
                bsel = sbt(ar, "b_bsel", [8, 512], BF16)
                bself = sbt(ar, "b_bself", [8, 512], F32)
                Bbsl = Buf("bsel")
                op(POOL, lambda: G.memset(bself[:], 1.0), writes=[Bbsl])
                op(POOL, lambda: G.affine_select(out=bself[:], in_=bself[:], pattern=[[1, 512]], compare_op=ALU.is_ge,
                                                 fill=0.0, base=0, channel_multiplier=-64), reads=[Bbsl], writes=[Bbsl])
                op(POOL, lambda: G.affine_select(out=bself[:], in_=bself[:], pattern=[[-1, 512]], compare_op=ALU.is_ge,
                                                 fill=0.0, base=63, channel_multiplier=64), reads=[Bbsl], writes=[Bbsl])
                op(DVE, lambda: V.tensor_copy(out=bsel[:], in_=bself[:]), reads=[Bbsl], writes=[Bbsl])
                dma(SP, lng_t[:], lng[l].partition_broadcast(128), writes=[Blg])
                dma(SP, lnb_t[:], lnb[l].partition_broadcast(128), writes=[Blg])
                dma(SP, wl[:], sgw[l].rearrange("g t s -> t g s"), writes=[Bwl])
                dma(SP, bsf[:], sgb[l], writes=[Bbs])
                u = uidx
                uidx += 1
                load_unit(u)
                load_unit(u + 1)
                wt, Bw = WA[u % 2], BW[u % 2]
                for i in range(NT):
                    j = i % 2
                    P, BP = PS[j], BPS[j]
                    for kc in range(8):
                        op(PE, lambda P=P, kc=kc, i=i: TE.matmul(P[:], lhsT=hT[:, kc, tcols(i)], rhs=wt[:, kc, :], start=(kc == 0), stop=(kc == 7)),
                           reads=[BhT[i], Bw], writes=[BP])
                    op(DVE, lambda P=P: V.bn_stats(out=st6[:], in_=P[:]), reads=[BP], writes=[Bst6])
                    op(DVE, lambda: V.bn_aggr(out=mv[:], in_=st6[:]), reads=[Bst6], writes=[Bmv])
                    op(ACT, lambda: A.activation(out=lnv[:], in_=mv[:, 1:2], func=AF.Ln, scale=1.0, bias=epsc[:, 0:1]), reads=[Bmv, Bc], writes=[Blnv])
                    op(ACT, lambda: A.activation(out=rsb[:], in_=lnv[:], func=AF.Exp, scale=-0.5), reads=[Blnv], writes=[Brsb])
                    op(DVE, lambda P=P: V.tensor_scalar(out=vnf[:], in0=P[:], scalar1=mv[:, 0:1], scalar2=rsb[:, 0:1], op0=ALU.subtract, op1=ALU.mult),
                       reads=[BP, Bmv, Brsb], writes=[Bvnf])
                    op(POOL, lambda: G.tensor_tensor(out=vnf[:], in0=vnf[:], in1=lng_t[:], op=ALU.mult), reads=[Bvnf, Blg], writes=[Bvnf])
                    op(POOL, lambda j=j: G.tensor_tensor(out=vnf2[j][:], in0=vnf[:], in1=lnb_t[:], op=ALU.add), reads=[Bvnf, Blg], writes=[Bvnf2[j]])
                    op(ACT, lambda i=i, j=j: A.activation(out=vn[:, i, :], in_=vnf2[j][:], func=AF.Copy), reads=[Bvnf2[j]], writes=[Bvn_])
                    if i == 16:
                        for b in range(4):
                            dma(SP, osg[l, b:b + 1, :], vnf2[j][32 * b:32 * b + 1, :], reads=[Bvnf2[j]])
                T.ck(f"B1_{l}")
                op(DVE, lambda: V.tensor_copy(out=wlb[:], in_=wl[:]), reads=[Bwl], writes=[Bwlb])
                for g in range(8):
                    op(PE, lambda g=g: TE.transpose(PB[0][:, tcols(g)], wlb[:, g, :], ident_bf[:]), reads=[Bwlb, Bc], writes=[BPB[0]])
                op(DVE, lambda: V.tensor_tensor(out=wT[:], in0=PB[0][:].rearrange("p (g t) -> p g t", g=8),
                                                in1=maskcur[:].unsqueeze(1).to_broadcast([128, 8, 128]), op=ALU.mult), reads=[BPB[0], Bc], writes=[BwT])
                op(PE, lambda: TE.matmul(PS[2][:, 0:8], lhsT=ones_f[0:1, :], rhs=wl[0:1, :, 0], start=True, stop=True), reads=[Bwl, Bc], writes=[BPS[2]])
                op(DVE, lambda: V.tensor_copy(out=w00[:], in_=PS[2][:, 0:8]), reads=[BPS[2]], writes=[BwT])
                op(DVE, lambda: V.tensor_tensor(out=wTs[:], in0=ident_bf[:].unsqueeze(1).to_broadcast([128, 8, 128]),
                                                in1=w00[:].unsqueeze(2).to_broadcast([128, 8, 128]), op=ALU.mult), reads=[BwT, Bc], writes=[BwT])
                op(DVE, lambda: V.tensor_copy(out=bsb[:], in_=bsf[:]), reads=[Bbs], writes=[Bbs])
                op(DVE, lambda: V.tensor_copy(out=bs0[:], in_=bsf[:, 0:1].to_broadcast([8, 128])), reads=[Bbs], writes=[Bbs])
                T.ck(f"B2_{l}")
                for cb in range(4):
                    u = uidx
                    uidx += 1
                    load_unit(u)
                    load_unit(u + 1)
                    wt, Bw = WA[u % 2], BW[u % 2]
                    for tg in range(5):
                        n = 512 if tg < 4 else 128
                        cols = slice(tg * 512, tg * 512 + n)
                        hb_ = BhT[4 * tg:4 * tg + 4] if tg < 4 else [BhT[16]]
                        Pu, BPu = PS[0 + tg % 2], BPS[0 + tg % 2]
                        Pz, BPz = PS[2 + tg % 2], BPS[2 + tg % 2]
                        Pm, BPm = PS[4 + tg % 2], BPS[4 + tg % 2]
                        for kc in range(8):
                            op(PE, lambda Pu=Pu, kc=kc, cols=cols, n=n: TE.matmul(Pu[:, 0:n], lhsT=wt[:, kc, 0:128], rhs=hT[:, kc, cols],
                                                                                start=(kc == 0), stop=(kc == 7)), reads=hb_ + [Bw], writes=[BPu])
                        for kc in range(8):
                            op(PE, lambda Pz=Pz, kc=kc, cols=cols, n=n: TE.matmul(Pz[:, 0:n], lhsT=wt[:, kc, 128:256], rhs=hT[:, kc, cols],
                                                                                start=(kc == 0), stop=(kc == 7)), reads=hb_ + [Bw], writes=[BPz])
                        for ti in range(n // 128):
                            i = 4 * tg + ti
                            for gg in range(2):
                                g = 2 * cb + gg
                                rws = slice(gg * 64, gg * 64 + 64)
                                wmat = wT if i < 16 else wTs
                                bmat = bsb if i < 16 else bs0
                                op(PE, lambda Pm=Pm, rws=rws, ti=ti, i=i, g=g, wmat=wmat: TE.matmul(
                                    Pm[rws, tcols(ti)], lhsT=vn[:, i, g * 64:(g + 1) * 64], rhs=wmat[:, g, :], start=True, stop=False),
                                   reads=[Bvn_, BwT], writes=[BPm])
                                op(PE, lambda Pm=Pm, rws=rws, ti=ti, g=g, bmat=bmat: TE.matmul(
                                    Pm[rws, tcols(ti)], lhsT=bsel[0:8, g * 64:(g + 1) * 64], rhs=bmat[0:8, :], start=False, stop=True),
                                   reads=[Bbs, Bbsl], writes=[BPm])
                        silu_from_psum(Pz, n, t1[:, 0:n], ezb, Bezb, BPz, [Bt1])
                        op(DVE, lambda Pu=Pu, n=n: V.tensor_tensor(out=t1[:, 0:n], in0=t1[:, 0:n], in1=Pu[:, 0:n], op=ALU.mult), reads=[Bt1, BPu], writes=[Bt1])
                        op(DVE, lambda Pm=Pm, n=n, cols=cols, cb=cb: V.tensor_tensor(out=yT[:, cb, cols], in0=t1[:, 0:n], in1=Pm[:, 0:n], op=ALU.mult),
                           reads=[Bt1, BPm], writes=[ByTp[cb] if tg < 4 else ByTs[cb]])
                T.barrier()
                T.ck(f"B3_{l}")
                out_proj(l, 4)
                T.barrier()
                T.ck(f"B_{l}")

            with contextlib.ExitStack() as ar:
                scanmask = sbt(ar, "c_scanm", [128, 512], F32)
                names = ["ef", "f", "g", "G", "eG", "kk", "kt", "khT", "eq", "q", "qt", "ez", "zs", "ln"]
                alias = {"eNG": "g", "rs": "ln", "sq": "eq", "o": "ez"}
                t = {nm: sbt(ar, "c_" + nm, [128, 512], F32) for nm in names}
                Bt_ = {nm: Buf("c_" + nm) for nm in names}
                for a_, b_ in alias.items():
                    t[a_] = t[b_]
                    Bt_[a_] = Bt_[b_]
                tv = sbt(ar, "c_tv", [128, 4, 128], F32)
                khtok = sbt(ar, "c_khtok", [128, 4, 4, 128], F32)
                ATm = sbt(ar, "c_ATm", [128, 4, 128], F32)
                Sring = sbt(ar, "c_Sring", [128, 9, 128], F32)
                BSr = [Buf(f"Sr{j}") for j in range(9)]
                S0 = [Sring[:, 1, :], Sring[:, 2, :]]
                Sn = [Sring[:, 3, :], Sring[:, 4, :]]
                BS0 = [BSr[1], BSr[2]]
                BSn = [BSr[3], BSr[4]]
                Btv, Bkh, BAT, Bsm = Buf("tv"), Buf("khtok"), Buf("ATm"), Buf("scanm")
                load_wo(l, 1280, 6)
                op(DVE, lambda: V.memset(scanmask[:], 1.0), writes=[Bsm])
                op(DVE, lambda: V.memset(scanmask[:].rearrange("p (c j) -> p c j", j=32)[:, :, 0:1], 0.0), reads=[Bsm], writes=[Bsm])
                for hd in range(6):
                    u = uidx
                    uidx += 1
                    load_unit(u)
                    load_unit(u + 1)
                    wt, Bw = WA[u % 2], BW[u % 2]
                    lbc = lbT[:, l * 6 + hd:l * 6 + hd + 1]
                    omc = omlbT[:, l * 6 + hd:l * 6 + hd + 1]
                    hgc = hgT[:, l * 6 + hd:l * 6 + hd + 1]
                    op(DVE, lambda: V.memset(Sring[:, 0, :], 0.0), writes=[BSr[0]])
                    for tg in range(5):
                        n = 512 if tg < 4 else 128
                        nti = n // 128
                        cols = slice(tg * 512, tg * 512 + n)
                        hb_ = BhT[4 * tg:4 * tg + 4] if tg < 4 else [BhT[16]]
                        for pi_, (P, BP, w0) in enumerate([(PS[0], BPS[0], 0), (PS[1], BPS[1], 128), (PS[2], BPS[2], 384)]):
                            for kc in range(8):
                                op(PE, lambda P=P, kc=kc, w0=w0, cols=cols, n=n: TE.matmul(P[:, 0:n], lhsT=wt[:, kc, w0:w0 + 128], rhs=hT[:, kc, cols],
                                                                                         start=(kc == 0), stop=(kc == 7)), reads=hb_ + [Bw], writes=[BP])
                        for ti in range(nti):
                            i = 4 * tg + ti
                            for kc in range(8):
                                op(PE, lambda ti=ti, i=i, kc=kc: TE.matmul(PS[3][:, tcols(ti)], lhsT=hT[:, kc, tcols(i)], rhs=wt[:, kc, 256:384],
                                                                         start=(kc == 0), stop=(kc == 7)), reads=[BhT[i], Bw], writes=[BPS[3]])
                        sl = slice(0, n)
                        sigmoid_act(PS[1][:, sl], t["ef"][:, sl], [BPS[1]], Bt_["ef"])
                        op(DVE, lambda: V.tensor_scalar(out=t["f"][:, sl], in0=t["ef"][:, sl], scalar1=omc, scalar2=lbc, op0=ALU.mult, op1=ALU.add),
                           reads=[Bt_["ef"], Bc], writes=[Bt_["f"]])
                        op(POOL, lambda: G.tensor_scalar(out=t["kk"][:, sl], in0=t["f"][:, sl], scalar1=-1.0, scalar2=1.0, op0=ALU.mult, op1=ALU.add),
                           reads=[Bt_["f"]], writes=[Bt_["kk"]])
                        sigmoid_act(PS[0][:, sl], t["eq"][:, sl], [BPS[0]], Bt_["eq"])
                        op(DVE, lambda: V.tensor_tensor(out=t["q"][:, sl], in0=PS[0][:, sl], in1=t["eq"][:, sl], op=ALU.mult), reads=[BPS[0], Bt_["eq"]], writes=[Bt_["q"]])
                        silu_from_psum(PS[2], n, t["zs"][:, sl], t["ez"], Bt_["ez"], BPS[2], [Bt_["zs"]])
                        op(ACT, lambda: A.activation(out=tv[:, 0:nti, :], in_=PS[3][:, sl].rearrange("p (a v) -> p a v", v=128), func=AF.Copy),
                           reads=[BPS[3]], writes=[Btv])
                        if tg < 4:
                            op(ACT, lambda: A.activation(out=t["g"][:], in_=t["f"][:], func=AF.Ln), reads=[Bt_["f"]], writes=[Bt_["g"]])
                            op(DVE, lambda: V.tensor_tensor_scan(out=t["G"][:], data0=scanmask[:], data1=t["g"][:], initial=0.0, op0=ALU.mult, op1=ALU.add),
                               reads=[Bt_["g"], Bsm], writes=[Bt_["G"]])
                            op(ACT, lambda: A.activation(out=t["eG"][:], in_=t["G"][:], func=AF.Exp), reads=[Bt_["G"]], writes=[Bt_["eG"]])
                            op(ACT, lambda: A.activation(out=t["eNG"][:], in_=t["G"][:], func=AF.Exp, scale=-1.0), reads=[Bt_["G"]], writes=[Bt_["eNG"]])
                            op(POOL, lambda: G.tensor_tensor(out=t["kt"][:], in0=t["kk"][:], in1=t["eNG"][:], op=ALU.mult), reads=[Bt_["kk"], Bt_["eNG"]], writes=[Bt_["kt"]])
                            op(POOL, lambda: G.tensor_tensor(out=t["khT"][:].rearrange("p (c j) -> p c j", j=32), in0=t["kt"][:].rearrange("p (c j) -> p c j", j=32),
                                                             in1=t["eG"][:, 31:512:32].unsqueeze(2).to_broadcast([128, 16, 32]), op=ALU.mult),
                               reads=[Bt_["kt"], Bt_["eG"]], writes=[Bt_["khT"]])
                            op(POOL, lambda: G.tensor_tensor(out=t["qt"][:], in0=t["q"][:], in1=t["eG"][:], op=ALU.mult), reads=[Bt_["q"], Bt_["eG"]], writes=[Bt_["qt"]])
                            T.ck(f"C1_{l}_{hd}_{tg}")
                            for ti in range(4):
                                op(PE, lambda ti=ti: TE.transpose(PS[0][:, tcols(ti)], t["khT"][:, tcols(ti)], ident_f[:]), reads=[Bt_["khT"], Bc], writes=[BPS[0]])
                            for ch in range(4):
                                if ch % 2 == 0:
                                    op(ACT, lambda ch=ch: A.activation(out=khtok[:, :, ch, :], in_=PS[0][:].rearrange("p (a k) -> p a k", a=4), func=AF.Copy,
                                                                       scale=rowmask[:, ch:ch + 1]), reads=[BPS[0], Bc], writes=[Bkh])
                                else:
                                    op(DVE, lambda ch=ch: V.tensor_scalar(out=khtok[:, :, ch, :], in0=PS[0][:].rearrange("p (a k) -> p a k", a=4),
                                                                          scalar1=rowmask[:, ch:ch + 1], scalar2=None, op0=ALU.mult), reads=[BPS[0], Bc], writes=[Bkh])
                            for ti in range(4):
                                op(PE, lambda ti=ti: TE.matmul(PS[1][:, tcols(ti)], lhsT=t["kt"][:, tcols(ti)], rhs=t["qt"][:, tcols(ti)], start=True, stop=True),
                                   reads=[Bt_["kt"], Bt_["qt"]], writes=[BPS[1]])
                            op(DVE, lambda: V.tensor_tensor(out=ATm[:], in0=PS[1][:].rearrange("p (a t) -> p a t", a=4),
                                                            in1=blockmask[:].unsqueeze(1).to_broadcast([128, 4, 128]), op=ALU.mult), reads=[BPS[1], Bc], writes=[BAT])
                            T.ck(f"C2_{l}_{hd}_{tg}")
                            banks_ = [0, 1, 3, 4]
                            for nchk in range(16):
                                ti, ch = nchk // 4, nchk % 4
                                bk, col = banks_[nchk // 4], (nchk % 4) * 128
                                op(PE, lambda bk=bk, col=col, ti=ti, ch=ch: TE.matmul(PS[bk][:, col:col + 128], lhsT=khtok[:, ti, ch, :], rhs=tv[:, ti, :], start=True, stop=True),
                                   reads=[Bkh, Btv], writes=[BPS[bk]], inc=(nchk % 4 == 3))
                            for half in range(2):
                                for r_ in range(8):
                                    nchk = half * 8 + r_
                                    bk, col = banks_[nchk // 4], (nchk % 4) * 128
                                    op(DVE, lambda bk=bk, col=col, nchk=nchk, r_=r_: V.scalar_tensor_tensor(
                                        out=Sring[:, r_ + 1, :], in0=Sring[:, r_, :], scalar=t["eG"][:, nchk * 32 + 31:nchk * 32 + 32], in1=PS[bk][:, col:col + 128],
                                        op0=ALU.mult, op1=ALU.add), reads=[BSr[r_], Bt_["eG"], BPS[bk]], writes=[BSr[r_ + 1]])
                                for r_ in range(8):
                                    nchk = half * 8 + r_
                                    ti, ch = nchk // 4, nchk % 4
                                    cc = slice(nchk * 32, nchk * 32 + 32)
                                    op(PE, lambda ti=ti, ch=ch, cc=cc: TE.matmul(PS[2][:, cc], lhsT=tv[:, ti, :], rhs=ATm[:, ti, ch * 32:(ch + 1) * 32], start=True, stop=False),
                                       reads=[Btv, BAT], writes=[BPS[2]])
                                    op(PE, lambda r_=r_, cc=cc: TE.matmul(PS[2][:, cc], lhsT=Sring[:, r_, :], rhs=t["qt"][:, cc], start=False, stop=True),
                                       reads=[BSr[r_], Bt_["qt"]], writes=[BPS[2]], inc=(r_ == 7))
                                op(DVE, lambda: V.tensor_copy(out=Sring[:, 0, :], in_=Sring[:, 8, :]), reads=[BSr[8]], writes=[BSr[0]])
                            no = 512
                        else:
                            op(PE, lambda: TE.transpose(PS[0][:, 0:128], t["kk"][:, 0:128], ident_f[:]), reads=[Bt_["kk"], Bc], writes=[BPS[0]])
                            for b in range(4):
                                op(DVE, lambda b=b: V.tensor_scalar(out=khtok[:, 0, b, :], in0=PS[0][:, 0:128], scalar1=rowmask_s[:, b:b + 1], scalar2=None, op0=ALU.mult),
                                   reads=[BPS[0], Bc], writes=[Bkh])
                            for b in range(4):
                                bj = b % 2
                                dma(SP, S0[bj], st[l, b, hd], writes=[BS0[bj]])
                                ku = 3 + bj
                                op(PE, lambda ku=ku, b=b: TE.matmul(PS[ku][:, 0:128], lhsT=khtok[:, 0, b, :], rhs=tv[:, 0, :], start=True, stop=True),
                                   reads=[Bkh, Btv], writes=[BPS[ku]])
                                op(DVE, lambda bj=bj, ku=ku, b=b: V.scalar_tensor_tensor(out=Sn[bj], in0=S0[bj], scalar=t["f"][:, 32 * b:32 * b + 1],
                                                                                       in1=PS[ku][:, 0:128], op0=ALU.mult, op1=ALU.add),
                                   reads=[BS0[bj], Bt_["f"], BPS[ku]], writes=[BSn[bj]])
                                dma(SP, ohs[l, b, hd], Sn[bj], reads=[BSn[bj]])
                                op(PE, lambda bj=bj, b=b: TE.matmul(PS[2][:, b:b + 1], lhsT=Sn[bj], rhs=t["q"][:, 32 * b:32 * b + 1], start=True, stop=True),
                                   reads=[BSn[bj], Bt_["q"]], writes=[BPS[2]])
                            no = 4
                        T.ck(f"C3_{l}_{hd}_{tg}")
                        so = slice(0, no)
                        op(ACT, lambda: A.activation(out=t["o"][:, so], in_=PS[2][:, so], func=AF.Copy), reads=[BPS[2]], writes=[Bt_["o"]])
                        op(ACT, lambda: A.activation(out=t["sq"][:, so], in_=PS[2][:, so], func=AF.Square), reads=[BPS[2]], writes=[Bt_["sq"]])
                        op(PE, lambda: TE.matmul(PS[5][:, so], lhsT=ones_f[:], rhs=t["sq"][:, so], start=True, stop=True), reads=[Bt_["sq"], Bc], writes=[BPS[5]])
                        op(ACT, lambda: A.activation(out=t["ln"][:, so], in_=PS[5][:, so], func=AF.Ln, scale=1.0 / 128, bias=epsc[:, 0:1]), reads=[BPS[5], Bc], writes=[Bt_["ln"]])
                        op(ACT, lambda: A.activation(out=t["rs"][:, so], in_=t["ln"][:, so], func=AF.Exp, scale=-0.5), reads=[Bt_["ln"]], writes=[Bt_["rs"]])
                        op(DVE, lambda: V.tensor_tensor(out=t["o"][:, so], in0=t["o"][:, so], in1=t["rs"][:, so], op=ALU.mult), reads=[Bt_["o"], Bt_["rs"]], writes=[Bt_["o"]])
                        if tg < 4:
                            op(DVE, lambda cols=cols, hd=hd: V.scalar_tensor_tensor(out=yT[:, hd, cols], in0=t["o"][:], scalar=hgc, in1=t["zs"][:], op0=ALU.mult, op1=ALU.mult),
                               reads=[Bt_["o"], Bt_["zs"], Bc], writes=[ByTp[hd]])
                        else:
                            op(DVE, lambda hd=hd: V.scalar_tensor_tensor(out=yT[:, hd, 2048:TOK:32], in0=t["o"][:, 0:4], scalar=hgc, in1=t["zs"][:, 0:128:32],
                                                                         op0=ALU.mult, op1=ALU.mult), reads=[Bt_["o"], Bt_["zs"], Bc], writes=[ByTs[hd]])
                        T.ck(f"C4_{l}_{hd}_{tg}")
                        if tg == 3:
                            dma(SP, ohp[l, hd], Sring[:, 0, :], reads=[BSr[0]])
                T.barrier()
                T.ck(f"C5_{l}")
                out_proj(l, 6)
                T.barrier()
                T.ck(f"L_{l}")

        T.force = True
        for i in range(16):
            dma(SP, yp[i * 128:(i + 1) * 128, :], xres[:, i, :], reads=[Bx[i]])
        for b in range(4):
            dma(SP, ys[b:b + 1, :], xres[32 * b:32 * b + 1, 16, :], reads=[Bx[16]])
        for Q in (SP, POOL):
            for s in Q.slots:
                if s.val > 0:
                    T._wait(SP, s.sem, s.val)
    return nc


_NC_CACHE = {}


def kernel(x_prompt, x_sample, cache_k, cache_v, state_hgrn, norm_g, w_in, q_norm_g, k_norm_g,
           sgu_ln_g, sgu_ln_b, sgu_w, sgu_b, hgrn_lb_logits, hgrn_norm_g, w_out):
    f = lambda a: np.ascontiguousarray(np.asarray(a, dtype=np.float32))
    x_prompt, x_sample, cache_k, cache_v, state_hgrn = map(f, (x_prompt, x_sample, cache_k, cache_v, state_hgrn))
    shared = {
        "norm_g": f(norm_g), "w_in": f(w_in), "q_norm_g": f(q_norm_g), "k_norm_g": f(k_norm_g),
        "sgu_ln_g": f(sgu_ln_g), "sgu_ln_b": f(sgu_ln_b), "sgu_w": f(sgu_w), "sgu_b": f(sgu_b),
        "hgrn_lb_logits": f(hgrn_lb_logits), "hgrn_norm_g": f(hgrn_norm_g), "w_out": f(w_out),
    }
    in_maps = []
    for c in range(NCORES):
        sb = slice(4 * c, 4 * c + 4)
        m = dict(shared)
        m["xp"] = np.ascontiguousarray(x_prompt[c])
        m["xs"] = np.ascontiguousarray(x_sample[sb, 0, :])
        m["ck"] = np.ascontiguousarray(cache_k[:, sb].reshape(2, 4, 2048, 768))
        m["cv"] = np.ascontiguousarray(cache_v[:, sb].reshape(2, 4, 2048, 768))
        m["st"] = np.ascontiguousarray(state_hgrn[:, sb])
        in_maps.append(m)
    if "nc" not in _NC_CACHE:
        _NC_CACHE["nc"] = build_nc()
    res = run_bass_kernel_spmd(_NC_CACHE["nc"], in_maps, core_ids=list(range(NCORES)))
    R = res.results
    y_prompt = np.stack([R[c]["yp"] for c in range(NCORES)]).astype(np.float32)
    y_sample = np.concatenate([R[c]["ys"] for c in range(NCORES)])[:, None, :].astype(np.float32)
    nkp = np.stack([R[c]["okp"] for c in range(NCORES)], axis=1).reshape(2, 8, 2048, 12, 64).astype(np.float32)
    nvp = np.stack([R[c]["ovp"] for c in range(NCORES)], axis=1).reshape(2, 8, 2048, 12, 64).astype(np.float32)
    nks = np.concatenate([R[c]["oks"] for c in range(NCORES)], axis=1).reshape(2, 32, 1, 12, 64).astype(np.float32)
    nvs = np.concatenate([R[c]["ovs"] for c in range(NCORES)], axis=1).reshape(2, 32, 1, 12, 64).astype(np.float32)
    nsg = np.concatenate([R[c]["osg"] for c in range(NCORES)], axis=1).reshape(2, 32, 1, 512).astype(np.float32)
    nhp = np.stack([R[c]["ohp"] for c in range(NCORES)], axis=1).astype(np.float32)
    nhs = np.concatenate([R[c]["ohs"] for c in range(NCORES)], axis=1).astype(np.float32)
    return (y_prompt, y_sample, nkp, nvp, nks, nvs, nsg, nhp, nhs)
```

```python
import bisect
import contextlib
import os

import numpy as np

import concourse.bass as bass
import concourse.mybir as mybir
from concourse.bass_utils import run_bass_kernel_spmd

F32 = mybir.dt.float32
BF16 = mybir.dt.bfloat16
AF = mybir.ActivationFunctionType
ALU = mybir.AluOpType
AX = mybir.AxisListType

NCORES = 8
NT = 17
TOK = NT * 128
EPS = 1e-6


class Eng:
    def __init__(self, name, eng, sem):
        self.name, self.eng, self.sem = name, eng, sem
        self.nseq = 0
        self.cnt = 0
        self.last = None
        self.inc_seq = []
        self.waited = {}
        self.slots = []
        self.rr = 0


class Slot:
    def __init__(self, sem):
        self.sem = sem
        self.val = 0


class Buf:
    __slots__ = ("name", "w", "r")

    def __init__(self, name):
        self.name = name
        self.w = None
        self.r = {}


class Tracker:
    def __init__(self):
        self.engs = []
        self.stopped = False
        self.force = False
        self.stop_at = os.environ.get("KSTOP", "")

    def ck(self, name):
        if self.stop_at and name == self.stop_at:
            self.stopped = True

    def resolve(self, ev):
        if ev[0] == "s":
            return ev[1], ev[2]
        E, seq = ev[1], ev[2]
        i = bisect.bisect_left(E.inc_seq, seq)
        if i < len(E.inc_seq):
            return E.sem, i + 1
        E.last.then_inc(E.sem, 1)
        E.cnt += 1
        E.inc_seq.append(E.nseq)
        return E.sem, E.cnt

    def _wait(self, E, sem, val):
        if E.waited.get(sem.num, 0) < val:
            E.eng.wait_ge(sem, val)
            E.waited[sem.num] = val

    def deps(self, E, reads, writes):
        evs = []
        for b in reads:
            if b.w is not None:
                evs.append(b.w)
        for b in writes:
            if b.w is not None:
                evs.append(b.w)
            for k, ev in b.r.items():
                if ev[0] == "e" and ev[1] is E:
                    continue
                evs.append(ev)
        need = {}
        for ev in evs:
            if ev[0] == "e" and ev[1] is E and E.name == "pe":
                continue
            sem, val = self.resolve(ev)
            if need.get(sem.num, (None, 0))[1] < val:
                need[sem.num] = (sem, val)
        for num, (sem, val) in need.items():
            self._wait(E, sem, val)

    def op(self, E, fn, reads=(), writes=(), inc=None):
        if self.stopped and not self.force:
            return None
        self.deps(E, reads, writes)
        ins = fn()
        E.nseq += 1
        E.last = ins
        if inc and os.environ.get("KINC", "1") == "1":
            ins.then_inc(E.sem, 1)
            E.cnt += 1
            E.inc_seq.append(E.nseq)
        ev = ("e", E, E.nseq)
        for b in writes:
            b.w = ev
            b.r = {}
        for b in reads:
            b.r[E.name] = ev
        return ins

    def dma(self, Q, out, in_, reads=(), writes=()):
        if self.stopped and not self.force:
            return
        self.deps(Q, reads, writes)
        slot = Q.slots[Q.rr % len(Q.slots)]
        Q.rr += 1
        if slot.val > 0:
            self._wait(Q, slot.sem, slot.val)
        Q.eng.dma_start(out=out, in_=in_).then_inc(slot.sem, 16)
        slot.val += 16
        ev = ("s", slot.sem, slot.val)
        for b in writes:
            b.w = ev
            b.r = {}
        for b in reads:
            b.r[("d", slot.sem.num)] = ev

    def barrier(self):
        if self.stopped and not self.force:
            return
        pts = []
        for F in self.engs:
            if F.nseq > 0:
                pts.append(self.resolve(("e", F, F.nseq)))
            for s in F.slots:
                if s.val > 0:
                    pts.append((s.sem, s.val))
        for E in self.engs:
            for sem, val in pts:
                if sem is E.sem and E.name == "pe":
                    continue
                self._wait(E, sem, val)


def build_nc():
    nc = bass.Bass("TRN2", target_bir_lowering=False)

    def din(name, shape):
        return nc.dram_tensor(name, shape, F32, kind="ExternalInput").ap()

    def dout(name, shape):
        return nc.dram_tensor(name, shape, F32, kind="ExternalOutput").ap()

    xp = din("xp", [2048, 1024])
    xs = din("xs", [4, 1024])
    ck = din("ck", [2, 4, 2048, 768])
    cv = din("cv", [2, 4, 2048, 768])
    st = din("st", [2, 4, 6, 128, 128])
    norm_g = din("norm_g", [2, 1024])
    w_in = din("w_in", [2, 1024, 7680])
    qng = din("q_norm_g", [2, 768])
    kng = din("k_norm_g", [2, 768])
    lng = din("sgu_ln_g", [2, 512])
    lnb = din("sgu_ln_b", [2, 512])
    sgw = din("sgu_w", [2, 8, 128, 128])
    sgb = din("sgu_b", [2, 8, 128])
    lbl = din("hgrn_lb_logits", [2, 768])
    hng = din("hgrn_norm_g", [2, 768])
    w_out = din("w_out", [2, 2048, 1024])
    yp = dout("yp", [2048, 1024])
    ys = dout("ys", [4, 1024])
    okp = dout("okp", [2, 2048, 768])
    ovp = dout("ovp", [2, 2048, 768])
    oks = dout("oks", [2, 4, 768])
    ovs = dout("ovs", [2, 4, 768])
    osg = dout("osg", [2, 4, 512])
    ohp = dout("ohp", [2, 6, 128, 128])
    ohs = dout("ohs", [2, 4, 6, 128, 128])

    T = Tracker()
    es = contextlib.ExitStack()
    with es:
        nmctr = [0]

        def sbt(stack, name, shape, dt):
            nmctr[0] += 1
            return stack.enter_context(nc.sbuf_tensor(f"{name}_{nmctr[0]}", shape, dt))

        sems = [es.enter_context(nc.semaphore(f"sem{i}")) for i in range(20)]
        PE = Eng("pe", nc.tensor, sems[0])
        ACT = Eng("act", nc.scalar, sems[1])
        DVE = Eng("dve", nc.vector, sems[2])
        POOL = Eng("pool", nc.gpsimd, sems[3])
        SP = Eng("sp", nc.sync, None)
        SP.slots = [Slot(s) for s in sems[4:12]]
        POOL.slots = [Slot(s) for s in sems[12:20]]
        T.engs = [PE, ACT, DVE, POOL, SP]
        op, dma = T.op, T.dma
        V, A, G, TE = nc.vector, nc.scalar, nc.gpsimd, nc.tensor

        PS = [es.enter_context(nc.psum_tensor(f"ps{i}", [128, 512], F32)) for i in range(6)]
        PB = [es.enter_context(nc.psum_tensor(f"pb{i}", [128, 1024], BF16)) for i in range(2)]
        BPS = [Buf(f"ps{i}") for i in range(6)]
        BPB = [Buf(f"pb{i}") for i in range(2)]

        xres = sbt(es, "xres", [128, NT, 1024], F32)
        hT = sbt(es, "hT", [128, 8, TOK], BF16)
        yT = sbt(es, "yT", [128, 6, TOK], BF16)
        WA = [sbt(es, f"wA{i}", [128, 8, 512], BF16) for i in range(2)]
        wo = sbt(es, "wo", [128, 6, 1024], BF16)
        ident_bf = sbt(es, "ident_bf", [128, 128], BF16)
        ident_f = sbt(es, "ident_f", [128, 128], F32)
        ones_bf = sbt(es, "ones_bf", [128, 128], BF16)
        ones_f = sbt(es, "ones_f", [128, 128], F32)
        mask01 = sbt(es, "mask01", [128, 512], BF16)
        maskcur = sbt(es, "maskcur", [128, 128], BF16)
        blockmask = sbt(es, "blockmask", [128, 128], F32)
        rowmask = sbt(es, "rowmask", [128, 4], F32)
        rowmask_s = sbt(es, "rowmask_s", [128, 4], F32)
        rowmask_s3 = sbt(es, "rowmask_s3", [128, 4], F32)
        epsc = sbt(es, "epsc", [128, 1], F32)
        lbT = sbt(es, "lbT", [128, 12], F32)
        omlbT = sbt(es, "omlbT", [128, 12], F32)
        hgT = sbt(es, "hgT", [128, 12], F32)

        Bx = [Buf(f"x{i}") for i in range(NT)]
        BhT = [Buf(f"hT{i}") for i in range(NT)]
        ByTp = [Buf(f"yTp{i}") for i in range(6)]
        ByTs = [Buf(f"yTs{i}") for i in range(6)]
        BW = [Buf("wA0"), Buf("wA1")]
        Bwo = Buf("wo")
        Bc = Buf("consts")

        def tcols(i):
            return slice(i * 128, (i + 1) * 128)

        units = []
        for l in range(2):
            for c in range(6):
                units.append((l, [(0, c * 128), (128, 768 + c * 128), (256, 1536 + c * 128), (384, 2304 + c * 128)], 128))
            units.append((l, [(0, 3584)], 512))
            for cb in range(4):
                units.append((l, [(0, 3072 + cb * 128), (128, 4096 + cb * 128)], 128))
            for hd in range(6):
                units.append((l, [(0, 4608 + hd * 128), (128, 5376 + hd * 128), (256, 6144 + hd * 128), (384, 6912 + hd * 128)], 128))
        ustate = {"loaded": 0}

        def load_unit(u):
            if u >= len(units) or u < ustate["loaded"]:
                return
            assert u == ustate["loaded"]
            ustate["loaded"] = u + 1
            l, parts, wdt = units[u]
            wt = WA[u % 2]
            for (dst, src, ) in [(p[0], p[1]) for p in parts]:
                dma(POOL, wt[:, :, dst:dst + wdt],
                    w_in[l, :, src:src + wdt].rearrange("(kc p) n -> p kc n", p=128), writes=[BW[u % 2]])

        def load_wo(l, r0, nch):
            dma(POOL, wo[:, 0:nch, :], w_out[l, r0:r0 + nch * 128, :].rearrange("(c p) d -> p c d", p=128), writes=[Bwo])

        with contextlib.ExitStack() as ar:
            tmpf = sbt(ar, "c_tmpf", [128, 512], F32)
            R4 = sbt(ar, "c_R4", [4, 128], F32)
            ld12 = sbt(ar, "c_ld12", [12, 128], F32)
            hg12 = sbt(ar, "c_hg12", [12, 128], F32)
            lgT = sbt(ar, "c_lgT", [128, 12], F32)
            Bt = Buf("c_tmp")
            BR4 = Buf("c_R4")
            Bl = Buf("c_ld")
            dma(SP, ld12[:], lbl.rearrange("l (h k) -> (l h) k", k=128), writes=[Bl])
            dma(SP, hg12[:], hng.rearrange("l (h k) -> (l h) k", k=128), writes=[Bl])
            for i in range(16):
                dma(SP, xres[:, i, :], xp[i * 128:(i + 1) * 128, :], writes=[Bx[i]])
            op(DVE, lambda: V.memset(xres[:, 16, :], 0.0), writes=[Bx[16]])
            for b in range(4):
                dma(SP, xres[32 * b:32 * b + 1, 16, :], xs[b:b + 1, :], writes=[Bx[16]])
            op(DVE, lambda: V.memset(yT[:, :, 2048:TOK], 0.0), writes=ByTs)
            op(DVE, lambda: V.memset(epsc[:], EPS), writes=[Bc])
            op(DVE, lambda: V.memset(ones_bf[:], 1.0), writes=[Bc])
            op(DVE, lambda: V.memset(ones_f[:], 1.0), writes=[Bc])
            op(POOL, lambda: G.memset(ident_f[:], 1.0), writes=[Bc])
            op(POOL, lambda: G.affine_select(out=ident_f[:], in_=ident_f[:], pattern=[[-1, 128]], compare_op=ALU.is_equal,
                                             fill=0.0, base=0, channel_multiplier=1), reads=[Bc], writes=[Bc])
            op(DVE, lambda: V.tensor_copy(out=ident_bf[:], in_=ident_f[:]), reads=[Bc], writes=[Bc])
            op(POOL, lambda: G.memset(tmpf[:, 0:256], 1.0), writes=[Bt])
            op(POOL, lambda: G.affine_select(out=tmpf[:, 0:128], in_=tmpf[:, 0:128], pattern=[[-1, 128]], compare_op=ALU.is_ge,
                                             fill=0.0, base=0, channel_multiplier=1), reads=[Bt], writes=[Bt])
            op(POOL, lambda: G.affine_select(out=tmpf[:, 128:256], in_=tmpf[:, 128:256], pattern=[[1, 128]], compare_op=ALU.is_ge,
                                             fill=0.0, base=0, channel_multiplier=-1), reads=[Bt], writes=[Bt])
            for kb in range(2):
                for h in range(2):
                    op(DVE, lambda kb=kb, h=h: V.tensor_copy(out=mask01[:, (h * 2 + kb) * 128:(h * 2 + kb + 1) * 128],
                                                            in_=tmpf[:, kb * 128:(kb + 1) * 128]), reads=[Bt], writes=[Bc])
            op(DVE, lambda: V.tensor_copy(out=maskcur[:], in_=tmpf[:, 128:256]), reads=[Bt], writes=[Bc])
            op(POOL, lambda: G.memset(R4[:], 1.0), writes=[BR4])
            op(POOL, lambda: G.affine_select(out=R4[:], in_=R4[:], pattern=[[1, 128]], compare_op=ALU.is_ge,
                                             fill=0.0, base=0, channel_multiplier=-32), reads=[BR4], writes=[BR4])
            op(POOL, lambda: G.affine_select(out=R4[:], in_=R4[:], pattern=[[-1, 128]], compare_op=ALU.is_ge,
                                             fill=0.0, base=31, channel_multiplier=32), reads=[BR4], writes=[BR4])
            op(PE, lambda: TE.matmul(PS[0][:, 0:128], lhsT=R4[:], rhs=R4[:], start=True, stop=True), reads=[BR4], writes=[BPS[0]])
            op(PE, lambda: TE.matmul(PS[0][:, 128:132], lhsT=R4[:], rhs=ident_f[0:4, 0:4], start=True, stop=True),
               reads=[BR4, Bc], writes=[BPS[0]])
            op(DVE, lambda: V.tensor_tensor(out=blockmask[:], in0=tmpf[:, 128:256], in1=PS[0][:, 0:128], op=ALU.mult),
               reads=[Bt, BPS[0]], writes=[Bc])
            op(DVE, lambda: V.tensor_copy(out=rowmask[:], in_=PS[0][:, 128:132]), reads=[BPS[0]], writes=[Bc])
            op(POOL, lambda: G.memset(rowmask_s[:], 1.0), writes=[Bc])
            op(POOL, lambda: G.affine_select(out=rowmask_s[:], in_=rowmask_s[:], pattern=[[-32, 4]], compare_op=ALU.is_equal,
                                             fill=0.0, base=0, channel_multiplier=1), reads=[Bc], writes=[Bc])
            op(DVE, lambda: V.tensor_scalar(out=rowmask_s3[:], in0=rowmask_s[:], scalar1=3.0, scalar2=None, op0=ALU.mult),
               reads=[Bc], writes=[Bc])
            op(PE, lambda: TE.transpose(PS[1][:, 0:12], ld12[:], ident_f[0:12, 0:12]), reads=[Bl, Bc], writes=[BPS[1]])
            op(PE, lambda: TE.transpose(PS[1][:, 16:28], hg12[:], ident_f[0:12, 0:12]), reads=[Bl, Bc], writes=[BPS[1]])
            op(DVE, lambda: V.tensor_copy(out=lgT[:], in_=PS[1][:, 0:12]), reads=[BPS[1]], writes=[Bt])
            op(DVE, lambda: V.tensor_copy(out=hgT[:], in_=PS[1][:, 16:28]), reads=[BPS[1]], writes=[Bc])
            op(DVE, lambda: V.memset(lbT[:], 0.0), writes=[Bc])
            op(DVE, lambda: V.tensor_tensor(out=lgT[:, 0:6], in0=lgT[:, 0:6], in1=lgT[:, 6:12], op=ALU.subtract), reads=[Bt], writes=[Bt])
            op(ACT, lambda: A.activation(out=lgT[:, 0:6], in_=lgT[:, 0:6], func=AF.Exp), reads=[Bt], writes=[Bt])
            op(DVE, lambda: V.tensor_scalar(out=lgT[:, 0:6], in0=lgT[:, 0:6], scalar1=1.0, scalar2=None, op0=ALU.add), reads=[Bt], writes=[Bt])
            op(DVE, lambda: V.reciprocal(out=lbT[:, 6:12], in_=lgT[:, 0:6]), reads=[Bt, Bc], writes=[Bc])
            op(DVE, lambda: V.tensor_scalar(out=omlbT[:], in0=lbT[:], scalar1=-1.0, scalar2=1.0, op0=ALU.mult, op1=ALU.add),
               reads=[Bc], writes=[Bc])
            load_unit(0)
            T.barrier()
            T.ck("const")

        def sigmoid_act(src_ap, dst_ap, rd, Bdst):
            op(ACT, lambda: A.activation(out=dst_ap, in_=src_ap, func=AF.Exp, scale=-1.0), reads=rd, writes=[Bdst])
            op(ACT, lambda: A.activation(out=dst_ap, in_=dst_ap, func=AF.Ln, scale=1.0, bias=1.0), reads=[Bdst], writes=[Bdst])
            op(ACT, lambda: A.activation(out=dst_ap, in_=dst_ap, func=AF.Exp, scale=-1.0), reads=[Bdst], writes=[Bdst])

        def silu_from_psum(P, n, out_ap, tmp, Btmp, BP, wr):
            sigmoid_act(P[:, 0:n], tmp[:, 0:n], [BP], Btmp)
            op(DVE, lambda: V.tensor_tensor(out=out_ap, in0=P[:, 0:n], in1=tmp[:, 0:n], op=ALU.mult), reads=[Btmp, BP], writes=wr)

        def out_proj(l, nch):
            for i in range(NT):
                ybufs = (ByTp if i < 16 else ByTs)[0:nch]
                for half in range(2):
                    k = (2 * i + half) % 4
                    P = PS[k]
                    for cc in range(nch):
                        op(PE, lambda P=P, cc=cc, i=i, half=half: TE.matmul(
                            P[:], lhsT=yT[:, cc, tcols(i)], rhs=wo[:, cc, half * 512:(half + 1) * 512],
                            start=(cc == 0), stop=(cc == nch - 1)), reads=ybufs + [Bwo], writes=[BPS[k]])
                    xs_ap = xres[:, i, half * 512:(half + 1) * 512]
                    op(DVE, lambda P=P, xs_ap=xs_ap: V.tensor_tensor(out=xs_ap, in0=xs_ap, in1=P[:], op=ALU.add),
                       reads=[Bx[i], BPS[k]], writes=[Bx[i]])

        uidx = 0
        for l in range(2):
            if os.environ.get("KSKIP0", "") == "1":
                if l == 0:
                    T.stopped = True
                else:
                    T.stopped = False
                    ustate["loaded"] = 17
            with contextlib.ExitStack() as ar:
                gn = sbt(ar, "n_gn", [128, 1024], F32)
                sqj = sbt(ar, "n_sqj", [128, 1024], BF16)
                ss = sbt(ar, "n_ss", [128, NT], F32)
                rstd = sbt(ar, "n_rstd", [128, NT], F32)
                hb = [sbt(ar, f"n_hb{j}", [128, 1024], BF16) for j in range(2)]
                Bgn, Bsq, Bss, Brs = Buf("gn"), Buf("sqj"), Buf("ss"), Buf("rstd")
                Bhb = [Buf("hb0"), Buf("hb1")]
                dma(SP, gn[:], norm_g[l].partition_broadcast(128), writes=[Bgn])
                op(DVE, lambda: V.memset(ss[:], 0.0), writes=[Bss])
                for i in range(NT):
                    op(ACT, lambda i=i: A.activation(out=sqj[:], in_=xres[:, i, :], func=AF.Square, accum_out=ss[:, i:i + 1]),
                       reads=[Bx[i], Bss], writes=[Bsq, Bss])
                op(ACT, lambda: A.activation(out=ss[:], in_=ss[:], func=AF.Ln, scale=1.0 / 1024, bias=epsc[:, 0:1]),
                   reads=[Bss, Bc], writes=[Bss])
                op(ACT, lambda: A.activation(out=rstd[:], in_=ss[:], func=AF.Exp, scale=-0.5), reads=[Bss], writes=[Brs])
                for i in range(NT):
                    j = i % 2
                    op(DVE, lambda i=i, j=j: V.scalar_tensor_tensor(out=hb[j][:], in0=xres[:, i, :], scalar=rstd[:, i:i + 1], in1=gn[:],
                                                                    op0=ALU.mult, op1=ALU.mult),
                       reads=[Bx[i], Brs, Bgn], writes=[Bhb[j]])
                    for kc in range(8):
                        op(PE, lambda j=j, kc=kc: TE.transpose(PB[j][:, tcols(kc)], hb[j][:, tcols(kc)], ident_bf[:]),
                           reads=[Bhb[j], Bc], writes=[BPB[j]])
                    op(ACT, lambda i=i, j=j: A.activation(out=hT[:, :, tcols(i)], in_=PB[j][:].rearrange("p (k t) -> p k t", k=8), func=AF.Copy),
                       reads=[BPB[j]], writes=[BhT[i]])
                T.barrier()
                T.ck(f"norm{l}")

            with contextlib.ExitStack() as arA:
                qs_tok = sbt(arA, "a_qs", [128, 768], BF16)
                ks_tok = sbt(arA, "a_ks", [128, 768], BF16)
                vs_tok = sbt(arA, "a_vs", [128, 768], BF16)
                zsamp = sbt(arA, "a_zs", [128, 6, 4], F32)
                Bst = Buf("a_stash")
                load_wo(l, 0, 6)
                with contextlib.ExitStack() as ar:
                    qkT = sbt(ar, "a_qkT", [128, 2, TOK], BF16)
                    vnat = sbt(ar, "a_vnat", [128, NT, 128], BF16)
                    vord = sbt(ar, "a_vord", [128, 16, 128], BF16)
                    zT = sbt(ar, "a_zT", [128, TOK], BF16)
                    UD = sbt(ar, "a_UD", [128, 2, 2048], F32)
                    pp = [sbt(ar, f"a_p{j}", [128, 512], BF16) for j in range(2)]
                    kfin = [sbt(ar, f"a_kfin{j}", [128, 128], F32) for j in range(2)]
                    vfin = [sbt(ar, f"a_vfin{j}", [128, 128], F32) for j in range(2)]
                    qb = [sbt(ar, f"a_qb{j}", [128, 128], BF16) for j in range(2)]
                    kb_ = [sbt(ar, f"a_kb{j}", [128, 128], BF16) for j in range(2)]
                    ss4 = sbt(ar, "a_ss4", [128, 4], F32)
                    rs4 = sbt(ar, "a_rs4", [128, 4], F32)
                    qgc = sbt(ar, "a_qgc", [128, 128], F32)
                    kgc = sbt(ar, "a_kgc", [128, 128], F32)
                    BqkT, Bvn, Bvo, BzT, BUD = Buf("qkT"), Buf("vnat"), Buf("vord"), Buf("zT"), Buf("UD")
                    Bp = [Buf("p0"), Buf("p1")]
                    Bpm = [Buf("pm0"), Buf("pm1")]
                    Bkf = [Buf("kf0"), Buf("kf1")]
                    Bvf = [Buf("vf0"), Buf("vf1")]
                    Btq, Btk, Bsq4, Bss4, Brs4, Bg, Bez = Buf("tq"), Buf("tk"), Buf("sq"), Buf("ss4"), Buf("rs4"), Buf("g"), Buf("ezt")
                    Bqb = [Buf("qb0"), Buf("qb1")]
                    Bkb = [Buf("kb0"), Buf("kb1")]
                    blk_ctr = [0]
                    sq = UD[:, 1, 0:256]
                    ezt = UD[:, 0, 0:512]
                    Bsq4 = BUD
                    Bez = BUD

                    for c in range(6):
                        u = uidx
                        uidx += 1
                        load_unit(u)
                        load_unit(u + 1)
                        wt, Bw = WA[u % 2], BW[u % 2]
                        csl = slice(c * 128, (c + 1) * 128)
                        dma(SP, qgc[:], qng[l, csl].partition_broadcast(128), writes=[Bg])
                        dma(SP, kgc[:], kng[l, csl].partition_broadcast(128), writes=[Bg])
                        op(DVE, lambda: V.tensor_scalar(out=qgc[:], in0=qgc[:], scalar1=0.125, scalar2=None, op0=ALU.mult), reads=[Bg], writes=[Bg])
                        def a1_front(i):
                            j = i % 2
                            P, BP = PS[j], BPS[j]
                            for kc in range(8):
                                op(PE, lambda P=P, kc=kc, i=i: TE.matmul(P[:, 0:384], lhsT=hT[:, kc, tcols(i)], rhs=wt[:, kc, 0:384],
                                                                        start=(kc == 0), stop=(kc == 7)), reads=[BhT[i], Bw], writes=[BP], inc=(kc == 7))
                            op(ACT, lambda P=P: A.activation(out=sq, in_=P[:, 0:256], func=AF.Square), reads=[BP], writes=[Bsq4])
                            op(DVE, lambda: V.tensor_reduce(out=ss4[:], in_=sq.rearrange("p (h e) -> p h e", e=64), axis=AX.X, op=ALU.add),
                               reads=[Bsq4], writes=[Bss4])
                            op(ACT, lambda: A.activation(out=ss4[:], in_=ss4[:], func=AF.Ln, scale=1.0 / 64, bias=epsc[:, 0:1]),
                               reads=[Bss4, Bc], writes=[Bss4])
                            op(ACT, lambda: A.activation(out=rs4[:], in_=ss4[:], func=AF.Exp, scale=-0.5), reads=[Bss4], writes=[Brs4])
                            for h in range(2):
                                hs = slice(h * 64, (h + 1) * 64)
                                op(DVE, lambda P=P, j=j, h=h, hs=hs: V.scalar_tensor_tensor(out=qb[j][:, hs], in0=P[:, hs], scalar=rs4[:, h:h + 1], in1=qgc[:, hs],
                                                                                        op0=ALU.mult, op1=ALU.mult), reads=[BP, Brs4, Bg], writes=[Bqb[j]])
                            for h in range(2):
                                hs = slice(h * 64, (h + 1) * 64)
                                op(DVE, lambda P=P, j=j, h=h, hs=hs: V.scalar_tensor_tensor(out=kfin[j][:, hs], in0=P[:, 128 + h * 64:128 + (h + 1) * 64],
                                                                                        scalar=rs4[:, 2 + h:3 + h], in1=kgc[:, hs], op0=ALU.mult, op1=ALU.mult),
                                   reads=[BP, Brs4, Bg], writes=[Bkf[j]])
                            op(POOL, lambda j=j: G.tensor_copy(out=kb_[j][:], in_=kfin[j][:]), reads=[Bkf[j]], writes=[Bkb[j]])
                            op(ACT, lambda P=P, j=j: A.activation(out=vfin[j][:], in_=P[:, 256:384], func=AF.Copy), reads=[BP], writes=[Bvf[j]])
                            op(POOL, lambda i=i, j=j: G.tensor_copy(out=vnat[:, i, :], in_=vfin[j][:]), reads=[Bvf[j]], writes=[Bvn])
                            if i < 16:
                                dma(SP, okp[l, tcols(i), csl], kfin[j][:], reads=[Bkf[j]])
                                dma(SP, ovp[l, tcols(i), csl], vfin[j][:], reads=[Bvf[j]])
                            else:
                                for b in range(4):
                                    dma(SP, oks[l, b:b + 1, csl], kfin[j][32 * b:32 * b + 1, :], reads=[Bkf[j]])
                                    dma(SP, ovs[l, b:b + 1, csl], vfin[j][32 * b:32 * b + 1, :], reads=[Bvf[j]])
                                op(POOL, lambda j=j: G.tensor_copy(out=qs_tok[:, csl], in_=qb[j][:]), reads=[Bqb[j]], writes=[Bst])
                                op(POOL, lambda j=j: G.tensor_copy(out=ks_tok[:, csl], in_=kb_[j][:]), reads=[Bkb[j]], writes=[Bst])
                                op(POOL, lambda j=j: G.tensor_copy(out=vs_tok[:, csl], in_=vfin[j][:]), reads=[Bvf[j]], writes=[Bst])
                        def a1_back(i):
                            j = i % 2
                            op(PE, lambda j=j: TE.transpose(PB[j][:, 0:128], qb[j][:], ident_bf[:]), reads=[Bqb[j], Bc], writes=[BPB[j]])
                            op(PE, lambda j=j: TE.transpose(PB[j][:, 128:256], kb_[j][:], ident_bf[:]), reads=[Bkb[j], Bc], writes=[BPB[j]], inc=True)
                            op(DVE, lambda i=i, j=j: V.tensor_copy(out=qkT[:, :, tcols(i)], in_=PB[j][:, 0:256].rearrange("p (a t) -> p a t", a=2)),
                               reads=[BPB[j]], writes=[BqkT])
                        if os.environ.get("KPIPE", "0") == "1":
                            a1_front(0)
                            for i in range(1, NT):
                                a1_front(i)
                                a1_back(i - 1)
                            a1_back(NT - 1)
                        else:
                            for i in range(NT):
                                a1_front(i)
                                a1_back(i)
                        T.ck(f"A1_{l}_{c}")
                        for tg in range(5):
                            n = 512 if tg < 4 else 128
                            cols = slice(tg * 512, tg * 512 + n)
                            k = 2 + tg % 2
                            P, BP = PS[k], BPS[k]
                            hb_ = BhT[4 * tg:4 * tg + 4] if tg < 4 else [BhT[16]]
                            for kc in range(8):
                                op(PE, lambda P=P, kc=kc, cols=cols, n=n: TE.matmul(P[:, 0:n], lhsT=wt[:, kc, 384:512], rhs=hT[:, kc, cols],
                                                                                  start=(kc == 0), stop=(kc == 7)), reads=hb_ + [Bw], writes=[BP])
                            silu_from_psum(P, n, zT[:, cols], ezt, Bez, BP, [BzT])
                        op(POOL, lambda c=c: G.tensor_copy(out=zsamp[:, c, :], in_=zT[:, 2048:TOK:32]), reads=[BzT], writes=[Bst])

                        T.ck(f"A2_{l}_{c}")
                        def attn_front(qsl, kcur, kprev, vcur, vprev, first):
                            n_ = blk_ctr[0]
                            blk_ctr[0] += 1
                            jj = n_ % 2
                            Sb = [(PS[n_ % 2], BPS[n_ % 2]), (PS[2 + n_ % 2], BPS[2 + n_ % 2])]
                            kbs = [1] if kprev is None else [0, 1]
                            lo = 0 if kprev is not None else 128
                            for h in range(2):
                                S, BS = Sb[h]
                                hs = slice(h * 64, (h + 1) * 64)
                                for kb in kbs:
                                    ks = kprev if kb == 0 else kcur
                                    op(PE, lambda S=S, hs=hs, ks=ks, kb=kb: TE.matmul(S[:, kb * 128:(kb + 1) * 128], lhsT=qkT[hs, 1, ks], rhs=qkT[hs, 0, qsl],
                                                                                   start=True, stop=True), reads=[BqkT], writes=[BS], inc=(kb == 1))
                            for h in range(2):
                                S, BS = Sb[h]
                                op(ACT, lambda S=S, h=h: A.activation(out=pp[jj][:, h * 256 + lo:(h + 1) * 256], in_=S[:, lo:256], func=AF.Exp),
                                   reads=[BS], writes=[Bp[jj]], inc=True)
                            pv = pp[jj][:].rearrange("p (h x) -> p h x", h=2)[:, :, lo:256]
                            mv_ = mask01[:].rearrange("p (h x) -> p h x", h=2)[:, :, lo:256]
                            op(POOL, lambda pv=pv, mv_=mv_: G.tensor_tensor(out=pv, in0=pv, in1=mv_, op=ALU.mult), reads=[Bp[jj], Bc], writes=[Bp[jj]], inc=True)
                            return (n_, qsl, kbs, vcur, vprev, first)

                        def attn_back(ctx):
                            n_, qsl, kbs, vcur, vprev, first = ctx
                            jj = n_ % 2
                            U, BU = PS[4 + n_ % 2], BPS[4 + n_ % 2]
                            for part in range(2):
                                for h in range(2):
                                    hs = slice(h * 64, (h + 1) * 64)
                                    for idx, kb in enumerate(kbs):
                                        vb = vprev if kb == 0 else vcur
                                        lhsT = vb[:, hs] if part == 0 else ones_bf[:, 0:64]
                                        o0 = (h * 2 + kb) * 128
                                        op(PE, lambda U=U, hs=hs, part=part, lhsT=lhsT, o0=o0, idx=idx: TE.matmul(
                                            U[hs, part * 128:(part + 1) * 128], lhsT=lhsT, rhs=pp[jj][:, o0:o0 + 128],
                                            start=(idx == 0), stop=(idx == len(kbs) - 1)), reads=[Bp[jj], Bvn, Bvo, Bc], writes=[BU],
                                           inc=(part == 1 and h == 1 and idx == len(kbs) - 1))
                            uv = U[:, 0:256].rearrange("p (a q) -> p a q", a=2)
                            if first:
                                op(DVE, lambda: V.tensor_copy(out=UD[:, :, qsl], in_=uv), reads=[BU], writes=[BUD], inc=True)
                            else:
                                op(DVE, lambda: V.tensor_tensor(out=UD[:, :, qsl], in0=UD[:, :, qsl], in1=uv, op=ALU.add), reads=[BU, BUD], writes=[BUD], inc=True)

                        def run_blocks(specs):
                            prev = None
                            for sp_ in specs:
                                ctx = attn_front(*sp_)
                                if prev is not None:
                                    attn_back(prev)
                                prev = ctx
                            attn_back(prev)

                        run_blocks([(tcols(i), tcols(i), tcols(i - 1) if i > 0 else None, vnat[:, i, :], vnat[:, i - 1, :] if i > 0 else None, True)
                                    for i in range(16)])
                        T.ck(f"A4a_{l}_{c}")
                        for dil in (4, 16):
                            def tsl(blk):
                                if dil == 4:
                                    jb, r4 = blk // 4, blk % 4
                                    return slice(512 * jb + r4, 512 * (jb + 1), 4)
                                return slice(blk, 2048, 16)
                            for g4 in range(4):
                                k = 2 + g4 % 2
                                P, BP = PS[k], BPS[k]
                                for bi in range(4):
                                    blk = 4 * g4 + bi
                                    for kc in range(8):
                                        op(PE, lambda P=P, bi=bi, blk=blk, kc=kc: TE.matmul(P[:, tcols(bi)], lhsT=hT[:, kc, tsl(blk)], rhs=wt[:, kc, 256:384],
                                                                                           start=(kc == 0), stop=(kc == 7)), reads=BhT[0:16] + [Bw], writes=[BP])
                                op(ACT, lambda P=P, g4=g4: A.activation(out=vord[:, 4 * g4:4 * g4 + 4, :], in_=P[:].rearrange("p (a t) -> p a t", a=4), func=AF.Copy),
                                   reads=[BP], writes=[Bvo])
                            run_blocks([(tsl(blk), tsl(blk), tsl(blk - 4), vord[:, blk, :], vord[:, blk - 4, :], False) if (dil == 4 and blk >= 4)
                                        else (tsl(blk), tsl(blk), None, vord[:, blk, :], None, False) for blk in range(16)])
                        T.ck(f"A4b_{l}_{c}")
                        op(ACT, lambda: A.activation(out=UD[:, 1, :], in_=UD[:, 1, :], func=AF.Ln), reads=[BUD], writes=[BUD])
                        op(ACT, lambda: A.activation(out=UD[:, 1, :], in_=UD[:, 1, :], func=AF.Exp, scale=-1.0), reads=[BUD], writes=[BUD])
                        op(DVE, lambda: V.tensor_tensor(out=UD[:, 0, :], in0=UD[:, 0, :], in1=UD[:, 1, :], op=ALU.mult), reads=[BUD], writes=[BUD])
                        op(POOL, lambda c=c: G.tensor_tensor(out=yT[:, c, 0:2048], in0=UD[:, 0, :], in1=zT[:, 0:2048], op=ALU.mult),
                           reads=[BUD, BzT], writes=[ByTp[c]])
                    T.barrier()

                T.ck(f"A4_{l}")
                with contextlib.ExitStack() as ar:
                    selb = sbt(ar, "s_selb", [128, 4, 128], BF16)
                    selbf = sbt(ar, "s_selbf", [128, 512], F32)
                    Kr = [sbt(ar, f"s_Kr{j}", [128, 768], BF16) for j in range(2)]
                    Vr = [sbt(ar, f"s_Vr{j}", [128, 3, 768], BF16) for j in range(2)]
                    prod = sbt(ar, "s_prod", [128, 768], F32)
                    sc = sbt(ar, "s_sc", [128, 12], F32)
                    pall = [sbt(ar, f"s_pall{j}", [128, 3, 12], BF16) for j in range(2)]
                    pnew = sbt(ar, "s_pnew", [128, 12], F32)
                    pnm = sbt(ar, "s_pnm", [128, 4, 12], BF16)
                    rd = sbt(ar, "s_rd", [128, 6], F32)
                    osb = sbt(ar, "s_osb", [128, 6], F32)
                    Bsel, Bprod, Bsc, Bpn, Bpnm, Brd, Bos = Buf("selb"), Buf("prod"), Buf("sc"), Buf("pnew"), Buf("pnm"), Buf("rd"), Buf("osb")
                    BKr = [Buf("Kr0"), Buf("Kr1")]
                    BVr = [Buf("Vr0"), Buf("Vr1")]
                    Bpa = [Buf("pa0"), Buf("pa1")]
                    op(POOL, lambda: G.memset(selbf[:], 1.0), writes=[Bsel])
                    op(POOL, lambda: G.affine_select(out=selbf[:].rearrange("p (b m) -> p b m", b=4), in_=selbf[:].rearrange("p (b m) -> p b m", b=4),
                                                     pattern=[[-32, 4], [0, 128]], compare_op=ALU.is_equal, fill=0.0, base=0, channel_multiplier=1),
                       reads=[Bsel], writes=[Bsel])
                    op(DVE, lambda: V.tensor_copy(out=selb[:].rearrange("p b m -> p (b m)"), in_=selbf[:]), reads=[Bsel], writes=[Bsel])
                    op(DVE, lambda: V.tensor_tensor(out=prod[:], in0=qs_tok[:], in1=ks_tok[:], op=ALU.mult), reads=[Bst], writes=[Bprod])
                    op(DVE, lambda: V.tensor_reduce(out=sc[:], in_=prod[:].rearrange("p (h e) -> p h e", e=64), axis=AX.X, op=ALU.add),
                       reads=[Bprod], writes=[Bsc])
                    op(ACT, lambda: A.activation(out=pnew[:], in_=sc[:], func=AF.Exp), reads=[Bsc], writes=[Bpn])
                    for b in range(4):
                        op(DVE, lambda b=b: V.tensor_scalar(out=pnm[:, b, :], in0=pnew[:], scalar1=rowmask_s3[:, b:b + 1], scalar2=None, op0=ALU.mult),
                           reads=[Bpn, Bc], writes=[Bpnm])
                    kctr = 0
                    for b in range(4):
                        bj = b % 2
                        for pi, (r0, step) in enumerate([(1920, 1), (1536, 4), (0, 16)]):
                            rows = slice(r0, 2048, step)
                            kj = kctr % 2
                            kctr += 1
                            dma(POOL, Kr[kj][:], ck[l, b, rows, :], writes=[BKr[kj]])
                            dma(POOL, Vr[bj][:, pi, :], cv[l, b, rows, :], writes=[BVr[bj]])
                            for hf in range(2):
                                op(PE, lambda b=b, hf=hf: TE.matmul(PS[hf][:, 0:384], lhsT=selb[:, b, :], rhs=qs_tok[:, hf * 384:(hf + 1) * 384],
                                                                  start=True, stop=True), reads=[Bsel, Bst], writes=[BPS[hf]])
                            for hf in range(2):
                                op(DVE, lambda kj=kj, hf=hf: V.tensor_tensor(out=prod[:, hf * 384:(hf + 1) * 384], in0=Kr[kj][:, hf * 384:(hf + 1) * 384],
                                                                           in1=PS[hf][:, 0:384], op=ALU.mult), reads=[BKr[kj], BPS[hf]], writes=[Bprod])
                            op(DVE, lambda: V.tensor_reduce(out=sc[:], in_=prod[:].rearrange("p (h e) -> p h e", e=64), axis=AX.X, op=ALU.add),
                               reads=[Bprod], writes=[Bsc])
                            op(ACT, lambda bj=bj, pi=pi: A.activation(out=pall[bj][:, pi, :], in_=sc[:], func=AF.Exp), reads=[Bsc], writes=[Bpa[bj]])
                        PO, BPO = PS[2 + bj], BPS[2 + bj]
                        for part in range(2):
                            for c in range(6):
                                o0 = part * 16 + 2 * c
                                for pi in range(3):
                                    lhsT = Vr[bj][:, pi, c * 128:(c + 1) * 128] if part == 0 else ones_bf[:]
                                    op(PE, lambda PO=PO, o0=o0, lhsT=lhsT, pi=pi, c=c: TE.matmul(PO[:, o0:o0 + 2], lhsT=lhsT, rhs=pall[bj][:, pi, 2 * c:2 * c + 2],
                                                                                              start=(pi == 0), stop=False), reads=[BVr[bj], Bpa[bj], Bc], writes=[BPO])
                                lhsT = vs_tok[:, c * 128:(c + 1) * 128] if part == 0 else ones_bf[:]
                                op(PE, lambda PO=PO, o0=o0, lhsT=lhsT, b=b, c=c: TE.matmul(PO[:, o0:o0 + 2], lhsT=lhsT, rhs=pnm[:, b, 2 * c:2 * c + 2],
                                                                                        start=False, stop=True), reads=[Bst, Bpnm, Bc], writes=[BPO])
                        for hh in range(2):
                            rws = slice(hh * 64, hh * 64 + 64)
                            op(DVE, lambda PO=PO, rws=rws, hh=hh: V.reciprocal(out=rd[rws, :], in_=PO[rws, 16 + hh:28:2]), reads=[BPO], writes=[Brd])
                            op(DVE, lambda PO=PO, rws=rws, hh=hh: V.tensor_tensor(out=osb[rws, :], in0=PO[rws, hh:12:2], in1=rd[rws, :], op=ALU.mult),
                               reads=[BPO, Brd], writes=[Bos])
                        op(DVE, lambda b=b: V.tensor_tensor(out=yT[:, :, 2048 + 32 * b:2048 + 32 * b + 1], in0=osb[:].unsqueeze(2),
                                                            in1=zsamp[:, :, b:b + 1], op=ALU.mult), reads=[Bos, Bst], writes=ByTs)
                    T.barrier()
                T.ck(f"A5_{l}")
                out_proj(l, 6)
                T.barrier()
                T.ck(f"A_{l}")

            with contextlib.ExitStack() as ar:
                vn = sbt(ar, "b_vn", [128, NT, 512], BF16)
                lng_t = sbt(ar, "b_lng", [128, 512], F32)
                lnb_t = sbt(ar, "b_lnb", [128, 512], F32)
                st6 = sbt(ar, "b_st6", [128, 6], F32)
                mv = sbt(ar, "b_mv", [128, 2], F32)
                lnv = sbt(ar, "b_lnv", [128, 1], F32)
                rsb = sbt(ar, "b_rsb", [128, 1], F32)
                vnf = sbt(ar, "b_vnf", [128, 512], F32)
                vnf2 = [sbt(ar, f"b_vnf2{j}", [128, 512], F32) for j in range(2)]
                wl = sbt(ar, "b_wl", [128, 8, 128], F32)
                wlb = sbt(ar, "b_wlb", [128, 8, 128], BF16)
                wT = sbt(ar, "b_wT", [128, 8, 128], BF16)
                wTs = sbt(ar, "b_wTs", [128, 8, 128], BF16)
                w00 = sbt(ar, "b_w00", [128, 8], F32)
                bsf = sbt(ar, "b_bsf", [8, 128], F32)
                bsb = sbt(ar, "b_bsb", [8, 128], BF16)
                bs0 = sbt(ar, "b_bs0", [8, 128], BF16)
                ezb = sbt(ar, "b_ez", [128, 512], F32)
                t1 = sbt(ar, "b_t1", [128, 512], F32)
                Bvn_, Blg, Bst6, Bmv, Blnv, Brsb, Bvnf = Buf("vn"), Buf("lng"), Buf("st6"), Buf("mv"), Buf("lnv"), Buf("rsb"), Buf("vnf")
                Bvnf2 = [Buf("vnf20"), Buf("vnf21")]
                Bwl, Bwlb, BwT, Bbs, Bezb, Bt1 = Buf("wl"), Buf("wlb"), Buf("wT"), Buf("bs"), Buf("ezb"), Buf("t1")
                load_wo(l, 768, 4)
                bsel = sbt(ar, "b_bsel", [8, 512], BF16)
                bself = sbt(ar, "b_bself", [8, 512], F32)
                Bbsl = Buf("bsel")
                op(POOL, lambda: G.memset(bself[:], 1.0), writes=[Bbsl])
                op(POOL, lambda: G.affine_select(out=bself[:], in_=bself[:], pattern=[[1, 512]], compare_op=ALU.is_ge,
                                                 fill=0.0, base=0, channel_multiplier=-64), reads=[Bbsl], writes=[Bbsl])
                op(POOL, lambda: G.affine_select(out=bself[:], in_=bself[:], pattern=[[-1, 512]], compare_op=ALU.is_ge,
                                                 fill=0.0, base=63, channel_multiplier=64), reads=[Bbsl], writes=[Bbsl])
                op(DVE, lambda: V.tensor_copy(out=bsel[:], in_=bself[:]), reads=[Bbsl], writes=[Bbsl])
                dma(SP, lng_t[:], lng[l].partition_broadcast(128), writes=[Blg])
                dma(SP, lnb_t[:], lnb[l].partition_broadcast(128), writes=[Blg])
                dma(SP, wl[:], sgw[l].rearrange("g t s -> t g s"), writes=[Bwl])
                dma(SP, bsf[:], sgb[l], writes=[Bbs])
                u = uidx
                uidx += 1
                load_unit(u)
                load_unit(u + 1)
                wt, Bw = WA[u % 2], BW[u % 2]
                for i in range(NT):
                    j = i % 2
                    P, BP = PS[j], BPS[j]
                    for kc in range(8):
                        op(PE, lambda P=P, kc=kc, i=i: TE.matmul(P[:], lhsT=hT[:, kc, tcols(i)], rhs=wt[:, kc, :], start=(kc == 0), stop=(kc == 7)),
                           reads=[BhT[i], Bw], writes=[BP])
                    op(DVE, lambda P=P: V.bn_stats(out=st6[:], in_=P[:]), reads=[BP], writes=[Bst6])
                    op(DVE, lambda: V.bn_aggr(out=mv[:], in_=st6[:]), reads=[Bst6], writes=[Bmv])
                    op(ACT, lambda: A.activation(out=lnv[:], in_=mv[:, 1:2], func=AF.Ln, scale=1.0, bias=epsc[:, 0:1]), reads=[Bmv, Bc], writes=[Blnv])
                    op(ACT, lambda: A.activation(out=rsb[:], in_=lnv[:], func=AF.Exp, scale=-0.5), reads=[Blnv], writes=[Brsb])
                    op(DVE, lambda P=P: V.tensor_scalar(out=vnf[:], in0=P[:], scalar1=mv[:, 0:1], scalar2=rsb[:, 0:1], op0=ALU.subtract, op1=ALU.mult),
                       reads=[BP, Bmv, Brsb], writes=[Bvnf])
                    op(POOL, lambda: G.tensor_tensor(out=vnf[:], in0=vnf[:], in1=lng_t[:], op=ALU.mult), reads=[Bvnf, Blg], writes=[Bvnf])
                    op(POOL, lambda j=j: G.tensor_tensor(out=vnf2[j][:], in0=vnf[:], in1=lnb_t[:], op=ALU.add), reads=[Bvnf, Blg], writes=[Bvnf2[j]])
                    op(ACT, lambda i=i, j=j: A.activation(out=vn[:, i, :], in_=vnf2[j][:], func=AF.Copy), reads=[Bvnf2[j]], writes=[Bvn_])
                    if i == 16:
                        for b in range(4):
                            dma(SP, osg[l, b:b + 1, :], vnf2[j][32 * b:32 * b + 1, :], reads=[Bvnf2[j]])
                T.ck(f"B1_{l}")
                op(DVE, lambda: V.tensor_copy(out=wlb[:], in_=wl[:]), reads=[Bwl], writes=[Bwlb])
                for g in range(8):
                    op(PE, lambda g=g: TE.transpose(PB[0][:, tcols(g)], wlb[:, g, :], ident_bf[:]), reads=[Bwlb, Bc], writes=[BPB[0]])
                op(DVE, lambda: V.tensor_tensor(out=wT[:], in0=PB[0][:].rearrange("p (g t) -> p g t", g=8),
                                                in1=maskcur[:].unsqueeze(1).to_broadcast([128, 8, 128]), op=ALU.mult), reads=[BPB[0], Bc], writes=[BwT])
                op(PE, lambda: TE.matmul(PS[2][:, 0:8], lhsT=ones_f[0:1, :], rhs=wl[0:1, :, 0], start=True, stop=True), reads=[Bwl, Bc], writes=[BPS[2]])
                op(DVE, lambda: V.tensor_copy(out=w00[:], in_=PS[2][:, 0:8]), reads=[BPS[2]], writes=[BwT])
                op(DVE, lambda: V.tensor_tensor(out=wTs[:], in0=ident_bf[:].unsqueeze(1).to_broadcast([128, 8, 128]),
                                                in1=w00[:].unsqueeze(2).to_broadcast([128, 8, 128]), op=ALU.mult), reads=[BwT, Bc], writes=[BwT])
                op(DVE, lambda: V.tensor_copy(out=bsb[:], in_=bsf[:]), reads=[Bbs], writes=[Bbs])
                op(DVE, lambda: V.tensor_copy(out=bs0[:], in_=bsf[:, 0:1].to_broadcast([8, 128])), reads=[Bbs], writes=[Bbs])
                T.ck(f"B2_{l}")
                for cb in range(4):
                    u = uidx
                    uidx += 1
                    load_unit(u)
                    load_unit(u + 1)
                    wt, Bw = WA[u % 2], BW[u % 2]
                    for tg in range(5):
                        n = 512 if tg < 4 else 128
                        cols = slice(tg * 512, tg * 512 + n)
                        hb_ = BhT[4 * tg:4 * tg + 4] if tg < 4 else [BhT[16]]
                        Pu, BPu = PS[0 + tg % 2], BPS[0 + tg % 2]
                        Pz, BPz = PS[2 + tg % 2], BPS[2 + tg % 2]
                        Pm, BPm = PS[4 + tg % 2], BPS[4 + tg % 2]
                        for kc in range(8):
                            op(PE, lambda Pu=Pu, kc=kc, cols=cols, n=n: TE.matmul(Pu[:, 0:n], lhsT=wt[:, kc, 0:128], rhs=hT[:, kc, cols],
                                                                                start=(kc == 0), stop=(kc == 7)), reads=hb_ + [Bw], writes=[BPu])
                        for kc in range(8):
                            op(PE, lambda Pz=Pz, kc=kc, cols=cols, n=n: TE.matmul(Pz[:, 0:n], lhsT=wt[:, kc, 128:256], rhs=hT[:, kc, cols],
                                                                                start=(kc == 0), stop=(kc == 7)), reads=hb_ + [Bw], writes=[BPz])
                        for ti in range(n // 128):
                            i = 4 * tg + ti
                            for gg in range(2):
                                g = 2 * cb + gg
                                rws = slice(gg * 64, gg * 64 + 64)
                                wmat = wT if i < 16 else wTs
                                bmat = bsb if i < 16 else bs0
                                op(PE, lambda Pm=Pm, rws=rws, ti=ti, i=i, g=g, wmat=wmat: TE.matmul(
                                    Pm[rws, tcols(ti)], lhsT=vn[:, i, g * 64:(g + 1) * 64], rhs=wmat[:, g, :], start=True, stop=False),
                                   reads=[Bvn_, BwT], writes=[BPm])
                                op(PE, lambda Pm=Pm, rws=rws, ti=ti, g=g, bmat=bmat: TE.matmul(
                                    Pm[rws, tcols(ti)], lhsT=bsel[0:8, g * 64:(g + 1) * 64], rhs=bmat[0:8, :], start=False, stop=True),
                                   reads=[Bbs, Bbsl], writes=[BPm])
                        silu_from_psum(Pz, n, t1[:, 0:n], ezb, Bezb, BPz, [Bt1])
                        op(DVE, lambda Pu=Pu, n=n: V.tensor_tensor(out=t1[:, 0:n], in0=t1[:, 0:n], in1=Pu[:, 0:n], op=ALU.mult), reads=[Bt1, BPu], writes=[Bt1])
                        op(DVE, lambda Pm=Pm, n=n, cols=cols, cb=cb: V.tensor_tensor(out=yT[:, cb, cols], in0=t1[:, 0:n], in1=Pm[:, 0:n], op=ALU.mult),
                           reads=[Bt1, BPm], writes=[ByTp[cb] if tg < 4 else ByTs[cb]])
                T.barrier()
                T.ck(f"B3_{l}")
                out_proj(l, 4)
                T.barrier()
                T.ck(f"B_{l}")

            with contextlib.ExitStack() as ar:
                scanmask = sbt(ar, "c_scanm", [128, 512], F32)
                names = ["ef", "f", "g", "G", "eG", "kk", "kt", "khT", "eq", "q", "qt", "ez", "zs", "ln"]
                alias = {"eNG": "g", "rs": "ln", "sq": "eq", "o": "ez"}
                t = {nm: sbt(ar, "c_" + nm, [128, 512], F32) for nm in names}
                Bt_ = {nm: Buf("c_" + nm) for nm in names}
                for a_, b_ in alias.items():
                    t[a_] = t[b_]
                    Bt_[a_] = Bt_[b_]
                tv = sbt(ar, "c_tv", [128, 4, 128], F32)
                khtok = sbt(ar, "c_khtok", [128, 4, 4, 128], F32)
                ATm = sbt(ar, "c_ATm", [128, 4, 128], F32)
                Sring = sbt(ar, "c_Sring", [128, 9, 128], F32)
                BSr = [Buf(f"Sr{j}") for j in range(9)]
                S0 = [Sring[:, 1, :], Sring[:, 2, :]]
                Sn = [Sring[:, 3, :], Sring[:, 4, :]]
                BS0 = [BSr[1], BSr[2]]
                BSn = [BSr[3], BSr[4]]
                Btv, Bkh, BAT, Bsm = Buf("tv"), Buf("khtok"), Buf("ATm"), Buf("scanm")
                load_wo(l, 1280, 6)
                op(DVE, lambda: V.memset(scanmask[:], 1.0), writes=[Bsm])
                op(DVE, lambda: V.memset(scanmask[:].rearrange("p (c j) -> p c j", j=32)[:, :, 0:1], 0.0), reads=[Bsm], writes=[Bsm])
                for hd in range(6):
                    u = uidx
                    uidx += 1
                    load_unit(u)
                    load_unit(u + 1)
                    wt, Bw = WA[u % 2], BW[u % 2]
                    lbc = lbT[:, l * 6 + hd:l * 6 + hd + 1]
                    omc = omlbT[:, l * 6 + hd:l * 6 + hd + 1]
                    hgc = hgT[:, l * 6 + hd:l * 6 + hd + 1]
                    op(DVE, lambda: V.memset(Sring[:, 0, :], 0.0), writes=[BSr[0]])
                    for tg in range(5):
                        n = 512 if tg < 4 else 128
                        nti = n // 128
                        cols = slice(tg * 512, tg * 512 + n)
                        hb_ = BhT[4 * tg:4 * tg + 4] if tg < 4 else [BhT[16]]
                        for pi_, (P, BP, w0) in enumerate([(PS[0], BPS[0], 0), (PS[1], BPS[1], 128), (PS[2], BPS[2], 384)]):
                            for kc in range(8):
                                op(PE, lambda P=P, kc=kc, w0=w0, cols=cols, n=n: TE.matmul(P[:, 0:n], lhsT=wt[:, kc, w0:w0 + 128], rhs=hT[:, kc, cols],
                                                                                         start=(kc == 0), stop=(kc == 7)), reads=hb_ + [Bw], writes=[BP])
                        for ti in range(nti):
                            i = 4 * tg + ti
                            for kc in range(8):
                                op(PE, lambda ti=ti, i=i, kc=kc: TE.matmul(PS[3][:, tcols(ti)], lhsT=hT[:, kc, tcols(i)], rhs=wt[:, kc, 256:384],
                                                                         start=(kc == 0), stop=(kc == 7)), reads=[BhT[i], Bw], writes=[BPS[3]])
                        sl = slice(0, n)
                        sigmoid_act(PS[1][:, sl], t["ef"][:, sl], [BPS[1]], Bt_["ef"])
                        op(DVE, lambda: V.tensor_scalar(out=t["f"][:, sl], in0=t["ef"][:, sl], scalar1=omc, scalar2=lbc, op0=ALU.mult, op1=ALU.add),
                           reads=[Bt_["ef"], Bc], writes=[Bt_["f"]])
                        op(POOL, lambda: G.tensor_scalar(out=t["kk"][:, sl], in0=t["f"][:, sl], scalar1=-1.0, scalar2=1.0, op0=ALU.mult, op1=ALU.add),
                           reads=[Bt_["f"]], writes=[Bt_["kk"]])
                        sigmoid_act(PS[0][:, sl], t["eq"][:, sl], [BPS[0]], Bt_["eq"])
                        op(DVE, lambda: V.tensor_tensor(out=t["q"][:, sl], in0=PS[0][:, sl], in1=t["eq"][:, sl], op=ALU.mult), reads=[BPS[0], Bt_["eq"]], writes=[Bt_["q"]])
                        silu_from_psum(PS[2], n, t["zs"][:, sl], t["ez"], Bt_["ez"], BPS[2], [Bt_["zs"]])
                        op(ACT, lambda: A.activation(out=tv[:, 0:nti, :], in_=PS[3][:, sl].rearrange("p (a v) -> p a v", v=128), func=AF.Copy),
                           reads=[BPS[3]], writes=[Btv])
                        if tg < 4:
                            op(ACT, lambda: A.activation(out=t["g"][:], in_=t["f"][:], func=AF.Ln), reads=[Bt_["f"]], writes=[Bt_["g"]])
                            op(DVE, lambda: V.tensor_tensor_scan(out=t["G"][:], data0=scanmask[:], data1=t["g"][:], initial=0.0, op0=ALU.mult, op1=ALU.add),
                               reads=[Bt_["g"], Bsm], writes=[Bt_["G"]])
                            op(ACT, lambda: A.activation(out=t["eG"][:], in_=t["G"][:], func=AF.Exp), reads=[Bt_["G"]], writes=[Bt_["eG"]])
                            op(ACT, lambda: A.activation(out=t["eNG"][:], in_=t["G"][:], func=AF.Exp, scale=-1.0), reads=[Bt_["G"]], writes=[Bt_["eNG"]])
                            op(POOL, lambda: G.tensor_tensor(out=t["kt"][:], in0=t["kk"][:], in1=t["eNG"][:], op=ALU.mult), reads=[Bt_["kk"], Bt_["eNG"]], writes=[Bt_["kt"]])
                            op(POOL, lambda: G.tensor_tensor(out=t["khT"][:].rearrange("p (c j) -> p c j", j=32), in0=t["kt"][:].rearrange("p (c j) -> p c j", j=32),
                                                             in1=t["eG"][:, 31:512:32].unsqueeze(2).to_broadcast([128, 16, 32]), op=ALU.mult),
                               reads=[Bt_["kt"], Bt_["eG"]], writes=[Bt_["khT"]])
                            op(POOL, lambda: G.tensor_tensor(out=t["qt"][:], in0=t["q"][:], in1=t["eG"][:], op=ALU.mult), reads=[Bt_["q"], Bt_["eG"]], writes=[Bt_["qt"]])
                            T.ck(f"C1_{l}_{hd}_{tg}")
                            for ti in range(4):
                                op(PE, lambda ti=ti: TE.transpose(PS[0][:, tcols(ti)], t["khT"][:, tcols(ti)], ident_f[:]), reads=[Bt_["khT"], Bc], writes=[BPS[0]])
                            for ch in range(4):
                                if ch % 2 == 0:
                                    op(ACT, lambda ch=ch: A.activation(out=khtok[:, :, ch, :], in_=PS[0][:].rearrange("p (a k) -> p a k", a=4), func=AF.Copy,
                                                                       scale=rowmask[:, ch:ch + 1]), reads=[BPS[0], Bc], writes=[Bkh])
                                else:
                                    op(DVE, lambda ch=ch: V.tensor_scalar(out=khtok[:, :, ch, :], in0=PS[0][:].rearrange("p (a k) -> p a k", a=4),
                                                                          scalar1=rowmask[:, ch:ch + 1], scalar2=None, op0=ALU.mult), reads=[BPS[0], Bc], writes=[Bkh])
                            for ti in range(4):
                                op(PE, lambda ti=ti: TE.matmul(PS[1][:, tcols(ti)], lhsT=t["kt"][:, tcols(ti)], rhs=t["qt"][:, tcols(ti)], start=True, stop=True),
                                   reads=[Bt_["kt"], Bt_["qt"]], writes=[BPS[1]])
                            op(DVE, lambda: V.tensor_tensor(out=ATm[:], in0=PS[1][:].rearrange("p (a t) -> p a t", a=4),
                                                            in1=blockmask[:].unsqueeze(1).to_broadcast([128, 4, 128]), op=ALU.mult), reads=[BPS[1], Bc], writes=[BAT])
                            T.ck(f"C2_{l}_{hd}_{tg}")
                            banks_ = [0, 1, 3, 4]
                            for nchk in range(16):
                                ti, ch = nchk // 4, nchk % 4
                                bk, col = banks_[nchk // 4], (nchk % 4) * 128
                                op(PE, lambda bk=bk, col=col, ti=ti, ch=ch: TE.matmul(PS[bk][:, col:col + 128], lhsT=khtok[:, ti, ch, :], rhs=tv[:, ti, :], start=True, stop=True),
                                   reads=[Bkh, Btv], writes=[BPS[bk]], inc=(nchk % 4 == 3))
                            for half in range(2):
                                for r_ in range(8):
                                    nchk = half * 8 + r_
                                    bk, col = banks_[nchk // 4], (nchk % 4) * 128
                                    op(DVE, lambda bk=bk, col=col, nchk=nchk, r_=r_: V.scalar_tensor_tensor(
                                        out=Sring[:, r_ + 1, :], in0=Sring[:, r_, :], scalar=t["eG"][:, nchk * 32 + 31:nchk * 32 + 32], in1=PS[bk][:, col:col + 128],
                                        op0=ALU.mult, op1=ALU.add), reads=[BSr[r_], Bt_["eG"], BPS[bk]], writes=[BSr[r_ + 1]])
                                for r_ in range(8):
                                    nchk = half * 8 + r_
                                    ti, ch = nchk // 4, nchk % 4
                                    cc = slice(nchk * 32, nchk * 32 + 32)
                                    op(PE, lambda ti=ti, ch=ch, cc=cc: TE.matmul(PS[2][:, cc], lhsT=tv[:, ti, :], rhs=ATm[:, ti, ch * 32:(ch + 1) * 32], start=True, stop=False),
                                       reads=[Btv, BAT], writes=[BPS[2]])
                                    op(PE, lambda r_=r_, cc=cc: TE.matmul(PS[2][:, cc], lhsT=Sring[:, r_, :], rhs=t["qt"][:, cc], start=False, stop=True),
                                       reads=[BSr[r_], Bt_["qt"]], writes=[BPS[2]], inc=(r_ == 7))
                                op(DVE, lambda: V.tensor_copy(out=Sring[:, 0, :], in_=Sring[:, 8, :]), reads=[BSr[8]], writes=[BSr[0]])
                            no = 512
                        else:
                            op(PE, lambda: TE.transpose(PS[0][:, 0:128], t["kk"][:, 0:128], ident_f[:]), reads=[Bt_["kk"], Bc], writes=[BPS[0]])
                            for b in range(4):
                                op(DVE, lambda b=b: V.tensor_scalar(out=khtok[:, 0, b, :], in0=PS[0][:, 0:128], scalar1=rowmask_s[:, b:b + 1], scalar2=None, op0=ALU.mult),
                                   reads=[BPS[0], Bc], writes=[Bkh])
                            for b in range(4):
                                bj = b % 2
                                dma(SP, S0[bj], st[l, b, hd], writes=[BS0[bj]])
                                ku = 3 + bj
                                op(PE, lambda ku=ku, b=b: TE.matmul(PS[ku][:, 0:128], lhsT=khtok[:, 0, b, :], rhs=tv[:, 0, :], start=True, stop=True),
                                   reads=[Bkh, Btv], writes=[BPS[ku]])
                                op(DVE, lambda bj=bj, ku=ku, b=b: V.scalar_tensor_tensor(out=Sn[bj], in0=S0[bj], scalar=t["f"][:, 32 * b:32 * b + 1],
                                                                                       in1=PS[ku][:, 0:128], op0=ALU.mult, op1=ALU.add),
                                   reads=[BS0[bj], Bt_["f"], BPS[ku]], writes=[BSn[bj]])
                                dma(SP, ohs[l, b, hd], Sn[bj], reads=[BSn[bj]])
                                op(PE, lambda bj=bj, b=b: TE.matmul(PS[2][:, b:b + 1], lhsT=Sn[bj], rhs=t["q"][:, 32 * b:32 * b + 1], start=True, stop=True),
                                   reads=[BSn[bj], Bt_["q"]], writes=[BPS[2]])
                            no = 4
                        T.ck(f"C3_{l}_{hd}_{tg}")
                        so = slice(0, no)
                        op(ACT, lambda: A.activation(out=t["o"][:, so], in_=PS[2][:, so], func=AF.Copy), reads=[BPS[2]], writes=[Bt_["o"]])
                        op(ACT, lambda: A.activation(out=t["sq"][:, so], in_=PS[2][:, so], func=AF.Square), reads=[BPS[2]], writes=[Bt_["sq"]])
                        op(PE, lambda: TE.matmul(PS[5][:, so], lhsT=ones_f[:], rhs=t["sq"][:, so], start=True, stop=True), reads=[Bt_["sq"], Bc], writes=[BPS[5]])
                        op(ACT, lambda: A.activation(out=t["ln"][:, so], in_=PS[5][:, so], func=AF.Ln, scale=1.0 / 128, bias=epsc[:, 0:1]), reads=[BPS[5], Bc], writes=[Bt_["ln"]])
                        op(ACT, lambda: A.activation(out=t["rs"][:, so], in_=t["ln"][:, so], func=AF.Exp, scale=-0.5), reads=[Bt_["ln"]], writes=[Bt_["rs"]])
                        op(DVE, lambda: V.tensor_tensor(out=t["o"][:, so], in0=t["o"][:, so], in1=t["rs"][:, so], op=ALU.mult), reads=[Bt_["o"], Bt_["rs"]], writes=[Bt_["o"]])
                        if tg < 4:
                            op(DVE, lambda cols=cols, hd=hd: V.scalar_tensor_tensor(out=yT[:, hd, cols], in0=t["o"][:], scalar=hgc, in1=t["zs"][:], op0=ALU.mult, op1=ALU.mult),
                               reads=[Bt_["o"], Bt_["zs"], Bc], writes=[ByTp[hd]])
                        else:
                            op(DVE, lambda hd=hd: V.scalar_tensor_tensor(out=yT[:, hd, 2048:TOK:32], in0=t["o"][:, 0:4], scalar=hgc, in1=t["zs"][:, 0:128:32],
                                                                         op0=ALU.mult, op1=ALU.mult), reads=[Bt_["o"], Bt_["zs"], Bc], writes=[ByTs[hd]])
                        T.ck(f"C4_{l}_{hd}_{tg}")
                        if tg == 3:
                            dma(SP, ohp[l, hd], Sring[:, 0, :], reads=[BSr[0]])
                T.barrier()
                T.ck(f"C5_{l}")
                out_proj(l, 6)
                T.barrier()
                T.ck(f"L_{l}")

        T.force = True
        for i in range(16):
            dma(SP, yp[i * 128:(i + 1) * 128, :], xres[:, i, :], reads=[Bx[i]])
        for b in range(4):
            dma(SP, ys[b:b + 1, :], xres[32 * b:32 * b + 1, 16, :], reads=[Bx[16]])
        for Q in (SP, POOL):
            for s in Q.slots:
                if s.val > 0:
                    T._wait(SP, s.sem, s.val)
    return nc


_NC_CACHE = {}


def kernel(x_prompt, x_sample, cache_k, cache_v, state_hgrn, norm_g, w_in, q_norm_g, k_norm_g,
           sgu_ln_g, sgu_ln_b, sgu_w, sgu_b, hgrn_lb_logits, hgrn_norm_g, w_out):
    f = lambda a: np.ascontiguousarray(np.asarray(a, dtype=np.float32))
    x_prompt, x_sample, cache_k, cache_v, state_hgrn = map(f, (x_prompt, x_sample, cache_k, cache_v, state_hgrn))
    shared = {
        "norm_g": f(norm_g), "w_in": f(w_in), "q_norm_g": f(q_norm_g), "k_norm_g": f(k_norm_g),
        "sgu_ln_g": f(sgu_ln_g), "sgu_ln_b": f(sgu_ln_b), "sgu_w": f(sgu_w), "sgu_b": f(sgu_b),
        "hgrn_lb_logits": f(hgrn_lb_logits), "hgrn_norm_g": f(hgrn_norm_g), "w_out": f(w_out),
    }
    in_maps = []
    for c in range(NCORES):
        sb = slice(4 * c, 4 * c + 4)
        m = dict(shared)
        m["xp"] = np.ascontiguousarray(x_prompt[c])
        m["xs"] = np.ascontiguousarray(x_sample[sb, 0, :])
        m["ck"] = np.ascontiguousarray(cache_k[:, sb].reshape(2, 4, 2048, 768))
        m["cv"] = np.ascontiguousarray(cache_v[:, sb].reshape(2, 4, 2048, 768))
        m["st"] = np.ascontiguousarray(state_hgrn[:, sb])
        in_maps.append(m)
    if "nc" not in _NC_CACHE:
        _NC_CACHE["nc"] = build_nc()
    res = run_bass_kernel_spmd(_NC_CACHE["nc"], in_maps, core_ids=list(range(NCORES)))
    R = res.results
    y_prompt = np.stack([R[c]["yp"] for c in range(NCORES)]).astype(np.float32)
    y_sample = np.concatenate([R[c]["ys"] for c in range(NCORES)])[:, None, :].astype(np.float32)
    nkp = np.stack([R[c]["okp"] for c in range(NCORES)], axis=1).reshape(2, 8, 2048, 12, 64).astype(np.float32)
    nvp = np.stack([R[c]["ovp"] for c in range(NCORES)], axis=1).reshape(2, 8, 2048, 12, 64).astype(np.float32)
    nks = np.concatenate([R[c]["oks"] for c in range(NCORES)], axis=1).reshape(2, 32, 1, 12, 64).astype(np.float32)
    nvs = np.concatenate([R[c]["ovs"] for c in range(NCORES)], axis=1).reshape(2, 32, 1, 12, 64).astype(np.float32)
    nsg = np.concatenate([R[c]["osg"] for c in range(NCORES)], axis=1).reshape(2, 32, 1, 512).astype(np.float32)
    nhp = np.stack([R[c]["ohp"] for c in range(NCORES)], axis=1).astype(np.float32)
    nhs = np.concatenate([R[c]["ohs"] for c in range(NCORES)], axis=1).astype(np.float32)
    return (y_prompt, y_sample, nkp, nvp, nks, nvs, nsg, nhp, nhs)
```

```python
import bisect
import contextlib
import os

import numpy as np

import concourse.bass as bass
import concourse.mybir as mybir
from concourse.bass_utils import run_bass_kernel_spmd

F32 = mybir.dt.float32
BF16 = mybir.dt.bfloat16
AF = mybir.ActivationFunctionType
ALU = mybir.AluOpType
AX = mybir.AxisListType

NCORES = 8
NT = 17
TOK = NT * 128
EPS = 1e-6


class Eng:
    def __init__(self, name, eng, sem):
        self.name, self.eng, self.sem = name, eng, sem
        self.nseq = 0
        self.cnt = 0
        self.last = None
        self.inc_seq = []
        self.waited = {}
        self.slots = []
        self.rr = 0


class Slot:
    def __init__(self, sem):
        self.sem = sem
        self.val = 0


class Buf:
    __slots__ = ("name", "w", "r")

    def __init__(self, name):
        self.name = name
        self.w = None
        self.r = {}


class Tracker:
    def __init__(self):
        self.engs = []
        self.stopped = False
        self.force = False
        self.stop_at = os.environ.get("KSTOP", "")

    def ck(self, name):
        if self.stop_at and name == self.stop_at:
            self.stopped = True

    def resolve(self, ev):
        if ev[0] == "s":
            return ev[1], ev[2]
        E, seq = ev[1], ev[2]
        i = bisect.bisect_left(E.inc_seq, seq)
        if i < len(E.inc_seq):
            return E.sem, i + 1
        E.last.then_inc(E.sem, 1)
        E.cnt += 1
        E.inc_seq.append(E.nseq)
        return E.sem, E.cnt

    def _wait(self, E, sem, val):
        if E.waited.get(sem.num, 0) < val:
            E.eng.wait_ge(sem, val)
            E.waited[sem.num] = val

    def deps(self, E, reads, writes):
        evs = []
        for b in reads:
            if b.w is not None:
                evs.append(b.w)
        for b in writes:
            if b.w is not None:
                evs.append(b.w)
            for k, ev in b.r.items():
                if ev[0] == "e" and ev[1] is E:
                    continue
                evs.append(ev)
        need = {}
        for ev in evs:
            if ev[0] == "e" and ev[1] is E and E.name == "pe":
                continue
            sem, val = self.resolve(ev)
            if need.get(sem.num, (None, 0))[1] < val:
                need[sem.num] = (sem, val)
        for num, (sem, val) in need.items():
            self._wait(E, sem, val)

    def op(self, E, fn, reads=(), writes=(), inc=None):
        if self.stopped and not self.force:
            return None
        self.deps(E, reads, writes)
        ins = fn()
        E.nseq += 1
        E.last = ins
        if inc and os.environ.get("KINC", "1") == "1":
            ins.then_inc(E.sem, 1)
            E.cnt += 1
            E.inc_seq.append(E.nseq)
        ev = ("e", E, E.nseq)
        for b in writes:
            b.w = ev
            b.r = {}
        for b in reads:
            b.r[E.name] = ev
        return ins

    def dma(self, Q, out, in_, reads=(), writes=()):
        if self.stopped and not self.force:
            return
        self.deps(Q, reads, writes)
        slot = Q.slots[Q.rr % len(Q.slots)]
        Q.rr += 1
        if slot.val > 0:
            self._wait(Q, slot.sem, slot.val)
        Q.eng.dma_start(out=out, in_=in_).then_inc(slot.sem, 16)
        slot.val += 16
        ev = ("s", slot.sem, slot.val)
        for b in writes:
            b.w = ev
            b.r = {}
        for b in reads:
            b.r[("d", slot.sem.num)] = ev

    def barrier(self):
        if self.stopped and not self.force:
            return
        pts = []
        for F in self.engs:
            if F.nseq > 0:
                pts.append(self.resolve(("e", F, F.nseq)))
            for s in F.slots:
                if s.val > 0:
                    pts.append((s.sem, s.val))
        for E in self.engs:
            for sem, val in pts:
                if sem is E.sem and E.name == "pe":
                    continue
                self._wait(E, sem, val)


def build_nc():
    nc = bass.Bass("TRN2", target_bir_lowering=False)

    def din(name, shape):
        return nc.dram_tensor(name, shape, F32, kind="ExternalInput").ap()

    def dout(name, shape):
        return nc.dram_tensor(name, shape, F32, kind="ExternalOutput").ap()

    xp = din("xp", [2048, 1024])
    xs = din("xs", [4, 1024])
    ck = din("ck", [2, 4, 2048, 768])
    cv = din("cv", [2, 4, 2048, 768])
    st = din("st", [2, 4, 6, 128, 128])
    norm_g = din("norm_g", [2, 1024])
    w_in = din("w_in", [2, 1024, 7680])
    qng = din("q_norm_g", [2, 768])
    kng = din("k_norm_g", [2, 768])
    lng = din("sgu_ln_g", [2, 512])
    lnb = din("sgu_ln_b", [2, 512])
    sgw = din("sgu_w", [2, 8, 128, 128])
    sgb = din("sgu_b", [2, 8, 128])
    lbl = din("hgrn_lb_logits", [2, 768])
    hng = din("hgrn_norm_g", [2, 768])
    w_out = din("w_out", [2, 2048, 1024])
    yp = dout("yp", [2048, 1024])
    ys = dout("ys", [4, 1024])
    okp = dout("okp", [2, 2048, 768])
    ovp = dout("ovp", [2, 2048, 768])
    oks = dout("oks", [2, 4, 768])
    ovs = dout("ovs", [2, 4, 768])
    osg = dout("osg", [2, 4, 512])
    ohp = dout("ohp", [2, 6, 128, 128])
    ohs = dout("ohs", [2, 4, 6, 128, 128])

    T = Tracker()
    es = contextlib.ExitStack()
    with es:
        nmctr = [0]

        def sbt(stack, name, shape, dt):
            nmctr[0] += 1
            return stack.enter_context(nc.sbuf_tensor(f"{name}_{nmctr[0]}", shape, dt))

        sems = [es.enter_context(nc.semaphore(f"sem{i}")) for i in range(20)]
        PE = Eng("pe", nc.tensor, sems[0])
        ACT = Eng("act", nc.scalar, sems[1])
        DVE = Eng("dve", nc.vector, sems[2])
        POOL = Eng("pool", nc.gpsimd, sems[3])
        SP = Eng("sp", nc.sync, None)
        SP.slots = [Slot(s) for s in sems[4:12]]
        POOL.slots = [Slot(s) for s in sems[12:20]]
        T.engs = [PE, ACT, DVE, POOL, SP]
        op, dma = T.op, T.dma
        V, A, G, TE = nc.vector, nc.scalar, nc.gpsimd, nc.tensor

        PS = [es.enter_context(nc.psum_tensor(f"ps{i}", [128, 512], F32)) for i in range(6)]
        PB = [es.enter_context(nc.psum_tensor(f"pb{i}", [128, 1024], BF16)) for i in range(2)]
        BPS = [Buf(f"ps{i}") for i in range(6)]
        BPB = [Buf(f"pb{i}") for i in range(2)]

        xres = sbt(es, "xres", [128, NT, 1024], F32)
        hT = sbt(es, "hT", [128, 8, TOK], BF16)
        yT = sbt(es, "yT", [128, 6, TOK], BF16)
        WA = [sbt(es, f"wA{i}", [128, 8, 512], BF16) for i in range(2)]
        wo = sbt(es, "wo", [128, 6, 1024], BF16)
        ident_bf = sbt(es, "ident_bf", [128, 128], BF16)
        ident_f = sbt(es, "ident_f", [128, 128], F32)
        ones_bf = sbt(es, "ones_bf", [128, 128], BF16)
        ones_f = sbt(es, "ones_f", [128, 128], F32)
        mask01 = sbt(es, "mask01", [128, 512], BF16)
        maskcur = sbt(es, "maskcur", [128, 128], BF16)
        blockmask = sbt(es, "blockmask", [128, 128], F32)
        rowmask = sbt(es, "rowmask", [128, 4], F32)
        rowmask_s = sbt(es, "rowmask_s", [128, 4], F32)
        rowmask_s3 = sbt(es, "rowmask_s3", [128, 4], F32)
        epsc = sbt(es, "epsc", [128, 1], F32)
        lbT = sbt(es, "lbT", [128, 12], F32)
        omlbT = sbt(es, "omlbT", [128, 12], F32)
        hgT = sbt(es, "hgT", [128, 12], F32)

        Bx = [Buf(f"x{i}") for i in range(NT)]
        BhT = [Buf(f"hT{i}") for i in range(NT)]
        ByTp = [Buf(f"yTp{i}") for i in range(6)]
        ByTs = [Buf(f"yTs{i}") for i in range(6)]
        BW = [Buf("wA0"), Buf("wA1")]
        Bwo = Buf("wo")
        Bc = Buf("consts")

        def tcols(i):
            return slice(i * 128, (i + 1) * 128)

        units = []
        for l in range(2):
            for c in range(6):
                units.append((l, [(0, c * 128), (128, 768 + c * 128), (256, 1536 + c * 128), (384, 2304 + c * 128)], 128))
            units.append((l, [(0, 3584)], 512))
            for cb in range(4):
                units.append((l, [(0, 3072 + cb * 128), (128, 4096 + cb * 128)], 128))
            for hd in range(6):
                units.append((l, [(0, 4608 + hd * 128), (128, 5376 + hd * 128), (256, 6144 + hd * 128), (384, 6912 + hd * 128)], 128))
        ustate = {"loaded": 0}

        def load_unit(u):
            if u >= len(units) or u < ustate["loaded"]:
                return
            assert u == ustate["loaded"]
            ustate["loaded"] = u + 1
            l, parts, wdt = units[u]
            wt = WA[u % 2]
            for (dst, src, ) in [(p[0], p[1]) for p in parts]:
                dma(POOL, wt[:, :, dst:dst + wdt],
                    w_in[l, :, src:src + wdt].rearrange("(kc p) n -> p kc n", p=128), writes=[BW[u % 2]])

        def load_wo(l, r0, nch):
            dma(POOL, wo[:, 0:nch, :], w_out[l, r0:r0 + nch * 128, :].rearrange("(c p) d -> p c d", p=128), writes=[Bwo])

        with contextlib.ExitStack() as ar:
            tmpf = sbt(ar, "c_tmpf", [128, 512], F32)
            R4 = sbt(ar, "c_R4", [4, 128], F32)
            ld12 = sbt(ar, "c_ld12", [12, 128], F32)
            hg12 = sbt(ar, "c_hg12", [12, 128], F32)
            lgT = sbt(ar, "c_lgT", [128, 12], F32)
            Bt = Buf("c_tmp")
            BR4 = Buf("c_R4")
            Bl = Buf("c_ld")
            dma(SP, ld12[:], lbl.rearrange("l (h k) -> (l h) k", k=128), writes=[Bl])
            dma(SP, hg12[:], hng.rearrange("l (h k) -> (l h) k", k=128), writes=[Bl])
            for i in range(16):
                dma(SP, xres[:, i, :], xp[i * 128:(i + 1) * 128, :], writes=[Bx[i]])
            op(DVE, lambda: V.memset(xres[:, 16, :], 0.0), writes=[Bx[16]])
            for b in range(4):
                dma(SP, xres[32 * b:32 * b + 1, 16, :], xs[b:b + 1, :], writes=[Bx[16]])
            op(DVE, lambda: V.memset(yT[:, :, 2048:TOK], 0.0), writes=ByTs)
            op(DVE, lambda: V.memset(epsc[:], EPS), writes=[Bc])
            op(DVE, lambda: V.memset(ones_bf[:], 1.0), writes=[Bc])
            op(DVE, lambda: V.memset(ones_f[:], 1.0), writes=[Bc])
            op(POOL, lambda: G.memset(ident_f[:], 1.0), writes=[Bc])
            op(POOL, lambda: G.affine_select(out=ident_f[:], in_=ident_f[:], pattern=[[-1, 128]], compare_op=ALU.is_equal,
                                             fill=0.0, base=0, channel_multiplier=1), reads=[Bc], writes=[Bc])
            op(DVE, lambda: V.tensor_copy(out=ident_bf[:], in_=ident_f[:]), reads=[Bc], writes=[Bc])
            op(POOL, lambda: G.memset(tmpf[:, 0:256], 1.0), writes=[Bt])
            op(POOL, lambda: G.affine_select(out=tmpf[:, 0:128], in_=tmpf[:, 0:128], pattern=[[-1, 128]], compare_op=ALU.is_ge,
                                             fill=0.0, base=0, channel_multiplier=1), reads=[Bt], writes=[Bt])
            op(POOL, lambda: G.affine_select(out=tmpf[:, 128:256], in_=tmpf[:, 128:256], pattern=[[1, 128]], compare_op=ALU.is_ge,
                                             fill=0.0, base=0, channel_multiplier=-1), reads=[Bt], writes=[Bt])
            for kb in range(2):
                for h in range(2):
                    op(DVE, lambda kb=kb, h=h: V.tensor_copy(out=mask01[:, (h * 2 + kb) * 128:(h * 2 + kb + 1) * 128],
                                                            in_=tmpf[:, kb * 128:(kb + 1) * 128]), reads=[Bt], writes=[Bc])
            op(DVE, lambda: V.tensor_copy(out=maskcur[:], in_=tmpf[:, 128:256]), reads=[Bt], writes=[Bc])
            op(POOL, lambda: G.memset(R4[:], 1.0), writes=[BR4])
            op(POOL, lambda: G.affine_select(out=R4[:], in_=R4[:], pattern=[[1, 128]], compare_op=ALU.is_ge,
                                             fill=0.0, base=0, channel_multiplier=-32), reads=[BR4], writes=[BR4])
            op(POOL, lambda: G.affine_select(out=R4[:], in_=R4[:], pattern=[[-1, 128]], compare_op=ALU.is_ge,
                                             fill=0.0, base=31, channel_multiplier=32), reads=[BR4], writes=[BR4])
            op(PE, lambda: TE.matmul(PS[0][:, 0:128], lhsT=R4[:], rhs=R4[:], start=True, stop=True), reads=[BR4], writes=[BPS[0]])
            op(PE, lambda: TE.matmul(PS[0][:, 128:132], lhsT=R4[:], rhs=ident_f[0:4, 0:4], start=True, stop=True),
               reads=[BR4, Bc], writes=[BPS[0]])
            op(DVE, lambda: V.tensor_tensor(out=blockmask[:], in0=tmpf[:, 128:256], in1=PS[0][:, 0:128], op=ALU.mult),
               reads=[Bt, BPS[0]], writes=[Bc])
            op(DVE, lambda: V.tensor_copy(out=rowmask[:], in_=PS[0][:, 128:132]), reads=[BPS[0]], writes=[Bc])
            op(POOL, lambda: G.memset(rowmask_s[:], 1.0), writes=[Bc])
            op(POOL, lambda: G.affine_select(out=rowmask_s[:], in_=rowmask_s[:], pattern=[[-32, 4]], compare_op=ALU.is_equal,
                                             fill=0.0, base=0, channel_multiplier=1), reads=[Bc], writes=[Bc])
            op(DVE, lambda: V.tensor_scalar(out=rowmask_s3[:], in0=rowmask_s[:], scalar1=3.0, scalar2=None, op0=ALU.mult),
               reads=[Bc], writes=[Bc])
            op(PE, lambda: TE.transpose(PS[1][:, 0:12], ld12[:], ident_f[0:12, 0:12]), reads=[Bl, Bc], writes=[BPS[1]])
            op(PE, lambda: TE.transpose(PS[1][:, 16:28], hg12[:], ident_f[0:12, 0:12]), reads=[Bl, Bc], writes=[BPS[1]])
            op(DVE, lambda: V.tensor_copy(out=lgT[:], in_=PS[1][:, 0:12]), reads=[BPS[1]], writes=[Bt])
            op(DVE, lambda: V.tensor_copy(out=hgT[:], in_=PS[1][:, 16:28]), reads=[BPS[1]], writes=[Bc])
            op(DVE, lambda: V.memset(lbT[:], 0.0), writes=[Bc])
            op(DVE, lambda: V.tensor_tensor(out=lgT[:, 0:6], in0=lgT[:, 0:6], in1=lgT[:, 6:12], op=ALU.subtract), reads=[Bt], writes=[Bt])
            op(ACT, lambda: A.activation(out=lgT[:, 0:6], in_=lgT[:, 0:6], func=AF.Exp), reads=[Bt], writes=[Bt])
            op(DVE, lambda: V.tensor_scalar(out=lgT[:, 0:6], in0=lgT[:, 0:6], scalar1=1.0, scalar2=None, op0=ALU.add), reads=[Bt], writes=[Bt])
            op(DVE, lambda: V.reciprocal(out=lbT[:, 6:12], in_=lgT[:, 0:6]), reads=[Bt, Bc], writes=[Bc])
            op(DVE, lambda: V.tensor_scalar(out=omlbT[:], in0=lbT[:], scalar1=-1.0, scalar2=1.0, op0=ALU.mult, op1=ALU.add),
               reads=[Bc], writes=[Bc])
            load_unit(0)
            T.barrier()
            T.ck("const")

        def sigmoid_act(src_ap, dst_ap, rd, Bdst):
            op(ACT, lambda: A.activation(out=dst_ap, in_=src_ap, func=AF.Exp, scale=-1.0), reads=rd, writes=[Bdst])
            op(ACT, lambda: A.activation(out=dst_ap, in_=dst_ap, func=AF.Ln, scale=1.0, bias=1.0), reads=[Bdst], writes=[Bdst])
            op(ACT, lambda: A.activation(out=dst_ap, in_=dst_ap, func=AF.Exp, scale=-1.0), reads=[Bdst], writes=[Bdst])

        def silu_from_psum(P, n, out_ap, tmp, Btmp, BP, wr):
            sigmoid_act(P[:, 0:n], tmp[:, 0:n], [BP], Btmp)
            op(DVE, lambda: V.tensor_tensor(out=out_ap, in0=P[:, 0:n], in1=tmp[:, 0:n], op=ALU.mult), reads=[Btmp, BP], writes=wr)

        def out_proj(l, nch):
            for i in range(NT):
                ybufs = (ByTp if i < 16 else ByTs)[0:nch]
                for half in range(2):
                    k = (2 * i + half) % 4
                    P = PS[k]
                    for cc in range(nch):
                        op(PE, lambda P=P, cc=cc, i=i, half=half: TE.matmul(
                            P[:], lhsT=yT[:, cc, tcols(i)], rhs=wo[:, cc, half * 512:(half + 1) * 512],
                            start=(cc == 0), stop=(cc == nch - 1)), reads=ybufs + [Bwo], writes=[BPS[k]])
                    xs_ap = xres[:, i, half * 512:(half + 1) * 512]
                    op(DVE, lambda P=P, xs_ap=xs_ap: V.tensor_tensor(out=xs_ap, in0=xs_ap, in1=P[:], op=ALU.add),
                       reads=[Bx[i], BPS[k]], writes=[Bx[i]])

        uidx = 0
        for l in range(2):
            if os.environ.get("KSKIP0", "") == "1":
                if l == 0:
                    T.stopped = True
                else:
                    T.stopped = False
                    ustate["loaded"] = 17
            with contextlib.ExitStack() as ar:
                gn = sbt(ar, "n_gn", [128, 1024], F32)
                sqj = sbt(ar, "n_sqj", [128, 1024], BF16)
                ss = sbt(ar, "n_ss", [128, NT], F32)
                rstd = sbt(ar, "n_rstd", [128, NT], F32)
                hb = [sbt(ar, f"n_hb{j}", [128, 1024], BF16) for j in range(2)]
                Bgn, Bsq, Bss, Brs = Buf("gn"), Buf("sqj"), Buf("ss"), Buf("rstd")
                Bhb = [Buf("hb0"), Buf("hb1")]
                dma(SP, gn[:], norm_g[l].partition_broadcast(128), writes=[Bgn])
                op(DVE, lambda: V.memset(ss[:], 0.0), writes=[Bss])
                for i in range(NT):
                    op(ACT, lambda i=i: A.activation(out=sqj[:], in_=xres[:, i, :], func=AF.Square, accum_out=ss[:, i:i + 1]),
                       reads=[Bx[i], Bss], writes=[Bsq, Bss])
                op(ACT, lambda: A.activation(out=ss[:], in_=ss[:], func=AF.Ln, scale=1.0 / 1024, bias=epsc[:, 0:1]),
                   reads=[Bss, Bc], writes=[Bss])
                op(ACT, lambda: A.activation(out=rstd[:], in_=ss[:], func=AF.Exp, scale=-0.5), reads=[Bss], writes=[Brs])
                for i in range(NT):
                    j = i % 2
                    op(DVE, lambda i=i, j=j: V.scalar_tensor_tensor(out=hb[j][:], in0=xres[:, i, :], scalar=rstd[:, i:i + 1], in1=gn[:],
                                                                    op0=ALU.mult, op1=ALU.mult),
                       reads=[Bx[i], Brs, Bgn], writes=[Bhb[j]])
                    for kc in range(8):
                        op(PE, lambda j=j, kc=kc: TE.transpose(PB[j][:, tcols(kc)], hb[j][:, tcols(kc)], ident_bf[:]),
                           reads=[Bhb[j], Bc], writes=[BPB[j]])
                    op(ACT, lambda i=i, j=j: A.activation(out=hT[:, :, tcols(i)], in_=PB[j][:].rearrange("p (k t) -> p k t", k=8), func=AF.Copy),
                       reads=[BPB[j]], writes=[BhT[i]])
                T.barrier()
                T.ck(f"norm{l}")

            with contextlib.ExitStack() as arA:
                qs_tok = sbt(arA, "a_qs", [128, 768], BF16)
                ks_tok = sbt(arA, "a_ks", [128, 768], BF16)
                vs_tok = sbt(arA, "a_vs", [128, 768], BF16)
                zsamp = sbt(arA, "a_zs", [128, 6, 4], F32)
                Bst = Buf("a_stash")
                load_wo(l, 0, 6)
                with contextlib.ExitStack() as ar:
                    qkT = sbt(ar, "a_qkT", [128, 2, TOK], BF16)
                    vnat = sbt(ar, "a_vnat", [128, NT, 128], BF16)
                    vord = sbt(ar, "a_vord", [128, 16, 128], BF16)
                    zT = sbt(ar, "a_zT", [128, TOK], BF16)
                    UD = sbt(ar, "a_UD", [128, 2, 2048], F32)
                    pp = [sbt(ar, f"a_p{j}", [128, 512], BF16) for j in range(2)]
                    kfin = [sbt(ar, f"a_kfin{j}", [128, 128], F32) for j in range(2)]
                    vfin = [sbt(ar, f"a_vfin{j}", [128, 128], F32) for j in range(2)]
                    qb = [sbt(ar, f"a_qb{j}", [128, 128], BF16) for j in range(2)]
                    kb_ = [sbt(ar, f"a_kb{j}", [128, 128], BF16) for j in range(2)]
                    ss4 = sbt(ar, "a_ss4", [128, 4], F32)
                    rs4 = sbt(ar, "a_rs4", [128, 4], F32)
                    qgc = sbt(ar, "a_qgc", [128, 128], F32)
                    kgc = sbt(ar, "a_kgc", [128, 128], F32)
                    BqkT, Bvn, Bvo, BzT, BUD = Buf("qkT"), Buf("vnat"), Buf("vord"), Buf("zT"), Buf("UD")
                    Bp = [Buf("p0"), Buf("p1")]
                    Bpm = [Buf("pm0"), Buf("pm1")]
                    Bkf = [Buf("kf0"), Buf("kf1")]
                    Bvf = [Buf("vf0"), Buf("vf1")]
                    Btq, Btk, Bsq4, Bss4, Brs4, Bg, Bez = Buf("tq"), Buf("tk"), Buf("sq"), Buf("ss4"), Buf("rs4"), Buf("g"), Buf("ezt")
                    Bqb = [Buf("qb0"), Buf("qb1")]
                    Bkb = [Buf("kb0"), Buf("kb1")]
                    blk_ctr = [0]
                    sq = UD[:, 1, 0:256]
                    ezt = UD[:, 0, 0:512]
                    Bsq4 = BUD
                    Bez = BUD

                    for c in range(6):
                        u = uidx
                        uidx += 1
                        load_unit(u)
                        load_unit(u + 1)
                        wt, Bw = WA[u % 2], BW[u % 2]
                        csl = slice(c * 128, (c + 1) * 128)
                        dma(SP, qgc[:], qng[l, csl].partition_broadcast(128), writes=[Bg])
                        dma(SP, kgc[:], kng[l, csl].partition_broadcast(128), writes=[Bg])
                        op(DVE, lambda: V.tensor_scalar(out=qgc[:], in0=qgc[:], scalar1=0.125, scalar2=None, op0=ALU.mult), reads=[Bg], writes=[Bg])
                        def a1_front(i):
                            j = i % 2
                            P, BP = PS[j], BPS[j]
                            for kc in range(8):
                                op(PE, lambda P=P, kc=kc, i=i: TE.matmul(P[:, 0:384], lhsT=hT[:, kc, tcols(i)], rhs=wt[:, kc, 0:384],
                                                                        start=(kc == 0), stop=(kc == 7)), reads=[BhT[i], Bw], writes=[BP], inc=(kc == 7))
                            op(ACT, lambda P=P: A.activation(out=sq, in_=P[:, 0:256], func=AF.Square), reads=[BP], writes=[Bsq4])
                            op(DVE, lambda: V.tensor_reduce(out=ss4[:], in_=sq.rearrange("p (h e) -> p h e", e=64), axis=AX.X, op=ALU.add),
                               reads=[Bsq4], writes=[Bss4])
                            op(ACT, lambda: A.activation(out=ss4[:], in_=ss4[:], func=AF.Ln, scale=1.0 / 64, bias=epsc[:, 0:1]),
                               reads=[Bss4, Bc], writes=[Bss4])
                            op(ACT, lambda: A.activation(out=rs4[:], in_=ss4[:], func=AF.Exp, scale=-0.5), reads=[Bss4], writes=[Brs4])
                            for h in range(2):
                                hs = slice(h * 64, (h + 1) * 64)
                                op(DVE, lambda P=P, j=j, h=h, hs=hs: V.scalar_tensor_tensor(out=qb[j][:, hs], in0=P[:, hs], scalar=rs4[:, h:h + 1], in1=qgc[:, hs],
                                                                                        op0=ALU.mult, op1=ALU.mult), reads=[BP, Brs4, Bg], writes=[Bqb[j]])
                            for h in range(2):
                                hs = slice(h * 64, (h + 1) * 64)
                                op(DVE, lambda P=P, j=j, h=h, hs=hs: V.scalar_tensor_tensor(out=kfin[j][:, hs], in0=P[:, 128 + h * 64:128 + (h + 1) * 64],
                                                                                        scalar=rs4[:, 2 + h:3 + h], in1=kgc[:, hs], op0=ALU.mult, op1=ALU.mult),
                                   reads=[BP, Brs4, Bg], writes=[Bkf[j]])
                            op(POOL, lambda j=j: G.tensor_copy(out=kb_[j][:], in_=kfin[j][:]), reads=[Bkf[j]], writes=[Bkb[j]])
                            op(ACT, lambda P=P, j=j: A.activation(out=vfin[j][:], in_=P[:, 256:384], func=AF.Copy), reads=[BP], writes=[Bvf[j]])
                            op(POOL, lambda i=i, j=j: G.tensor_copy(out=vnat[:, i, :], in_=vfin[j][:]), reads=[Bvf[j]], writes=[Bvn])
                            if i < 16:
                                dma(SP, okp[l, tcols(i), csl], kfin[j][:], reads=[Bkf[j]])
                                dma(SP, ovp[l, tcols(i), csl], vfin[j][:], reads=[Bvf[j]])
                            else:
                                for b in range(4):
                                    dma(SP, oks[l, b:b + 1, csl], kfin[j][32 * b:32 * b + 1, :], reads=[Bkf[j]])
                                    dma(SP, ovs[l, b:b + 1, csl], vfin[j][32 * b:32 * b + 1, :], reads=[Bvf[j]])
                                op(POOL, lambda j=j: G.tensor_copy(out=qs_tok[:, csl], in_=qb[j][:]), reads=[Bqb[j]], writes=[Bst])
                                op(POOL, lambda j=j: G.tensor_copy(out=ks_tok[:, csl], in_=kb_[j][:]), reads=[Bkb[j]], writes=[Bst])
                                op(POOL, lambda j=j: G.tensor_copy(out=vs_tok[:, csl], in_=vfin[j][:]), reads=[Bvf[j]], writes=[Bst])
                        def a1_back(i):
                            j = i % 2
                            op(PE, lambda j=j: TE.transpose(PB[j][:, 0:128], qb[j][:], ident_bf[:]), reads=[Bqb[j], Bc], writes=[BPB[j]])
                            op(PE, lambda j=j: TE.transpose(PB[j][:, 128:256], kb_[j][:], ident_bf[:]), reads=[Bkb[j], Bc], writes=[BPB[j]], inc=True)
                            op(DVE, lambda i=i, j=j: V.tensor_copy(out=qkT[:, :, tcols(i)], in_=PB[j][:, 0:256].rearrange("p (a t) -> p a t", a=2)),
                               reads=[BPB[j]], writes=[BqkT])
                        if os.environ.get("KPIPE", "0") == "1":
                            a1_front(0)
                            for i in range(1, NT):
                                a1_front(i)
                                a1_back(i - 1)
                            a1_back(NT - 1)
                        else:
                            for i in range(NT):
                                a1_front(i)
                                a1_back(i)
                        T.ck(f"A1_{l}_{c}")
                        for tg in range(5):
                            n = 512 if tg < 4 else 128
                            cols = slice(tg * 512, tg * 512 + n)
                            k = 2 + tg % 2
                            P, BP = PS[k], BPS[k]
                            hb_ = BhT[4 * tg:4 * tg + 4] if tg < 4 else [BhT[16]]
                            for kc in range(8):
                                op(PE, lambda P=P, kc=kc, cols=cols, n=n: TE.matmul(P[:, 0:n], lhsT=wt[:, kc, 384:512], rhs=hT[:, kc, cols],
                                                                                  start=(kc == 0), stop=(kc == 7)), reads=hb_ + [Bw], writes=[BP])
                            silu_from_psum(P, n, zT[:, cols], ezt, Bez, BP, [BzT])
                        op(POOL, lambda c=c: G.tensor_copy(out=zsamp[:, c, :], in_=zT[:, 2048:TOK:32]), reads=[BzT], writes=[Bst])

                        T.ck(f"A2_{l}_{c}")
                        def attn_front(qsl, kcur, kprev, vcur, vprev, first):
                            n_ = blk_ctr[0]
                            blk_ctr[0] += 1
                            jj = n_ % 2
                            Sb = [(PS[n_ % 2], BPS[n_ % 2]), (PS[2 + n_ % 2], BPS[2 + n_ % 2])]
                            kbs = [1] if kprev is None else [0, 1]
                            lo = 0 if kprev is not None else 128
                            for h in range(2):
                                S, BS = Sb[h]
                                hs = slice(h * 64, (h + 1) * 64)
                                for kb in kbs:
                                    ks = kprev if kb == 0 else kcur
                                    op(PE, lambda S=S, hs=hs, ks=ks, kb=kb: TE.matmul(S[:, kb * 128:(kb + 1) * 128], lhsT=qkT[hs, 1, ks], rhs=qkT[hs, 0, qsl],
                                                                                   start=True, stop=True), reads=[BqkT], writes=[BS], inc=(kb == 1))
                            for h in range(2):
                                S, BS = Sb[h]
                                op(ACT, lambda S=S, h=h: A.activation(out=pp[jj][:, h * 256 + lo:(h + 1) * 256], in_=S[:, lo:256], func=AF.Exp),
                                   reads=[BS], writes=[Bp[jj]], inc=True)
                            pv = pp[jj][:].rearrange("p (h x) -> p h x", h=2)[:, :, lo:256]
                            mv_ = mask01[:].rearrange("p (h x) -> p h x", h=2)[:, :, lo:256]
                            op(POOL, lambda pv=pv, mv_=mv_: G.tensor_tensor(out=pv, in0=pv, in1=mv_, op=ALU.mult), reads=[Bp[jj], Bc], writes=[Bp[jj]], inc=True)
                            return (n_, qsl, kbs, vcur, vprev, first)

                        def attn_back(ctx):
                            n_, qsl, kbs, vcur, vprev, first = ctx
                            jj = n_ % 2
                            U, BU = PS[4 + n_ % 2], BPS[4 + n_ % 2]
                            for part in range(2):
                                for h in range(2):
                                    hs = slice(h * 64, (h + 1) * 64)
                                    for idx, kb in enumerate(kbs):
                                        vb = vprev if kb == 0 else vcur
                                        lhsT = vb[:, hs] if part == 0 else ones_bf[:, 0:64]
                                        o0 = (h * 2 + kb) * 128
                                        op(PE, lambda U=U, hs=hs, part=part, lhsT=lhsT, o0=o0, idx=idx: TE.matmul(
                                            U[hs, part * 128:(part + 1) * 128], lhsT=lhsT, rhs=pp[jj][:, o0:o0 + 128],
                                            start=(idx == 0), stop=(idx == len(kbs) - 1)), reads=[Bp[jj], Bvn, Bvo, Bc], writes=[BU],
                                           inc=(part == 1 and h == 1 and idx == len(kbs) - 1))
                            uv = U[:, 0:256].rearrange("p (a q) -> p a q", a=2)
                            if first:
                                op(DVE, lambda: V.tensor_copy(out=UD[:, :, qsl], in_=uv), reads=[BU], writes=[BUD], inc=True)
                            else:
                                op(DVE, lambda: V.tensor_tensor(out=UD[:, :, qsl], in0=UD[:, :, qsl], in1=uv, op=ALU.add), reads=[BU, BUD], writes=[BUD], inc=True)

                        def run_blocks(specs):
                            prev = None
                            for sp_ in specs:
                                ctx = attn_front(*sp_)
                                if prev is not None:
                                    attn_back(prev)
                                prev = ctx
                            attn_back(prev)

                        run_blocks([(tcols(i), tcols(i), tcols(i - 1) if i > 0 else None, vnat[:, i, :], vnat[:, i - 1, :] if i > 0 else None, True)
                                    for i in range(16)])
                        T.ck(f"A4a_{l}_{c}")
                        for dil in (4, 16):
                            def tsl(blk):
                                if dil == 4:
                                    jb, r4 = blk // 4, blk % 4
                                    return slice(512 * jb + r4, 512 * (jb + 1), 4)
                                return slice(blk, 2048, 16)
                            for g4 in range(4):
                                k = 2 + g4 % 2
                                P, BP = PS[k], BPS[k]
                                for bi in range(4):
                                    blk = 4 * g4 + bi
                                    for kc in range(8):
                                        op(PE, lambda P=P, bi=bi, blk=blk, kc=kc: TE.matmul(P[:, tcols(bi)], lhsT=hT[:, kc, tsl(blk)], rhs=wt[:, kc, 256:384],
                                                                                           start=(kc == 0), stop=(kc == 7)), reads=BhT[0:16] + [Bw], writes=[BP])
                                op(ACT, lambda P=P, g4=g4: A.activation(out=vord[:, 4 * g4:4 * g4 + 4, :], in_=P[:].rearrange("p (a t) -> p a t", a=4), func=AF.Copy),
                                   reads=[BP], writes=[Bvo])
                            run_blocks([(tsl(blk), tsl(blk), tsl(blk - 4), vord[:, blk, :], vord[:, blk - 4, :], False) if (dil == 4 and blk >= 4)
                                        else (tsl(blk), tsl(blk), None, vord[:, blk, :], None, False) for blk in range(16)])
                        T.ck(f"A4b_{l}_{c}")
                        op(ACT, lambda: A.activation(out=UD[:, 1, :], in_=UD[:, 1, :], func=AF.Ln), reads=[BUD], writes=[BUD])
                        op(ACT, lambda: A.activation(out=UD[:, 1, :], in_=UD[:, 1, :], func=AF.Exp, scale=-1.0), reads=[BUD], writes=[BUD])
                        op(DVE, lambda: V.tensor_tensor(out=UD[:, 0, :], in0=UD[:, 0, :], in1=UD[:, 1, :], op=ALU.mult), reads=[BUD], writes=[BUD])
                        op(POOL, lambda c=c: G.tensor_tensor(out=yT[:, c, 0:2048], in0=UD[:, 0, :], in1=zT[:, 0:2048], op=ALU.mult),
                           reads=[BUD, BzT], writes=[ByTp[c]])
                    T.barrier()

                T.ck(f"A4_{l}")
                with contextlib.ExitStack() as ar:
                    selb = sbt(ar, "s_selb", [128, 4, 128], BF16)
                    selbf = sbt(ar, "s_selbf", [128, 512], F32)
                    Kr = [sbt(ar, f"s_Kr{j}", [128, 768], BF16) for j in range(2)]
                    Vr = [sbt(ar, f"s_Vr{j}", [128, 3, 768], BF16) for j in range(2)]
                    prod = sbt(ar, "s_prod", [128, 768], F32)
                    sc = sbt(ar, "s_sc", [128, 12], F32)
                    pall = [sbt(ar, f"s_pall{j}", [128, 3, 12], BF16) for j in range(2)]
                    pnew = sbt(ar, "s_pnew", [128, 12], F32)
                    pnm = sbt(ar, "s_pnm", [128, 4, 12], BF16)
                    rd = sbt(ar, "s_rd", [128, 6], F32)
                    osb = sbt(ar, "s_osb", [128, 6], F32)
                    Bsel, Bprod, Bsc, Bpn, Bpnm, Brd, Bos = Buf("selb"), Buf("prod"), Buf("sc"), Buf("pnew"), Buf("pnm"), Buf("rd"), Buf("osb")
                    BKr = [Buf("Kr0"), Buf("Kr1")]
                    BVr = [Buf("Vr0"), Buf("Vr1")]
                    Bpa = [Buf("pa0"), Buf("pa1")]
                    op(POOL, lambda: G.memset(selbf[:], 1.0), writes=[Bsel])
                    op(POOL, lambda: G.affine_select(out=selbf[:].rearrange("p (b m) -> p b m", b=4), in_=selbf[:].rearrange("p (b m) -> p b m", b=4),
                                                     pattern=[[-32, 4], [0, 128]], compare_op=ALU.is_equal, fill=0.0, base=0, channel_multiplier=1),
                       reads=[Bsel], writes=[Bsel])
                    op(DVE, lambda: V.tensor_copy(out=selb[:].rearrange("p b m -> p (b m)"), in_=selbf[:]), reads=[Bsel], writes=[Bsel])
                    op(DVE, lambda: V.tensor_tensor(out=prod[:], in0=qs_tok[:], in1=ks_tok[:], op=ALU.mult), reads=[Bst], writes=[Bprod])
                    op(DVE, lambda: V.tensor_reduce(out=sc[:], in_=prod[:].rearrange("p (h e) -> p h e", e=64), axis=AX.X, op=ALU.add),
                       reads=[Bprod], writes=[Bsc])
                    op(ACT, lambda: A.activation(out=pnew[:], in_=sc[:], func=AF.Exp), reads=[Bsc], writes=[Bpn])
                    for b in range(4):
                        op(DVE, lambda b=b: V.tensor_scalar(out=pnm[:, b, :], in0=pnew[:], scalar1=rowmask_s3[:, b:b + 1], scalar2=None, op0=ALU.mult),
                           reads=[Bpn, Bc], writes=[Bpnm])
                    kctr = 0
                    for b in range(4):
                        bj = b % 2
                        for pi, (r0, step) in enumerate([(1920, 1), (1536, 4), (0, 16)]):
                            rows = slice(r0, 2048, step)
                            kj = kctr % 2
                            kctr += 1
                            dma(POOL, Kr[kj][:], ck[l, b, rows, :], writes=[BKr[kj]])
                            dma(POOL, Vr[bj][:, pi, :], cv[l, b, rows, :], writes=[BVr[bj]])
                            for hf in range(2):
                                op(PE, lambda b=b, hf=hf: TE.matmul(PS[hf][:, 0:384], lhsT=selb[:, b, :], rhs=qs_tok[:, hf * 384:(hf + 1) * 384],
                                                                  start=True, stop=True), reads=[Bsel, Bst], writes=[BPS[hf]])
                            for hf in range(2):
                                op(DVE, lambda kj=kj, hf=hf: V.tensor_tensor(out=prod[:, hf * 384:(hf + 1) * 384], in0=Kr[kj][:, hf * 384:(hf + 1) * 384],
                                                                           in1=PS[hf][:, 0:384], op=ALU.mult), reads=[BKr[kj], BPS[hf]], writes=[Bprod])
                            op(DVE, lambda: V.tensor_reduce(out=sc[:], in_=prod[:].rearrange("p (h e) -> p h e", e=64), axis=AX.X, op=ALU.add),
                               reads=[Bprod], writes=[Bsc])
                            op(ACT, lambda bj=bj, pi=pi: A.activation(out=pall[bj][:, pi, :], in_=sc[:], func=AF.Exp), reads=[Bsc], writes=[Bpa[bj]])
                        PO, BPO = PS[2 + bj], BPS[2 + bj]
                        for part in range(2):
                            for c in range(6):
                                o0 = part * 16 + 2 * c
                                for pi in range(3):
                                    lhsT = Vr[bj][:, pi, c * 128:(c + 1) * 128] if part == 0 else ones_bf[:]
                                    op(PE, lambda PO=PO, o0=o0, lhsT=lhsT, pi=pi, c=c: TE.matmul(PO[:, o0:o0 + 2], lhsT=lhsT, rhs=pall[bj][:, pi, 2 * c:2 * c + 2],
                                                                                              start=(pi == 0), stop=False), reads=[BVr[bj], Bpa[bj], Bc], writes=[BPO])
                                lhsT = vs_tok[:, c * 128:(c + 1) * 128] if part == 0 else ones_bf[:]
                                op(PE, lambda PO=PO, o0=o0, lhsT=lhsT, b=b, c=c: TE.matmul(PO[:, o0:o0 + 2], lhsT=lhsT, rhs=pnm[:, b, 2 * c:2 * c + 2],
                                                                                        start=False, stop=True), reads=[Bst, Bpnm, Bc], writes=[BPO])
                        for hh in range(2):
                            rws = slice(hh * 64, hh * 64 + 64)
                            op(DVE, lambda PO=PO, rws=rws, hh=hh: V.reciprocal(out=rd[rws, :], in_=PO[rws, 16 + hh:28:2]), reads=[BPO], writes=[Brd])
                            op(DVE, lambda PO=PO, rws=rws, hh=hh: V.tensor_tensor(out=osb[rws, :], in0=PO[rws, hh:12:2], in1=rd[rws, :], op=ALU.mult),
                               reads=[BPO, Brd], writes=[Bos])
                        op(DVE, lambda b=b: V.tensor_tensor(out=yT[:, :, 2048 + 32 * b:2048 + 32 * b + 1], in0=osb[:].unsqueeze(2),
                                                            in1=zsamp[:, :, b:b + 1], op=ALU.mult), reads=[Bos, Bst], writes=ByTs)
                    T.barrier()
                T.ck(f"A5_{l}")
                out_proj(l, 6)
                T.barrier()
                T.ck(f"A_{l}")

            with contextlib.ExitStack() as ar:
                vn = sbt(ar, "b_vn", [128, NT, 512], BF16)
                lng_t = sbt(ar, "b_lng", [128, 512], F32)
                lnb_t = sbt(ar, "b_lnb", [128, 512], F32)
                st6 = sbt(ar, "b_st6", [128, 6], F32)
                mv = sbt(ar, "b_mv", [128, 2], F32)
                lnv = sbt(ar, "b_lnv", [128, 1], F32)
                rsb = sbt(ar, "b_rsb", [128, 1], F32)
                vnf = sbt(ar, "b_vnf", [128, 512], F32)
                vnf2 = [sbt(ar, f"b_vnf2{j}", [128, 512], F32) for j in range(2)]
                wl = sbt(ar, "b_wl", [128, 8, 128], F32)
                wlb = sbt(ar, "b_wlb", [128, 8, 128], BF16)
                wT = sbt(ar, "b_wT", [128, 8, 128], BF16)
                wTs = sbt(ar, "b_wTs", [128, 8, 128], BF16)
                w00 = sbt(ar, "b_w00", [128, 8], F32)
                bsf = sbt(ar, "b_bsf", [8, 128], F32)
                bsb = sbt(ar, "b_bsb", [8, 128], BF16)
                bs0 = sbt(ar, "b_bs0", [8, 128], BF16)
                ezb = sbt(ar, "b_ez", [128, 512], F32)
                t1 = sbt(ar, "b_t1", [128, 512], F32)
                Bvn_, Blg, Bst6, Bmv, Blnv, Brsb, Bvnf = Buf("vn"), Buf("lng"), Buf("st6"), Buf("mv"), Buf("lnv"), Buf("rsb"), Buf("vnf")
                Bvnf2 = [Buf("vnf20"), Buf("vnf21")]
                Bwl, Bwlb, BwT, Bbs, Bezb, Bt1 = Buf("wl"), Buf("wlb"), Buf("wT"), Buf("bs"), Buf("ezb"), Buf("t1")
                load_wo(l, 768, 4)
                bsel = sbt(ar, "b_bsel", [8, 512], BF16)
                bself = sbt(ar, "b_bself", [8, 512], F32)
                Bbsl = Buf("bsel")
                op(POOL, lambda: G.memset(bself[:], 1.0), writes=[Bbsl])
                op(POOL, lambda: G.affine_select(out=bself[:], in_=bself[:], pattern=[[1, 512]], compare_op=ALU.is_ge,
                                                 fill=0.0, base=0, channel_multiplier=-64), reads=[Bbsl], writes=[Bbsl])
                op(POOL, lambda: G.affine_select(out=bself[:], in_=bself[:], pattern=[[-1, 512]], compare_op=ALU.is_ge,
                                                 fill=0.0, base=63, channel_multiplier=64), reads=[Bbsl], writes=[Bbsl])
                op(DVE, lambda: V.tensor_copy(out=bsel[:], in_=bself[:]), reads=[Bbsl], writes=[Bbsl])
                dma(SP, lng_t[:], lng[l].partition_broadcast(128), writes=[Blg])
                dma(SP, lnb_t[:], lnb[l].partition_broadcast(128), writes=[Blg])
                dma(SP, wl[:], sgw[l].rearrange("g t s -> t g s"), writes=[Bwl])
                dma(SP, bsf[:], sgb[l], writes=[Bbs])
                u = uidx
                uidx += 1
                load_unit(u)
                load_unit(u + 1)
                wt, Bw = WA[u % 2], BW[u % 2]
                for i in range(NT):
                    j = i % 2
                    P, BP = PS[j], BPS[j]
                    for kc in range(8):
                        op(PE, lambda P=P, kc=kc, i=i: TE.matmul(P[:], lhsT=hT[:, kc, tcols(i)], rhs=wt[:, kc, :], start=(kc == 0), stop=(kc == 7)),
                           reads=[BhT[i], Bw], writes=[BP])
                    op(DVE, lambda P=P: V.bn_stats(out=st6[:], in_=P[:]), reads=[BP], writes=[Bst6])
                    op(DVE, lambda: V.bn_aggr(out=mv[:], in_=st6[:]), reads=[Bst6], writes=[Bmv])
                    op(ACT, lambda: A.activation(out=lnv[:], in_=mv[:, 1:2], func=AF.Ln, scale=1.0, bias=epsc[:, 0:1]), reads=[Bmv, Bc], writes=[Blnv])
                    op(ACT, lambda: A.activation(out=rsb[:], in_=lnv[:], func=AF.Exp, scale=-0.5), reads=[Blnv], writes=[Brsb])
                    op(DVE, lambda P=P: V.tensor_scalar(out=vnf[:], in0=P[:], scalar1=mv[:, 0:1], scalar2=rsb[:, 0:1], op0=ALU.subtract, op1=ALU.mult),
                       reads=[BP, Bmv, Brsb], writes=[Bvnf])
                    op(POOL, lambda: G.tensor_tensor(out=vnf[:], in0=vnf[:], in1=lng_t[:], op=ALU.mult), reads=[Bvnf, Blg], writes=[Bvnf])
                    op(POOL, lambda j=j: G.tensor_tensor(out=vnf2[j][:], in0=vnf[:], in1=lnb_t[:], op=ALU.add), reads=[Bvnf, Blg], writes=[Bvnf2[j]])
                    op(ACT, lambda i=i, j=j: A.activation(out=vn[:, i, :], in_=vnf2[j][:], func=AF.Copy), reads=[Bvnf2[j]], writes=[Bvn_])
                    if i == 16:
                        for b in range(4):
                            dma(SP, osg[l, b:b + 1, :], vnf2[j][32 * b:32 * b + 1, :], reads=[Bvnf2[j]])
                T.ck(f"B1_{l}")
                op(DVE, lambda: V.tensor_copy(out=wlb[:], in_=wl[:]), reads=[Bwl], writes=[Bwlb])
                for g in range(8):
                    op(PE, lambda g=g: TE.transpose(PB[0][:, tcols(g)], wlb[:, g, :], ident_bf[:]), reads=[Bwlb, Bc], writes=[BPB[0]])
                op(DVE, lambda: V.tensor_tensor(out=wT[:], in0=PB[0][:].rearrange("p (g t) -> p g t", g=8),
                                                in1=maskcur[:].unsqueeze(1).to_broadcast([128, 8, 128]), op=ALU.mult), reads=[BPB[0], Bc], writes=[BwT])
                op(PE, lambda: TE.matmul(PS[2][:, 0:8], lhsT=ones_f[0:1, :], rhs=wl[0:1, :, 0], start=True, stop=True), reads=[Bwl, Bc], writes=[BPS[2]])
                op(DVE, lambda: V.tensor_copy(out=w00[:], in_=PS[2][:, 0:8]), reads=[BPS[2]], writes=[BwT])
                op(DVE, lambda: V.tensor_tensor(out=wTs[:], in0=ident_bf[:].unsqueeze(1).to_broadcast([128, 8, 128]),
                                                in1=w00[:].unsqueeze(2).to_broadcast([128, 8, 128]), op=ALU.mult), reads=[BwT, Bc], writes=[BwT])
                op(DVE, lambda: V.tensor_copy(out=bsb[:], in_=bsf[:]), reads=[Bbs], writes=[Bbs])
                op(DVE, lambda: V.tensor_copy(out=bs0[:], in_=bsf[:, 0:1].to_broadcast([8, 128])), reads=[Bbs], writes=[Bbs])
                T.ck(f"B2_{l}")
                for cb in range(4):
                    u = uidx
                    uidx += 1
                    load_unit(u)
                    load_unit(u + 1)
                    wt, Bw = WA[u % 2], BW[u % 2]
                    for tg in range(5):
                        n = 512 if tg < 4 else 128
                        cols = slice(tg * 512, tg * 512 + n)
                        hb_ = BhT[4 * tg:4 * tg + 4] if tg < 4 else [BhT[16]]
                        Pu, BPu = PS[0 + tg % 2], BPS[0 + tg % 2]
                        Pz, BPz = PS[2 + tg % 2], BPS[2 + tg % 2]
                        Pm, BPm = PS[4 + tg % 2], BPS[4 + tg % 2]
                        for kc in range(8):
                            op(PE, lambda Pu=Pu, kc=kc, cols=cols, n=n: TE.matmul(Pu[:, 0:n], lhsT=wt[:, kc, 0:128], rhs=hT[:, kc, cols],
                                                                                start=(kc == 0), stop=(kc == 7)), reads=hb_ + [Bw], writes=[BPu])
                        for kc in range(8):
                            op(PE, lambda Pz=Pz, kc=kc, cols=cols, n=n: TE.matmul(Pz[:, 0:n], lhsT=wt[:, kc, 128:256], rhs=hT[:, kc, cols],
                                                                                start=(kc == 0), stop=(kc == 7)), reads=hb_ + [Bw], writes=[BPz])
                        for ti in range(n // 128):
                            i = 4 * tg + ti
                            for gg in range(2):
                                g = 2 * cb + gg
                                rws = slice(gg * 64, gg * 64 + 64)
                                wmat = wT if i < 16 else wTs
                                bmat = bsb if i < 16 else bs0
                                op(PE, lambda Pm=Pm, rws=rws, ti=ti, i=i, g=g, wmat=wmat: TE.matmul(
                                    Pm[rws, tcols(ti)], lhsT=vn[:, i, g * 64:(g + 1) * 64], rhs=wmat[:, g, :], start=True, stop=False),
                                   reads=[Bvn_, BwT], writes=[BPm])
                                op(PE, lambda Pm=Pm, rws=rws, ti=ti, g=g, bmat=bmat: TE.matmul(
                                    Pm[rws, tcols(ti)], lhsT=bsel[0:8, g * 64:(g + 1) * 64], rhs=bmat[0:8, :], start=False, stop=True),
                                   reads=[Bbs, Bbsl], writes=[BPm])
                        silu_from_psum(Pz, n, t1[:, 0:n], ezb, Bezb, BPz, [Bt1])
                        op(DVE, lambda Pu=Pu, n=n: V.tensor_tensor(out=t1[:, 0:n], in0=t1[:, 0:n], in1=Pu[:, 0:n], op=ALU.mult), reads=[Bt1, BPu], writes=[Bt1])
                        op(DVE, lambda Pm=Pm, n=n, cols=cols, cb=cb: V.tensor_tensor(out=yT[:, cb, cols], in0=t1[:, 0:n], in1=Pm[:, 0:n], op=ALU.mult),
                           reads=[Bt1, BPm], writes=[ByTp[cb] if tg < 4 else ByTs[cb]])
                T.barrier()
                T.ck(f"B3_{l}")
                out_proj(l, 4)
                T.barrier()
                T.ck(f"B_{l}")

            with contextlib.ExitStack() as ar:
                scanmask = sbt(ar, "c_scanm", [128, 512], F32)
                names = ["ef", "f", "g", "G", "eG", "kk", "eq", "q", "ez", "zs", "ln"]
                alias = {"eNG": "g", "rs": "ln", "sq": "eq", "o": "ez"}
                t = {nm: sbt(ar, "c_" + nm, [128, 512], F32) for nm in names}
                Bt_ = {nm: Buf("c_" + nm) for nm in names}
                for a_, b_ in alias.items():
                    t[a_] = t[b_]
                    Bt_[a_] = Bt_[b_]
                for nm in ("kt", "khT", "qt"):
                    t[nm] = sbt(ar, "c_" + nm, [128, 512], BF16)
                    Bt_[nm] = Buf("c_" + nm)
                Sbf = sbt(ar, "c_Sbf", [128, 9, 128], BF16)
                BSb = [Buf(f"Sb{j}") for j in range(9)]
                tv = sbt(ar, "c_tv", [128, 4, 128], BF16)
                khtok = sbt(ar, "c_khtok", [128, 4, 4, 128], BF16)
                ATm = sbt(ar, "c_ATm", [128, 4, 128], BF16)
                Sring = sbt(ar, "c_Sring", [128, 9, 128], F32)
                BSr = [Buf(f"Sr{j}") for j in range(9)]
                S0 = [Sring[:, 1, :], Sring[:, 2, :]]
                Sn = [Sring[:, 3, :], Sring[:, 4, :]]
                BS0 = [BSr[1], BSr[2]]
                BSn = [BSr[3], BSr[4]]
                Btv, Bkh, BAT, Bsm = Buf("tv"), Buf("khtok"), Buf("ATm"), Buf("scanm")
                load_wo(l, 1280, 6)
                op(DVE, lambda: V.memset(scanmask[:], 1.0), writes=[Bsm])
                op(DVE, lambda: V.memset(scanmask[:].rearrange("p (c j) -> p c j", j=32)[:, :, 0:1], 0.0), reads=[Bsm], writes=[Bsm])
                for hd in range(6):
                    u = uidx
                    uidx += 1
                    load_unit(u)
                    load_unit(u + 1)
                    wt, Bw = WA[u % 2], BW[u % 2]
                    lbc = lbT[:, l * 6 + hd:l * 6 + hd + 1]
                    omc = omlbT[:, l * 6 + hd:l * 6 + hd + 1]
                    hgc = hgT[:, l * 6 + hd:l * 6 + hd + 1]
                    op(DVE, lambda: V.memset(Sring[:, 0, :], 0.0), writes=[BSr[0]])
                    op(POOL, lambda: G.memset(Sbf[:, 0, :], 0.0), writes=[BSb[0]])
                    for tg in range(5):
                        n = 512 if tg < 4 else 128
                        nti = n // 128
                        cols = slice(tg * 512, tg * 512 + n)
                        hb_ = BhT[4 * tg:4 * tg + 4] if tg < 4 else [BhT[16]]
                        for pi_, (P, BP, w0) in enumerate([(PS[0], BPS[0], 0), (PS[1], BPS[1], 128), (PS[2], BPS[2], 384)]):
                            for kc in range(8):
                                op(PE, lambda P=P, kc=kc, w0=w0, cols=cols, n=n: TE.matmul(P[:, 0:n], lhsT=wt[:, kc, w0:w0 + 128], rhs=hT[:, kc, cols],
                                                                                         start=(kc == 0), stop=(kc == 7)), reads=hb_ + [Bw], writes=[BP])
                        for ti in range(nti):
                            i = 4 * tg + ti
                            for kc in range(8):
                                op(PE, lambda ti=ti, i=i, kc=kc: TE.matmul(PS[3][:, tcols(ti)], lhsT=hT[:, kc, tcols(i)], rhs=wt[:, kc, 256:384],
                                                                         start=(kc == 0), stop=(kc == 7)), reads=[BhT[i], Bw], writes=[BPS[3]])
                        sl = slice(0, n)
                        sigmoid_act(PS[1][:, sl], t["ef"][:, sl], [BPS[1]], Bt_["ef"])
                        op(DVE, lambda: V.tensor_scalar(out=t["f"][:, sl], in0=t["ef"][:, sl], scalar1=omc, scalar2=lbc, op0=ALU.mult, op1=ALU.add),
                           reads=[Bt_["ef"], Bc], writes=[Bt_["f"]])
                        op(POOL, lambda: G.tensor_scalar(out=t["kk"][:, sl], in0=t["f"][:, sl], scalar1=-1.0, scalar2=1.0, op0=ALU.mult, op1=ALU.add),
                           reads=[Bt_["f"]], writes=[Bt_["kk"]])
                        sigmoid_act(PS[0][:, sl], t["eq"][:, sl], [BPS[0]], Bt_["eq"])
                        op(DVE, lambda: V.tensor_tensor(out=t["q"][:, sl], in0=PS[0][:, sl], in1=t["eq"][:, sl], op=ALU.mult), reads=[BPS[0], Bt_["eq"]], writes=[Bt_["q"]])
                        silu_from_psum(PS[2], n, t["zs"][:, sl], t["ez"], Bt_["ez"], BPS[2], [Bt_["zs"]])
                        op(ACT, lambda: A.activation(out=tv[:, 0:nti, :], in_=PS[3][:, sl].rearrange("p (a v) -> p a v", v=128), func=AF.Copy),
                           reads=[BPS[3]], writes=[Btv])
                        if tg < 4:
                            op(ACT, lambda: A.activation(out=t["g"][:], in_=t["f"][:], func=AF.Ln), reads=[Bt_["f"]], writes=[Bt_["g"]])
                            op(DVE, lambda: V.tensor_tensor_scan(out=t["G"][:], data0=scanmask[:], data1=t["g"][:], initial=0.0, op0=ALU.mult, op1=ALU.add),
                               reads=[Bt_["g"], Bsm], writes=[Bt_["G"]])
                            op(ACT, lambda: A.activation(out=t["eG"][:], in_=t["G"][:], func=AF.Exp), reads=[Bt_["G"]], writes=[Bt_["eG"]])
                            op(ACT, lambda: A.activation(out=t["eNG"][:], in_=t["G"][:], func=AF.Exp, scale=-1.0), reads=[Bt_["G"]], writes=[Bt_["eNG"]])
                            op(POOL, lambda: G.tensor_tensor(out=t["kt"][:], in0=t["kk"][:], in1=t["eNG"][:], op=ALU.mult), reads=[Bt_["kk"], Bt_["eNG"]], writes=[Bt_["kt"]])
                            op(POOL, lambda: G.tensor_tensor(out=t["khT"][:].rearrange("p (c j) -> p c j", j=32), in0=t["kt"][:].rearrange("p (c j) -> p c j", j=32),
                                                             in1=t["eG"][:, 31:512:32].unsqueeze(2).to_broadcast([128, 16, 32]), op=ALU.mult),
                               reads=[Bt_["kt"], Bt_["eG"]], writes=[Bt_["khT"]])
                            op(POOL, lambda: G.tensor_tensor(out=t["qt"][:], in0=t["q"][:], in1=t["eG"][:], op=ALU.mult), reads=[Bt_["q"], Bt_["eG"]], writes=[Bt_["qt"]])
                            T.ck(f"C1_{l}_{hd}_{tg}")
                            for ti in range(4):
                                op(PE, lambda ti=ti: TE.transpose(PB[0][:, tcols(ti)], t["khT"][:, tcols(ti)], ident_bf[:]), reads=[Bt_["khT"], Bc], writes=[BPB[0]], inc=(ti == 3))
                            for ch in range(4):
                                if ch % 2 == 0:
                                    op(ACT, lambda ch=ch: A.activation(out=khtok[:, :, ch, :], in_=PB[0][:, 0:512].rearrange("p (a k) -> p a k", a=4), func=AF.Copy,
                                                                       scale=rowmask[:, ch:ch + 1]), reads=[BPB[0], Bc], writes=[Bkh])
                                else:
                                    op(DVE, lambda ch=ch: V.tensor_scalar(out=khtok[:, :, ch, :], in0=PB[0][:, 0:512].rearrange("p (a k) -> p a k", a=4),
                                                                          scalar1=rowmask[:, ch:ch + 1], scalar2=None, op0=ALU.mult), reads=[BPB[0], Bc], writes=[Bkh])
                            for ti in range(4):
                                op(PE, lambda ti=ti: TE.matmul(PS[1][:, tcols(ti)], lhsT=t["kt"][:, tcols(ti)], rhs=t["qt"][:, tcols(ti)], start=True, stop=True),
                                   reads=[Bt_["kt"], Bt_["qt"]], writes=[BPS[1]])
                            op(DVE, lambda: V.tensor_tensor(out=ATm[:], in0=PS[1][:].rearrange("p (a t) -> p a t", a=4),
                                                            in1=blockmask[:].unsqueeze(1).to_broadcast([128, 4, 128]), op=ALU.mult), reads=[BPS[1], Bc], writes=[BAT])
                            T.ck(f"C2_{l}_{hd}_{tg}")
                            banks_ = [0, 1, 3, 4]
                            for nchk in range(16):
                                ti, ch = nchk // 4, nchk % 4
                                bk, col = banks_[nchk // 4], (nchk % 4) * 128
                                op(PE, lambda bk=bk, col=col, ti=ti, ch=ch: TE.matmul(PS[bk][:, col:col + 128], lhsT=khtok[:, ti, ch, :], rhs=tv[:, ti, :], start=True, stop=True),
                                   reads=[Bkh, Btv], writes=[BPS[bk]], inc=(nchk % 4 == 3))
                            for half in range(2):
                                for r_ in range(8):
                                    nchk = half * 8 + r_
                                    bk, col = banks_[nchk // 4], (nchk % 4) * 128
                                    op(DVE, lambda bk=bk, col=col, nchk=nchk, r_=r_: V.scalar_tensor_tensor(
                                        out=Sring[:, r_ + 1, :], in0=Sring[:, r_, :], scalar=t["eG"][:, nchk * 32 + 31:nchk * 32 + 32], in1=PS[bk][:, col:col + 128],
                                        op0=ALU.mult, op1=ALU.add), reads=[BSr[r_], Bt_["eG"], BPS[bk]], writes=[BSr[r_ + 1]], inc=True)
                                    op(POOL, lambda r_=r_: G.tensor_copy(out=Sbf[:, r_ + 1, :], in_=Sring[:, r_ + 1, :]), reads=[BSr[r_ + 1]], writes=[BSb[r_ + 1]], inc=True)
                                for r_ in range(8):
                                    nchk = half * 8 + r_
                                    ti, ch = nchk // 4, nchk % 4
                                    cc = slice(nchk * 32, nchk * 32 + 32)
                                    op(PE, lambda ti=ti, ch=ch, cc=cc: TE.matmul(PS[2][:, cc], lhsT=tv[:, ti, :], rhs=ATm[:, ti, ch * 32:(ch + 1) * 32], start=True, stop=False),
                                       reads=[Btv, BAT], writes=[BPS[2]])
                                    op(PE, lambda r_=r_, cc=cc: TE.matmul(PS[2][:, cc], lhsT=Sbf[:, r_, :], rhs=t["qt"][:, cc], start=False, stop=True),
                                       reads=[BSb[r_], Bt_["qt"]], writes=[BPS[2]], inc=(r_ == 7))
                                op(DVE, lambda: V.tensor_copy(out=Sring[:, 0, :], in_=Sring[:, 8, :]), reads=[BSr[8]], writes=[BSr[0]])
                                op(POOL, lambda: G.tensor_copy(out=Sbf[:, 0, :], in_=Sbf[:, 8, :]), reads=[BSb[8]], writes=[BSb[0]])
                            no = 512
                        else:
                            op(PE, lambda: TE.transpose(PS[0][:, 0:128], t["kk"][:, 0:128], ident_f[:]), reads=[Bt_["kk"], Bc], writes=[BPS[0]])
                            for b in range(4):
                                op(DVE, lambda b=b: V.tensor_scalar(out=khtok[:, 0, b, :], in0=PS[0][:, 0:128], scalar1=rowmask_s[:, b:b + 1], scalar2=None, op0=ALU.mult),
                                   reads=[BPS[0], Bc], writes=[Bkh])
                            for b in range(4):
                                bj = b % 2
                                dma(SP, S0[bj], st[l, b, hd], writes=[BS0[bj]])
                                ku = 3 + bj
                                op(PE, lambda ku=ku, b=b: TE.matmul(PS[ku][:, 0:128], lhsT=khtok[:, 0, b, :], rhs=tv[:, 0, :], start=True, stop=True),
                                   reads=[Bkh, Btv], writes=[BPS[ku]])
                                op(DVE, lambda bj=bj, ku=ku, b=b: V.scalar_tensor_tensor(out=Sn[bj], in0=S0[bj], scalar=t["f"][:, 32 * b:32 * b + 1],
                                                                                       in1=PS[ku][:, 0:128], op0=ALU.mult, op1=ALU.add),
                                   reads=[BS0[bj], Bt_["f"], BPS[ku]], writes=[BSn[bj]])
                                dma(SP, ohs[l, b, hd], Sn[bj], reads=[BSn[bj]])
                                op(PE, lambda bj=bj, b=b: TE.matmul(PS[2][:, b:b + 1], lhsT=Sn[bj], rhs=t["q"][:, 32 * b:32 * b + 1], start=True, stop=True),
                                   reads=[BSn[bj], Bt_["q"]], writes=[BPS[2]])
                            no = 4
                        T.ck(f"C3_{l}_{hd}_{tg}")
                        so = slice(0, no)
                        op(ACT, lambda: A.activation(out=t["o"][:, so], in_=PS[2][:, so], func=AF.Copy), reads=[BPS[2]], writes=[Bt_["o"]])
                        op(ACT, lambda: A.activation(out=t["sq"][:, so], in_=PS[2][:, so], func=AF.Square), reads=[BPS[2]], writes=[Bt_["sq"]])
                        op(PE, lambda: TE.matmul(PS[5][:, so], lhsT=ones_f[:], rhs=t["sq"][:, so], start=True, stop=True), reads=[Bt_["sq"], Bc], writes=[BPS[5]])
                        op(ACT, lambda: A.activation(out=t["ln"][:, so], in_=PS[5][:, so], func=AF.Ln, scale=1.0 / 128, bias=epsc[:, 0:1]), reads=[BPS[5], Bc], writes=[Bt_["ln"]])
                        op(ACT, lambda: A.activation(out=t["rs"][:, so], in_=t["ln"][:, so], func=AF.Exp, scale=-0.5), reads=[Bt_["ln"]], writes=[Bt_["rs"]])
                        op(DVE, lambda: V.tensor_tensor(out=t["o"][:, so], in0=t["o"][:, so], in1=t["rs"][:, so], op=ALU.mult), reads=[Bt_["o"], Bt_["rs"]], writes=[Bt_["o"]])
                        if tg < 4:
                            op(DVE, lambda cols=cols, hd=hd: V.scalar_tensor_tensor(out=yT[:, hd, cols], in0=t["o"][:], scalar=hgc, in1=t["zs"][:], op0=ALU.mult, op1=ALU.mult),
                               reads=[Bt_["o"], Bt_["zs"], Bc], writes=[ByTp[hd]])
                        else:
                            op(DVE, lambda hd=hd: V.scalar_tensor_tensor(out=yT[:, hd, 2048:TOK:32], in0=t["o"][:, 0:4], scalar=hgc, in1=t["zs"][:, 0:128:32],
                                                                         op0=ALU.mult, op1=ALU.mult), reads=[Bt_["o"], Bt_["zs"], Bc], writes=[ByTs[hd]])
                        T.ck(f"C4_{l}_{hd}_{tg}")
                        if tg == 3:
                            dma(SP, ohp[l, hd], Sring[:, 0, :], reads=[BSr[0]])
                T.barrier()
                T.ck(f"C5_{l}")
                out_proj(l, 6)
                T.barrier()
                T.ck(f"L_{l}")

        T.force = True
        for i in range(16):
            dma(SP, yp[i * 128:(i + 1) * 128, :], xres[:, i, :], reads=[Bx[i]])
        for b in range(4):
            dma(SP, ys[b:b + 1, :], xres[32 * b:32 * b + 1, 16, :], reads=[Bx[16]])
        for Q in (SP, POOL):
            for s in Q.slots:
                if s.val > 0:
                    T._wait(SP, s.sem, s.val)
    return nc


_NC_CACHE = {}


def kernel(x_prompt, x_sample, cache_k, cache_v, state_hgrn, norm_g, w_in, q_norm_g, k_norm_g,
           sgu_ln_g, sgu_ln_b, sgu_w, sgu_b, hgrn_lb_logits, hgrn_norm_g, w_out):
    f = lambda a: np.ascontiguousarray(np.asarray(a, dtype=np.float32))
    x_prompt, x_sample, cache_k, cache_v, state_hgrn = map(f, (x_prompt, x_sample, cache_k, cache_v, state_hgrn))
    shared = {
        "norm_g": f(norm_g), "w_in": f(w_in), "q_norm_g": f(q_norm_g), "k_norm_g": f(k_norm_g),
        "sgu_ln_g": f(sgu_ln_g), "sgu_ln_b": f(sgu_ln_b), "sgu_w": f(sgu_w), "sgu_b": f(sgu_b),
        "hgrn_lb_logits": f(hgrn_lb_logits), "hgrn_norm_g": f(hgrn_norm_g), "w_out": f(w_out),
    }
    in_maps = []
    for c in range(NCORES):
        sb = slice(4 * c, 4 * c + 4)
        m = dict(shared)
        m["xp"] = np.ascontiguousarray(x_prompt[c])
        m["xs"] = np.ascontiguousarray(x_sample[sb, 0, :])
        m["ck"] = np.ascontiguousarray(cache_k[:, sb].reshape(2, 4, 2048, 768))
        m["cv"] = np.ascontiguousarray(cache_v[:, sb].reshape(2, 4, 2048, 768))
        m["st"] = np.ascontiguousarray(state_hgrn[:, sb])
        in_maps.append(m)
    if "nc" not in _NC_CACHE:
        _NC_CACHE["nc"] = build_nc()
    res = run_bass_kernel_spmd(_NC_CACHE["nc"], in_maps, core_ids=list(range(NCORES)))
    R = res.results
    y_prompt = np.stack([R[c]["yp"] for c in range(NCORES)]).astype(np.float32)
    y_sample = np.concatenate([R[c]["ys"] for c in range(NCORES)])[:, None, :].astype(np.float32)
    nkp = np.stack([R[c]["okp"] for c in range(NCORES)], axis=1).reshape(2, 8, 2048, 12, 64).astype(np.float32)
    nvp = np.stack([R[c]["ovp"] for c in range(NCORES)], axis=1).reshape(2, 8, 2048, 12, 64).astype(np.float32)
    nks = np.concatenate([R[c]["oks"] for c in range(NCORES)], axis=1).reshape(2, 32, 1, 12, 64).astype(np.float32)
    nvs = np.concatenate([R[c]["ovs"] for c in range(NCORES)], axis=1).reshape(2, 32, 1, 12, 64).astype(np.float32)
    nsg = np.concatenate([R[c]["osg"] for c in range(NCORES)], axis=1).reshape(2, 32, 1, 512).astype(np.float32)
    nhp = np.stack([R[c]["ohp"] for c in range(NCORES)], axis=1).astype(np.float32)
    nhs = np.concatenate([R[c]["ohs"] for c in range(NCORES)], axis=1).astype(np.float32)
    return (y_prompt, y_sample, nkp, nvp, nks, nvs, nsg, nhp, nhs)
```

```python
import bisect
import contextlib
import os

import numpy as np

import concourse.bass as bass
import concourse.mybir as mybir
from concourse.bass_utils import run_bass_kernel_spmd

F32 = mybir.dt.float32
BF16 = mybir.dt.bfloat16
AF = mybir.ActivationFunctionType
ALU = mybir.AluOpType
AX = mybir.AxisListType

NCORES = 8
NT = 17
TOK = NT * 128
EPS = 1e-6


class Eng:
    def __init__(self, name, eng, sem):
        self.name, self.eng, self.sem = name, eng, sem
        self.nseq = 0
        self.cnt = 0
        self.last = None
        self.inc_seq = []
        self.waited = {}
        self.slots = []
        self.rr = 0


class Slot:
    def __init__(self, sem):
        self.sem = sem
        self.val = 0


class Buf:
    __slots__ = ("name", "w", "r")

    def __init__(self, name):
        self.name = name
        self.w = None
        self.r = {}


class Tracker:
    def __init__(self):
        self.engs = []
        self.stopped = False
        self.force = False
        self.stop_at = os.environ.get("KSTOP", "")

    def ck(self, name):
        if self.stop_at and name == self.stop_at:
            self.stopped = True

    def resolve(self, ev):
        if ev[0] == "s":
            return ev[1], ev[2]
        E, seq = ev[1], ev[2]
        i = bisect.bisect_left(E.inc_seq, seq)
        if i < len(E.inc_seq):
            return E.sem, i + 1
        E.last.then_inc(E.sem, 1)
        E.cnt += 1
        E.inc_seq.append(E.nseq)
        return E.sem, E.cnt

    def _wait(self, E, sem, val):
        if E.waited.get(sem.num, 0) < val:
            E.eng.wait_ge(sem, val)
            E.waited[sem.num] = val

    def deps(self, E, reads, writes):
        evs = []
        for b in reads:
            if b.w is not None:
                evs.append(b.w)
        for b in writes:
            if b.w is not None:
                evs.append(b.w)
            for k, ev in b.r.items():
                if ev[0] == "e" and ev[1] is E:
                    continue
                evs.append(ev)
        need = {}
        for ev in evs:
            if ev[0] == "e" and ev[1] is E and E.name == "pe":
                continue
            sem, val = self.resolve(ev)
            if need.get(sem.num, (None, 0))[1] < val:
                need[sem.num] = (sem, val)
        for num, (sem, val) in need.items():
            self._wait(E, sem, val)

    def op(self, E, fn, reads=(), writes=(), inc=None):
        if self.stopped and not self.force:
            return None
        self.deps(E, reads, writes)
        ins = fn()
        E.nseq += 1
        E.last = ins
        if inc and os.environ.get("KINC", "1") == "1":
            ins.then_inc(E.sem, 1)
            E.cnt += 1
            E.inc_seq.append(E.nseq)
        ev = ("e", E, E.nseq)
        for b in writes:
            b.w = ev
            b.r = {}
        for b in reads:
            b.r[E.name] = ev
        return ins

    def dma(self, Q, out, in_, reads=(), writes=()):
        if self.stopped and not self.force:
            return
        self.deps(Q, reads, writes)
        slot = Q.slots[Q.rr % len(Q.slots)]
        Q.rr += 1
        if slot.val > 0:
            self._wait(Q, slot.sem, slot.val)
        Q.eng.dma_start(out=out, in_=in_).then_inc(slot.sem, 16)
        slot.val += 16
        ev = ("s", slot.sem, slot.val)
        for b in writes:
            b.w = ev
            b.r = {}
        for b in reads:
            b.r[("d", slot.sem.num)] = ev

    def barrier(self):
        if self.stopped and not self.force:
            return
        pts = []
        for F in self.engs:
            if F.nseq > 0:
                pts.append(self.resolve(("e", F, F.nseq)))
            for s in F.slots:
                if s.val > 0:
                    pts.append((s.sem, s.val))
        for E in self.engs:
            for sem, val in pts:
                if sem is E.sem and E.name == "pe":
                    continue
                self._wait(E, sem, val)


def build_nc():
    nc = bass.Bass("TRN2", target_bir_lowering=False)

    def din(name, shape):
        return nc.dram_tensor(name, shape, F32, kind="ExternalInput").ap()

    def dout(name, shape):
        return nc.dram_tensor(name, shape, F32, kind="ExternalOutput").ap()

    xp = din("xp", [2048, 1024])
    xs = din("xs", [4, 1024])
    ck = din("ck", [2, 4, 2048, 768])
    cv = din("cv", [2, 4, 2048, 768])
    st = din("st", [2, 4, 6, 128, 128])
    norm_g = din("norm_g", [2, 1024])
    w_in = din("w_in", [2, 1024, 7680])
    qng = din("q_norm_g", [2, 768])
    kng = din("k_norm_g", [2, 768])
    lng = din("sgu_ln_g", [2, 512])
    lnb = din("sgu_ln_b", [2, 512])
    sgw = din("sgu_w", [2, 8, 128, 128])
    sgb = din("sgu_b", [2, 8, 128])
    lbl = din("hgrn_lb_logits", [2, 768])
    hng = din("hgrn_norm_g", [2, 768])
    w_out = din("w_out", [2, 2048, 1024])
    yp = dout("yp", [2048, 1024])
    ys = dout("ys", [4, 1024])
    okp = dout("okp", [2, 2048, 768])
    ovp = dout("ovp", [2, 2048, 768])
    oks = dout("oks", [2, 4, 768])
    ovs = dout("ovs", [2, 4, 768])
    osg = dout("osg", [2, 4, 512])
    ohp = dout("ohp", [2, 6, 128, 128])
    ohs = dout("ohs", [2, 4, 6, 128, 128])

    T = Tracker()
    es = contextlib.ExitStack()
    with es:
        nmctr = [0]

        def sbt(stack, name, shape, dt):
            nmctr[0] += 1
            return stack.enter_context(nc.sbuf_tensor(f"{name}_{nmctr[0]}", shape, dt))

        sems = [es.enter_context(nc.semaphore(f"sem{i}")) for i in range(20)]
        PE = Eng("pe", nc.tensor, sems[0])
        ACT = Eng("act", nc.scalar, sems[1])
        DVE = Eng("dve", nc.vector, sems[2])
        POOL = Eng("pool", nc.gpsimd, sems[3])
        SP = Eng("sp", nc.sync, None)
        SP.slots = [Slot(s) for s in sems[4:12]]
        POOL.slots = [Slot(s) for s in sems[12:20]]
        T.engs = [PE, ACT, DVE, POOL, SP]
        op, dma = T.op, T.dma
        V, A, G, TE = nc.vector, nc.scalar, nc.gpsimd, nc.tensor

        PS = [es.enter_context(nc.psum_tensor(f"ps{i}", [128, 512], F32)) for i in range(6)]
        PB = [es.enter_context(nc.psum_tensor(f"pb{i}", [128, 1024], BF16)) for i in range(2)]
        BPS = [Buf(f"ps{i}") for i in range(6)]
        BPB = [Buf(f"pb{i}") for i in range(2)]

        xres = sbt(es, "xres", [128, NT, 1024], F32)
        hT = sbt(es, "hT", [128, 8, TOK], BF16)
        yT = sbt(es, "yT", [128, 6, TOK], BF16)
        WA = [sbt(es, f"wA{i}", [128, 8, 512], BF16) for i in range(2)]
        wo = sbt(es, "wo", [128, 6, 1024], BF16)
        ident_bf = sbt(es, "ident_bf", [128, 128], BF16)
        ident_f = sbt(es, "ident_f", [128, 128], F32)
        ones_bf = sbt(es, "ones_bf", [128, 128], BF16)
        ones_f = sbt(es, "ones_f", [128, 128], F32)
        mask01 = sbt(es, "mask01", [128, 512], BF16)
        maskcur = sbt(es, "maskcur", [128, 128], BF16)
        blockmask = sbt(es, "blockmask", [128, 128], F32)
        rowmask = sbt(es, "rowmask", [128, 4], F32)
        rowmask_s = sbt(es, "rowmask_s", [128, 4], F32)
        rowmask_s3 = sbt(es, "rowmask_s3", [128, 4], F32)
        epsc = sbt(es, "epsc", [128, 1], F32)
        lbT = sbt(es, "lbT", [128, 12], F32)
        omlbT = sbt(es, "omlbT", [128, 12], F32)
        hgT = sbt(es, "hgT", [128, 12], F32)

        Bx = [Buf(f"x{i}") for i in range(NT)]
        BhT = [Buf(f"hT{i}") for i in range(NT)]
        ByTp = [Buf(f"yTp{i}") for i in range(6)]
        ByTs = [Buf(f"yTs{i}") for i in range(6)]
        BW = [Buf("wA0"), Buf("wA1")]
        Bwo = Buf("wo")
        Bc = Buf("consts")

        def tcols(i):
            return slice(i * 128, (i + 1) * 128)

        units = []
        for l in range(2):
            for c in range(6):
                units.append((l, [(0, c * 128), (128, 768 + c * 128), (256, 1536 + c * 128), (384, 2304 + c * 128)], 128))
            units.append((l, [(0, 3584)], 512))
            for cb in range(4):
                units.append((l, [(0, 3072 + cb * 128), (128, 4096 + cb * 128)], 128))
            for hd in range(6):
                units.append((l, [(0, 4608 + hd * 128), (128, 5376 + hd * 128), (256, 6144 + hd * 128), (384, 6912 + hd * 128)], 128))
        ustate = {"loaded": 0}

        def load_unit(u):
            if u >= len(units) or u < ustate["loaded"]:
                return
            assert u == ustate["loaded"]
            ustate["loaded"] = u + 1
            l, parts, wdt = units[u]
            wt = WA[u % 2]
            for (dst, src, ) in [(p[0], p[1]) for p in parts]:
                dma(POOL, wt[:, :, dst:dst + wdt],
                    w_in[l, :, src:src + wdt].rearrange("(kc p) n -> p kc n", p=128), writes=[BW[u % 2]])

        def load_wo(l, r0, nch):
            dma(POOL, wo[:, 0:nch, :], w_out[l, r0:r0 + nch * 128, :].rearrange("(c p) d -> p c d", p=128), writes=[Bwo])

        with contextlib.ExitStack() as ar:
            tmpf = sbt(ar, "c_tmpf", [128, 512], F32)
            R4 = sbt(ar, "c_R4", [4, 128], F32)
            ld12 = sbt(ar, "c_ld12", [12, 128], F32)
            hg12 = sbt(ar, "c_hg12", [12, 128], F32)
            lgT = sbt(ar, "c_lgT", [128, 12], F32)
            Bt = Buf("c_tmp")
            BR4 = Buf("c_R4")
            Bl = Buf("c_ld")
            dma(SP, ld12[:], lbl.rearrange("l (h k) -> (l h) k", k=128), writes=[Bl])
            dma(SP, hg12[:], hng.rearrange("l (h k) -> (l h) k", k=128), writes=[Bl])
            for i in range(16):
                dma(SP, xres[:, i, :], xp[i * 128:(i + 1) * 128, :], writes=[Bx[i]])
            op(DVE, lambda: V.memset(xres[:, 16, :], 0.0), writes=[Bx[16]])
            for b in range(4):
                dma(SP, xres[32 * b:32 * b + 1, 16, :], xs[b:b + 1, :], writes=[Bx[16]])
            op(DVE, lambda: V.memset(yT[:, :, 2048:TOK], 0.0), writes=ByTs)
            op(DVE, lambda: V.memset(epsc[:], EPS), writes=[Bc])
            op(DVE, lambda: V.memset(ones_bf[:], 1.0), writes=[Bc])
            op(DVE, lambda: V.memset(ones_f[:], 1.0), writes=[Bc])
            op(POOL, lambda: G.memset(ident_f[:], 1.0), writes=[Bc])
            op(POOL, lambda: G.affine_select(out=ident_f[:], in_=ident_f[:], pattern=[[-1, 128]], compare_op=ALU.is_equal,
                                             fill=0.0, base=0, channel_multiplier=1), reads=[Bc], writes=[Bc])
            op(DVE, lambda: V.tensor_copy(out=ident_bf[:], in_=ident_f[:]), reads=[Bc], writes=[Bc])
            op(POOL, lambda: G.memset(tmpf[:, 0:256], 1.0), writes=[Bt])
            op(POOL, lambda: G.affine_select(out=tmpf[:, 0:128], in_=tmpf[:, 0:128], pattern=[[-1, 128]], compare_op=ALU.is_ge,
                                             fill=0.0, base=0, channel_multiplier=1), reads=[Bt], writes=[Bt])
            op(POOL, lambda: G.affine_select(out=tmpf[:, 128:256], in_=tmpf[:, 128:256], pattern=[[1, 128]], compare_op=ALU.is_ge,
                                             fill=0.0, base=0, channel_multiplier=-1), reads=[Bt], writes=[Bt])
            for kb in range(2):
                for h in range(2):
                    op(DVE, lambda kb=kb, h=h: V.tensor_copy(out=mask01[:, (h * 2 + kb) * 128:(h * 2 + kb + 1) * 128],
                                                            in_=tmpf[:, kb * 128:(kb + 1) * 128]), reads=[Bt], writes=[Bc])
            op(DVE, lambda: V.tensor_copy(out=maskcur[:], in_=tmpf[:, 128:256]), reads=[Bt], writes=[Bc])
            op(POOL, lambda: G.memset(R4[:], 1.0), writes=[BR4])
            op(POOL, lambda: G.affine_select(out=R4[:], in_=R4[:], pattern=[[1, 128]], compare_op=ALU.is_ge,
                                             fill=0.0, base=0, channel_multiplier=-32), reads=[BR4], writes=[BR4])
            op(POOL, lambda: G.affine_select(out=R4[:], in_=R4[:], pattern=[[-1, 128]], compare_op=ALU.is_ge,
                                             fill=0.0, base=31, channel_multiplier=32), reads=[BR4], writes=[BR4])
            op(PE, lambda: TE.matmul(PS[0][:, 0:128], lhsT=R4[:], rhs=R4[:], start=True, stop=True), reads=[BR4], writes=[BPS[0]])
            op(PE, lambda: TE.matmul(PS[0][:, 128:132], lhsT=R4[:], rhs=ident_f[0:4, 0:4], start=True, stop=True),
               reads=[BR4, Bc], writes=[BPS[0]])
            op(DVE, lambda: V.tensor_tensor(out=blockmask[:], in0=tmpf[:, 128:256], in1=PS[0][:, 0:128], op=ALU.mult),
               reads=[Bt, BPS[0]], writes=[Bc])
            op(DVE, lambda: V.tensor_copy(out=rowmask[:], in_=PS[0][:, 128:132]), reads=[BPS[0]], writes=[Bc])
            op(POOL, lambda: G.memset(rowmask_s[:], 1.0), writes=[Bc])
            op(POOL, lambda: G.affine_select(out=rowmask_s[:], in_=rowmask_s[:], pattern=[[-32, 4]], compare_op=ALU.is_equal,
                                             fill=0.0, base=0, channel_multiplier=1), reads=[Bc], writes=[Bc])
            op(DVE, lambda: V.tensor_scalar(out=rowmask_s3[:], in0=rowmask_s[:], scalar1=3.0, scalar2=None, op0=ALU.mult),
               reads=[Bc], writes=[Bc])
            op(PE, lambda: TE.transpose(PS[1][:, 0:12], ld12[:], ident_f[0:12, 0:12]), reads=[Bl, Bc], writes=[BPS[1]])
            op(PE, lambda: TE.transpose(PS[1][:, 16:28], hg12[:], ident_f[0:12, 0:12]), reads=[Bl, Bc], writes=[BPS[1]])
            op(DVE, lambda: V.tensor_copy(out=lgT[:], in_=PS[1][:, 0:12]), reads=[BPS[1]], writes=[Bt])
            op(DVE, lambda: V.tensor_copy(out=hgT[:], in_=PS[1][:, 16:28]), reads=[BPS[1]], writes=[Bc])
            op(DVE, lambda: V.memset(lbT[:], 0.0), writes=[Bc])
            op(DVE, lambda: V.tensor_tensor(out=lgT[:, 0:6], in0=lgT[:, 0:6], in1=lgT[:, 6:12], op=ALU.subtract), reads=[Bt], writes=[Bt])
            op(ACT, lambda: A.activation(out=lgT[:, 0:6], in_=lgT[:, 0:6], func=AF.Exp), reads=[Bt], writes=[Bt])
            op(DVE, lambda: V.tensor_scalar(out=lgT[:, 0:6], in0=lgT[:, 0:6], scalar1=1.0, scalar2=None, op0=ALU.add), reads=[Bt], writes=[Bt])
            op(DVE, lambda: V.reciprocal(out=lbT[:, 6:12], in_=lgT[:, 0:6]), reads=[Bt, Bc], writes=[Bc])
            op(DVE, lambda: V.tensor_scalar(out=omlbT[:], in0=lbT[:], scalar1=-1.0, scalar2=1.0, op0=ALU.mult, op1=ALU.add),
               reads=[Bc], writes=[Bc])
            load_unit(0)
            T.barrier()
            T.ck("const")

        def sigmoid_act(src_ap, dst_ap, rd, Bdst):
            op(ACT, lambda: A.activation(out=dst_ap, in_=src_ap, func=AF.Exp, scale=-1.0), reads=rd, writes=[Bdst])
            op(ACT, lambda: A.activation(out=dst_ap, in_=dst_ap, func=AF.Ln, scale=1.0, bias=1.0), reads=[Bdst], writes=[Bdst])
            op(ACT, lambda: A.activation(out=dst_ap, in_=dst_ap, func=AF.Exp, scale=-1.0), reads=[Bdst], writes=[Bdst])

        def silu_from_psum(P, n, out_ap, tmp, Btmp, BP, wr):
            sigmoid_act(P[:, 0:n], tmp[:, 0:n], [BP], Btmp)
            op(DVE, lambda: V.tensor_tensor(out=out_ap, in0=P[:, 0:n], in1=tmp[:, 0:n], op=ALU.mult), reads=[Btmp, BP], writes=wr)

        def out_proj(l, nch):
            for i in range(NT):
                ybufs = (ByTp if i < 16 else ByTs)[0:nch]
                for half in range(2):
                    k = (2 * i + half) % 4
                    P = PS[k]
                    for cc in range(nch):
                        op(PE, lambda P=P, cc=cc, i=i, half=half: TE.matmul(
                            P[:], lhsT=yT[:, cc, tcols(i)], rhs=wo[:, cc, half * 512:(half + 1) * 512],
                            start=(cc == 0), stop=(cc == nch - 1)), reads=ybufs + [Bwo], writes=[BPS[k]])
                    xs_ap = xres[:, i, half * 512:(half + 1) * 512]
                    op(DVE, lambda P=P, xs_ap=xs_ap: V.tensor_tensor(out=xs_ap, in0=xs_ap, in1=P[:], op=ALU.add),
                       reads=[Bx[i], BPS[k]], writes=[Bx[i]])

        uidx = 0
        for l in range(2):
            if os.environ.get("KSKIP0", "") == "1":
                if l == 0:
                    T.stopped = True
                else:
                    T.stopped = False
                    ustate["loaded"] = 17
            with contextlib.ExitStack() as ar:
                gn = sbt(ar, "n_gn", [128, 1024], F32)
                sqj = sbt(ar, "n_sqj", [128, 1024], BF16)
                ss = sbt(ar, "n_ss", [128, NT], F32)
                rstd = sbt(ar, "n_rstd", [128, NT], F32)
                hb = [sbt(ar, f"n_hb{j}", [128, 1024], BF16) for j in range(2)]
                Bgn, Bsq, Bss, Brs = Buf("gn"), Buf("sqj"), Buf("ss"), Buf("rstd")
                Bhb = [Buf("hb0"), Buf("hb1")]
                dma(SP, gn[:], norm_g[l].partition_broadcast(128), writes=[Bgn])
                op(DVE, lambda: V.memset(ss[:], 0.0), writes=[Bss])
                for i in range(NT):
                    op(ACT, lambda i=i: A.activation(out=sqj[:], in_=xres[:, i, :], func=AF.Square, accum_out=ss[:, i:i + 1]),
                       reads=[Bx[i], Bss], writes=[Bsq, Bss])
                op(ACT, lambda: A.activation(out=ss[:], in_=ss[:], func=AF.Ln, scale=1.0 / 1024, bias=epsc[:, 0:1]),
                   reads=[Bss, Bc], writes=[Bss])
                op(ACT, lambda: A.activation(out=rstd[:], in_=ss[:], func=AF.Exp, scale=-0.5), reads=[Bss], writes=[Brs])
                for i in range(NT):
                    j = i % 2
                    op(DVE, lambda i=i, j=j: V.scalar_tensor_tensor(out=hb[j][:], in0=xres[:, i, :], scalar=rstd[:, i:i + 1], in1=gn[:],
                                                                    op0=ALU.mult, op1=ALU.mult),
                       reads=[Bx[i], Brs, Bgn], writes=[Bhb[j]])
                    for kc in range(8):
                        op(PE, lambda j=j, kc=kc: TE.transpose(PB[j][:, tcols(kc)], hb[j][:, tcols(kc)], ident_bf[:]),
                           reads=[Bhb[j], Bc], writes=[BPB[j]])
                    op(ACT, lambda i=i, j=j: A.activation(out=hT[:, :, tcols(i)], in_=PB[j][:].rearrange("p (k t) -> p k t", k=8), func=AF.Copy),
                       reads=[BPB[j]], writes=[BhT[i]])
                T.barrier()
                T.ck(f"norm{l}")

            with contextlib.ExitStack() as arA:
                qs_tok = sbt(arA, "a_qs", [128, 768], BF16)
                ks_tok = sbt(arA, "a_ks", [128, 768], BF16)
                vs_tok = sbt(arA, "a_vs", [128, 768], BF16)
                zsamp = sbt(arA, "a_zs", [128, 6, 4], F32)
                Bst = Buf("a_stash")
                load_wo(l, 0, 6)
                with contextlib.ExitStack() as ar:
                    qkT = sbt(ar, "a_qkT", [128, 2, TOK], BF16)
                    vnat = sbt(ar, "a_vnat", [128, NT, 128], BF16)
                    vord = sbt(ar, "a_vord", [128, 16, 128], BF16)
                    zT = sbt(ar, "a_zT", [128, TOK], BF16)
                    UD = sbt(ar, "a_UD", [128, 2, 2048], F32)
                    pp = [sbt(ar, f"a_p{j}", [128, 512], BF16) for j in range(2)]
                    kfin = [sbt(ar, f"a_kfin{j}", [128, 128], F32) for j in range(2)]
                    vfin = [sbt(ar, f"a_vfin{j}", [128, 128], F32) for j in range(2)]
                    qb = [sbt(ar, f"a_qb{j}", [128, 128], BF16) for j in range(2)]
                    kb_ = [sbt(ar, f"a_kb{j}", [128, 128], BF16) for j in range(2)]
                    ss4 = sbt(ar, "a_ss4", [128, 4], F32)
                    rs4 = sbt(ar, "a_rs4", [128, 4], F32)
                    qgc = sbt(ar, "a_qgc", [128, 128], F32)
                    kgc = sbt(ar, "a_kgc", [128, 128], F32)
                    BqkT, Bvn, Bvo, BzT, BUD = Buf("qkT"), Buf("vnat"), Buf("vord"), Buf("zT"), Buf("UD")
                    Bp = [Buf("p0"), Buf("p1")]
                    Bpm = [Buf("pm0"), Buf("pm1")]
                    Bkf = [Buf("kf0"), Buf("kf1")]
                    Bvf = [Buf("vf0"), Buf("vf1")]
                    Btq, Btk, Bsq4, Bss4, Brs4, Bg, Bez = Buf("tq"), Buf("tk"), Buf("sq"), Buf("ss4"), Buf("rs4"), Buf("g"), Buf("ezt")
                    Bqb = [Buf("qb0"), Buf("qb1")]
                    Bkb = [Buf("kb0"), Buf("kb1")]
                    blk_ctr = [0]
                    sq = UD[:, 1, 0:256]
                    ezt = UD[:, 0, 0:512]
                    Bsq4 = BUD
                    Bez = BUD

                    for c in range(6):
                        u = uidx
                        uidx += 1
                        load_unit(u)
                        load_unit(u + 1)
                        wt, Bw = WA[u % 2], BW[u % 2]
                        csl = slice(c * 128, (c + 1) * 128)
                        dma(SP, qgc[:], qng[l, csl].partition_broadcast(128), writes=[Bg])
                        dma(SP, kgc[:], kng[l, csl].partition_broadcast(128), writes=[Bg])
                        op(DVE, lambda: V.tensor_scalar(out=qgc[:], in0=qgc[:], scalar1=0.125, scalar2=None, op0=ALU.mult), reads=[Bg], writes=[Bg])
                        def a1_front(i):
                            j = i % 2
                            P, BP = PS[j], BPS[j]
                            for kc in range(8):
                                op(PE, lambda P=P, kc=kc, i=i: TE.matmul(P[:, 0:384], lhsT=hT[:, kc, tcols(i)], rhs=wt[:, kc, 0:384],
                                                                        start=(kc == 0), stop=(kc == 7)), reads=[BhT[i], Bw], writes=[BP], inc=(kc == 7))
                            op(ACT, lambda P=P: A.activation(out=sq, in_=P[:, 0:256], func=AF.Square), reads=[BP], writes=[Bsq4])
                            op(DVE, lambda: V.tensor_reduce(out=ss4[:], in_=sq.rearrange("p (h e) -> p h e", e=64), axis=AX.X, op=ALU.add),
                               reads=[Bsq4], writes=[Bss4])
                            op(ACT, lambda: A.activation(out=ss4[:], in_=ss4[:], func=AF.Ln, scale=1.0 / 64, bias=epsc[:, 0:1]),
                               reads=[Bss4, Bc], writes=[Bss4])
                            op(ACT, lambda: A.activation(out=rs4[:], in_=ss4[:], func=AF.Exp, scale=-0.5), reads=[Bss4], writes=[Brs4])
                            for h in range(2):
                                hs = slice(h * 64, (h + 1) * 64)
                                op(DVE, lambda P=P, j=j, h=h, hs=hs: V.scalar_tensor_tensor(out=qb[j][:, hs], in0=P[:, hs], scalar=rs4[:, h:h + 1], in1=qgc[:, hs],
                                                                                        op0=ALU.mult, op1=ALU.mult), reads=[BP, Brs4, Bg], writes=[Bqb[j]])
                            for h in range(2):
                                hs = slice(h * 64, (h + 1) * 64)
                                op(DVE, lambda P=P, j=j, h=h, hs=hs: V.scalar_tensor_tensor(out=kfin[j][:, hs], in0=P[:, 128 + h * 64:128 + (h + 1) * 64],
                                                                                        scalar=rs4[:, 2 + h:3 + h], in1=kgc[:, hs], op0=ALU.mult, op1=ALU.mult),
                                   reads=[BP, Brs4, Bg], writes=[Bkf[j]])
                            op(POOL, lambda j=j: G.tensor_copy(out=kb_[j][:], in_=kfin[j][:]), reads=[Bkf[j]], writes=[Bkb[j]])
                            op(ACT, lambda P=P, j=j: A.activation(out=vfin[j][:], in_=P[:, 256:384], func=AF.Copy), reads=[BP], writes=[Bvf[j]])
                            op(POOL, lambda i=i, j=j: G.tensor_copy(out=vnat[:, i, :], in_=vfin[j][:]), reads=[Bvf[j]], writes=[Bvn])
                            if i < 16:
                                dma(SP, okp[l, tcols(i), csl], kfin[j][:], reads=[Bkf[j]])
                                dma(SP, ovp[l, tcols(i), csl], vfin[j][:], reads=[Bvf[j]])
                            else:
                                for b in range(4):
                                    dma(SP, oks[l, b:b + 1, csl], kfin[j][32 * b:32 * b + 1, :], reads=[Bkf[j]])
                                    dma(SP, ovs[l, b:b + 1, csl], vfin[j][32 * b:32 * b + 1, :], reads=[Bvf[j]])
                                op(POOL, lambda j=j: G.tensor_copy(out=qs_tok[:, csl], in_=qb[j][:]), reads=[Bqb[j]], writes=[Bst])
                                op(POOL, lambda j=j: G.tensor_copy(out=ks_tok[:, csl], in_=kb_[j][:]), reads=[Bkb[j]], writes=[Bst])
                                op(POOL, lambda j=j: G.tensor_copy(out=vs_tok[:, csl], in_=vfin[j][:]), reads=[Bvf[j]], writes=[Bst])
                        def a1_back(i):
                            j = i % 2
                            op(PE, lambda j=j: TE.transpose(PB[j][:, 0:128], qb[j][:], ident_bf[:]), reads=[Bqb[j], Bc], writes=[BPB[j]])
                            op(PE, lambda j=j: TE.transpose(PB[j][:, 128:256], kb_[j][:], ident_bf[:]), reads=[Bkb[j], Bc], writes=[BPB[j]], inc=True)
                            op(DVE, lambda i=i, j=j: V.tensor_copy(out=qkT[:, :, tcols(i)], in_=PB[j][:, 0:256].rearrange("p (a t) -> p a t", a=2)),
                               reads=[BPB[j]], writes=[BqkT])
                        if os.environ.get("KPIPE", "0") == "1":
                            a1_front(0)
                            for i in range(1, NT):
                                a1_front(i)
                                a1_back(i - 1)
                            a1_back(NT - 1)
                        else:
                            for i in range(NT):
                                a1_front(i)
                                a1_back(i)
                        T.ck(f"A1_{l}_{c}")

                        def tsl_of(dil):
                            def tsl(blk):
                                if dil == 4:
                                    jb, r4 = blk // 4, blk % 4
                                    return slice(512 * jb + r4, 512 * (jb + 1), 4)
                                return slice(blk, 2048, 16)
                            return tsl

                        def vproj(dil):
                            tsl = tsl_of(dil)
                            for g4 in range(4):
                                k = 2 + g4 % 2
                                P, BP = PS[k], BPS[k]
                                for bi in range(4):
                                    blk = 4 * g4 + bi
                                    for kc in range(8):
                                        op(PE, lambda P=P, bi=bi, blk=blk, kc=kc: TE.matmul(P[:, tcols(bi)], lhsT=hT[:, kc, tsl(blk)], rhs=wt[:, kc, 256:384],
                                                                                           start=(kc == 0), stop=(kc == 7)), reads=BhT[0:16] + [Bw], writes=[BP])
                                op(ACT, lambda P=P, g4=g4: A.activation(out=vord[:, 4 * g4:4 * g4 + 4, :], in_=P[:].rearrange("p (a t) -> p a t", a=4), func=AF.Copy),
                                   reads=[BP], writes=[Bvo])

                        vproj(4)
                        for tg in range(5):
                            n = 512 if tg < 4 else 128
                            cols = slice(tg * 512, tg * 512 + n)
                            k = 4 + tg % 2
                            P, BP = PS[k], BPS[k]
                            hb_ = BhT[4 * tg:4 * tg + 4] if tg < 4 else [BhT[16]]
                            for kc in range(8):
                                op(PE, lambda P=P, kc=kc, cols=cols, n=n: TE.matmul(P[:, 0:n], lhsT=wt[:, kc, 384:512], rhs=hT[:, kc, cols],
                                                                                  start=(kc == 0), stop=(kc == 7)), reads=hb_ + [Bw], writes=[BP])
                            silu_from_psum(P, n, zT[:, cols], ezt, Bez, BP, [BzT])
                        op(POOL, lambda c=c: G.tensor_copy(out=zsamp[:, c, :], in_=zT[:, 2048:TOK:32]), reads=[BzT], writes=[Bst])

                        T.ck(f"A2_{l}_{c}")
                        def attn_front(qsl, kcur, kprev, vcur, vprev, first):
                            n_ = blk_ctr[0]
                            blk_ctr[0] += 1
                            jj = n_ % 2
                            Sb = [(PS[n_ % 2], BPS[n_ % 2]), (PS[2 + n_ % 2], BPS[2 + n_ % 2])]
                            kbs = [1] if kprev is None else [0, 1]
                            lo = 0 if kprev is not None else 128
                            for h in range(2):
                                S, BS = Sb[h]
                                hs = slice(h * 64, (h + 1) * 64)
                                for kb in kbs:
                                    ks = kprev if kb == 0 else kcur
                                    op(PE, lambda S=S, hs=hs, ks=ks, kb=kb: TE.matmul(S[:, kb * 128:(kb + 1) * 128], lhsT=qkT[hs, 1, ks], rhs=qkT[hs, 0, qsl],
                                                                                   start=True, stop=True), reads=[BqkT], writes=[BS], inc=(kb == 1))
                            for h in range(2):
                                S, BS = Sb[h]
                                op(ACT, lambda S=S, h=h: A.activation(out=pp[jj][:, h * 256 + lo:(h + 1) * 256], in_=S[:, lo:256], func=AF.Exp),
                                   reads=[BS], writes=[Bp[jj]], inc=True)
                            pv = pp[jj][:].rearrange("p (h x) -> p h x", h=2)[:, :, lo:256]
                            mv_ = mask01[:].rearrange("p (h x) -> p h x", h=2)[:, :, lo:256]
                            op(POOL, lambda pv=pv, mv_=mv_: G.tensor_tensor(out=pv, in0=pv, in1=mv_, op=ALU.mult), reads=[Bp[jj], Bc], writes=[Bp[jj]], inc=True)
                            return (n_, qsl, kbs, vcur, vprev, first)

                        def attn_back(ctx):
                            n_, qsl, kbs, vcur, vprev, first = ctx
                            jj = n_ % 2
                            U, BU = PS[4 + n_ % 2], BPS[4 + n_ % 2]
                            for part in range(2):
                                for h in range(2):
                                    hs = slice(h * 64, (h + 1) * 64)
                                    for idx, kb in enumerate(kbs):
                                        vb = vprev if kb == 0 else vcur
                                        lhsT = vb[:, hs] if part == 0 else ones_bf[:, 0:64]
                                        o0 = (h * 2 + kb) * 128
                                        op(PE, lambda U=U, hs=hs, part=part, lhsT=lhsT, o0=o0, idx=idx: TE.matmul(
                                            U[hs, part * 128:(part + 1) * 128], lhsT=lhsT, rhs=pp[jj][:, o0:o0 + 128],
                                            start=(idx == 0), stop=(idx == len(kbs) - 1)), reads=[Bp[jj], Bvn, Bvo, Bc], writes=[BU],
                                           inc=(part == 1 and h == 1 and idx == len(kbs) - 1))
                            uv = U[:, 0:256].rearrange("p (a q) -> p a q", a=2)
                            if first:
                                op(DVE, lambda: V.tensor_copy(out=UD[:, :, qsl], in_=uv), reads=[BU], writes=[BUD], inc=True)
                            else:
                                op(DVE, lambda: V.tensor_tensor(out=UD[:, :, qsl], in0=UD[:, :, qsl], in1=uv, op=ALU.add), reads=[BU, BUD], writes=[BUD], inc=True)

                        def run_blocks(specs):
                            prev = None
                            for sp_ in specs:
                                ctx = attn_front(*sp_)
                                if prev is not None:
                                    attn_back(prev)
                                prev = ctx
                            attn_back(prev)

                        run_blocks([(tcols(i), tcols(i), tcols(i - 1) if i > 0 else None, vnat[:, i, :], vnat[:, i - 1, :] if i > 0 else None, True)
                                    for i in range(16)])
                        T.ck(f"A4a_{l}_{c}")
                        for dil in (4, 16):
                            tsl = tsl_of(dil)
                            if dil == 16:
                                vproj(16)
                            run_blocks([(tsl(blk), tsl(blk), tsl(blk - 4), vord[:, blk, :], vord[:, blk - 4, :], False) if (dil == 4 and blk >= 4)
                                        else (tsl(blk), tsl(blk), None, vord[:, blk, :], None, False) for blk in range(16)])
                        T.ck(f"A4b_{l}_{c}")
                        op(ACT, lambda: A.activation(out=UD[:, 1, :], in_=UD[:, 1, :], func=AF.Ln), reads=[BUD], writes=[BUD])
                        op(ACT, lambda: A.activation(out=UD[:, 1, :], in_=UD[:, 1, :], func=AF.Exp, scale=-1.0), reads=[BUD], writes=[BUD])
                        op(DVE, lambda: V.tensor_tensor(out=UD[:, 0, :], in0=UD[:, 0, :], in1=UD[:, 1, :], op=ALU.mult), reads=[BUD], writes=[BUD])
                        op(POOL, lambda c=c: G.tensor_tensor(out=yT[:, c, 0:2048], in0=UD[:, 0, :], in1=zT[:, 0:2048], op=ALU.mult),
                           reads=[BUD, BzT], writes=[ByTp[c]])
                    T.barrier()

                T.ck(f"A4_{l}")
                with contextlib.ExitStack() as ar:
                    selb = sbt(ar, "s_selb", [128, 4, 128], BF16)
                    selbf = sbt(ar, "s_selbf", [128, 512], F32)
                    Kr = [sbt(ar, f"s_Kr{j}", [128, 768], BF16) for j in range(2)]
                    Vr = [sbt(ar, f"s_Vr{j}", [128, 3, 768], BF16) for j in range(2)]
                    prod = sbt(ar, "s_prod", [128, 768], F32)
                    sc = sbt(ar, "s_sc", [128, 12], F32)
                    pall = [sbt(ar, f"s_pall{j}", [128, 3, 12], BF16) for j in range(2)]
                    pnew = sbt(ar, "s_pnew", [128, 12], F32)
                    pnm = sbt(ar, "s_pnm", [128, 4, 12], BF16)
                    rd = sbt(ar, "s_rd", [128, 6], F32)
                    osb = sbt(ar, "s_osb", [128, 6], F32)
                    Bsel, Bprod, Bsc, Bpn, Bpnm, Brd, Bos = Buf("selb"), Buf("prod"), Buf("sc"), Buf("pnew"), Buf("pnm"), Buf("rd"), Buf("osb")
                    BKr = [Buf("Kr0"), Buf("Kr1")]
                    BVr = [Buf("Vr0"), Buf("Vr1")]
                    Bpa = [Buf("pa0"), Buf("pa1")]
                    op(POOL, lambda: G.memset(selbf[:], 1.0), writes=[Bsel])
                    op(POOL, lambda: G.affine_select(out=selbf[:].rearrange("p (b m) -> p b m", b=4), in_=selbf[:].rearrange("p (b m) -> p b m", b=4),
                                                     pattern=[[-32, 4], [0, 128]], compare_op=ALU.is_equal, fill=0.0, base=0, channel_multiplier=1),
                       reads=[Bsel], writes=[Bsel])
                    op(DVE, lambda: V.tensor_copy(out=selb[:].rearrange("p b m -> p (b m)"), in_=selbf[:]), reads=[Bsel], writes=[Bsel])
                    op(DVE, lambda: V.tensor_tensor(out=prod[:], in0=qs_tok[:], in1=ks_tok[:], op=ALU.mult), reads=[Bst], writes=[Bprod])
                    op(DVE, lambda: V.tensor_reduce(out=sc[:], in_=prod[:].rearrange("p (h e) -> p h e", e=64), axis=AX.X, op=ALU.add),
                       reads=[Bprod], writes=[Bsc])
                    op(ACT, lambda: A.activation(out=pnew[:], in_=sc[:], func=AF.Exp), reads=[Bsc], writes=[Bpn])
                    for b in range(4):
                        op(DVE, lambda b=b: V.tensor_scalar(out=pnm[:, b, :], in0=pnew[:], scalar1=rowmask_s3[:, b:b + 1], scalar2=None, op0=ALU.mult),
                           reads=[Bpn, Bc], writes=[Bpnm])
                    kctr = 0
                    for b in range(4):
                        bj = b % 2
                        for pi, (r0, step) in enumerate([(1920, 1), (1536, 4), (0, 16)]):
                            rows = slice(r0, 2048, step)
                            kj = kctr % 2
                            kctr += 1
                            dma(POOL, Kr[kj][:], ck[l, b, rows, :], writes=[BKr[kj]])
                            dma(POOL, Vr[bj][:, pi, :], cv[l, b, rows, :], writes=[BVr[bj]])
                            for hf in range(2):
                                op(PE, lambda b=b, hf=hf: TE.matmul(PS[hf][:, 0:384], lhsT=selb[:, b, :], rhs=qs_tok[:, hf * 384:(hf + 1) * 384],
                                                                  start=True, stop=True), reads=[Bsel, Bst], writes=[BPS[hf]])
                            for hf in range(2):
                                op(DVE, lambda kj=kj, hf=hf: V.tensor_tensor(out=prod[:, hf * 384:(hf + 1) * 384], in0=Kr[kj][:, hf * 384:(hf + 1) * 384],
                                                                           in1=PS[hf][:, 0:384], op=ALU.mult), reads=[BKr[kj], BPS[hf]], writes=[Bprod])
                            op(DVE, lambda: V.tensor_reduce(out=sc[:], in_=prod[:].rearrange("p (h e) -> p h e", e=64), axis=AX.X, op=ALU.add),
                               reads=[Bprod], writes=[Bsc])
                            op(ACT, lambda bj=bj, pi=pi: A.activation(out=pall[bj][:, pi, :], in_=sc[:], func=AF.Exp), reads=[Bsc], writes=[Bpa[bj]])
                        PO, BPO = PS[2 + bj], BPS[2 + bj]
                        for part in range(2):
                            for c in range(6):
                                o0 = part * 16 + 2 * c
                                for pi in range(3):
                                    lhsT = Vr[bj][:, pi, c * 128:(c + 1) * 128] if part == 0 else ones_bf[:]
                                    op(PE, lambda PO=PO, o0=o0, lhsT=lhsT, pi=pi, c=c: TE.matmul(PO[:, o0:o0 + 2], lhsT=lhsT, rhs=pall[bj][:, pi, 2 * c:2 * c + 2],
                                                                                              start=(pi == 0), stop=False), reads=[BVr[bj], Bpa[bj], Bc], writes=[BPO])
                                lhsT = vs_tok[:, c * 128:(c + 1) * 128] if part == 0 else ones_bf[:]
                                op(PE, lambda PO=PO, o0=o0, lhsT=lhsT, b=b, c=c: TE.matmul(PO[:, o0:o0 + 2], lhsT=lhsT, rhs=pnm[:, b, 2 * c:2 * c + 2],
                                                                                        start=False, stop=True), reads=[Bst, Bpnm, Bc], writes=[BPO])
                        for hh in range(2):
                            rws = slice(hh * 64, hh * 64 + 64)
                            op(DVE, lambda PO=PO, rws=rws, hh=hh: V.reciprocal(out=rd[rws, :], in_=PO[rws, 16 + hh:28:2]), reads=[BPO], writes=[Brd])
                            op(DVE, lambda PO=PO, rws=rws, hh=hh: V.tensor_tensor(out=osb[rws, :], in0=PO[rws, hh:12:2], in1=rd[rws, :], op=ALU.mult),
                               reads=[BPO, Brd], writes=[Bos])
                        op(DVE, lambda b=b: V.tensor_tensor(out=yT[:, :, 2048 + 32 * b:2048 + 32 * b + 1], in0=osb[:].unsqueeze(2),
                                                            in1=zsamp[:, :, b:b + 1], op=ALU.mult), reads=[Bos, Bst], writes=ByTs)
                    T.barrier()
                T.ck(f"A5_{l}")
                out_proj(l, 6)
                T.barrier()
                T.ck(f"A_{l}")

            with contextlib.ExitStack() as ar:
                vn = sbt(ar, "b_vn", [128, NT, 512], BF16)
                lng_t = sbt(ar, "b_lng", [128, 512], F32)
                lnb_t = sbt(ar, "b_lnb", [128, 512], F32)
                st6 = sbt(ar, "b_st6", [128, 6], F32)
                mv = sbt(ar, "b_mv", [128, 2], F32)
                lnv = sbt(ar, "b_lnv", [128, 1], F32)
                rsb = sbt(ar, "b_rsb", [128, 1], F32)
                vnf = sbt(ar, "b_vnf", [128, 512], F32)
                vnf2 = [sbt(ar, f"b_vnf2{j}", [128, 512], F32) for j in range(2)]
                wl = sbt(ar, "b_wl", [128, 8, 128], F32)
                wlb = sbt(ar, "b_wlb", [128, 8, 128], BF16)
                wT = sbt(ar, "b_wT", [128, 8, 128], BF16)
                wTs = sbt(ar, "b_wTs", [128, 8, 128], BF16)
                w00 = sbt(ar, "b_w00", [128, 8], F32)
                bsf = sbt(ar, "b_bsf", [8, 128], F32)
                bsb = sbt(ar, "b_bsb", [8, 128], BF16)
                bs0 = sbt(ar, "b_bs0", [8, 128], BF16)
                ezb = sbt(ar, "b_ez", [128, 512], F32)
                t1 = sbt(ar, "b_t1", [128, 512], F32)
                Bvn_, Blg, Bst6, Bmv, Blnv, Brsb, Bvnf = Buf("vn"), Buf("lng"), Buf("st6"), Buf("mv"), Buf("lnv"), Buf("rsb"), Buf("vnf")
                Bvnf2 = [Buf("vnf20"), Buf("vnf21")]
                Bwl, Bwlb, BwT, Bbs, Bezb, Bt1 = Buf("wl"), Buf("wlb"), Buf("wT"), Buf("bs"), Buf("ezb"), Buf("t1")
                load_wo(l, 768, 4)
                bsel = sbt(ar, "b_bsel", [8, 512], BF16)
                bself = sbt(ar, "b_bself", [8, 512], F32)
                Bbsl = Buf("bsel")
                op(POOL, lambda: G.memset(bself[:], 1.0), writes=[Bbsl])
                op(POOL, lambda: G.affine_select(out=bself[:], in_=bself[:], pattern=[[1, 512]], compare_op=ALU.is_ge,
                                                 fill=0.0, base=0, channel_multiplier=-64), reads=[Bbsl], writes=[Bbsl])
                op(POOL, lambda: G.affine_select(out=bself[:], in_=bself[:], pattern=[[-1, 512]], compare_op=ALU.is_ge,
                                                 fill=0.0, base=63, channel_multiplier=64), reads=[Bbsl], writes=[Bbsl])
                op(DVE, lambda: V.tensor_copy(out=bsel[:], in_=bself[:]), reads=[Bbsl], writes=[Bbsl])
                dma(SP, lng_t[:], lng[l].partition_broadcast(128), writes=[Blg])
                dma(SP, lnb_t[:], lnb[l].partition_broadcast(128), writes=[Blg])
                dma(SP, wl[:], sgw[l].rearrange("g t s -> t g s"), writes=[Bwl])
                dma(SP, bsf[:], sgb[l], writes=[Bbs])
                u = uidx
                uidx += 1
                load_unit(u)
                load_unit(u + 1)
                wt, Bw = WA[u % 2], BW[u % 2]
                for i in range(NT):
                    j = i % 2
                    P, BP = PS[j], BPS[j]
                    for kc in range(8):
                        op(PE, lambda P=P, kc=kc, i=i: TE.matmul(P[:], lhsT=hT[:, kc, tcols(i)], rhs=wt[:, kc, :], start=(kc == 0), stop=(kc == 7)),
                           reads=[BhT[i], Bw], writes=[BP])
                    op(DVE, lambda P=P: V.bn_stats(out=st6[:], in_=P[:]), reads=[BP], writes=[Bst6])
                    op(DVE, lambda: V.bn_aggr(out=mv[:], in_=st6[:]), reads=[Bst6], writes=[Bmv])
                    op(ACT, lambda: A.activation(out=lnv[:], in_=mv[:, 1:2], func=AF.Ln, scale=1.0, bias=epsc[:, 0:1]), reads=[Bmv, Bc], writes=[Blnv])
                    op(ACT, lambda: A.activation(out=rsb[:], in_=lnv[:], func=AF.Exp, scale=-0.5), reads=[Blnv], writes=[Brsb])
                    op(DVE, lambda P=P: V.tensor_scalar(out=vnf[:], in0=P[:], scalar1=mv[:, 0:1], scalar2=rsb[:, 0:1], op0=ALU.subtract, op1=ALU.mult),
                       reads=[BP, Bmv, Brsb], writes=[Bvnf])
                    op(POOL, lambda: G.tensor_tensor(out=vnf[:], in0=vnf[:], in1=lng_t[:], op=ALU.mult), reads=[Bvnf, Blg], writes=[Bvnf])
                    op(POOL, lambda j=j: G.tensor_tensor(out=vnf2[j][:], in0=vnf[:], in1=lnb_t[:], op=ALU.add), reads=[Bvnf, Blg], writes=[Bvnf2[j]])
                    op(ACT, lambda i=i, j=j: A.activation(out=vn[:, i, :], in_=vnf2[j][:], func=AF.Copy), reads=[Bvnf2[j]], writes=[Bvn_])
                    if i == 16:
                        for b in range(4):
                            dma(SP, osg[l, b:b + 1, :], vnf2[j][32 * b:32 * b + 1, :], reads=[Bvnf2[j]])
                T.ck(f"B1_{l}")
                op(DVE, lambda: V.tensor_copy(out=wlb[:], in_=wl[:]), reads=[Bwl], writes=[Bwlb])
                for g in range(8):
                    op(PE, lambda g=g: TE.transpose(PB[0][:, tcols(g)], wlb[:, g, :], ident_bf[:]), reads=[Bwlb, Bc], writes=[BPB[0]])
                op(DVE, lambda: V.tensor_tensor(out=wT[:], in0=PB[0][:].rearrange("p (g t) -> p g t", g=8),
                                                in1=maskcur[:].unsqueeze(1).to_broadcast([128, 8, 128]), op=ALU.mult), reads=[BPB[0], Bc], writes=[BwT])
                op(PE, lambda: TE.matmul(PS[2][:, 0:8], lhsT=ones_f[0:1, :], rhs=wl[0:1, :, 0], start=True, stop=True), reads=[Bwl, Bc], writes=[BPS[2]])
                op(DVE, lambda: V.tensor_copy(out=w00[:], in_=PS[2][:, 0:8]), reads=[BPS[2]], writes=[BwT])
                op(DVE, lambda: V.tensor_tensor(out=wTs[:], in0=ident_bf[:].unsqueeze(1).to_broadcast([128, 8, 128]),
                                                in1=w00[:].unsqueeze(2).to_broadcast([128, 8, 128]), op=ALU.mult), reads=[BwT, Bc], writes=[BwT])
                op(DVE, lambda: V.tensor_copy(out=bsb[:], in_=bsf[:]), reads=[Bbs], writes=[Bbs])
                op(DVE, lambda: V.tensor_copy(out=bs0[:], in_=bsf[:, 0:1].to_broadcast([8, 128])), reads=[Bbs], writes=[Bbs])
                T.ck(f"B2_{l}")
                for cb in range(4):
                    u = uidx
                    uidx += 1
                    load_unit(u)
                    load_unit(u + 1)
                    wt, Bw = WA[u % 2], BW[u % 2]
                    for tg in range(5):
                        n = 512 if tg < 4 else 128
                        cols = slice(tg * 512, tg * 512 + n)
                        hb_ = BhT[4 * tg:4 * tg + 4] if tg < 4 else [BhT[16]]
                        Pu, BPu = PS[0 + tg % 2], BPS[0 + tg % 2]
                        Pz, BPz = PS[2 + tg % 2], BPS[2 + tg % 2]
                        Pm, BPm = PS[4 + tg % 2], BPS[4 + tg % 2]
                        for kc in range(8):
                            op(PE, lambda Pu=Pu, kc=kc, cols=cols, n=n: TE.matmul(Pu[:, 0:n], lhsT=wt[:, kc, 0:128], rhs=hT[:, kc, cols],
                                                                                start=(kc == 0), stop=(kc == 7)), reads=hb_ + [Bw], writes=[BPu])
                        for kc in range(8):
                            op(PE, lambda Pz=Pz, kc=kc, cols=cols, n=n: TE.matmul(Pz[:, 0:n], lhsT=wt[:, kc, 128:256], rhs=hT[:, kc, cols],
                                                                                start=(kc == 0), stop=(kc == 7)), reads=hb_ + [Bw], writes=[BPz])
                        for ti in range(n // 128):
                            i = 4 * tg + ti
                            for gg in range(2):
                                g = 2 * cb + gg
                                rws = slice(gg * 64, gg * 64 + 64)
                                wmat = wT if i < 16 else wTs
                                bmat = bsb if i < 16 else bs0
                                op(PE, lambda Pm=Pm, rws=rws, ti=ti, i=i, g=g, wmat=wmat: TE.matmul(
                                    Pm[rws, tcols(ti)], lhsT=vn[:, i, g * 64:(g + 1) * 64], rhs=wmat[:, g, :], start=True, stop=False),
                                   reads=[Bvn_, BwT], writes=[BPm])
                                op(PE, lambda Pm=Pm, rws=rws, ti=ti, g=g, bmat=bmat: TE.matmul(
                                    Pm[rws, tcols(ti)], lhsT=bsel[0:8, g * 64:(g + 1) * 64], rhs=bmat[0:8, :], start=False, stop=True),
                                   reads=[Bbs, Bbsl], writes=[BPm])
                        silu_from_psum(Pz, n, t1[:, 0:n], ezb, Bezb, BPz, [Bt1])
                        op(DVE, lambda Pu=Pu, n=n: V.tensor_tensor(out=t1[:, 0:n], in0=t1[:, 0:n], in1=Pu[:, 0:n], op=ALU.mult), reads=[Bt1, BPu], writes=[Bt1])
                        op(DVE, lambda Pm=Pm, n=n, cols=cols, cb=cb: V.tensor_tensor(out=yT[:, cb, cols], in0=t1[:, 0:n], in1=Pm[:, 0:n], op=ALU.mult),
                           reads=[Bt1, BPm], writes=[ByTp[cb] if tg < 4 else ByTs[cb]])
                T.barrier()
                T.ck(f"B3_{l}")
                out_proj(l, 4)
                T.barrier()
                T.ck(f"B_{l}")

            with contextlib.ExitStack() as ar:
                scanmask = sbt(ar, "c_scanm", [128, 512], F32)
                names = ["ef", "f", "g", "G", "eG", "kk", "eq", "q", "ez", "zs", "ln"]
                alias = {"eNG": "g", "rs": "ln", "sq": "eq", "o": "ez"}
                t = {nm: sbt(ar, "c_" + nm, [128, 512], F32) for nm in names}
                Bt_ = {nm: Buf("c_" + nm) for nm in names}
                for a_, b_ in alias.items():
                    t[a_] = t[b_]
                    Bt_[a_] = Bt_[b_]
                for nm in ("kt", "khT", "qt"):
                    t[nm] = sbt(ar, "c_" + nm, [128, 512], BF16)
                    Bt_[nm] = Buf("c_" + nm)
                Sbf = sbt(ar, "c_Sbf", [128, 9, 128], BF16)
                BSb = [Buf(f"Sb{j}") for j in range(9)]
                tv = sbt(ar, "c_tv", [128, 4, 128], BF16)
                khtok = sbt(ar, "c_khtok", [128, 4, 4, 128], BF16)
                ATm = sbt(ar, "c_ATm", [128, 4, 128], BF16)
                Sring = sbt(ar, "c_Sring", [128, 9, 128], F32)
                BSr = [Buf(f"Sr{j}") for j in range(9)]
                S0 = [Sring[:, 1, :], Sring[:, 2, :]]
                Sn = [Sring[:, 3, :], Sring[:, 4, :]]
                BS0 = [BSr[1], BSr[2]]
                BSn = [BSr[3], BSr[4]]
                Btv, Bkh, BAT, Bsm = Buf("tv"), Buf("khtok"), Buf("ATm"), Buf("scanm")
                load_wo(l, 1280, 6)
                op(DVE, lambda: V.memset(scanmask[:], 1.0), writes=[Bsm])
                op(DVE, lambda: V.memset(scanmask[:].rearrange("p (c j) -> p c j", j=32)[:, :, 0:1], 0.0), reads=[Bsm], writes=[Bsm])
                for hd in range(6):
                    u = uidx
                    uidx += 1
                    load_unit(u)
                    load_unit(u + 1)
                    wt, Bw = WA[u % 2], BW[u % 2]
                    lbc = lbT[:, l * 6 + hd:l * 6 + hd + 1]
                    omc = omlbT[:, l * 6 + hd:l * 6 + hd + 1]
                    hgc = hgT[:, l * 6 + hd:l * 6 + hd + 1]
                    op(DVE, lambda: V.memset(Sring[:, 0, :], 0.0), writes=[BSr[0]])
                    op(POOL, lambda: G.memset(Sbf[:, 0, :], 0.0), writes=[BSb[0]])
                    for tg in range(5):
                        n = 512 if tg < 4 else 128
                        nti = n // 128
                        cols = slice(tg * 512, tg * 512 + n)
                        hb_ = BhT[4 * tg:4 * tg + 4] if tg < 4 else [BhT[16]]
                        for pi_, (P, BP, w0) in enumerate([(PS[0], BPS[0], 0), (PS[1], BPS[1], 128), (PS[2], BPS[2], 384)]):
                            for kc in range(8):
                                op(PE, lambda P=P, kc=kc, w0=w0, cols=cols, n=n: TE.matmul(P[:, 0:n], lhsT=wt[:, kc, w0:w0 + 128], rhs=hT[:, kc, cols],
                                                                                         start=(kc == 0), stop=(kc == 7)), reads=hb_ + [Bw], writes=[BP])
                        for ti in range(nti):
                            i = 4 * tg + ti
                            for kc in range(8):
                                op(PE, lambda ti=ti, i=i, kc=kc: TE.matmul(PS[3][:, tcols(ti)], lhsT=hT[:, kc, tcols(i)], rhs=wt[:, kc, 256:384],
                                                                         start=(kc == 0), stop=(kc == 7)), reads=[BhT[i], Bw], writes=[BPS[3]])
                        sl = slice(0, n)
                        sigmoid_act(PS[1][:, sl], t["ef"][:, sl], [BPS[1]], Bt_["ef"])
                        op(DVE, lambda: V.tensor_scalar(out=t["f"][:, sl], in0=t["ef"][:, sl], scalar1=omc, scalar2=lbc, op0=ALU.mult, op1=ALU.add),
                           reads=[Bt_["ef"], Bc], writes=[Bt_["f"]])
                        op(POOL, lambda: G.tensor_scalar(out=t["kk"][:, sl], in0=t["f"][:, sl], scalar1=-1.0, scalar2=1.0, op0=ALU.mult, op1=ALU.add),
                           reads=[Bt_["f"]], writes=[Bt_["kk"]])
                        sigmoid_act(PS[0][:, sl], t["eq"][:, sl], [BPS[0]], Bt_["eq"])
                        op(DVE, lambda: V.tensor_tensor(out=t["q"][:, sl], in0=PS[0][:, sl], in1=t["eq"][:, sl], op=ALU.mult), reads=[BPS[0], Bt_["eq"]], writes=[Bt_["q"]])
                        silu_from_psum(PS[2], n, t["zs"][:, sl], t["ez"], Bt_["ez"], BPS[2], [Bt_["zs"]])
                        op(ACT, lambda: A.activation(out=tv[:, 0:nti, :], in_=PS[3][:, sl].rearrange("p (a v) -> p a v", v=128), func=AF.Copy),
                           reads=[BPS[3]], writes=[Btv])
                        if tg < 4:
                            op(ACT, lambda: A.activation(out=t["g"][:], in_=t["f"][:], func=AF.Ln), reads=[Bt_["f"]], writes=[Bt_["g"]])
                            op(DVE, lambda: V.tensor_tensor_scan(out=t["G"][:], data0=scanmask[:], data1=t["g"][:], initial=0.0, op0=ALU.mult, op1=ALU.add),
                               reads=[Bt_["g"], Bsm], writes=[Bt_["G"]])
                            op(ACT, lambda: A.activation(out=t["eG"][:], in_=t["G"][:], func=AF.Exp), reads=[Bt_["G"]], writes=[Bt_["eG"]])
                            op(ACT, lambda: A.activation(out=t["eNG"][:], in_=t["G"][:], func=AF.Exp, scale=-1.0), reads=[Bt_["G"]], writes=[Bt_["eNG"]])
                            op(POOL, lambda: G.tensor_tensor(out=t["kt"][:], in0=t["kk"][:], in1=t["eNG"][:], op=ALU.mult), reads=[Bt_["kk"], Bt_["eNG"]], writes=[Bt_["kt"]])
                            op(POOL, lambda: G.tensor_tensor(out=t["khT"][:].rearrange("p (c j) -> p c j", j=32), in0=t["kt"][:].rearrange("p (c j) -> p c j", j=32),
                                                             in1=t["eG"][:, 31:512:32].unsqueeze(2).to_broadcast([128, 16, 32]), op=ALU.mult),
                               reads=[Bt_["kt"], Bt_["eG"]], writes=[Bt_["khT"]])
                            op(POOL, lambda: G.tensor_tensor(out=t["qt"][:], in0=t["q"][:], in1=t["eG"][:], op=ALU.mult), reads=[Bt_["q"], Bt_["eG"]], writes=[Bt_["qt"]])
                            T.ck(f"C1_{l}_{hd}_{tg}")
                            for ti in range(4):
                                op(PE, lambda ti=ti: TE.transpose(PB[0][:, tcols(ti)], t["khT"][:, tcols(ti)], ident_bf[:]), reads=[Bt_["khT"], Bc], writes=[BPB[0]], inc=(ti == 3))
                            for ch in range(4):
                                if ch % 2 == 0:
                                    op(ACT, lambda ch=ch: A.activation(out=khtok[:, :, ch, :], in_=PB[0][:, 0:512].rearrange("p (a k) -> p a k", a=4), func=AF.Copy,
                                                                       scale=rowmask[:, ch:ch + 1]), reads=[BPB[0], Bc], writes=[Bkh])
                                else:
                                    op(DVE, lambda ch=ch: V.tensor_scalar(out=khtok[:, :, ch, :], in0=PB[0][:, 0:512].rearrange("p (a k) -> p a k", a=4),
                                                                          scalar1=rowmask[:, ch:ch + 1], scalar2=None, op0=ALU.mult), reads=[BPB[0], Bc], writes=[Bkh])
                            for ti in range(4):
                                op(PE, lambda ti=ti: TE.matmul(PS[1][:, tcols(ti)], lhsT=t["kt"][:, tcols(ti)], rhs=t["qt"][:, tcols(ti)], start=True, stop=True),
                                   reads=[Bt_["kt"], Bt_["qt"]], writes=[BPS[1]])
                            op(DVE, lambda: V.tensor_tensor(out=ATm[:], in0=PS[1][:].rearrange("p (a t) -> p a t", a=4),
                                                            in1=blockmask[:].unsqueeze(1).to_broadcast([128, 4, 128]), op=ALU.mult), reads=[BPS[1], Bc], writes=[BAT])
                            T.ck(f"C2_{l}_{hd}_{tg}")
                            banks_ = [0, 1, 3, 4]
                            for nchk in range(16):
                                ti, ch = nchk // 4, nchk % 4
                                bk, col = banks_[nchk // 4], (nchk % 4) * 128
                                op(PE, lambda bk=bk, col=col, ti=ti, ch=ch: TE.matmul(PS[bk][:, col:col + 128], lhsT=khtok[:, ti, ch, :], rhs=tv[:, ti, :], start=True, stop=True),
                                   reads=[Bkh, Btv], writes=[BPS[bk]], inc=(nchk % 4 == 3))
                            for half in range(2):
                                for r_ in range(8):
                                    nchk = half * 8 + r_
                                    bk, col = banks_[nchk // 4], (nchk % 4) * 128
                                    op(DVE, lambda bk=bk, col=col, nchk=nchk, r_=r_: V.scalar_tensor_tensor(
                                        out=Sring[:, r_ + 1, :], in0=Sring[:, r_, :], scalar=t["eG"][:, nchk * 32 + 31:nchk * 32 + 32], in1=PS[bk][:, col:col + 128],
                                        op0=ALU.mult, op1=ALU.add), reads=[BSr[r_], Bt_["eG"], BPS[bk]], writes=[BSr[r_ + 1]], inc=True)
                                    op(POOL, lambda r_=r_: G.tensor_copy(out=Sbf[:, r_ + 1, :], in_=Sring[:, r_ + 1, :]), reads=[BSr[r_ + 1]], writes=[BSb[r_ + 1]], inc=True)
                                for r_ in range(8):
                                    nchk = half * 8 + r_
                                    ti, ch = nchk // 4, nchk % 4
                                    cc = slice(nchk * 32, nchk * 32 + 32)
                                    op(PE, lambda ti=ti, ch=ch, cc=cc: TE.matmul(PS[2][:, cc], lhsT=tv[:, ti, :], rhs=ATm[:, ti, ch * 32:(ch + 1) * 32], start=True, stop=False),
                                       reads=[Btv, BAT], writes=[BPS[2]])
                                    op(PE, lambda r_=r_, cc=cc: TE.matmul(PS[2][:, cc], lhsT=Sbf[:, r_, :], rhs=t["qt"][:, cc], start=False, stop=True),
                                       reads=[BSb[r_], Bt_["qt"]], writes=[BPS[2]], inc=(r_ == 7))
                                op(DVE, lambda: V.tensor_copy(out=Sring[:, 0, :], in_=Sring[:, 8, :]), reads=[BSr[8]], writes=[BSr[0]])
                                op(POOL, lambda: G.tensor_copy(out=Sbf[:, 0, :], in_=Sbf[:, 8, :]), reads=[BSb[8]], writes=[BSb[0]])
                            no = 512
                        else:
                            op(PE, lambda: TE.transpose(PS[0][:, 0:128], t["kk"][:, 0:128], ident_f[:]), reads=[Bt_["kk"], Bc], writes=[BPS[0]])
                            for b in range(4):
                                op(DVE, lambda b=b: V.tensor_scalar(out=khtok[:, 0, b, :], in0=PS[0][:, 0:128], scalar1=rowmask_s[:, b:b + 1], scalar2=None, op0=ALU.mult),
                                   reads=[BPS[0], Bc], writes=[Bkh])
                            for b in range(4):
                                bj = b % 2
                                dma(SP, S0[bj], st[l, b, hd], writes=[BS0[bj]])
                                ku = 3 + bj
                                op(PE, lambda ku=ku, b=b: TE.matmul(PS[ku][:, 0:128], lhsT=khtok[:, 0, b, :], rhs=tv[:, 0, :], start=True, stop=True),
                                   reads=[Bkh, Btv], writes=[BPS[ku]])
                                op(DVE, lambda bj=bj, ku=ku, b=b: V.scalar_tensor_tensor(out=Sn[bj], in0=S0[bj], scalar=t["f"][:, 32 * b:32 * b + 1],
                                                                                       in1=PS[ku][:, 0:128], op0=ALU.mult, op1=ALU.add),
                                   reads=[BS0[bj], Bt_["f"], BPS[ku]], writes=[BSn[bj]])
                                dma(SP, ohs[l, b, hd], Sn[bj], reads=[BSn[bj]])
                                op(PE, lambda bj=bj, b=b: TE.matmul(PS[2][:, b:b + 1], lhsT=Sn[bj], rhs=t["q"][:, 32 * b:32 * b + 1], start=True, stop=True),
                                   reads=[BSn[bj], Bt_["q"]], writes=[BPS[2]])
                            no = 4
                        T.ck(f"C3_{l}_{hd}_{tg}")
                        so = slice(0, no)
                        op(ACT, lambda: A.activation(out=t["o"][:, so], in_=PS[2][:, so], func=AF.Copy), reads=[BPS[2]], writes=[Bt_["o"]])
                        op(ACT, lambda: A.activation(out=t["sq"][:, so], in_=PS[2][:, so], func=AF.Square), reads=[BPS[2]], writes=[Bt_["sq"]])
                        op(PE, lambda: TE.matmul(PS[5][:, so], lhsT=ones_f[:], rhs=t["sq"][:, so], start=True, stop=True), reads=[Bt_["sq"], Bc], writes=[BPS[5]])
                        op(ACT, lambda: A.activation(out=t["ln"][:, so], in_=PS[5][:, so], func=AF.Ln, scale=1.0 / 128, bias=epsc[:, 0:1]), reads=[BPS[5], Bc], writes=[Bt_["ln"]])
                        op(ACT, lambda: A.activation(out=t["rs"][:, so], in_=t["ln"][:, so], func=AF.Exp, scale=-0.5), reads=[Bt_["ln"]], writes=[Bt_["rs"]])
                        op(DVE, lambda: V.tensor_tensor(out=t["o"][:, so], in0=t["o"][:, so], in1=t["rs"][:, so], op=ALU.mult), reads=[Bt_["o"], Bt_["rs"]], writes=[Bt_["o"]])
                        if tg < 4:
                            op(DVE, lambda cols=cols, hd=hd: V.scalar_tensor_tensor(out=yT[:, hd, cols], in0=t["o"][:], scalar=hgc, in1=t["zs"][:], op0=ALU.mult, op1=ALU.mult),
                               reads=[Bt_["o"], Bt_["zs"], Bc], writes=[ByTp[hd]])
                        else:
                            op(DVE, lambda hd=hd: V.scalar_tensor_tensor(out=yT[:, hd, 2048:TOK:32], in0=t["o"][:, 0:4], scalar=hgc, in1=t["zs"][:, 0:128:32],
                                                                         op0=ALU.mult, op1=ALU.mult), reads=[Bt_["o"], Bt_["zs"], Bc], writes=[ByTs[hd]])
                        T.ck(f"C4_{l}_{hd}_{tg}")
                        if tg == 3:
                            dma(SP, ohp[l, hd], Sring[:, 0, :], reads=[BSr[0]])
                T.barrier()
                T.ck(f"C5_{l}")
                out_proj(l, 6)
                T.barrier()
                T.ck(f"L_{l}")

        T.force = True
        for i in range(16):
            dma(SP, yp[i * 128:(i + 1) * 128, :], xres[:, i, :], reads=[Bx[i]])
        for b in range(4):
            dma(SP, ys[b:b + 1, :], xres[32 * b:32 * b + 1, 16, :], reads=[Bx[16]])
        for Q in (SP, POOL):
            for s in Q.slots:
                if s.val > 0:
                    T._wait(SP, s.sem, s.val)
    return nc


_NC_CACHE = {}


def kernel(x_prompt, x_sample, cache_k, cache_v, state_hgrn, norm_g, w_in, q_norm_g, k_norm_g,
           sgu_ln_g, sgu_ln_b, sgu_w, sgu_b, hgrn_lb_logits, hgrn_norm_g, w_out):
    f = lambda a: np.ascontiguousarray(np.asarray(a, dtype=np.float32))
    x_prompt, x_sample, cache_k, cache_v, state_hgrn = map(f, (x_prompt, x_sample, cache_k, cache_v, state_hgrn))
    shared = {
        "norm_g": f(norm_g), "w_in": f(w_in), "q_norm_g": f(q_norm_g), "k_norm_g": f(k_norm_g),
        "sgu_ln_g": f(sgu_ln_g), "sgu_ln_b": f(sgu_ln_b), "sgu_w": f(sgu_w), "sgu_b": f(sgu_b),
        "hgrn_lb_logits": f(hgrn_lb_logits), "hgrn_norm_g": f(hgrn_norm_g), "w_out": f(w_out),
    }
    in_maps = []
    for c in range(NCORES):
        sb = slice(4 * c, 4 * c + 4)
        m = dict(shared)
        m["xp"] = np.ascontiguousarray(x_prompt[c])
        m["xs"] = np.ascontiguousarray(x_sample[sb, 0, :])
        m["ck"] = np.ascontiguousarray(cache_k[:, sb].reshape(2, 4, 2048, 768))
        m["cv"] = np.ascontiguousarray(cache_v[:, sb].reshape(2, 4, 2048, 768))
        m["st"] = np.ascontiguousarray(state_hgrn[:, sb])
        in_maps.append(m)
    if "nc" not in _NC_CACHE:
        _NC_CACHE["nc"] = build_nc()
    res = run_bass_kernel_spmd(_NC_CACHE["nc"], in_maps, core_ids=list(range(NCORES)))
    R = res.results
    y_prompt = np.stack([R[c]["yp"] for c in range(NCORES)]).astype(np.float32)
    y_sample = np.concatenate([R[c]["ys"] for c in range(NCORES)])[:, None, :].astype(np.float32)
    nkp = np.stack([R[c]["okp"] for c in range(NCORES)], axis=1).reshape(2, 8, 2048, 12, 64).astype(np.float32)
    nvp = np.stack([R[c]["ovp"] for c in range(NCORES)], axis=1).reshape(2, 8, 2048, 12, 64).astype(np.float32)
    nks = np.concatenate([R[c]["oks"] for c in range(NCORES)], axis=1).reshape(2, 32, 1, 12, 64).astype(np.float32)
    nvs = np.concatenate([R[c]["ovs"] for c in range(NCORES)], axis=1).reshape(2, 32, 1, 12, 64).astype(np.float32)
    nsg = np.concatenate([R[c]["osg"] for c in range(NCORES)], axis=1).reshape(2, 32, 1, 512).astype(np.float32)
    nhp = np.stack([R[c]["ohp"] for c in range(NCORES)], axis=1).astype(np.float32)
    nhs = np.concatenate([R[c]["ohs"] for c in range(NCORES)], axis=1).astype(np.float32)
    return (y_prompt, y_sample, nkp, nvp, nks, nvs, nsg, nhp, nhs)
```

```python
import bisect
import contextlib
import os

import numpy as np

import concourse.bass as bass
import concourse.mybir as mybir
from concourse.bass_utils import run_bass_kernel_spmd

F32 = mybir.dt.float32
BF16 = mybir.dt.bfloat16
AF = mybir.ActivationFunctionType
ALU = mybir.AluOpType
AX = mybir.AxisListType

NCORES = 8
NT = 17
TOK = NT * 128
EPS = 1e-6


class Eng:
    def __init__(self, name, eng, sem):
        self.name, self.eng, self.sem = name, eng, sem
        self.nseq = 0
        self.cnt = 0
        self.last = None
        self.inc_seq = []
        self.waited = {}
        self.slots = []
        self.rr = 0


class Slot:
    def __init__(self, sem):
        self.sem = sem
        self.val = 0


class Buf:
    __slots__ = ("name", "w", "r")

    def __init__(self, name):
        self.name = name
        self.w = None
        self.r = {}


class Tracker:
    def __init__(self):
        self.engs = []
        self.stopped = False
        self.force = False
        self.stop_at = os.environ.get("KSTOP", "")

    def ck(self, name):
        if self.stop_at and name == self.stop_at:
            self.stopped = True

    def resolve(self, ev):
        if ev[0] == "s":
            return ev[1], ev[2]
        E, seq = ev[1], ev[2]
        i = bisect.bisect_left(E.inc_seq, seq)
        if i < len(E.inc_seq):
            return E.sem, i + 1
        E.last.then_inc(E.sem, 1)
        E.cnt += 1
        E.inc_seq.append(E.nseq)
        return E.sem, E.cnt

    def _wait(self, E, sem, val):
        if E.waited.get(sem.num, 0) < val:
            E.eng.wait_ge(sem, val)
            E.waited[sem.num] = val

    def deps(self, E, reads, writes):
        evs = []
        for b in reads:
            if b.w is not None:
                evs.append(b.w)
        for b in writes:
            if b.w is not None:
                evs.append(b.w)
            for k, ev in b.r.items():
                if ev[0] == "e" and ev[1] is E:
                    continue
                evs.append(ev)
        need = {}
        for ev in evs:
            if ev[0] == "e" and ev[1] is E and E.name == "pe":
                continue
            sem, val = self.resolve(ev)
            if need.get(sem.num, (None, 0))[1] < val:
                need[sem.num] = (sem, val)
        for num, (sem, val) in need.items():
            self._wait(E, sem, val)

    def op(self, E, fn, reads=(), writes=(), inc=None):
        if self.stopped and not self.force:
            return None
        self.deps(E, reads, writes)
        ins = fn()
        E.nseq += 1
        E.last = ins
        if inc and os.environ.get("KINC", "1") == "1":
            ins.then_inc(E.sem, 1)
            E.cnt += 1
            E.inc_seq.append(E.nseq)
        ev = ("e", E, E.nseq)
        for b in writes:
            b.w = ev
            b.r = {}
        for b in reads:
            b.r[E.name] = ev
        return ins

    def dma(self, Q, out, in_, reads=(), writes=()):
        if self.stopped and not self.force:
            return
        self.deps(Q, reads, writes)
        slot = Q.slots[Q.rr % len(Q.slots)]
        Q.rr += 1
        if slot.val > 0:
            self._wait(Q, slot.sem, slot.val)
        Q.eng.dma_start(out=out, in_=in_).then_inc(slot.sem, 16)
        slot.val += 16
        ev = ("s", slot.sem, slot.val)
        for b in writes:
            b.w = ev
            b.r = {}
        for b in reads:
            b.r[("d", slot.sem.num)] = ev

    def barrier(self):
        if self.stopped and not self.force:
            return
        pts = []
        for F in self.engs:
            if F.nseq > 0:
                pts.append(self.resolve(("e", F, F.nseq)))
            for s in F.slots:
                if s.val > 0:
                    pts.append((s.sem, s.val))
        for E in self.engs:
            for sem, val in pts:
                if sem is E.sem and E.name == "pe":
                    continue
                self._wait(E, sem, val)


def build_nc():
    nc = bass.Bass("TRN2", target_bir_lowering=False)

    def din(name, shape):
        return nc.dram_tensor(name, shape, F32, kind="ExternalInput").ap()

    def dout(name, shape):
        return nc.dram_tensor(name, shape, F32, kind="ExternalOutput").ap()

    xp = din("xp", [2048, 1024])
    xs = din("xs", [4, 1024])
    ck = din("ck", [2, 4, 2048, 768])
    cv = din("cv", [2, 4, 2048, 768])
    st = din("st", [2, 4, 6, 128, 128])
    norm_g = din("norm_g", [2, 1024])
    w_in = din("w_in", [2, 1024, 7680])
    qng = din("q_norm_g", [2, 768])
    kng = din("k_norm_g", [2, 768])
    lng = din("sgu_ln_g", [2, 512])
    lnb = din("sgu_ln_b", [2, 512])
    sgw = din("sgu_w", [2, 8, 128, 128])
    sgb = din("sgu_b", [2, 8, 128])
    lbl = din("hgrn_lb_logits", [2, 768])
    hng = din("hgrn_norm_g", [2, 768])
    w_out = din("w_out", [2, 2048, 1024])
    yp = dout("yp", [2048, 1024])
    ys = dout("ys", [4, 1024])
    okp = dout("okp", [2, 2048, 768])
    ovp = dout("ovp", [2, 2048, 768])
    oks = dout("oks", [2, 4, 768])
    ovs = dout("ovs", [2, 4, 768])
    osg = dout("osg", [2, 4, 512])
    ohp = dout("ohp", [2, 6, 128, 128])
    ohs = dout("ohs", [2, 4, 6, 128, 128])

    T = Tracker()
    es = contextlib.ExitStack()
    with es:
        nmctr = [0]

        def sbt(stack, name, shape, dt):
            nmctr[0] += 1
            return stack.enter_context(nc.sbuf_tensor(f"{name}_{nmctr[0]}", shape, dt))

        sems = [es.enter_context(nc.semaphore(f"sem{i}")) for i in range(20)]
        PE = Eng("pe", nc.tensor, sems[0])
        ACT = Eng("act", nc.scalar, sems[1])
        DVE = Eng("dve", nc.vector, sems[2])
        POOL = Eng("pool", nc.gpsimd, sems[3])
        SP = Eng("sp", nc.sync, None)
        SP.slots = [Slot(s) for s in sems[4:12]]
        POOL.slots = [Slot(s) for s in sems[12:20]]
        T.engs = [PE, ACT, DVE, POOL, SP]
        op, dma = T.op, T.dma
        V, A, G, TE = nc.vector, nc.scalar, nc.gpsimd, nc.tensor

        PS = [es.enter_context(nc.psum_tensor(f"ps{i}", [128, 512], F32)) for i in range(6)]
        PB = [es.enter_context(nc.psum_tensor(f"pb{i}", [128, 1024], BF16)) for i in range(2)]
        BPS = [Buf(f"ps{i}") for i in range(6)]
        BPB = [Buf(f"pb{i}") for i in range(2)]

        xres = sbt(es, "xres", [128, NT, 1024], F32)
        hT = sbt(es, "hT", [128, 8, TOK], BF16)
        yT = sbt(es, "yT", [128, 6, TOK], BF16)
        WA = [sbt(es, f"wA{i}", [128, 8, 512], BF16) for i in range(2)]
        wo = sbt(es, "wo", [128, 6, 1024], BF16)
        ident_bf = sbt(es, "ident_bf", [128, 128], BF16)
        ident_f = sbt(es, "ident_f", [128, 128], F32)
        ones_bf = sbt(es, "ones_bf", [128, 128], BF16)
        ones_f = sbt(es, "ones_f", [128, 128], F32)
        mask01 = sbt(es, "mask01", [128, 512], BF16)
        maskcur = sbt(es, "maskcur", [128, 128], BF16)
        blockmask = sbt(es, "blockmask", [128, 128], F32)
        rowmask = sbt(es, "rowmask", [128, 4], F32)
        rowmask_s = sbt(es, "rowmask_s", [128, 4], F32)
        rowmask_s3 = sbt(es, "rowmask_s3", [128, 4], F32)
        epsc = sbt(es, "epsc", [128, 1], F32)
        lbT = sbt(es, "lbT", [128, 12], F32)
        omlbT = sbt(es, "omlbT", [128, 12], F32)
        hgT = sbt(es, "hgT", [128, 12], F32)

        Bx = [Buf(f"x{i}") for i in range(NT)]
        BhT = [Buf(f"hT{i}") for i in range(NT)]
        ByTp = [Buf(f"yTp{i}") for i in range(6)]
        ByTs = [Buf(f"yTs{i}") for i in range(6)]
        BW = [Buf("wA0"), Buf("wA1")]
        Bwo = Buf("wo")
        Bc = Buf("consts")

        def tcols(i):
            return slice(i * 128, (i + 1) * 128)

        units = []
        for l in range(2):
            for c in range(6):
                units.append((l, [(0, c * 128), (128, 768 + c * 128), (256, 1536 + c * 128), (384, 2304 + c * 128)], 128))
            units.append((l, [(0, 3584)], 512))
            for cb in range(4):
                units.append((l, [(0, 3072 + cb * 128), (128, 4096 + cb * 128)], 128))
            for hd in range(6):
                units.append((l, [(0, 4608 + hd * 128), (128, 5376 + hd * 128), (256, 6144 + hd * 128), (384, 6912 + hd * 128)], 128))
        ustate = {"loaded": 0}

        def load_unit(u):
            if u >= len(units) or u < ustate["loaded"]:
                return
            assert u == ustate["loaded"]
            ustate["loaded"] = u + 1
            l, parts, wdt = units[u]
            wt = WA[u % 2]
            for (dst, src, ) in [(p[0], p[1]) for p in parts]:
                dma(POOL, wt[:, :, dst:dst + wdt],
                    w_in[l, :, src:src + wdt].rearrange("(kc p) n -> p kc n", p=128), writes=[BW[u % 2]])

        def load_wo(l, r0, nch):
            dma(POOL, wo[:, 0:nch, :], w_out[l, r0:r0 + nch * 128, :].rearrange("(c p) d -> p c d", p=128), writes=[Bwo])

        with contextlib.ExitStack() as ar:
            tmpf = sbt(ar, "c_tmpf", [128, 512], F32)
            R4 = sbt(ar, "c_R4", [4, 128], F32)
            ld12 = sbt(ar, "c_ld12", [12, 128], F32)
            hg12 = sbt(ar, "c_hg12", [12, 128], F32)
            lgT = sbt(ar, "c_lgT", [128, 12], F32)
            Bt = Buf("c_tmp")
            BR4 = Buf("c_R4")
            Bl = Buf("c_ld")
            dma(SP, ld12[:], lbl.rearrange("l (h k) -> (l h) k", k=128), writes=[Bl])
            dma(SP, hg12[:], hng.rearrange("l (h k) -> (l h) k", k=128), writes=[Bl])
            for i in range(16):
                dma(SP, xres[:, i, :], xp[i * 128:(i + 1) * 128, :], writes=[Bx[i]])
            op(DVE, lambda: V.memset(xres[:, 16, :], 0.0), writes=[Bx[16]])
            for b in range(4):
                dma(SP, xres[32 * b:32 * b + 1, 16, :], xs[b:b + 1, :], writes=[Bx[16]])
            op(DVE, lambda: V.memset(yT[:, :, 2048:TOK], 0.0), writes=ByTs)
            op(DVE, lambda: V.memset(epsc[:], EPS), writes=[Bc])
            op(DVE, lambda: V.memset(ones_bf[:], 1.0), writes=[Bc])
            op(DVE, lambda: V.memset(ones_f[:], 1.0), writes=[Bc])
            op(POOL, lambda: G.memset(ident_f[:], 1.0), writes=[Bc])
            op(POOL, lambda: G.affine_select(out=ident_f[:], in_=ident_f[:], pattern=[[-1, 128]], compare_op=ALU.is_equal,
                                             fill=0.0, base=0, channel_multiplier=1), reads=[Bc], writes=[Bc])
            op(DVE, lambda: V.tensor_copy(out=ident_bf[:], in_=ident_f[:]), reads=[Bc], writes=[Bc])
            op(POOL, lambda: G.memset(tmpf[:, 0:256], 1.0), writes=[Bt])
            op(POOL, lambda: G.affine_select(out=tmpf[:, 0:128], in_=tmpf[:, 0:128], pattern=[[-1, 128]], compare_op=ALU.is_ge,
                                             fill=0.0, base=0, channel_multiplier=1), reads=[Bt], writes=[Bt])
            op(POOL, lambda: G.affine_select(out=tmpf[:, 128:256], in_=tmpf[:, 128:256], pattern=[[1, 128]], compare_op=ALU.is_ge,
                                             fill=0.0, base=0, channel_multiplier=-1), reads=[Bt], writes=[Bt])
            for kb in range(2):
                for h in range(2):
                    op(DVE, lambda kb=kb, h=h: V.tensor_copy(out=mask01[:, (h * 2 + kb) * 128:(h * 2 + kb + 1) * 128],
                                                            in_=tmpf[:, kb * 128:(kb + 1) * 128]), reads=[Bt], writes=[Bc])
            op(DVE, lambda: V.tensor_copy(out=maskcur[:], in_=tmpf[:, 128:256]), reads=[Bt], writes=[Bc])
            op(POOL, lambda: G.memset(R4[:], 1.0), writes=[BR4])
            op(POOL, lambda: G.affine_select(out=R4[:], in_=R4[:], pattern=[[1, 128]], compare_op=ALU.is_ge,
                                             fill=0.0, base=0, channel_multiplier=-32), reads=[BR4], writes=[BR4])
            op(POOL, lambda: G.affine_select(out=R4[:], in_=R4[:], pattern=[[-1, 128]], compare_op=ALU.is_ge,
                                             fill=0.0, base=31, channel_multiplier=32), reads=[BR4], writes=[BR4])
            op(PE, lambda: TE.matmul(PS[0][:, 0:128], lhsT=R4[:], rhs=R4[:], start=True, stop=True), reads=[BR4], writes=[BPS[0]])
            op(PE, lambda: TE.matmul(PS[0][:, 128:132], lhsT=R4[:], rhs=ident_f[0:4, 0:4], start=True, stop=True),
               reads=[BR4, Bc], writes=[BPS[0]])
            op(DVE, lambda: V.tensor_tensor(out=blockmask[:], in0=tmpf[:, 128:256], in1=PS[0][:, 0:128], op=ALU.mult),
               reads=[Bt, BPS[0]], writes=[Bc])
            op(DVE, lambda: V.tensor_copy(out=rowmask[:], in_=PS[0][:, 128:132]), reads=[BPS[0]], writes=[Bc])
            op(POOL, lambda: G.memset(rowmask_s[:], 1.0), writes=[Bc])
            op(POOL, lambda: G.affine_select(out=rowmask_s[:], in_=rowmask_s[:], pattern=[[-32, 4]], compare_op=ALU.is_equal,
                                             fill=0.0, base=0, channel_multiplier=1), reads=[Bc], writes=[Bc])
            op(DVE, lambda: V.tensor_scalar(out=rowmask_s3[:], in0=rowmask_s[:], scalar1=3.0, scalar2=None, op0=ALU.mult),
               reads=[Bc], writes=[Bc])
            op(PE, lambda: TE.transpose(PS[1][:, 0:12], ld12[:], ident_f[0:12, 0:12]), reads=[Bl, Bc], writes=[BPS[1]])
            op(PE, lambda: TE.transpose(PS[1][:, 16:28], hg12[:], ident_f[0:12, 0:12]), reads=[Bl, Bc], writes=[BPS[1]])
            op(DVE, lambda: V.tensor_copy(out=lgT[:], in_=PS[1][:, 0:12]), reads=[BPS[1]], writes=[Bt])
            op(DVE, lambda: V.tensor_copy(out=hgT[:], in_=PS[1][:, 16:28]), reads=[BPS[1]], writes=[Bc])
            op(DVE, lambda: V.memset(lbT[:], 0.0), writes=[Bc])
            op(DVE, lambda: V.tensor_tensor(out=lgT[:, 0:6], in0=lgT[:, 0:6], in1=lgT[:, 6:12], op=ALU.subtract), reads=[Bt], writes=[Bt])
            op(ACT, lambda: A.activation(out=lgT[:, 0:6], in_=lgT[:, 0:6], func=AF.Exp), reads=[Bt], writes=[Bt])
            op(DVE, lambda: V.tensor_scalar(out=lgT[:, 0:6], in0=lgT[:, 0:6], scalar1=1.0, scalar2=None, op0=ALU.add), reads=[Bt], writes=[Bt])
            op(DVE, lambda: V.reciprocal(out=lbT[:, 6:12], in_=lgT[:, 0:6]), reads=[Bt, Bc], writes=[Bc])
            op(DVE, lambda: V.tensor_scalar(out=omlbT[:], in0=lbT[:], scalar1=-1.0, scalar2=1.0, op0=ALU.mult, op1=ALU.add),
               reads=[Bc], writes=[Bc])
            load_unit(0)
            T.barrier()
            T.ck("const")

        def sigmoid_act(src_ap, dst_ap, rd, Bdst):
            op(ACT, lambda: A.activation(out=dst_ap, in_=src_ap, func=AF.Exp, scale=-1.0), reads=rd, writes=[Bdst])
            op(ACT, lambda: A.activation(out=dst_ap, in_=dst_ap, func=AF.Ln, scale=1.0, bias=1.0), reads=[Bdst], writes=[Bdst])
            op(ACT, lambda: A.activation(out=dst_ap, in_=dst_ap, func=AF.Exp, scale=-1.0), reads=[Bdst], writes=[Bdst])

        def silu_from_psum(P, n, out_ap, tmp, Btmp, BP, wr):
            sigmoid_act(P[:, 0:n], tmp[:, 0:n], [BP], Btmp)
            op(DVE, lambda: V.tensor_tensor(out=out_ap, in0=P[:, 0:n], in1=tmp[:, 0:n], op=ALU.mult), reads=[Btmp, BP], writes=wr)

        def out_proj(l, nch):
            for i in range(NT):
                ybufs = (ByTp if i < 16 else ByTs)[0:nch]
                for half in range(2):
                    k = (2 * i + half) % 4
                    P = PS[k]
                    for cc in range(nch):
                        op(PE, lambda P=P, cc=cc, i=i, half=half: TE.matmul(
                            P[:], lhsT=yT[:, cc, tcols(i)], rhs=wo[:, cc, half * 512:(half + 1) * 512],
                            start=(cc == 0), stop=(cc == nch - 1)), reads=ybufs + [Bwo], writes=[BPS[k]])
                    xs_ap = xres[:, i, half * 512:(half + 1) * 512]
                    op(DVE, lambda P=P, xs_ap=xs_ap: V.tensor_tensor(out=xs_ap, in0=xs_ap, in1=P[:], op=ALU.add),
                       reads=[Bx[i], BPS[k]], writes=[Bx[i]])

        uidx = 0
        for l in range(2):
            if os.environ.get("KSKIP0", "") == "1":
                if l == 0:
                    T.stopped = True
                else:
                    T.stopped = False
                    ustate["loaded"] = 17
            with contextlib.ExitStack() as ar:
                gn = sbt(ar, "n_gn", [128, 1024], F32)
                sqj = sbt(ar, "n_sqj", [128, 1024], BF16)
                ss = sbt(ar, "n_ss", [128, NT], F32)
                rstd = sbt(ar, "n_rstd", [128, NT], F32)
                hb = [sbt(ar, f"n_hb{j}", [128, 1024], BF16) for j in range(2)]
                Bgn, Bsq, Bss, Brs = Buf("gn"), Buf("sqj"), Buf("ss"), Buf("rstd")
                Bhb = [Buf("hb0"), Buf("hb1")]
                dma(SP, gn[:], norm_g[l].partition_broadcast(128), writes=[Bgn])
                op(DVE, lambda: V.memset(ss[:], 0.0), writes=[Bss])
                for i in range(NT):
                    op(ACT, lambda i=i: A.activation(out=sqj[:], in_=xres[:, i, :], func=AF.Square, accum_out=ss[:, i:i + 1]),
                       reads=[Bx[i], Bss], writes=[Bsq, Bss])
                op(ACT, lambda: A.activation(out=ss[:], in_=ss[:], func=AF.Ln, scale=1.0 / 1024, bias=epsc[:, 0:1]),
                   reads=[Bss, Bc], writes=[Bss])
                op(ACT, lambda: A.activation(out=rstd[:], in_=ss[:], func=AF.Exp, scale=-0.5), reads=[Bss], writes=[Brs])
                for i in range(NT):
                    j = i % 2
                    op(DVE, lambda i=i, j=j: V.scalar_tensor_tensor(out=hb[j][:], in0=xres[:, i, :], scalar=rstd[:, i:i + 1], in1=gn[:],
                                                                    op0=ALU.mult, op1=ALU.mult),
                       reads=[Bx[i], Brs, Bgn], writes=[Bhb[j]])
                    for kc in range(8):
                        op(PE, lambda j=j, kc=kc: TE.transpose(PB[j][:, tcols(kc)], hb[j][:, tcols(kc)], ident_bf[:]),
                           reads=[Bhb[j], Bc], writes=[BPB[j]])
                    op(ACT, lambda i=i, j=j: A.activation(out=hT[:, :, tcols(i)], in_=PB[j][:].rearrange("p (k t) -> p k t", k=8), func=AF.Copy),
                       reads=[BPB[j]], writes=[BhT[i]])
                T.barrier()
                T.ck(f"norm{l}")

            with contextlib.ExitStack() as arA:
                qs_tok = sbt(arA, "a_qs", [128, 768], BF16)
                ks_tok = sbt(arA, "a_ks", [128, 768], BF16)
                vs_tok = sbt(arA, "a_vs", [128, 768], BF16)
                zsamp = sbt(arA, "a_zs", [128, 6, 4], F32)
                Bst = Buf("a_stash")
                load_wo(l, 0, 6)
                with contextlib.ExitStack() as ar:
                    qkT = sbt(ar, "a_qkT", [128, 2, TOK], BF16)
                    vnat = sbt(ar, "a_vnat", [128, NT, 128], BF16)
                    vord = sbt(ar, "a_vord", [128, 16, 128], BF16)
                    zT = sbt(ar, "a_zT", [128, TOK], BF16)
                    UD = sbt(ar, "a_UD", [128, 2, 2048], F32)
                    pp = [sbt(ar, f"a_p{j}", [128, 512], BF16) for j in range(2)]
                    kfin = [sbt(ar, f"a_kfin{j}", [128, 128], F32) for j in range(2)]
                    vfin = [sbt(ar, f"a_vfin{j}", [128, 128], F32) for j in range(2)]
                    qb = [sbt(ar, f"a_qb{j}", [128, 128], BF16) for j in range(2)]
                    kb_ = [sbt(ar, f"a_kb{j}", [128, 128], BF16) for j in range(2)]
                    ss4 = sbt(ar, "a_ss4", [128, 4], F32)
                    rs4 = sbt(ar, "a_rs4", [128, 4], F32)
                    qgc = sbt(ar, "a_qgc", [128, 128], F32)
                    kgc = sbt(ar, "a_kgc", [128, 128], F32)
                    BqkT, Bvn, Bvo, BzT, BUD = Buf("qkT"), Buf("vnat"), Buf("vord"), Buf("zT"), Buf("UD")
                    Bp = [Buf("p0"), Buf("p1")]
                    Bpm = [Buf("pm0"), Buf("pm1")]
                    Bkf = [Buf("kf0"), Buf("kf1")]
                    Bvf = [Buf("vf0"), Buf("vf1")]
                    Btq, Btk, Bsq4, Bss4, Brs4, Bg, Bez = Buf("tq"), Buf("tk"), Buf("sq"), Buf("ss4"), Buf("rs4"), Buf("g"), Buf("ezt")
                    Bqb = [Buf("qb0"), Buf("qb1")]
                    Bkb = [Buf("kb0"), Buf("kb1")]
                    blk_ctr = [0]
                    sq = UD[:, 1, 0:256]
                    ezt = UD[:, 0, 0:512]
                    Bsq4 = BUD
                    Bez = BUD

                    for c in range(6):
                        u = uidx
                        uidx += 1
                        load_unit(u)
                        load_unit(u + 1)
                        wt, Bw = WA[u % 2], BW[u % 2]
                        csl = slice(c * 128, (c + 1) * 128)
                        dma(SP, qgc[:], qng[l, csl].partition_broadcast(128), writes=[Bg])
                        dma(SP, kgc[:], kng[l, csl].partition_broadcast(128), writes=[Bg])
                        op(DVE, lambda: V.tensor_scalar(out=qgc[:], in0=qgc[:], scalar1=0.125, scalar2=None, op0=ALU.mult), reads=[Bg], writes=[Bg])
                        def a1_front(i):
                            j = i % 2
                            P, BP = PS[j], BPS[j]
                            for kc in range(8):
                                op(PE, lambda P=P, kc=kc, i=i: TE.matmul(P[:, 0:384], lhsT=hT[:, kc, tcols(i)], rhs=wt[:, kc, 0:384],
                                                                        start=(kc == 0), stop=(kc == 7)), reads=[BhT[i], Bw], writes=[BP], inc=(kc == 7))
                            op(ACT, lambda P=P: A.activation(out=sq, in_=P[:, 0:256], func=AF.Square), reads=[BP], writes=[Bsq4])
                            op(DVE, lambda: V.tensor_reduce(out=ss4[:], in_=sq.rearrange("p (h e) -> p h e", e=64), axis=AX.X, op=ALU.add),
                               reads=[Bsq4], writes=[Bss4])
                            op(ACT, lambda: A.activation(out=ss4[:], in_=ss4[:], func=AF.Ln, scale=1.0 / 64, bias=epsc[:, 0:1]),
                               reads=[Bss4, Bc], writes=[Bss4])
                            op(ACT, lambda: A.activation(out=rs4[:], in_=ss4[:], func=AF.Exp, scale=-0.5), reads=[Bss4], writes=[Brs4])
                            for h in range(2):
                                hs = slice(h * 64, (h + 1) * 64)
                                op(DVE, lambda P=P, j=j, h=h, hs=hs: V.scalar_tensor_tensor(out=qb[j][:, hs], in0=P[:, hs], scalar=rs4[:, h:h + 1], in1=qgc[:, hs],
                                                                                        op0=ALU.mult, op1=ALU.mult), reads=[BP, Brs4, Bg], writes=[Bqb[j]])
                            for h in range(2):
                                hs = slice(h * 64, (h + 1) * 64)
                                op(DVE, lambda P=P, j=j, h=h, hs=hs: V.scalar_tensor_tensor(out=kfin[j][:, hs], in0=P[:, 128 + h * 64:128 + (h + 1) * 64],
                                                                                        scalar=rs4[:, 2 + h:3 + h], in1=kgc[:, hs], op0=ALU.mult, op1=ALU.mult),
                                   reads=[BP, Brs4, Bg], writes=[Bkf[j]])
                            op(POOL, lambda j=j: G.tensor_copy(out=kb_[j][:], in_=kfin[j][:]), reads=[Bkf[j]], writes=[Bkb[j]])
                            op(ACT, lambda P=P, j=j: A.activation(out=vfin[j][:], in_=P[:, 256:384], func=AF.Copy), reads=[BP], writes=[Bvf[j]])
                            op(POOL, lambda i=i, j=j: G.tensor_copy(out=vnat[:, i, :], in_=vfin[j][:]), reads=[Bvf[j]], writes=[Bvn])
                            if i < 16:
                                dma(SP, okp[l, tcols(i), csl], kfin[j][:], reads=[Bkf[j]])
                                dma(SP, ovp[l, tcols(i), csl], vfin[j][:], reads=[Bvf[j]])
                            else:
                                for b in range(4):
                                    dma(SP, oks[l, b:b + 1, csl], kfin[j][32 * b:32 * b + 1, :], reads=[Bkf[j]])
                                    dma(SP, ovs[l, b:b + 1, csl], vfin[j][32 * b:32 * b + 1, :], reads=[Bvf[j]])
                                op(POOL, lambda j=j: G.tensor_copy(out=qs_tok[:, csl], in_=qb[j][:]), reads=[Bqb[j]], writes=[Bst])
                                op(POOL, lambda j=j: G.tensor_copy(out=ks_tok[:, csl], in_=kb_[j][:]), reads=[Bkb[j]], writes=[Bst])
                                op(POOL, lambda j=j: G.tensor_copy(out=vs_tok[:, csl], in_=vfin[j][:]), reads=[Bvf[j]], writes=[Bst])
                        def a1_back(i):
                            j = i % 2
                            op(PE, lambda j=j: TE.transpose(PB[j][:, 0:128], qb[j][:], ident_bf[:]), reads=[Bqb[j], Bc], writes=[BPB[j]])
                            op(PE, lambda j=j: TE.transpose(PB[j][:, 128:256], kb_[j][:], ident_bf[:]), reads=[Bkb[j], Bc], writes=[BPB[j]], inc=True)
                            op(DVE, lambda i=i, j=j: V.tensor_copy(out=qkT[:, :, tcols(i)], in_=PB[j][:, 0:256].rearrange("p (a t) -> p a t", a=2)),
                               reads=[BPB[j]], writes=[BqkT])
                        if os.environ.get("KPIPE", "0") == "1":
                            a1_front(0)
                            for i in range(1, NT):
                                a1_front(i)
                                a1_back(i - 1)
                            a1_back(NT - 1)
                        else:
                            for i in range(NT):
                                a1_front(i)
                                a1_back(i)
                        T.ck(f"A1_{l}_{c}")

                        def tsl_of(dil):
                            def tsl(blk):
                                if dil == 4:
                                    jb, r4 = blk // 4, blk % 4
                                    return slice(512 * jb + r4, 512 * (jb + 1), 4)
                                return slice(blk, 2048, 16)
                            return tsl

                        def vproj(dil):
                            tsl = tsl_of(dil)
                            for g4 in range(4):
                                k = 2 + g4 % 2
                                P, BP = PS[k], BPS[k]
                                for bi in range(4):
                                    blk = 4 * g4 + bi
                                    for kc in range(8):
                                        op(PE, lambda P=P, bi=bi, blk=blk, kc=kc: TE.matmul(P[:, tcols(bi)], lhsT=hT[:, kc, tsl(blk)], rhs=wt[:, kc, 256:384],
                                                                                           start=(kc == 0), stop=(kc == 7)), reads=BhT[0:16] + [Bw], writes=[BP])
                                op(ACT, lambda P=P, g4=g4: A.activation(out=vord[:, 4 * g4:4 * g4 + 4, :], in_=P[:].rearrange("p (a t) -> p a t", a=4), func=AF.Copy),
                                   reads=[BP], writes=[Bvo])

                        vproj(4)
                        for tg in range(5):
                            n = 512 if tg < 4 else 128
                            cols = slice(tg * 512, tg * 512 + n)
                            k = 4 + tg % 2
                            P, BP = PS[k], BPS[k]
                            hb_ = BhT[4 * tg:4 * tg + 4] if tg < 4 else [BhT[16]]
                            for kc in range(8):
                                op(PE, lambda P=P, kc=kc, cols=cols, n=n: TE.matmul(P[:, 0:n], lhsT=wt[:, kc, 384:512], rhs=hT[:, kc, cols],
                                                                                  start=(kc == 0), stop=(kc == 7)), reads=hb_ + [Bw], writes=[BP])
                            silu_from_psum(P, n, zT[:, cols], ezt, Bez, BP, [BzT])
                        op(POOL, lambda c=c: G.tensor_copy(out=zsamp[:, c, :], in_=zT[:, 2048:TOK:32]), reads=[BzT], writes=[Bst])

                        T.ck(f"A2_{l}_{c}")
                        def attn_front(qsl, kcur, kprev, vcur, vprev, first):
                            n_ = blk_ctr[0]
                            blk_ctr[0] += 1
                            jj = n_ % 2
                            Sb = [(PS[n_ % 2], BPS[n_ % 2]), (PS[2 + n_ % 2], BPS[2 + n_ % 2])]
                            kbs = [1] if kprev is None else [0, 1]
                            lo = 0 if kprev is not None else 128
                            for h in range(2):
                                S, BS = Sb[h]
                                hs = slice(h * 64, (h + 1) * 64)
                                for kb in kbs:
                                    ks = kprev if kb == 0 else kcur
                                    op(PE, lambda S=S, hs=hs, ks=ks, kb=kb: TE.matmul(S[:, kb * 128:(kb + 1) * 128], lhsT=qkT[hs, 1, ks], rhs=qkT[hs, 0, qsl],
                                                                                   start=True, stop=True), reads=[BqkT], writes=[BS], inc=(kb == 1))
                            for h in range(2):
                                S, BS = Sb[h]
                                op(ACT, lambda S=S, h=h: A.activation(out=pp[jj][:, h * 256 + lo:(h + 1) * 256], in_=S[:, lo:256], func=AF.Exp),
                                   reads=[BS], writes=[Bp[jj]], inc=True)
                            pv = pp[jj][:].rearrange("p (h x) -> p h x", h=2)[:, :, lo:256]
                            mv_ = mask01[:].rearrange("p (h x) -> p h x", h=2)[:, :, lo:256]
                            op(POOL, lambda pv=pv, mv_=mv_: G.tensor_tensor(out=pv, in0=pv, in1=mv_, op=ALU.mult), reads=[Bp[jj], Bc], writes=[Bp[jj]], inc=True)
                            return (n_, qsl, kbs, vcur, vprev, first)

                        def attn_back(ctx):
                            n_, qsl, kbs, vcur, vprev, first = ctx
                            jj = n_ % 2
                            U, BU = PS[4 + n_ % 2], BPS[4 + n_ % 2]
                            for part in range(2):
                                for h in range(2):
                                    hs = slice(h * 64, (h + 1) * 64)
                                    for idx, kb in enumerate(kbs):
                                        vb = vprev if kb == 0 else vcur
                                        lhsT = vb[:, hs] if part == 0 else ones_bf[:, 0:64]
                                        o0 = (h * 2 + kb) * 128
                                        op(PE, lambda U=U, hs=hs, part=part, lhsT=lhsT, o0=o0, idx=idx: TE.matmul(
                                            U[hs, part * 128:(part + 1) * 128], lhsT=lhsT, rhs=pp[jj][:, o0:o0 + 128],
                                            start=(idx == 0), stop=(idx == len(kbs) - 1)), reads=[Bp[jj], Bvn, Bvo, Bc], writes=[BU],
                                           inc=(part == 1 and h == 1 and idx == len(kbs) - 1))
                            uv = U[:, 0:256].rearrange("p (a q) -> p a q", a=2)
                            if first:
                                op(DVE, lambda: V.tensor_copy(out=UD[:, :, qsl], in_=uv), reads=[BU], writes=[BUD], inc=True)
                            else:
                                op(DVE, lambda: V.tensor_tensor(out=UD[:, :, qsl], in0=UD[:, :, qsl], in1=uv, op=ALU.add), reads=[BU, BUD], writes=[BUD], inc=True)

                        def run_blocks(specs):
                            prev = None
                            for sp_ in specs:
                                ctx = attn_front(*sp_)
                                if prev is not None:
                                    attn_back(prev)
                                prev = ctx
                            attn_back(prev)

                        run_blocks([(tcols(i), tcols(i), tcols(i - 1) if i > 0 else None, vnat[:, i, :], vnat[:, i - 1, :] if i > 0 else None, True)
                                    for i in range(16)])
                        T.ck(f"A4a_{l}_{c}")
                        for dil in (4, 16):
                            tsl = tsl_of(dil)
                            if dil == 16:
                                vproj(16)
                            run_blocks([(tsl(blk), tsl(blk), tsl(blk - 4), vord[:, blk, :], vord[:, blk - 4, :], False) if (dil == 4 and blk >= 4)
                                        else (tsl(blk), tsl(blk), None, vord[:, blk, :], None, False) for blk in range(16)])
                        T.ck(f"A4b_{l}_{c}")
                        op(ACT, lambda: A.activation(out=UD[:, 1, :], in_=UD[:, 1, :], func=AF.Ln), reads=[BUD], writes=[BUD])
                        op(ACT, lambda: A.activation(out=UD[:, 1, :], in_=UD[:, 1, :], func=AF.Exp, scale=-1.0), reads=[BUD], writes=[BUD])
                        op(DVE, lambda: V.tensor_tensor(out=UD[:, 0, :], in0=UD[:, 0, :], in1=UD[:, 1, :], op=ALU.mult), reads=[BUD], writes=[BUD])
                        op(POOL, lambda c=c: G.tensor_tensor(out=yT[:, c, 0:2048], in0=UD[:, 0, :], in1=zT[:, 0:2048], op=ALU.mult),
                           reads=[BUD, BzT], writes=[ByTp[c]])
                    T.barrier()

                T.ck(f"A4_{l}")
                with contextlib.ExitStack() as ar:
                    selb = sbt(ar, "s_selb", [128, 4, 128], BF16)
                    selbf = sbt(ar, "s_selbf", [128, 512], F32)
                    Kr = [sbt(ar, f"s_Kr{j}", [128, 768], BF16) for j in range(2)]
                    Vr = [sbt(ar, f"s_Vr{j}", [128, 3, 768], BF16) for j in range(2)]
                    prod = sbt(ar, "s_prod", [128, 768], F32)
                    sc = sbt(ar, "s_sc", [128, 12], F32)
                    pall = [sbt(ar, f"s_pall{j}", [128, 3, 12], BF16) for j in range(2)]
                    pnew = sbt(ar, "s_pnew", [128, 12], F32)
                    pnm = sbt(ar, "s_pnm", [128, 4, 12], BF16)
                    rd = sbt(ar, "s_rd", [128, 6], F32)
                    osb = sbt(ar, "s_osb", [128, 6], F32)
                    Bsel, Bprod, Bsc, Bpn, Bpnm, Brd, Bos = Buf("selb"), Buf("prod"), Buf("sc"), Buf("pnew"), Buf("pnm"), Buf("rd"), Buf("osb")
                    BKr = [Buf("Kr0"), Buf("Kr1")]
                    BVr = [Buf("Vr0"), Buf("Vr1")]
                    Bpa = [Buf("pa0"), Buf("pa1")]
                    op(POOL, lambda: G.memset(selbf[:], 1.0), writes=[Bsel])
                    op(POOL, lambda: G.affine_select(out=selbf[:].rearrange("p (b m) -> p b m", b=4), in_=selbf[:].rearrange("p (b m) -> p b m", b=4),
                                                     pattern=[[-32, 4], [0, 128]], compare_op=ALU.is_equal, fill=0.0, base=0, channel_multiplier=1),
                       reads=[Bsel], writes=[Bsel])
                    op(DVE, lambda: V.tensor_copy(out=selb[:].rearrange("p b m -> p (b m)"), in_=selbf[:]), reads=[Bsel], writes=[Bsel])
                    op(DVE, lambda: V.tensor_tensor(out=prod[:], in0=qs_tok[:], in1=ks_tok[:], op=ALU.mult), reads=[Bst], writes=[Bprod])
                    op(DVE, lambda: V.tensor_reduce(out=sc[:], in_=prod[:].rearrange("p (h e) -> p h e", e=64), axis=AX.X, op=ALU.add),
                       reads=[Bprod], writes=[Bsc])
                    op(ACT, lambda: A.activation(out=pnew[:], in_=sc[:], func=AF.Exp), reads=[Bsc], writes=[Bpn])
                    for b in range(4):
                        op(DVE, lambda b=b: V.tensor_scalar(out=pnm[:, b, :], in0=pnew[:], scalar1=rowmask_s3[:, b:b + 1], scalar2=None, op0=ALU.mult),
                           reads=[Bpn, Bc], writes=[Bpnm])
                    kctr = 0
                    for b in range(4):
                        bj = b % 2
                        for pi, (r0, step) in enumerate([(1920, 1), (1536, 4), (0, 16)]):
                            rows = slice(r0, 2048, step)
                            kj = kctr % 2
                            kctr += 1
                            dma(POOL, Kr[kj][:], ck[l, b, rows, :], writes=[BKr[kj]])
                            dma(POOL, Vr[bj][:, pi, :], cv[l, b, rows, :], writes=[BVr[bj]])
                            for hf in range(2):
                                op(PE, lambda b=b, hf=hf: TE.matmul(PS[hf][:, 0:384], lhsT=selb[:, b, :], rhs=qs_tok[:, hf * 384:(hf + 1) * 384],
                                                                  start=True, stop=True), reads=[Bsel, Bst], writes=[BPS[hf]])
                            for hf in range(2):
                                op(DVE, lambda kj=kj, hf=hf: V.tensor_tensor(out=prod[:, hf * 384:(hf + 1) * 384], in0=Kr[kj][:, hf * 384:(hf + 1) * 384],
                                                                           in1=PS[hf][:, 0:384], op=ALU.mult), reads=[BKr[kj], BPS[hf]], writes=[Bprod])
                            op(DVE, lambda: V.tensor_reduce(out=sc[:], in_=prod[:].rearrange("p (h e) -> p h e", e=64), axis=AX.X, op=ALU.add),
                               reads=[Bprod], writes=[Bsc])
                            op(ACT, lambda bj=bj, pi=pi: A.activation(out=pall[bj][:, pi, :], in_=sc[:], func=AF.Exp), reads=[Bsc], writes=[Bpa[bj]])
                        PO, BPO = PS[2 + bj], BPS[2 + bj]
                        for part in range(2):
                            for c in range(6):
                                o0 = part * 16 + 2 * c
                                for pi in range(3):
                                    lhsT = Vr[bj][:, pi, c * 128:(c + 1) * 128] if part == 0 else ones_bf[:]
                                    op(PE, lambda PO=PO, o0=o0, lhsT=lhsT, pi=pi, c=c: TE.matmul(PO[:, o0:o0 + 2], lhsT=lhsT, rhs=pall[bj][:, pi, 2 * c:2 * c + 2],
                                                                                              start=(pi == 0), stop=False), reads=[BVr[bj], Bpa[bj], Bc], writes=[BPO])
                                lhsT = vs_tok[:, c * 128:(c + 1) * 128] if part == 0 else ones_bf[:]
                                op(PE, lambda PO=PO, o0=o0, lhsT=lhsT, b=b, c=c: TE.matmul(PO[:, o0:o0 + 2], lhsT=lhsT, rhs=pnm[:, b, 2 * c:2 * c + 2],
                                                                                        start=False, stop=True), reads=[Bst, Bpnm, Bc], writes=[BPO])
                        for hh in range(2):
                            rws = slice(hh * 64, hh * 64 + 64)
                            op(DVE, lambda PO=PO, rws=rws, hh=hh: V.reciprocal(out=rd[rws, :], in_=PO[rws, 16 + hh:28:2]), reads=[BPO], writes=[Brd])
                            op(DVE, lambda PO=PO, rws=rws, hh=hh: V.tensor_tensor(out=osb[rws, :], in0=PO[rws, hh:12:2], in1=rd[rws, :], op=ALU.mult),
                               reads=[BPO, Brd], writes=[Bos])
                        op(DVE, lambda b=b: V.tensor_tensor(out=yT[:, :, 2048 + 32 * b:2048 + 32 * b + 1], in0=osb[:].unsqueeze(2),
                                                            in1=zsamp[:, :, b:b + 1], op=ALU.mult), reads=[Bos, Bst], writes=ByTs)
                    T.barrier()
                T.ck(f"A5_{l}")
                out_proj(l, 6)
                T.barrier()
                T.ck(f"A_{l}")

            with contextlib.ExitStack() as ar:
                vn = sbt(ar, "b_vn", [128, NT, 512], BF16)
                lng_t = sbt(ar, "b_lng", [128, 512], F32)
                lnb_t = sbt(ar, "b_lnb", [128, 512], F32)
                st6 = sbt(ar, "b_st6", [128, 6], F32)
                mv = sbt(ar, "b_mv", [128, 2], F32)
                lnv = sbt(ar, "b_lnv", [128, 1], F32)
                rsb = sbt(ar, "b_rsb", [128, 1], F32)
                vnf = sbt(ar, "b_vnf", [128, 512], F32)
                vnf2 = [sbt(ar, f"b_vnf2{j}", [128, 512], F32) for j in range(2)]
                wl = sbt(ar, "b_wl", [128, 8, 128], F32)
                wlb = sbt(ar, "b_wlb", [128, 8, 128], BF16)
                wT = sbt(ar, "b_wT", [128, 8, 128], BF16)
                wTs = sbt(ar, "b_wTs", [128, 8, 128], BF16)
                w00 = sbt(ar, "b_w00", [128, 8], F32)
                bsf = sbt(ar, "b_bsf", [8, 128], F32)
                bsb = sbt(ar, "b_bsb", [8, 128], BF16)
                bs0 = sbt(ar, "b_bs0", [8, 128], BF16)
                ezb = sbt(ar, "b_ez", [128, 512], F32)
                t1 = sbt(ar, "b_t1", [128, 512], F32)
                Bvn_, Blg, Bst6, Bmv, Blnv, Brsb, Bvnf = Buf("vn"), Buf("lng"), Buf("st6"), Buf("mv"), Buf("lnv"), Buf("rsb"), Buf("vnf")
                Bvnf2 = [Buf("vnf20"), Buf("vnf21")]
                Bwl, Bwlb, BwT, Bbs, Bezb, Bt1 = Buf("wl"), Buf("wlb"), Buf("wT"), Buf("bs"), Buf("ezb"), Buf("t1")
                load_wo(l, 768, 4)
                bsel = sbt(ar, "b_bsel", [8, 512], BF16)
                bself = sbt(ar, "b_bself", [8, 512], F32)
                Bbsl = Buf("bsel")
                op(POOL, lambda: G.memset(bself[:], 1.0), writes=[Bbsl])
                op(POOL, lambda: G.affine_select(out=bself[:], in_=bself[:], pattern=[[1, 512]], compare_op=ALU.is_ge,
                                                 fill=0.0, base=0, channel_multiplier=-64), reads=[Bbsl], writes=[Bbsl])
                op(POOL, lambda: G.affine_select(out=bself[:], in_=bself[:], pattern=[[-1, 512]], compare_op=ALU.is_ge,
                                                 fill=0.0, base=63, channel_multiplier=64), reads=[Bbsl], writes=[Bbsl])
                op(DVE, lambda: V.tensor_copy(out=bsel[:], in_=bself[:]), reads=[Bbsl], writes=[Bbsl])
                dma(SP, lng_t[:], lng[l].partition_broadcast(128), writes=[Blg])
                dma(SP, lnb_t[:], lnb[l].partition_broadcast(128), writes=[Blg])
                dma(SP, wl[:], sgw[l].rearrange("g t s -> t g s"), writes=[Bwl])
                dma(SP, bsf[:], sgb[l], writes=[Bbs])
                u = uidx
                uidx += 1
                load_unit(u)
                load_unit(u + 1)
                wt, Bw = WA[u % 2], BW[u % 2]
                for i in range(NT):
                    j = i % 2
                    P, BP = PS[j], BPS[j]
                    for kc in range(8):
                        op(PE, lambda P=P, kc=kc, i=i: TE.matmul(P[:], lhsT=hT[:, kc, tcols(i)], rhs=wt[:, kc, :], start=(kc == 0), stop=(kc == 7)),
                           reads=[BhT[i], Bw], writes=[BP])
                    op(DVE, lambda P=P: V.bn_stats(out=st6[:], in_=P[:]), reads=[BP], writes=[Bst6])
                    op(DVE, lambda: V.bn_aggr(out=mv[:], in_=st6[:]), reads=[Bst6], writes=[Bmv])
                    op(ACT, lambda: A.activation(out=lnv[:], in_=mv[:, 1:2], func=AF.Ln, scale=1.0, bias=epsc[:, 0:1]), reads=[Bmv, Bc], writes=[Blnv])
                    op(ACT, lambda: A.activation(out=rsb[:], in_=lnv[:], func=AF.Exp, scale=-0.5), reads=[Blnv], writes=[Brsb])
                    op(DVE, lambda P=P: V.tensor_scalar(out=vnf[:], in0=P[:], scalar1=mv[:, 0:1], scalar2=rsb[:, 0:1], op0=ALU.subtract, op1=ALU.mult),
                       reads=[BP, Bmv, Brsb], writes=[Bvnf])
                    op(POOL, lambda: G.tensor_tensor(out=vnf[:], in0=vnf[:], in1=lng_t[:], op=ALU.mult), reads=[Bvnf, Blg], writes=[Bvnf])
                    op(POOL, lambda j=j: G.tensor_tensor(out=vnf2[j][:], in0=vnf[:], in1=lnb_t[:], op=ALU.add), reads=[Bvnf, Blg], writes=[Bvnf2[j]])
                    op(ACT, lambda i=i, j=j: A.activation(out=vn[:, i, :], in_=vnf2[j][:], func=AF.Copy), reads=[Bvnf2[j]], writes=[Bvn_])
                    if i == 16:
                        for b in range(4):
                            dma(SP, osg[l, b:b + 1, :], vnf2[j][32 * b:32 * b + 1, :], reads=[Bvnf2[j]])
                T.ck(f"B1_{l}")
                op(DVE, lambda: V.tensor_copy(out=wlb[:], in_=wl[:]), reads=[Bwl], writes=[Bwlb])
                for g in range(8):
                    op(PE, lambda g=g: TE.transpose(PB[0][:, tcols(g)], wlb[:, g, :], ident_bf[:]), reads=[Bwlb, Bc], writes=[BPB[0]])
                op(DVE, lambda: V.tensor_tensor(out=wT[:], in0=PB[0][:].rearrange("p (g t) -> p g t", g=8),
                                                in1=maskcur[:].unsqueeze(1).to_broadcast([128, 8, 128]), op=ALU.mult), reads=[BPB[0], Bc], writes=[BwT])
                op(PE, lambda: TE.matmul(PS[2][:, 0:8], lhsT=ones_f[0:1, :], rhs=wl[0:1, :, 0], start=True, stop=True), reads=[Bwl, Bc], writes=[BPS[2]])
                op(DVE, lambda: V.tensor_copy(out=w00[:], in_=PS[2][:, 0:8]), reads=[BPS[2]], writes=[BwT])
                op(DVE, lambda: V.tensor_tensor(out=wTs[:], in0=ident_bf[:].unsqueeze(1).to_broadcast([128, 8, 128]),
                                                in1=w00[:].unsqueeze(2).to_broadcast([128, 8, 128]), op=ALU.mult), reads=[BwT, Bc], writes=[BwT])
                op(DVE, lambda: V.tensor_copy(out=bsb[:], in_=bsf[:]), reads=[Bbs], writes=[Bbs])
                op(DVE, lambda: V.tensor_copy(out=bs0[:], in_=bsf[:, 0:1].to_broadcast([8, 128])), reads=[Bbs], writes=[Bbs])
                T.ck(f"B2_{l}")
                for cb in range(4):
                    u = uidx
                    uidx += 1
                    load_unit(u)
                    load_unit(u + 1)
                    wt, Bw = WA[u % 2], BW[u % 2]
                    for tg in range(5):
                        n = 512 if tg < 4 else 128
                        cols = slice(tg * 512, tg * 512 + n)
                        hb_ = BhT[4 * tg:4 * tg + 4] if tg < 4 else [BhT[16]]
                        Pu, BPu = PS[0 + tg % 2], BPS[0 + tg % 2]
                        Pz, BPz = PS[2 + tg % 2], BPS[2 + tg % 2]
                        Pm, BPm = PS[4 + tg % 2], BPS[4 + tg % 2]
                        for kc in range(8):
                            op(PE, lambda Pu=Pu, kc=kc, cols=cols, n=n: TE.matmul(Pu[:, 0:n], lhsT=wt[:, kc, 0:128], rhs=hT[:, kc, cols],
                                                                                start=(kc == 0), stop=(kc == 7)), reads=hb_ + [Bw], writes=[BPu])
                        for kc in range(8):
                            op(PE, lambda Pz=Pz, kc=kc, cols=cols, n=n: TE.matmul(Pz[:, 0:n], lhsT=wt[:, kc, 128:256], rhs=hT[:, kc, cols],
                                                                                start=(kc == 0), stop=(kc == 7)), reads=hb_ + [Bw], writes=[BPz])
                        for ti in range(n // 128):
                            i = 4 * tg + ti
                            for gg in range(2):
                                g = 2 * cb + gg
                                rws = slice(gg * 64, gg * 64 + 64)
                                wmat = wT if i < 16 else wTs
                                bmat = bsb if i < 16 else bs0
                                op(PE, lambda Pm=Pm, rws=rws, ti=ti, i=i, g=g, wmat=wmat: TE.matmul(
                                    Pm[rws, tcols(ti)], lhsT=vn[:, i, g * 64:(g + 1) * 64], rhs=wmat[:, g, :], start=True, stop=False),
                                   reads=[Bvn_, BwT], writes=[BPm])
                                op(PE, lambda Pm=Pm, rws=rws, ti=ti, g=g, bmat=bmat: TE.matmul(
                                    Pm[rws, tcols(ti)], lhsT=bsel[0:8, g * 64:(g + 1) * 64], rhs=bmat[0:8, :], start=False, stop=True),
                                   reads=[Bbs, Bbsl], writes=[BPm])
                        silu_from_psum(Pz, n, t1[:, 0:n], ezb, Bezb, BPz, [Bt1])
                        op(DVE, lambda Pu=Pu, n=n: V.tensor_tensor(out=t1[:, 0:n], in0=t1[:, 0:n], in1=Pu[:, 0:n], op=ALU.mult), reads=[Bt1, BPu], writes=[Bt1])
                        op(DVE, lambda Pm=Pm, n=n, cols=cols, cb=cb: V.tensor_tensor(out=yT[:, cb, cols], in0=t1[:, 0:n], in1=Pm[:, 0:n], op=ALU.mult),
                           reads=[Bt1, BPm], writes=[ByTp[cb] if tg < 4 else ByTs[cb]])
                T.barrier()
                T.ck(f"B3_{l}")
                out_proj(l, 4)
                T.barrier()
                T.ck(f"B_{l}")

            with contextlib.ExitStack() as ar:
                scanmask = sbt(ar, "c_scanm", [128, 512], F32)
                names = ["ef", "f", "g", "G", "eG", "kk", "eq", "q", "ez", "zs", "ln"]
                alias = {"eNG": "g", "rs": "ln", "sq": "eq", "o": "ez"}
                t = {nm: sbt(ar, "c_" + nm, [128, 512], F32) for nm in names}
                Bt_ = {nm: Buf("c_" + nm) for nm in names}
                for a_, b_ in alias.items():
                    t[a_] = t[b_]
                    Bt_[a_] = Bt_[b_]
                for nm in ("kt", "khT", "qt"):
                    t[nm] = sbt(ar, "c_" + nm, [128, 512], BF16)
                    Bt_[nm] = Buf("c_" + nm)
                Sbf = sbt(ar, "c_Sbf", [128, 9, 128], BF16)
                BSb = [Buf(f"Sb{j}") for j in range(9)]
                tv = sbt(ar, "c_tv", [128, 4, 128], BF16)
                khtok = sbt(ar, "c_khtok", [128, 4, 4, 128], BF16)
                ATm = sbt(ar, "c_ATm", [128, 4, 128], BF16)
                Sring = sbt(ar, "c_Sring", [128, 9, 128], F32)
                BSr = [Buf(f"Sr{j}") for j in range(9)]
                S0 = [Sring[:, 1, :], Sring[:, 2, :]]
                Sn = [Sring[:, 3, :], Sring[:, 4, :]]
                BS0 = [BSr[1], BSr[2]]
                BSn = [BSr[3], BSr[4]]
                Btv, Bkh, BAT, Bsm = Buf("tv"), Buf("khtok"), Buf("ATm"), Buf("scanm")
                load_wo(l, 1280, 6)
                op(DVE, lambda: V.memset(scanmask[:], 1.0), writes=[Bsm])
                op(DVE, lambda: V.memset(scanmask[:].rearrange("p (c j) -> p c j", j=32)[:, :, 0:1], 0.0), reads=[Bsm], writes=[Bsm])
                for hd in range(6):
                    u = uidx
                    uidx += 1
                    load_unit(u)
                    load_unit(u + 1)
                    wt, Bw = WA[u % 2], BW[u % 2]
                    lbc = lbT[:, l * 6 + hd:l * 6 + hd + 1]
                    omc = omlbT[:, l * 6 + hd:l * 6 + hd + 1]
                    hgc = hgT[:, l * 6 + hd:l * 6 + hd + 1]
                    op(DVE, lambda: V.memset(Sring[:, 0, :], 0.0), writes=[BSr[0]])
                    op(POOL, lambda: G.memset(Sbf[:, 0, :], 0.0), writes=[BSb[0]])
                    for tg in range(5):
                        n = 512 if tg < 4 else 128
                        nti = n // 128
                        cols = slice(tg * 512, tg * 512 + n)
                        hb_ = BhT[4 * tg:4 * tg + 4] if tg < 4 else [BhT[16]]
                        for pi_, (P, BP, w0) in enumerate([(PS[0], BPS[0], 0), (PS[1], BPS[1], 128), (PS[2], BPS[2], 384)]):
                            for kc in range(8):
                                op(PE, lambda P=P, kc=kc, w0=w0, cols=cols, n=n: TE.matmul(P[:, 0:n], lhsT=wt[:, kc, w0:w0 + 128], rhs=hT[:, kc, cols],
                                                                                         start=(kc == 0), stop=(kc == 7)), reads=hb_ + [Bw], writes=[BP])
                        for ti in range(nti):
                            i = 4 * tg + ti
                            for kc in range(8):
                                op(PE, lambda ti=ti, i=i, kc=kc: TE.matmul(PS[3][:, tcols(ti)], lhsT=hT[:, kc, tcols(i)], rhs=wt[:, kc, 256:384],
                                                                         start=(kc == 0), stop=(kc == 7)), reads=[BhT[i], Bw], writes=[BPS[3]])
                        sl = slice(0, n)
                        sigmoid_act(PS[1][:, sl], t["ef"][:, sl], [BPS[1]], Bt_["ef"])
                        op(DVE, lambda: V.tensor_scalar(out=t["f"][:, sl], in0=t["ef"][:, sl], scalar1=omc, scalar2=lbc, op0=ALU.mult, op1=ALU.add),
                           reads=[Bt_["ef"], Bc], writes=[Bt_["f"]])
                        op(POOL, lambda: G.tensor_scalar(out=t["kk"][:, sl], in0=t["f"][:, sl], scalar1=-1.0, scalar2=1.0, op0=ALU.mult, op1=ALU.add),
                           reads=[Bt_["f"]], writes=[Bt_["kk"]])
                        def qz_part():
                            sigmoid_act(PS[0][:, sl], t["eq"][:, sl], [BPS[0]], Bt_["eq"])
                            op(DVE, lambda: V.tensor_tensor(out=t["q"][:, sl], in0=PS[0][:, sl], in1=t["eq"][:, sl], op=ALU.mult), reads=[BPS[0], Bt_["eq"]], writes=[Bt_["q"]])
                            silu_from_psum(PS[2], n, t["zs"][:, sl], t["ez"], Bt_["ez"], BPS[2], [Bt_["zs"]])
                        op(ACT, lambda: A.activation(out=tv[:, 0:nti, :], in_=PS[3][:, sl].rearrange("p (a v) -> p a v", v=128), func=AF.Copy),
                           reads=[BPS[3]], writes=[Btv])
                        if tg < 4:
                            op(ACT, lambda: A.activation(out=t["g"][:], in_=t["f"][:], func=AF.Ln), reads=[Bt_["f"]], writes=[Bt_["g"]])
                            op(DVE, lambda: V.tensor_tensor_scan(out=t["G"][:], data0=scanmask[:], data1=t["g"][:], initial=0.0, op0=ALU.mult, op1=ALU.add),
                               reads=[Bt_["g"], Bsm], writes=[Bt_["G"]])
                            op(ACT, lambda: A.activation(out=t["eG"][:], in_=t["G"][:], func=AF.Exp), reads=[Bt_["G"]], writes=[Bt_["eG"]])
                            op(ACT, lambda: A.activation(out=t["eNG"][:], in_=t["G"][:], func=AF.Exp, scale=-1.0), reads=[Bt_["G"]], writes=[Bt_["eNG"]])
                            op(POOL, lambda: G.tensor_tensor(out=t["kt"][:], in0=t["kk"][:], in1=t["eNG"][:], op=ALU.mult), reads=[Bt_["kk"], Bt_["eNG"]], writes=[Bt_["kt"]])
                            op(POOL, lambda: G.tensor_tensor(out=t["khT"][:].rearrange("p (c j) -> p c j", j=32), in0=t["kt"][:].rearrange("p (c j) -> p c j", j=32),
                                                             in1=t["eG"][:, 31:512:32].unsqueeze(2).to_broadcast([128, 16, 32]), op=ALU.mult),
                               reads=[Bt_["kt"], Bt_["eG"]], writes=[Bt_["khT"]])
                            qz_part()
                            op(POOL, lambda: G.tensor_tensor(out=t["qt"][:], in0=t["q"][:], in1=t["eG"][:], op=ALU.mult), reads=[Bt_["q"], Bt_["eG"]], writes=[Bt_["qt"]])
                            T.ck(f"C1_{l}_{hd}_{tg}")
                            for ti in range(4):
                                op(PE, lambda ti=ti: TE.transpose(PB[0][:, tcols(ti)], t["khT"][:, tcols(ti)], ident_bf[:]), reads=[Bt_["khT"], Bc], writes=[BPB[0]], inc=(ti == 3))
                            for ch in range(4):
                                if ch % 2 == 0:
                                    op(ACT, lambda ch=ch: A.activation(out=khtok[:, :, ch, :], in_=PB[0][:, 0:512].rearrange("p (a k) -> p a k", a=4), func=AF.Copy,
                                                                       scale=rowmask[:, ch:ch + 1]), reads=[BPB[0], Bc], writes=[Bkh])
                                else:
                                    op(DVE, lambda ch=ch: V.tensor_scalar(out=khtok[:, :, ch, :], in0=PB[0][:, 0:512].rearrange("p (a k) -> p a k", a=4),
                                                                          scalar1=rowmask[:, ch:ch + 1], scalar2=None, op0=ALU.mult), reads=[BPB[0], Bc], writes=[Bkh])
                            for ti in range(4):
                                op(PE, lambda ti=ti: TE.matmul(PS[1][:, tcols(ti)], lhsT=t["kt"][:, tcols(ti)], rhs=t["qt"][:, tcols(ti)], start=True, stop=True),
                                   reads=[Bt_["kt"], Bt_["qt"]], writes=[BPS[1]])
                            op(DVE, lambda: V.tensor_tensor(out=ATm[:], in0=PS[1][:].rearrange("p (a t) -> p a t", a=4),
                                                            in1=blockmask[:].unsqueeze(1).to_broadcast([128, 4, 128]), op=ALU.mult), reads=[BPS[1], Bc], writes=[BAT])
                            T.ck(f"C2_{l}_{hd}_{tg}")
                            banks_ = [3, 4, 1, 0]
                            for nchk in range(16):
                                ti, ch = nchk // 4, nchk % 4
                                bk, col = banks_[nchk // 4], (nchk % 4) * 128
                                op(PE, lambda bk=bk, col=col, ti=ti, ch=ch: TE.matmul(PS[bk][:, col:col + 128], lhsT=khtok[:, ti, ch, :], rhs=tv[:, ti, :], start=True, stop=True),
                                   reads=[Bkh, Btv], writes=[BPS[bk]], inc=(nchk % 4 == 3))
                            for half in range(2):
                                for r_ in range(8):
                                    nchk = half * 8 + r_
                                    bk, col = banks_[nchk // 4], (nchk % 4) * 128
                                    op(DVE, lambda bk=bk, col=col, nchk=nchk, r_=r_: V.scalar_tensor_tensor(
                                        out=Sring[:, r_ + 1, :], in0=Sring[:, r_, :], scalar=t["eG"][:, nchk * 32 + 31:nchk * 32 + 32], in1=PS[bk][:, col:col + 128],
                                        op0=ALU.mult, op1=ALU.add), reads=[BSr[r_], Bt_["eG"], BPS[bk]], writes=[BSr[r_ + 1]], inc=True)
                                    op(POOL, lambda r_=r_: G.tensor_copy(out=Sbf[:, r_ + 1, :], in_=Sring[:, r_ + 1, :]), reads=[BSr[r_ + 1]], writes=[BSb[r_ + 1]], inc=True)
                                for r_ in range(8):
                                    nchk = half * 8 + r_
                                    ti, ch = nchk // 4, nchk % 4
                                    cc = slice(nchk * 32, nchk * 32 + 32)
                                    op(PE, lambda ti=ti, ch=ch, cc=cc: TE.matmul(PS[2][:, cc], lhsT=tv[:, ti, :], rhs=ATm[:, ti, ch * 32:(ch + 1) * 32], start=True, stop=False),
                                       reads=[Btv, BAT], writes=[BPS[2]])
                                    op(PE, lambda r_=r_, cc=cc: TE.matmul(PS[2][:, cc], lhsT=Sbf[:, r_, :], rhs=t["qt"][:, cc], start=False, stop=True),
                                       reads=[BSb[r_], Bt_["qt"]], writes=[BPS[2]], inc=(r_ == 7))
                                op(DVE, lambda: V.tensor_copy(out=Sring[:, 0, :], in_=Sring[:, 8, :]), reads=[BSr[8]], writes=[BSr[0]])
                                op(POOL, lambda: G.tensor_copy(out=Sbf[:, 0, :], in_=Sbf[:, 8, :]), reads=[BSb[8]], writes=[BSb[0]])
                            no = 512
                        else:
                            qz_part()
                            op(PE, lambda: TE.transpose(PS[0][:, 0:128], t["kk"][:, 0:128], ident_f[:]), reads=[Bt_["kk"], Bc], writes=[BPS[0]])
                            for b in range(4):
                                op(DVE, lambda b=b: V.tensor_scalar(out=khtok[:, 0, b, :], in0=PS[0][:, 0:128], scalar1=rowmask_s[:, b:b + 1], scalar2=None, op0=ALU.mult),
                                   reads=[BPS[0], Bc], writes=[Bkh])
                            for b in range(4):
                                bj = b % 2
                                dma(SP, S0[bj], st[l, b, hd], writes=[BS0[bj]])
                                ku = 3 + bj
                                op(PE, lambda ku=ku, b=b: TE.matmul(PS[ku][:, 0:128], lhsT=khtok[:, 0, b, :], rhs=tv[:, 0, :], start=True, stop=True),
                                   reads=[Bkh, Btv], writes=[BPS[ku]])
                                op(DVE, lambda bj=bj, ku=ku, b=b: V.scalar_tensor_tensor(out=Sn[bj], in0=S0[bj], scalar=t["f"][:, 32 * b:32 * b + 1],
                                                                                       in1=PS[ku][:, 0:128], op0=ALU.mult, op1=ALU.add),
                                   reads=[BS0[bj], Bt_["f"], BPS[ku]], writes=[BSn[bj]])
                                dma(SP, ohs[l, b, hd], Sn[bj], reads=[BSn[bj]])
                                op(PE, lambda bj=bj, b=b: TE.matmul(PS[2][:, b:b + 1], lhsT=Sn[bj], rhs=t["q"][:, 32 * b:32 * b + 1], start=True, stop=True),
                                   reads=[BSn[bj], Bt_["q"]], writes=[BPS[2]])
                            no = 4
                        T.ck(f"C3_{l}_{hd}_{tg}")
                        so = slice(0, no)
                        op(ACT, lambda: A.activation(out=t["o"][:, so], in_=PS[2][:, so], func=AF.Copy), reads=[BPS[2]], writes=[Bt_["o"]])
                        op(ACT, lambda: A.activation(out=t["sq"][:, so], in_=PS[2][:, so], func=AF.Square), reads=[BPS[2]], writes=[Bt_["sq"]])
                        op(PE, lambda: TE.matmul(PS[5][:, so], lhsT=ones_f[:], rhs=t["sq"][:, so], start=True, stop=True), reads=[Bt_["sq"], Bc], writes=[BPS[5]])
                        op(ACT, lambda: A.activation(out=t["ln"][:, so], in_=PS[5][:, so], func=AF.Ln, scale=1.0 / 128, bias=epsc[:, 0:1]), reads=[BPS[5], Bc], writes=[Bt_["ln"]])
                        op(ACT, lambda: A.activation(out=t["rs"][:, so], in_=t["ln"][:, so], func=AF.Exp, scale=-0.5), reads=[Bt_["ln"]], writes=[Bt_["rs"]])
                        op(DVE, lambda: V.tensor_tensor(out=t["o"][:, so], in0=t["o"][:, so], in1=t["rs"][:, so], op=ALU.mult), reads=[Bt_["o"], Bt_["rs"]], writes=[Bt_["o"]])
                        if tg < 4:
                            op(DVE, lambda cols=cols, hd=hd: V.scalar_tensor_tensor(out=yT[:, hd, cols], in0=t["o"][:], scalar=hgc, in1=t["zs"][:], op0=ALU.mult, op1=ALU.mult),
                               reads=[Bt_["o"], Bt_["zs"], Bc], writes=[ByTp[hd]])
                        else:
                            op(DVE, lambda hd=hd: V.scalar_tensor_tensor(out=yT[:, hd, 2048:TOK:32], in0=t["o"][:, 0:4], scalar=hgc, in1=t["zs"][:, 0:128:32],
                                                                         op0=ALU.mult, op1=ALU.mult), reads=[Bt_["o"], Bt_["zs"], Bc], writes=[ByTs[hd]])
                        T.ck(f"C4_{l}_{hd}_{tg}")
                        if tg == 3:
                            dma(SP, ohp[l, hd], Sring[:, 0, :], reads=[BSr[0]])
                T.barrier()
                T.ck(f"C5_{l}")
                out_proj(l, 6)
                T.barrier()
                T.ck(f"L_{l}")

        T.force = True
        for i in range(16):
            dma(SP, yp[i * 128:(i + 1) * 128, :], xres[:, i, :], reads=[Bx[i]])
        for b in range(4):
            dma(SP, ys[b:b + 1, :], xres[32 * b:32 * b + 1, 16, :], reads=[Bx[16]])
        for Q in (SP, POOL):
            for s in Q.slots:
                if s.val > 0:
                    T._wait(SP, s.sem, s.val)
    return nc


_NC_CACHE = {}


def kernel(x_prompt, x_sample, cache_k, cache_v, state_hgrn, norm_g, w_in, q_norm_g, k_norm_g,
           sgu_ln_g, sgu_ln_b, sgu_w, sgu_b, hgrn_lb_logits, hgrn_norm_g, w_out):
    f = lambda a: np.ascontiguousarray(np.asarray(a, dtype=np.float32))
    x_prompt, x_sample, cache_k, cache_v, state_hgrn = map(f, (x_prompt, x_sample, cache_k, cache_v, state_hgrn))
    shared = {
        "norm_g": f(norm_g), "w_in": f(w_in), "q_norm_g": f(q_norm_g), "k_norm_g": f(k_norm_g),
        "sgu_ln_g": f(sgu_ln_g), "sgu_ln_b": f(sgu_ln_b), "sgu_w": f(sgu_w), "sgu_b": f(sgu_b),
        "hgrn_lb_logits": f(hgrn_lb_logits), "hgrn_norm_g": f(hgrn_norm_g), "w_out": f(w_out),
    }
    in_maps = []
    for c in range(NCORES):
        sb = slice(4 * c, 4 * c + 4)
        m = dict(shared)
        m["xp"] = np.ascontiguousarray(x_prompt[c])
        m["xs"] = np.ascontiguousarray(x_sample[sb, 0, :])
        m["ck"] = np.ascontiguousarray(cache_k[:, sb].reshape(2, 4, 2048, 768))
        m["cv"] = np.ascontiguousarray(cache_v[:, sb].reshape(2, 4, 2048, 768))
        m["st"] = np.ascontiguousarray(state_hgrn[:, sb])
        in_maps.append(m)
    if "nc" not in _NC_CACHE:
        _NC_CACHE["nc"] = build_nc()
    res = run_bass_kernel_spmd(_NC_CACHE["nc"], in_maps, core_ids=list(range(NCORES)))
    R = res.results
    y_prompt = np.stack([R[c]["yp"] for c in range(NCORES)]).astype(np.float32)
    y_sample = np.concatenate([R[c]["ys"] for c in range(NCORES)])[:, None, :].astype(np.float32)
    nkp = np.stack([R[c]["okp"] for c in range(NCORES)], axis=1).reshape(2, 8, 2048, 12, 64).astype(np.float32)
    nvp = np.stack([R[c]["ovp"] for c in range(NCORES)], axis=1).reshape(2, 8, 2048, 12, 64).astype(np.float32)
    nks = np.concatenate([R[c]["oks"] for c in range(NCORES)], axis=1).reshape(2, 32, 1, 12, 64).astype(np.float32)
    nvs = np.concatenate([R[c]["ovs"] for c in range(NCORES)], axis=1).reshape(2, 32, 1, 12, 64).astype(np.float32)
    nsg = np.concatenate([R[c]["osg"] for c in range(NCORES)], axis=1).reshape(2, 32, 1, 512).astype(np.float32)
    nhp = np.stack([R[c]["ohp"] for c in range(NCORES)], axis=1).astype(np.float32)
    nhs = np.concatenate([R[c]["ohs"] for c in range(NCORES)], axis=1).astype(np.float32)
    return (y_prompt, y_sample, nkp, nvp, nks, nvs, nsg, nhp, nhs)
```

```python
import bisect
import contextlib
import os

import numpy as np

import concourse.bass as bass
import concourse.mybir as mybir
from concourse.bass_utils import run_bass_kernel_spmd

F32 = mybir.dt.float32
BF16 = mybir.dt.bfloat16
AF = mybir.ActivationFunctionType
ALU = mybir.AluOpType
AX = mybir.AxisListType

NCORES = 8
NT = 17
TOK = NT * 128
EPS = 1e-6


class Eng:
    def __init__(self, name, eng, sem):
        self.name, self.eng, self.sem = name, eng, sem
        self.nseq = 0
        self.cnt = 0
        self.last = None
        self.inc_seq = []
        self.waited = {}
        self.slots = []
        self.rr = 0


class Slot:
    def __init__(self, sem):
        self.sem = sem
        self.val = 0


class Buf:
    __slots__ = ("name", "w", "r")

    def __init__(self, name):
        self.name = name
        self.w = None
        self.r = {}


class Tracker:
    def __init__(self):
        self.engs = []
        self.stopped = False
        self.force = False
        self.stop_at = os.environ.get("KSTOP", "")

    def ck(self, name):
        if self.stop_at and name == self.stop_at:
            self.stopped = True

    def resolve(self, ev):
        if ev[0] == "s":
            return ev[1], ev[2]
        E, seq = ev[1], ev[2]
        i = bisect.bisect_left(E.inc_seq, seq)
        if i < len(E.inc_seq):
            return E.sem, i + 1
        E.last.then_inc(E.sem, 1)
        E.cnt += 1
        E.inc_seq.append(E.nseq)
        return E.sem, E.cnt

    def _wait(self, E, sem, val):
        if E.waited.get(sem.num, 0) < val:
            E.eng.wait_ge(sem, val)
            E.waited[sem.num] = val

    def deps(self, E, reads, writes):
        evs = []
        for b in reads:
            if b.w is not None:
                evs.append(b.w)
        for b in writes:
            if b.w is not None:
                evs.append(b.w)
            for k, ev in b.r.items():
                evs.append(ev)
        need = {}
        for ev in evs:
            if ev[0] == "e" and ev[1] is E and E.name == "pe":
                continue
            sem, val = self.resolve(ev)
            if need.get(sem.num, (None, 0))[1] < val:
                need[sem.num] = (sem, val)
        for num, (sem, val) in need.items():
            self._wait(E, sem, val)

    def op(self, E, fn, reads=(), writes=(), inc=None):
        if self.stopped and not self.force:
            return None
        self.deps(E, reads, writes)
        ins = fn()
        E.nseq += 1
        E.last = ins
        if inc and os.environ.get("KINC", "1") == "1":
            ins.then_inc(E.sem, 1)
            E.cnt += 1
            E.inc_seq.append(E.nseq)
        ev = ("e", E, E.nseq)
        for b in writes:
            b.w = ev
            b.r = {}
        for b in reads:
            b.r[E.name] = ev
        return ins

    def dma(self, Q, out, in_, reads=(), writes=()):
        if self.stopped and not self.force:
            return
        self.deps(Q, reads, writes)
        slot = Q.slots[Q.rr % len(Q.slots)]
        Q.rr += 1
        if slot.val > 0:
            self._wait(Q, slot.sem, slot.val)
        Q.eng.dma_start(out=out, in_=in_).then_inc(slot.sem, 16)
        slot.val += 16
        ev = ("s", slot.sem, slot.val)
        for b in writes:
            b.w = ev
            b.r = {}
        for b in reads:
            b.r[("d", slot.sem.num)] = ev

    def barrier(self):
        if self.stopped and not self.force:
            return
        pts = []
        for F in self.engs:
            if F.nseq > 0:
                pts.append(self.resolve(("e", F, F.nseq)))
            for s in F.slots:
                if s.val > 0:
                    pts.append((s.sem, s.val))
        for E in self.engs:
            for sem, val in pts:
                if sem is E.sem and E.name == "pe":
                    continue
                self._wait(E, sem, val)


def build_nc():
    nc = bass.Bass("TRN2", target_bir_lowering=False)

    def din(name, shape):
        return nc.dram_tensor(name, shape, F32, kind="ExternalInput").ap()

    def dout(name, shape):
        return nc.dram_tensor(name, shape, F32, kind="ExternalOutput").ap()

    xp = din("xp", [2048, 1024])
    xs = din("xs", [4, 1024])
    ck = din("ck", [2, 4, 2048, 768])
    cv = din("cv", [2, 4, 2048, 768])
    st = din("st", [2, 4, 6, 128, 128])
    norm_g = din("norm_g", [2, 1024])
    w_in = din("w_in", [2, 1024, 7680])
    qng = din("q_norm_g", [2, 768])
    kng = din("k_norm_g", [2, 768])
    lng = din("sgu_ln_g", [2, 512])
    lnb = din("sgu_ln_b", [2, 512])
    sgw = din("sgu_w", [2, 8, 128, 128])
    sgb = din("sgu_b", [2, 8, 128])
    lbl = din("hgrn_lb_logits", [2, 768])
    hng = din("hgrn_norm_g", [2, 768])
    w_out = din("w_out", [2, 2048, 1024])
    yp = dout("yp", [2048, 1024])
    ys = dout("ys", [4, 1024])
    okp = dout("okp", [2, 2048, 768])
    ovp = dout("ovp", [2, 2048, 768])
    oks = dout("oks", [2, 4, 768])
    ovs = dout("ovs", [2, 4, 768])
    osg = dout("osg", [2, 4, 512])
    ohp = dout("ohp", [2, 6, 128, 128])
    ohs = dout("ohs", [2, 4, 6, 128, 128])

    T = Tracker()
    es = contextlib.ExitStack()
    with es:
        nmctr = [0]

        def sbt(stack, name, shape, dt):
            nmctr[0] += 1
            return stack.enter_context(nc.sbuf_tensor(f"{name}_{nmctr[0]}", shape, dt))

        sems = [es.enter_context(nc.semaphore(f"sem{i}")) for i in range(20)]
        PE = Eng("pe", nc.tensor, sems[0])
        ACT = Eng("act", nc.scalar, sems[1])
        DVE = Eng("dve", nc.vector, sems[2])
        POOL = Eng("pool", nc.gpsimd, sems[3])
        SP = Eng("sp", nc.sync, None)
        SP.slots = [Slot(s) for s in sems[4:12]]
        POOL.slots = [Slot(s) for s in sems[12:20]]
        T.engs = [PE, ACT, DVE, POOL, SP]
        op, dma = T.op, T.dma
        V, A, G, TE = nc.vector, nc.scalar, nc.gpsimd, nc.tensor

        PS = [es.enter_context(nc.psum_tensor(f"ps{i}", [128, 512], F32)) for i in range(6)]
        PB = [es.enter_context(nc.psum_tensor(f"pb{i}", [128, 1024], BF16)) for i in range(2)]
        BPS = [Buf(f"ps{i}") for i in range(6)]
        BPB = [Buf(f"pb{i}") for i in range(2)]

        xres = sbt(es, "xres", [128, NT, 1024], F32)
        hT = sbt(es, "hT", [128, 8, TOK], BF16)
        yT = sbt(es, "yT", [128, 6, TOK], BF16)
        WA = [sbt(es, f"wA{i}", [128, 8, 512], BF16) for i in range(2)]
        wo = sbt(es, "wo", [128, 6, 1024], BF16)
        ident_bf = sbt(es, "ident_bf", [128, 128], BF16)
        ident_f = sbt(es, "ident_f", [128, 128], F32)
        ones_bf = sbt(es, "ones_bf", [128, 128], BF16)
        ones_f = sbt(es, "ones_f", [128, 128], F32)
        mask01 = sbt(es, "mask01", [128, 512], BF16)
        maskcur = sbt(es, "maskcur", [128, 128], BF16)
        blockmask = sbt(es, "blockmask", [128, 128], F32)
        rowmask = sbt(es, "rowmask", [128, 4], F32)
        rowmask_s = sbt(es, "rowmask_s", [128, 4], F32)
        rowmask_s3 = sbt(es, "rowmask_s3", [128, 4], F32)
        epsc = sbt(es, "epsc", [128, 1], F32)
        lbT = sbt(es, "lbT", [128, 12], F32)
        omlbT = sbt(es, "omlbT", [128, 12], F32)
        hgT = sbt(es, "hgT", [128, 12], F32)

        Bx = [Buf(f"x{i}") for i in range(NT)]
        BhT = [Buf(f"hT{i}") for i in range(NT)]
        ByTp = [Buf(f"yTp{i}") for i in range(6)]
        ByTs = [Buf(f"yTs{i}") for i in range(6)]
        BW = [Buf("wA0"), Buf("wA1")]
        Bwo = Buf("wo")
        Bc = Buf("consts")

        def tcols(i):
            return slice(i * 128, (i + 1) * 128)

        units = []
        for l in range(2):
            for c in range(6):
                units.append((l, [(0, c * 128), (128, 768 + c * 128), (256, 1536 + c * 128), (384, 2304 + c * 128)], 128))
            units.append((l, [(0, 3584)], 512))
            for cb in range(4):
                units.append((l, [(0, 3072 + cb * 128), (128, 4096 + cb * 128)], 128))
            for hd in range(6):
                units.append((l, [(0, 4608 + hd * 128), (128, 5376 + hd * 128), (256, 6144 + hd * 128), (384, 6912 + hd * 128)], 128))
        ustate = {"loaded": 0}

        def load_unit(u):
            if u >= len(units) or u < ustate["loaded"]:
                return
            assert u == ustate["loaded"]
            ustate["loaded"] = u + 1
            l, parts, wdt = units[u]
            wt = WA[u % 2]
            for (dst, src, ) in [(p[0], p[1]) for p in parts]:
                dma(POOL, wt[:, :, dst:dst + wdt],
                    w_in[l, :, src:src + wdt].rearrange("(kc p) n -> p kc n", p=128), writes=[BW[u % 2]])

        def load_wo(l, r0, nch):
            dma(POOL, wo[:, 0:nch, :], w_out[l, r0:r0 + nch * 128, :].rearrange("(c p) d -> p c d", p=128), writes=[Bwo])

        with contextlib.ExitStack() as ar:
            tmpf = sbt(ar, "c_tmpf", [128, 512], F32)
            R4 = sbt(ar, "c_R4", [4, 128], F32)
            ld12 = sbt(ar, "c_ld12", [12, 128], F32)
            hg12 = sbt(ar, "c_hg12", [12, 128], F32)
            lgT = sbt(ar, "c_lgT", [128, 12], F32)
            Bt = Buf("c_tmp")
            BR4 = Buf("c_R4")
            Bl = Buf("c_ld")
            dma(SP, ld12[:], lbl.rearrange("l (h k) -> (l h) k", k=128), writes=[Bl])
            dma(SP, hg12[:], hng.rearrange("l (h k) -> (l h) k", k=128), writes=[Bl])
            for i in range(16):
                dma(SP, xres[:, i, :], xp[i * 128:(i + 1) * 128, :], writes=[Bx[i]])
            op(DVE, lambda: V.memset(xres[:, 16, :], 0.0), writes=[Bx[16]])
            for b in range(4):
                dma(SP, xres[32 * b:32 * b + 1, 16, :], xs[b:b + 1, :], writes=[Bx[16]])
            op(DVE, lambda: V.memset(yT[:, :, 2048:TOK], 0.0), writes=ByTs)
            op(DVE, lambda: V.memset(epsc[:], EPS), writes=[Bc])
            op(DVE, lambda: V.memset(ones_bf[:], 1.0), writes=[Bc])
            op(DVE, lambda: V.memset(ones_f[:], 1.0), writes=[Bc])
            op(POOL, lambda: G.memset(ident_f[:], 1.0), writes=[Bc])
            op(POOL, lambda: G.affine_select(out=ident_f[:], in_=ident_f[:], pattern=[[-1, 128]], compare_op=ALU.is_equal,
                                             fill=0.0, base=0, channel_multiplier=1), reads=[Bc], writes=[Bc])
            op(DVE, lambda: V.tensor_copy(out=ident_bf[:], in_=ident_f[:]), reads=[Bc], writes=[Bc])
            op(POOL, lambda: G.memset(tmpf[:, 0:256], 1.0), writes=[Bt])
            op(POOL, lambda: G.affine_select(out=tmpf[:, 0:128], in_=tmpf[:, 0:128], pattern=[[-1, 128]], compare_op=ALU.is_ge,
                                             fill=0.0, base=0, channel_multiplier=1), reads=[Bt], writes=[Bt])
            op(POOL, lambda: G.affine_select(out=tmpf[:, 128:256], in_=tmpf[:, 128:256], pattern=[[1, 128]], compare_op=ALU.is_ge,
                                             fill=0.0, base=0, channel_multiplier=-1), reads=[Bt], writes=[Bt])
            for kb in range(2):
                for h in range(2):
                    op(DVE, lambda kb=kb, h=h: V.tensor_copy(out=mask01[:, (h * 2 + kb) * 128:(h * 2 + kb + 1) * 128],
                                                            in_=tmpf[:, kb * 128:(kb + 1) * 128]), reads=[Bt], writes=[Bc])
            op(DVE, lambda: V.tensor_copy(out=maskcur[:], in_=tmpf[:, 128:256]), reads=[Bt], writes=[Bc])
            op(POOL, lambda: G.memset(R4[:], 1.0), writes=[BR4])
            op(POOL, lambda: G.affine_select(out=R4[:], in_=R4[:], pattern=[[1, 128]], compare_op=ALU.is_ge,
                                             fill=0.0, base=0, channel_multiplier=-32), reads=[BR4], writes=[BR4])
            op(POOL, lambda: G.affine_select(out=R4[:], in_=R4[:], pattern=[[-1, 128]], compare_op=ALU.is_ge,
                                             fill=0.0, base=31, channel_multiplier=32), reads=[BR4], writes=[BR4])
            op(PE, lambda: TE.matmul(PS[0][:, 0:128], lhsT=R4[:], rhs=R4[:], start=True, stop=True), reads=[BR4], writes=[BPS[0]])
            op(PE, lambda: TE.matmul(PS[0][:, 128:132], lhsT=R4[:], rhs=ident_f[0:4, 0:4], start=True, stop=True),
               reads=[BR4, Bc], writes=[BPS[0]])
            op(DVE, lambda: V.tensor_tensor(out=blockmask[:], in0=tmpf[:, 128:256], in1=PS[0][:, 0:128], op=ALU.mult),
               reads=[Bt, BPS[0]], writes=[Bc])
            op(DVE, lambda: V.tensor_copy(out=rowmask[:], in_=PS[0][:, 128:132]), reads=[BPS[0]], writes=[Bc])
            op(POOL, lambda: G.memset(rowmask_s[:], 1.0), writes=[Bc])
            op(POOL, lambda: G.affine_select(out=rowmask_s[:], in_=rowmask_s[:], pattern=[[-32, 4]], compare_op=ALU.is_equal,
                                             fill=0.0, base=0, channel_multiplier=1), reads=[Bc], writes=[Bc])
            op(DVE, lambda: V.tensor_scalar(out=rowmask_s3[:], in0=rowmask_s[:], scalar1=3.0, scalar2=None, op0=ALU.mult),
               reads=[Bc], writes=[Bc])
            op(PE, lambda: TE.transpose(PS[1][:, 0:12], ld12[:], ident_f[0:12, 0:12]), reads=[Bl, Bc], writes=[BPS[1]])
            op(PE, lambda: TE.transpose(PS[1][:, 16:28], hg12[:], ident_f[0:12, 0:12]), reads=[Bl, Bc], writes=[BPS[1]])
            op(DVE, lambda: V.tensor_copy(out=lgT[:], in_=PS[1][:, 0:12]), reads=[BPS[1]], writes=[Bt])
            op(DVE, lambda: V.tensor_copy(out=hgT[:], in_=PS[1][:, 16:28]), reads=[BPS[1]], writes=[Bc])
            op(DVE, lambda: V.memset(lbT[:], 0.0), writes=[Bc])
            op(DVE, lambda: V.tensor_tensor(out=lgT[:, 0:6], in0=lgT[:, 0:6], in1=lgT[:, 6:12], op=ALU.subtract), reads=[Bt], writes=[Bt])
            op(ACT, lambda: A.activation(out=lgT[:, 0:6], in_=lgT[:, 0:6], func=AF.Exp), reads=[Bt], writes=[Bt])
            op(DVE, lambda: V.tensor_scalar(out=lgT[:, 0:6], in0=lgT[:, 0:6], scalar1=1.0, scalar2=None, op0=ALU.add), reads=[Bt], writes=[Bt])
            op(DVE, lambda: V.reciprocal(out=lbT[:, 6:12], in_=lgT[:, 0:6]), reads=[Bt, Bc], writes=[Bc])
            op(DVE, lambda: V.tensor_scalar(out=omlbT[:], in0=lbT[:], scalar1=-1.0, scalar2=1.0, op0=ALU.mult, op1=ALU.add),
               reads=[Bc], writes=[Bc])
            load_unit(0)
            T.barrier()
            T.ck("const")

        def sigmoid_act(src_ap, dst_ap, rd, Bdst):
            op(ACT, lambda: A.activation(out=dst_ap, in_=src_ap, func=AF.Exp, scale=-1.0), reads=rd, writes=[Bdst])
            op(ACT, lambda: A.activation(out=dst_ap, in_=dst_ap, func=AF.Ln, scale=1.0, bias=1.0), reads=[Bdst], writes=[Bdst])
            op(ACT, lambda: A.activation(out=dst_ap, in_=dst_ap, func=AF.Exp, scale=-1.0), reads=[Bdst], writes=[Bdst])

        def silu_from_psum(P, n, out_ap, tmp, Btmp, BP, wr):
            sigmoid_act(P[:, 0:n], tmp[:, 0:n], [BP], Btmp)
            op(DVE, lambda: V.tensor_tensor(out=out_ap, in0=P[:, 0:n], in1=tmp[:, 0:n], op=ALU.mult), reads=[Btmp, BP], writes=wr)

        def out_proj(l, nch):
            for i in range(NT):
                ybufs = (ByTp if i < 16 else ByTs)[0:nch]
                for half in range(2):
                    k = (2 * i + half) % 4
                    P = PS[k]
                    for cc in range(nch):
                        op(PE, lambda P=P, cc=cc, i=i, half=half: TE.matmul(
                            P[:], lhsT=yT[:, cc, tcols(i)], rhs=wo[:, cc, half * 512:(half + 1) * 512],
                            start=(cc == 0), stop=(cc == nch - 1)), reads=ybufs + [Bwo], writes=[BPS[k]])
                    xs_ap = xres[:, i, half * 512:(half + 1) * 512]
                    op(DVE, lambda P=P, xs_ap=xs_ap: V.tensor_tensor(out=xs_ap, in0=xs_ap, in1=P[:], op=ALU.add),
                       reads=[Bx[i], BPS[k]], writes=[Bx[i]])

        uidx = 0
        for l in range(2):
            if os.environ.get("KSKIP0", "") == "1":
                if l == 0:
                    T.stopped = True
                else:
                    T.stopped = False
                    ustate["loaded"] = 17
            with contextlib.ExitStack() as ar:
                gn = sbt(ar, "n_gn", [128, 1024], F32)
                sqj = sbt(ar, "n_sqj", [128, 1024], BF16)
                ss = sbt(ar, "n_ss", [128, NT], F32)
                rstd = sbt(ar, "n_rstd", [128, NT], F32)
                hb = [sbt(ar, f"n_hb{j}", [128, 1024], BF16) for j in range(2)]
                Bgn, Bsq, Bss, Brs = Buf("gn"), Buf("sqj"), Buf("ss"), Buf("rstd")
                Bhb = [Buf("hb0"), Buf("hb1")]
                dma(SP, gn[:], norm_g[l].partition_broadcast(128), writes=[Bgn])
                op(DVE, lambda: V.memset(ss[:], 0.0), writes=[Bss])
                for i in range(NT):
                    op(ACT, lambda i=i: A.activation(out=sqj[:], in_=xres[:, i, :], func=AF.Square, accum_out=ss[:, i:i + 1]),
                       reads=[Bx[i], Bss], writes=[Bsq, Bss])
                op(ACT, lambda: A.activation(out=ss[:], in_=ss[:], func=AF.Ln, scale=1.0 / 1024, bias=epsc[:, 0:1]),
                   reads=[Bss, Bc], writes=[Bss])
                op(ACT, lambda: A.activation(out=rstd[:], in_=ss[:], func=AF.Exp, scale=-0.5), reads=[Bss], writes=[Brs])
                for i in range(NT):
                    j = i % 2
                    op(DVE, lambda i=i, j=j: V.scalar_tensor_tensor(out=hb[j][:], in0=xres[:, i, :], scalar=rstd[:, i:i + 1], in1=gn[:],
                                                                    op0=ALU.mult, op1=ALU.mult),
                       reads=[Bx[i], Brs, Bgn], writes=[Bhb[j]])
                    for kc in range(8):
                        op(PE, lambda j=j, kc=kc: TE.transpose(PB[j][:, tcols(kc)], hb[j][:, tcols(kc)], ident_bf[:]),
                           reads=[Bhb[j], Bc], writes=[BPB[j]])
                    op(ACT, lambda i=i, j=j: A.activation(out=hT[:, :, tcols(i)], in_=PB[j][:].rearrange("p (k t) -> p k t", k=8), func=AF.Copy),
                       reads=[BPB[j]], writes=[BhT[i]])
                T.barrier()
                T.ck(f"norm{l}")

            with contextlib.ExitStack() as arA:
                qs_tok = sbt(arA, "a_qs", [128, 768], BF16)
                ks_tok = sbt(arA, "a_ks", [128, 768], BF16)
                vs_tok = sbt(arA, "a_vs", [128, 768], BF16)
                zsamp = sbt(arA, "a_zs", [128, 6, 4], F32)
                Bst = Buf("a_stash")
                load_wo(l, 0, 6)
                with contextlib.ExitStack() as ar:
                    qkT = sbt(ar, "a_qkT", [128, 2, TOK], BF16)
                    vnat = sbt(ar, "a_vnat", [128, NT, 128], BF16)
                    vord = sbt(ar, "a_vord", [128, 16, 128], BF16)
                    zT = sbt(ar, "a_zT", [128, TOK], BF16)
                    UD = sbt(ar, "a_UD", [128, 2, 2048], F32)
                    pp = [sbt(ar, f"a_p{j}", [128, 512], BF16) for j in range(2)]
                    kfin = [sbt(ar, f"a_kfin{j}", [128, 128], F32) for j in range(2)]
                    vfin = [sbt(ar, f"a_vfin{j}", [128, 128], F32) for j in range(2)]
                    qb = [sbt(ar, f"a_qb{j}", [128, 128], BF16) for j in range(2)]
                    kb_ = [sbt(ar, f"a_kb{j}", [128, 128], BF16) for j in range(2)]
                    ss4 = sbt(ar, "a_ss4", [128, 4], F32)
                    rs4 = sbt(ar, "a_rs4", [128, 4], F32)
                    qgc = sbt(ar, "a_qgc", [128, 128], F32)
                    kgc = sbt(ar, "a_kgc", [128, 128], F32)
                    BqkT, Bvn, Bvo, BzT, BUD = Buf("qkT"), Buf("vnat"), Buf("vord"), Buf("zT"), Buf("UD")
                    Bp = [Buf("p0"), Buf("p1")]
                    Bpm = [Buf("pm0"), Buf("pm1")]
                    Bkf = [Buf("kf0"), Buf("kf1")]
                    Bvf = [Buf("vf0"), Buf("vf1")]
                    Btq, Btk, Bsq4, Bss4, Brs4, Bg, Bez = Buf("tq"), Buf("tk"), Buf("sq"), Buf("ss4"), Buf("rs4"), Buf("g"), Buf("ezt")
                    Bqb = [Buf("qb0"), Buf("qb1")]
                    Bkb = [Buf("kb0"), Buf("kb1")]
                    blk_ctr = [0]
                    sq = UD[:, 1, 0:256]
                    ezt = UD[:, 0, 0:512]
                    Bsq4 = BUD
                    Bez = BUD

                    for c in range(6):
                        u = uidx
                        uidx += 1
                        load_unit(u)
                        load_unit(u + 1)
                        wt, Bw = WA[u % 2], BW[u % 2]
                        csl = slice(c * 128, (c + 1) * 128)
                        dma(SP, qgc[:], qng[l, csl].partition_broadcast(128), writes=[Bg])
                        dma(SP, kgc[:], kng[l, csl].partition_broadcast(128), writes=[Bg])
                        op(DVE, lambda: V.tensor_scalar(out=qgc[:], in0=qgc[:], scalar1=0.125, scalar2=None, op0=ALU.mult), reads=[Bg], writes=[Bg])
                        def a1_front(i):
                            j = i % 2
                            P, BP = PS[j], BPS[j]
                            for kc in range(8):
                                op(PE, lambda P=P, kc=kc, i=i: TE.matmul(P[:, 0:384], lhsT=hT[:, kc, tcols(i)], rhs=wt[:, kc, 0:384],
                                                                        start=(kc == 0), stop=(kc == 7)), reads=[BhT[i], Bw], writes=[BP], inc=(kc == 7))
                            op(ACT, lambda P=P: A.activation(out=sq, in_=P[:, 0:256], func=AF.Square), reads=[BP], writes=[Bsq4])
                            op(DVE, lambda: V.tensor_reduce(out=ss4[:], in_=sq.rearrange("p (h e) -> p h e", e=64), axis=AX.X, op=ALU.add),
                               reads=[Bsq4], writes=[Bss4])
                            op(ACT, lambda: A.activation(out=ss4[:], in_=ss4[:], func=AF.Ln, scale=1.0 / 64, bias=epsc[:, 0:1]),
                               reads=[Bss4, Bc], writes=[Bss4])
                            op(ACT, lambda: A.activation(out=rs4[:], in_=ss4[:], func=AF.Exp, scale=-0.5), reads=[Bss4], writes=[Brs4])
                            for h in range(2):
                                hs = slice(h * 64, (h + 1) * 64)
                                op(DVE, lambda P=P, j=j, h=h, hs=hs: V.scalar_tensor_tensor(out=qb[j][:, hs], in0=P[:, hs], scalar=rs4[:, h:h + 1], in1=qgc[:, hs],
                                                                                        op0=ALU.mult, op1=ALU.mult), reads=[BP, Brs4, Bg], writes=[Bqb[j]])
                            for h in range(2):
                                hs = slice(h * 64, (h + 1) * 64)
                                op(DVE, lambda P=P, j=j, h=h, hs=hs: V.scalar_tensor_tensor(out=kfin[j][:, hs], in0=P[:, 128 + h * 64:128 + (h + 1) * 64],
                                                                                        scalar=rs4[:, 2 + h:3 + h], in1=kgc[:, hs], op0=ALU.mult, op1=ALU.mult),
                                   reads=[BP, Brs4, Bg], writes=[Bkf[j]])
                            op(POOL, lambda j=j: G.tensor_copy(out=kb_[j][:], in_=kfin[j][:]), reads=[Bkf[j]], writes=[Bkb[j]])
                            op(ACT, lambda P=P, j=j: A.activation(out=vfin[j][:], in_=P[:, 256:384], func=AF.Copy), reads=[BP], writes=[Bvf[j]])
                            op(POOL, lambda i=i, j=j: G.tensor_copy(out=vnat[:, i, :], in_=vfin[j][:]), reads=[Bvf[j]], writes=[Bvn])
                            if i < 16:
                                dma(SP, okp[l, tcols(i), csl], kfin[j][:], reads=[Bkf[j]])
                                dma(SP, ovp[l, tcols(i), csl], vfin[j][:], reads=[Bvf[j]])
                            else:
                                for b in range(4):
                                    dma(SP, oks[l, b:b + 1, csl], kfin[j][32 * b:32 * b + 1, :], reads=[Bkf[j]])
                                    dma(SP, ovs[l, b:b + 1, csl], vfin[j][32 * b:32 * b + 1, :], reads=[Bvf[j]])
                                op(POOL, lambda j=j: G.tensor_copy(out=qs_tok[:, csl], in_=qb[j][:]), reads=[Bqb[j]], writes=[Bst])
                                op(POOL, lambda j=j: G.tensor_copy(out=ks_tok[:, csl], in_=kb_[j][:]), reads=[Bkb[j]], writes=[Bst])
                                op(POOL, lambda j=j: G.tensor_copy(out=vs_tok[:, csl], in_=vfin[j][:]), reads=[Bvf[j]], writes=[Bst])
                        def a1_back(i):
                            j = i % 2
                            op(PE, lambda j=j: TE.transpose(PB[j][:, 0:128], qb[j][:], ident_bf[:]), reads=[Bqb[j], Bc], writes=[BPB[j]])
                            op(PE, lambda j=j: TE.transpose(PB[j][:, 128:256], kb_[j][:], ident_bf[:]), reads=[Bkb[j], Bc], writes=[BPB[j]], inc=True)
                            op(DVE, lambda i=i, j=j: V.tensor_copy(out=qkT[:, :, tcols(i)], in_=PB[j][:, 0:256].rearrange("p (a t) -> p a t", a=2)),
                               reads=[BPB[j]], writes=[BqkT])
                        if os.environ.get("KPIPE", "0") == "1":
                            a1_front(0)
                            for i in range(1, NT):
                                a1_front(i)
                                a1_back(i - 1)
                            a1_back(NT - 1)
                        else:
                            for i in range(NT):
                                a1_front(i)
                                a1_back(i)
                        T.ck(f"A1_{l}_{c}")

                        def tsl_of(dil):
                            def tsl(blk):
                                if dil == 4:
                                    jb, r4 = blk // 4, blk % 4
                                    return slice(512 * jb + r4, 512 * (jb + 1), 4)
                                return slice(blk, 2048, 16)
                            return tsl

                        def vproj(dil):
                            tsl = tsl_of(dil)
                            for g4 in range(4):
                                k = 2 + g4 % 2
                                P, BP = PS[k], BPS[k]
                                for bi in range(4):
                                    blk = 4 * g4 + bi
                                    for kc in range(8):
                                        op(PE, lambda P=P, bi=bi, blk=blk, kc=kc: TE.matmul(P[:, tcols(bi)], lhsT=hT[:, kc, tsl(blk)], rhs=wt[:, kc, 256:384],
                                                                                           start=(kc == 0), stop=(kc == 7)), reads=BhT[0:16] + [Bw], writes=[BP])
                                op(ACT, lambda P=P, g4=g4: A.activation(out=vord[:, 4 * g4:4 * g4 + 4, :], in_=P[:].rearrange("p (a t) -> p a t", a=4), func=AF.Copy),
                                   reads=[BP], writes=[Bvo])

                        vproj(4)
                        for tg in range(5):
                            n = 512 if tg < 4 else 128
                            cols = slice(tg * 512, tg * 512 + n)
                            k = 4 + tg % 2
                            P, BP = PS[k], BPS[k]
                            hb_ = BhT[4 * tg:4 * tg + 4] if tg < 4 else [BhT[16]]
                            for kc in range(8):
                                op(PE, lambda P=P, kc=kc, cols=cols, n=n: TE.matmul(P[:, 0:n], lhsT=wt[:, kc, 384:512], rhs=hT[:, kc, cols],
                                                                                  start=(kc == 0), stop=(kc == 7)), reads=hb_ + [Bw], writes=[BP])
                            silu_from_psum(P, n, zT[:, cols], ezt, Bez, BP, [BzT])
                        op(POOL, lambda c=c: G.tensor_copy(out=zsamp[:, c, :], in_=zT[:, 2048:TOK:32]), reads=[BzT], writes=[Bst])

                        T.ck(f"A2_{l}_{c}")
                        def attn_front(qsl, kcur, kprev, vcur, vprev, first):
                            n_ = blk_ctr[0]
                            blk_ctr[0] += 1
                            jj = n_ % 2
                            Sb = [(PS[n_ % 2], BPS[n_ % 2]), (PS[2 + n_ % 2], BPS[2 + n_ % 2])]
                            kbs = [1] if kprev is None else [0, 1]
                            lo = 0 if kprev is not None else 128
                            for h in range(2):
                                S, BS = Sb[h]
                                hs = slice(h * 64, (h + 1) * 64)
                                for kb in kbs:
                                    ks = kprev if kb == 0 else kcur
                                    op(PE, lambda S=S, hs=hs, ks=ks, kb=kb: TE.matmul(S[:, kb * 128:(kb + 1) * 128], lhsT=qkT[hs, 1, ks], rhs=qkT[hs, 0, qsl],
                                                                                   start=True, stop=True), reads=[BqkT], writes=[BS], inc=(kb == 1))
                            for h in range(2):
                                S, BS = Sb[h]
                                op(ACT, lambda S=S, h=h: A.activation(out=pp[jj][:, h * 256 + lo:(h + 1) * 256], in_=S[:, lo:256], func=AF.Exp),
                                   reads=[BS], writes=[Bp[jj]], inc=True)
                            pv = pp[jj][:].rearrange("p (h x) -> p h x", h=2)[:, :, lo:256]
                            mv_ = mask01[:].rearrange("p (h x) -> p h x", h=2)[:, :, lo:256]
                            op(POOL, lambda pv=pv, mv_=mv_: G.tensor_tensor(out=pv, in0=pv, in1=mv_, op=ALU.mult), reads=[Bp[jj], Bc], writes=[Bp[jj]], inc=True)
                            return (n_, qsl, kbs, vcur, vprev, first)

                        def attn_back(ctx):
                            n_, qsl, kbs, vcur, vprev, first = ctx
                            jj = n_ % 2
                            U, BU = PS[4 + n_ % 2], BPS[4 + n_ % 2]
                            for part in range(2):
                                for h in range(2):
                                    hs = slice(h * 64, (h + 1) * 64)
                                    for idx, kb in enumerate(kbs):
                                        vb = vprev if kb == 0 else vcur
                                        lhsT = vb[:, hs] if part == 0 else ones_bf[:, 0:64]
                                        o0 = (h * 2 + kb) * 128
                                        op(PE, lambda U=U, hs=hs, part=part, lhsT=lhsT, o0=o0, idx=idx: TE.matmul(
                                            U[hs, part * 128:(part + 1) * 128], lhsT=lhsT, rhs=pp[jj][:, o0:o0 + 128],
                                            start=(idx == 0), stop=(idx == len(kbs) - 1)), reads=[Bp[jj], Bvn, Bvo, Bc], writes=[BU],
                                           inc=(part == 1 and h == 1 and idx == len(kbs) - 1))
                            uv = U[:, 0:256].rearrange("p (a q) -> p a q", a=2)
                            if first:
                                op(DVE, lambda: V.tensor_copy(out=UD[:, :, qsl], in_=uv), reads=[BU], writes=[BUD], inc=True)
                            else:
                                op(DVE, lambda: V.tensor_tensor(out=UD[:, :, qsl], in0=UD[:, :, qsl], in1=uv, op=ALU.add), reads=[BU, BUD], writes=[BUD], inc=True)

                        def run_blocks(specs):
                            prev = None
                            for sp_ in specs:
                                ctx = attn_front(*sp_)
                                if prev is not None:
                                    attn_back(prev)
                                prev = ctx
                            attn_back(prev)

                        run_blocks([(tcols(i), tcols(i), tcols(i - 1) if i > 0 else None, vnat[:, i, :], vnat[:, i - 1, :] if i > 0 else None, True)
                                    for i in range(16)])
                        T.ck(f"A4a_{l}_{c}")
                        for dil in (4, 16):
                            tsl = tsl_of(dil)
                            if dil == 16:
                                vproj(16)
                            run_blocks([(tsl(blk), tsl(blk), tsl(blk - 4), vord[:, blk, :], vord[:, blk - 4, :], False) if (dil == 4 and blk >= 4)
                                        else (tsl(blk), tsl(blk), None, vord[:, blk, :], None, False) for blk in range(16)])
                        T.ck(f"A4b_{l}_{c}")
                        op(ACT, lambda: A.activation(out=UD[:, 1, :], in_=UD[:, 1, :], func=AF.Ln), reads=[BUD], writes=[BUD])
                        op(ACT, lambda: A.activation(out=UD[:, 1, :], in_=UD[:, 1, :], func=AF.Exp, scale=-1.0), reads=[BUD], writes=[BUD])
                        op(DVE, lambda: V.tensor_tensor(out=UD[:, 0, :], in0=UD[:, 0, :], in1=UD[:, 1, :], op=ALU.mult), reads=[BUD], writes=[BUD])
                        op(POOL, lambda c=c: G.tensor_tensor(out=yT[:, c, 0:2048], in0=UD[:, 0, :], in1=zT[:, 0:2048], op=ALU.mult),
                           reads=[BUD, BzT], writes=[ByTp[c]])
                    T.barrier()

                T.ck(f"A4_{l}")
                with contextlib.ExitStack() as ar:
                    selb = sbt(ar, "s_selb", [128, 4, 128], BF16)
                    selbf = sbt(ar, "s_selbf", [128, 512], F32)
                    Kr = [sbt(ar, f"s_Kr{j}", [128, 768], BF16) for j in range(2)]
                    Vr = [sbt(ar, f"s_Vr{j}", [128, 3, 768], BF16) for j in range(2)]
                    prod = sbt(ar, "s_prod", [128, 768], F32)
                    sc = sbt(ar, "s_sc", [128, 12], F32)
                    pall = [sbt(ar, f"s_pall{j}", [128, 3, 12], BF16) for j in range(2)]
                    pnew = sbt(ar, "s_pnew", [128, 12], F32)
                    pnm = sbt(ar, "s_pnm", [128, 4, 12], BF16)
                    rd = sbt(ar, "s_rd", [128, 6], F32)
                    osb = sbt(ar, "s_osb", [128, 6], F32)
                    Bsel, Bprod, Bsc, Bpn, Bpnm, Brd, Bos = Buf("selb"), Buf("prod"), Buf("sc"), Buf("pnew"), Buf("pnm"), Buf("rd"), Buf("osb")
                    BKr = [Buf("Kr0"), Buf("Kr1")]
                    BVr = [Buf("Vr0"), Buf("Vr1")]
                    Bpa = [Buf("pa0"), Buf("pa1")]
                    op(POOL, lambda: G.memset(selbf[:], 1.0), writes=[Bsel])
                    op(POOL, lambda: G.affine_select(out=selbf[:].rearrange("p (b m) -> p b m", b=4), in_=selbf[:].rearrange("p (b m) -> p b m", b=4),
                                                     pattern=[[-32, 4], [0, 128]], compare_op=ALU.is_equal, fill=0.0, base=0, channel_multiplier=1),
                       reads=[Bsel], writes=[Bsel])
                    op(DVE, lambda: V.tensor_copy(out=selb[:].rearrange("p b m -> p (b m)"), in_=selbf[:]), reads=[Bsel], writes=[Bsel])
                    op(DVE, lambda: V.tensor_tensor(out=prod[:], in0=qs_tok[:], in1=ks_tok[:], op=ALU.mult), reads=[Bst], writes=[Bprod])
                    op(DVE, lambda: V.tensor_reduce(out=sc[:], in_=prod[:].rearrange("p (h e) -> p h e", e=64), axis=AX.X, op=ALU.add),
                       reads=[Bprod], writes=[Bsc])
                    op(ACT, lambda: A.activation(out=pnew[:], in_=sc[:], func=AF.Exp), reads=[Bsc], writes=[Bpn])
                    for b in range(4):
                        op(DVE, lambda b=b: V.tensor_scalar(out=pnm[:, b, :], in0=pnew[:], scalar1=rowmask_s3[:, b:b + 1], scalar2=None, op0=ALU.mult),
                           reads=[Bpn, Bc], writes=[Bpnm])
                    kctr = 0
                    for b in range(4):
                        bj = b % 2
                        for pi, (r0, step) in enumerate([(1920, 1), (1536, 4), (0, 16)]):
                            rows = slice(r0, 2048, step)
                            kj = kctr % 2
                            kctr += 1
                            dma(POOL, Kr[kj][:], ck[l, b, rows, :], writes=[BKr[kj]])
                            dma(POOL, Vr[bj][:, pi, :], cv[l, b, rows, :], writes=[BVr[bj]])
                            for hf in range(2):
                                op(PE, lambda b=b, hf=hf: TE.matmul(PS[hf][:, 0:384], lhsT=selb[:, b, :], rhs=qs_tok[:, hf * 384:(hf + 1) * 384],
                                                                  start=True, stop=True), reads=[Bsel, Bst], writes=[BPS[hf]])
                            for hf in range(2):
                                op(DVE, lambda kj=kj, hf=hf: V.tensor_tensor(out=prod[:, hf * 384:(hf + 1) * 384], in0=Kr[kj][:, hf * 384:(hf + 1) * 384],
                                                                           in1=PS[hf][:, 0:384], op=ALU.mult), reads=[BKr[kj], BPS[hf]], writes=[Bprod])
                            op(DVE, lambda: V.tensor_reduce(out=sc[:], in_=prod[:].rearrange("p (h e) -> p h e", e=64), axis=AX.X, op=ALU.add),
                               reads=[Bprod], writes=[Bsc])
                            op(ACT, lambda bj=bj, pi=pi: A.activation(out=pall[bj][:, pi, :], in_=sc[:], func=AF.Exp), reads=[Bsc], writes=[Bpa[bj]])
                        PO, BPO = PS[2 + bj], BPS[2 + bj]
                        for part in range(2):
                            for c in range(6):
                                o0 = part * 16 + 2 * c
                                for pi in range(3):
                                    lhsT = Vr[bj][:, pi, c * 128:(c + 1) * 128] if part == 0 else ones_bf[:]
                                    op(PE, lambda PO=PO, o0=o0, lhsT=lhsT, pi=pi, c=c: TE.matmul(PO[:, o0:o0 + 2], lhsT=lhsT, rhs=pall[bj][:, pi, 2 * c:2 * c + 2],
                                                                                              start=(pi == 0), stop=False), reads=[BVr[bj], Bpa[bj], Bc], writes=[BPO])
                                lhsT = vs_tok[:, c * 128:(c + 1) * 128] if part == 0 else ones_bf[:]
                                op(PE, lambda PO=PO, o0=o0, lhsT=lhsT, b=b, c=c: TE.matmul(PO[:, o0:o0 + 2], lhsT=lhsT, rhs=pnm[:, b, 2 * c:2 * c + 2],
                                                                                        start=False, stop=True), reads=[Bst, Bpnm, Bc], writes=[BPO])
                        for hh in range(2):
                            rws = slice(hh * 64, hh * 64 + 64)
                            op(DVE, lambda PO=PO, rws=rws, hh=hh: V.reciprocal(out=rd[rws, :], in_=PO[rws, 16 + hh:28:2]), reads=[BPO], writes=[Brd])
                            op(DVE, lambda PO=PO, rws=rws, hh=hh: V.tensor_tensor(out=osb[rws, :], in0=PO[rws, hh:12:2], in1=rd[rws, :], op=ALU.mult),
                               reads=[BPO, Brd], writes=[Bos])
                        op(DVE, lambda b=b: V.tensor_tensor(out=yT[:, :, 2048 + 32 * b:2048 + 32 * b + 1], in0=osb[:].unsqueeze(2),
                                                            in1=zsamp[:, :, b:b + 1], op=ALU.mult), reads=[Bos, Bst], writes=ByTs)
                    T.barrier()
                T.ck(f"A5_{l}")
                out_proj(l, 6)
                T.barrier()
                T.ck(f"A_{l}")

            with contextlib.ExitStack() as ar:
                vn = sbt(ar, "b_vn", [128, NT, 512], BF16)
                lng_t = sbt(ar, "b_lng", [128, 512], F32)
                lnb_t = sbt(ar, "b_lnb", [128, 512], F32)
                st6 = sbt(ar, "b_st6", [128, 6], F32)
                mv = sbt(ar, "b_mv", [128, 2], F32)
                lnv = sbt(ar, "b_lnv", [128, 1], F32)
                rsb = sbt(ar, "b_rsb", [128, 1], F32)
                vnf = sbt(ar, "b_vnf", [128, 512], F32)
                vnf2 = [sbt(ar, f"b_vnf2{j}", [128, 512], F32) for j in range(2)]
                wl = sbt(ar, "b_wl", [128, 8, 128], F32)
                wlb = sbt(ar, "b_wlb", [128, 8, 128], BF16)
                wT = sbt(ar, "b_wT", [128, 8, 128], BF16)
                wTs = sbt(ar, "b_wTs", [128, 8, 128], BF16)
                w00 = sbt(ar, "b_w00", [128, 8], F32)
                bsf = sbt(ar, "b_bsf", [8, 128], F32)
                bsb = sbt(ar, "b_bsb", [8, 128], BF16)
                bs0 = sbt(ar, "b_bs0", [8, 128], BF16)
                ezb = sbt(ar, "b_ez", [128, 512], F32)
                t1 = sbt(ar, "b_t1", [128, 512], F32)
                Bvn_, Blg, Bst6, Bmv, Blnv, Brsb, Bvnf = Buf("vn"), Buf("lng"), Buf("st6"), Buf("mv"), Buf("lnv"), Buf("rsb"), Buf("vnf")
                Bvnf2 = [Buf("vnf20"), Buf("vnf21")]
                Bwl, Bwlb, BwT, Bbs, Bezb, Bt1 = Buf("wl"), Buf("wlb"), Buf("wT"), Buf("bs"), Buf("ezb"), Buf("t1")
                load_wo(l, 768, 4)
                bsel = sbt(ar, "b_bsel", [8, 512], BF16)
                bself = sbt(ar, "b_bself", [8, 512], F32)
                Bbsl = Buf("bsel")
                op(POOL, lambda: G.memset(bself[:], 1.0), writes=[Bbsl])
                op(POOL, lambda: G.affine_select(out=bself[:], in_=bself[:], pattern=[[1, 512]], compare_op=ALU.is_ge,
                                                 fill=0.0, base=0, channel_multiplier=-64), reads=[Bbsl], writes=[Bbsl])
                op(POOL, lambda: G.affine_select(out=bself[:], in_=bself[:], pattern=[[-1, 512]], compare_op=ALU.is_ge,
                                                 fill=0.0, base=63, channel_multiplier=64), reads=[Bbsl], writes=[Bbsl])
                op(DVE, lambda: V.tensor_copy(out=bsel[:], in_=bself[:]), reads=[Bbsl], writes=[Bbsl])
                dma(SP, lng_t[:], lng[l].partition_broadcast(128), writes=[Blg])
                dma(SP, lnb_t[:], lnb[l].partition_broadcast(128), writes=[Blg])
                dma(SP, wl[:], sgw[l].rearrange("g t s -> t g s"), writes=[Bwl])
                dma(SP, bsf[:], sgb[l], writes=[Bbs])
                u = uidx
                uidx += 1
                load_unit(u)
                load_unit(u + 1)
                wt, Bw = WA[u % 2], BW[u % 2]
                for i in range(NT):
                    j = i % 2
                    P, BP = PS[j], BPS[j]
                    for kc in range(8):
                        op(PE, lambda P=P, kc=kc, i=i: TE.matmul(P[:], lhsT=hT[:, kc, tcols(i)], rhs=wt[:, kc, :], start=(kc == 0), stop=(kc == 7)),
                           reads=[BhT[i], Bw], writes=[BP])
                    op(DVE, lambda P=P: V.bn_stats(out=st6[:], in_=P[:]), reads=[BP], writes=[Bst6])
                    op(DVE, lambda: V.bn_aggr(out=mv[:], in_=st6[:]), reads=[Bst6], writes=[Bmv])
                    op(ACT, lambda: A.activation(out=lnv[:], in_=mv[:, 1:2], func=AF.Ln, scale=1.0, bias=epsc[:, 0:1]), reads=[Bmv, Bc], writes=[Blnv])
                    op(ACT, lambda: A.activation(out=rsb[:], in_=lnv[:], func=AF.Exp, scale=-0.5), reads=[Blnv], writes=[Brsb])
                    op(DVE, lambda P=P: V.tensor_scalar(out=vnf[:], in0=P[:], scalar1=mv[:, 0:1], scalar2=rsb[:, 0:1], op0=ALU.subtract, op1=ALU.mult),
                       reads=[BP, Bmv, Brsb], writes=[Bvnf])
                    op(POOL, lambda: G.tensor_tensor(out=vnf[:], in0=vnf[:], in1=lng_t[:], op=ALU.mult), reads=[Bvnf, Blg], writes=[Bvnf])
                    op(POOL, lambda j=j: G.tensor_tensor(out=vnf2[j][:], in0=vnf[:], in1=lnb_t[:], op=ALU.add), reads=[Bvnf, Blg], writes=[Bvnf2[j]])
                    op(ACT, lambda i=i, j=j: A.activation(out=vn[:, i, :], in_=vnf2[j][:], func=AF.Copy), reads=[Bvnf2[j]], writes=[Bvn_])
                    if i == 16:
                        for b in range(4):
                            dma(SP, osg[l, b:b + 1, :], vnf2[j][32 * b:32 * b + 1, :], reads=[Bvnf2[j]])
                T.ck(f"B1_{l}")
                op(DVE, lambda: V.tensor_copy(out=wlb[:], in_=wl[:]), reads=[Bwl], writes=[Bwlb])
                for g in range(8):
                    op(PE, lambda g=g: TE.transpose(PB[0][:, tcols(g)], wlb[:, g, :], ident_bf[:]), reads=[Bwlb, Bc], writes=[BPB[0]])
                op(DVE, lambda: V.tensor_tensor(out=wT[:], in0=PB[0][:].rearrange("p (g t) -> p g t", g=8),
                                                in1=maskcur[:].unsqueeze(1).to_broadcast([128, 8, 128]), op=ALU.mult), reads=[BPB[0], Bc], writes=[BwT])
                op(PE, lambda: TE.matmul(PS[2][:, 0:8], lhsT=ones_f[0:1, :], rhs=wl[0:1, :, 0], start=True, stop=True), reads=[Bwl, Bc], writes=[BPS[2]])
                op(DVE, lambda: V.tensor_copy(out=w00[:], in_=PS[2][:, 0:8]), reads=[BPS[2]], writes=[BwT])
                op(DVE, lambda: V.tensor_tensor(out=wTs[:], in0=ident_bf[:].unsqueeze(1).to_broadcast([128, 8, 128]),
                                                in1=w00[:].unsqueeze(2).to_broadcast([128, 8, 128]), op=ALU.mult), reads=[BwT, Bc], writes=[BwT])
                op(DVE, lambda: V.tensor_copy(out=bsb[:], in_=bsf[:]), reads=[Bbs], writes=[Bbs])
                op(DVE, lambda: V.tensor_copy(out=bs0[:], in_=bsf[:, 0:1].to_broadcast([8, 128])), reads=[Bbs], writes=[Bbs])
                T.ck(f"B2_{l}")
                for cb in range(4):
                    u = uidx
                    uidx += 1
                    load_unit(u)
                    load_unit(u + 1)
                    wt, Bw = WA[u % 2], BW[u % 2]
                    for tg in range(5):
                        n = 512 if tg < 4 else 128
                        cols = slice(tg * 512, tg * 512 + n)
                        hb_ = BhT[4 * tg:4 * tg + 4] if tg < 4 else [BhT[16]]
                        Pu, BPu = PS[0 + tg % 2], BPS[0 + tg % 2]
                        Pz, BPz = PS[2 + tg % 2], BPS[2 + tg % 2]
                        Pm, BPm = PS[4 + tg % 2], BPS[4 + tg % 2]
                        for kc in range(8):
                            op(PE, lambda Pu=Pu, kc=kc, cols=cols, n=n: TE.matmul(Pu[:, 0:n], lhsT=wt[:, kc, 0:128], rhs=hT[:, kc, cols],
                                                                                start=(kc == 0), stop=(kc == 7)), reads=hb_ + [Bw], writes=[BPu])
                        for kc in range(8):
                            op(PE, lambda Pz=Pz, kc=kc, cols=cols, n=n: TE.matmul(Pz[:, 0:n], lhsT=wt[:, kc, 128:256], rhs=hT[:, kc, cols],
                                                                                start=(kc == 0), stop=(kc == 7)), reads=hb_ + [Bw], writes=[BPz])
                        for ti in range(n // 128):
                            i = 4 * tg + ti
                            for gg in range(2):
                                g = 2 * cb + gg
                                rws = slice(gg * 64, gg * 64 + 64)
                                wmat = wT if i < 16 else wTs
                                bmat = bsb if i < 16 else bs0
                                op(PE, lambda Pm=Pm, rws=rws, ti=ti, i=i, g=g, wmat=wmat: TE.matmul(
                                    Pm[rws, tcols(ti)], lhsT=vn[:, i, g * 64:(g + 1) * 64], rhs=wmat[:, g, :], start=True, stop=False),
                                   reads=[Bvn_, BwT], writes=[BPm])
                                op(PE, lambda Pm=Pm, rws=rws, ti=ti, g=g, bmat=bmat: TE.matmul(
                                    Pm[rws, tcols(ti)], lhsT=bsel[0:8, g * 64:(g + 1) * 64], rhs=bmat[0:8, :], start=False, stop=True),
                                   reads=[Bbs, Bbsl], writes=[BPm])
                        silu_from_psum(Pz, n, t1[:, 0:n], ezb, Bezb, BPz, [Bt1])
                        op(DVE, lambda Pu=Pu, n=n: V.tensor_tensor(out=t1[:, 0:n], in0=t1[:, 0:n], in1=Pu[:, 0:n], op=ALU.mult), reads=[Bt1, BPu], writes=[Bt1])
                        op(DVE, lambda Pm=Pm, n=n, cols=cols, cb=cb: V.tensor_tensor(out=yT[:, cb, cols], in0=t1[:, 0:n], in1=Pm[:, 0:n], op=ALU.mult),
                           reads=[Bt1, BPm], writes=[ByTp[cb] if tg < 4 else ByTs[cb]])
                T.barrier()
                T.ck(f"B3_{l}")
                out_proj(l, 4)
                T.barrier()
                T.ck(f"B_{l}")

            with contextlib.ExitStack() as ar:
                scanmask = sbt(ar, "c_scanm", [128, 512], F32)
                names = ["ef", "f", "g", "G", "eG", "kk", "eq", "q", "ez", "zs", "ln"]
                alias = {"eNG": "g", "rs": "ln", "sq": "eq", "o": "ez"}
                t = {nm: sbt(ar, "c_" + nm, [128, 512], F32) for nm in names}
                Bt_ = {nm: Buf("c_" + nm) for nm in names}
                for a_, b_ in alias.items():
                    t[a_] = t[b_]
                    Bt_[a_] = Bt_[b_]
                for nm in ("kt", "khT", "qt"):
                    t[nm] = sbt(ar, "c_" + nm, [128, 512], BF16)
                    Bt_[nm] = Buf("c_" + nm)
                Sbf = sbt(ar, "c_Sbf", [128, 9, 128], BF16)
                BSb = [Buf(f"Sb{j}") for j in range(9)]
                tv = sbt(ar, "c_tv", [128, 4, 128], BF16)
                khtok = sbt(ar, "c_khtok", [128, 4, 4, 128], BF16)
                ATm = sbt(ar, "c_ATm", [128, 4, 128], BF16)
                Sring = sbt(ar, "c_Sring", [128, 9, 128], F32)
                BSr = [Buf(f"Sr{j}") for j in range(9)]
                S0 = [Sring[:, 1, :], Sring[:, 2, :]]
                Sn = [Sring[:, 3, :], Sring[:, 4, :]]
                BS0 = [BSr[1], BSr[2]]
                BSn = [BSr[3], BSr[4]]
                Btv, Bkh, BAT, Bsm = Buf("tv"), Buf("khtok"), Buf("ATm"), Buf("scanm")
                load_wo(l, 1280, 6)
                op(DVE, lambda: V.memset(scanmask[:], 1.0), writes=[Bsm])
                op(DVE, lambda: V.memset(scanmask[:].rearrange("p (c j) -> p c j", j=32)[:, :, 0:1], 0.0), reads=[Bsm], writes=[Bsm])
                for hd in range(6):
                    u = uidx
                    uidx += 1
                    load_unit(u)
                    load_unit(u + 1)
                    wt, Bw = WA[u % 2], BW[u % 2]
                    lbc = lbT[:, l * 6 + hd:l * 6 + hd + 1]
                    omc = omlbT[:, l * 6 + hd:l * 6 + hd + 1]
                    hgc = hgT[:, l * 6 + hd:l * 6 + hd + 1]
                    op(DVE, lambda: V.memset(Sring[:, 0, :], 0.0), writes=[BSr[0]])
                    op(POOL, lambda: G.memset(Sbf[:, 0, :], 0.0), writes=[BSb[0]])
                    for tg in range(5):
                        n = 512 if tg < 4 else 128
                        nti = n // 128
                        cols = slice(tg * 512, tg * 512 + n)
                        hb_ = BhT[4 * tg:4 * tg + 4] if tg < 4 else [BhT[16]]
                        for pi_, (P, BP, w0) in enumerate([(PS[0], BPS[0], 0), (PS[1], BPS[1], 128), (PS[2], BPS[2], 384)]):
                            for kc in range(8):
                                op(PE, lambda P=P, kc=kc, w0=w0, cols=cols, n=n: TE.matmul(P[:, 0:n], lhsT=wt[:, kc, w0:w0 + 128], rhs=hT[:, kc, cols],
                                                                                         start=(kc == 0), stop=(kc == 7)), reads=hb_ + [Bw], writes=[BP])
                        for ti in range(nti):
                            i = 4 * tg + ti
                            for kc in range(8):
                                op(PE, lambda ti=ti, i=i, kc=kc: TE.matmul(PS[3][:, tcols(ti)], lhsT=hT[:, kc, tcols(i)], rhs=wt[:, kc, 256:384],
                                                                         start=(kc == 0), stop=(kc == 7)), reads=[BhT[i], Bw], writes=[BPS[3]])
                        sl = slice(0, n)
                        sigmoid_act(PS[1][:, sl], t["ef"][:, sl], [BPS[1]], Bt_["ef"])
                        op(DVE, lambda: V.tensor_scalar(out=t["f"][:, sl], in0=t["ef"][:, sl], scalar1=omc, scalar2=lbc, op0=ALU.mult, op1=ALU.add),
                           reads=[Bt_["ef"], Bc], writes=[Bt_["f"]])
                        op(POOL, lambda: G.tensor_scalar(out=t["kk"][:, sl], in0=t["f"][:, sl], scalar1=-1.0, scalar2=1.0, op0=ALU.mult, op1=ALU.add),
                           reads=[Bt_["f"]], writes=[Bt_["kk"]])
                        def qz_part():
                            sigmoid_act(PS[0][:, sl], t["eq"][:, sl], [BPS[0]], Bt_["eq"])
                            op(DVE, lambda: V.tensor_tensor(out=t["q"][:, sl], in0=PS[0][:, sl], in1=t["eq"][:, sl], op=ALU.mult), reads=[BPS[0], Bt_["eq"]], writes=[Bt_["q"]])
                            silu_from_psum(PS[2], n, t["zs"][:, sl], t["ez"], Bt_["ez"], BPS[2], [Bt_["zs"]])
                        op(ACT, lambda: A.activation(out=tv[:, 0:nti, :], in_=PS[3][:, sl].rearrange("p (a v) -> p a v", v=128), func=AF.Copy),
                           reads=[BPS[3]], writes=[Btv])
                        if tg < 4:
                            op(ACT, lambda: A.activation(out=t["g"][:], in_=t["f"][:], func=AF.Ln), reads=[Bt_["f"]], writes=[Bt_["g"]])
                            op(DVE, lambda: V.tensor_tensor_scan(out=t["G"][:], data0=scanmask[:], data1=t["g"][:], initial=0.0, op0=ALU.mult, op1=ALU.add),
                               reads=[Bt_["g"], Bsm], writes=[Bt_["G"]])
                            op(ACT, lambda: A.activation(out=t["eG"][:], in_=t["G"][:], func=AF.Exp), reads=[Bt_["G"]], writes=[Bt_["eG"]])
                            op(ACT, lambda: A.activation(out=t["eNG"][:], in_=t["G"][:], func=AF.Exp, scale=-1.0), reads=[Bt_["G"]], writes=[Bt_["eNG"]])
                            op(POOL, lambda: G.tensor_tensor(out=t["kt"][:], in0=t["kk"][:], in1=t["eNG"][:], op=ALU.mult), reads=[Bt_["kk"], Bt_["eNG"]], writes=[Bt_["kt"]])
                            op(POOL, lambda: G.tensor_tensor(out=t["khT"][:].rearrange("p (c j) -> p c j", j=32), in0=t["kt"][:].rearrange("p (c j) -> p c j", j=32),
                                                             in1=t["eG"][:, 31:512:32].unsqueeze(2).to_broadcast([128, 16, 32]), op=ALU.mult),
                               reads=[Bt_["kt"], Bt_["eG"]], writes=[Bt_["khT"]])
                            qz_part()
                            op(POOL, lambda: G.tensor_tensor(out=t["qt"][:], in0=t["q"][:], in1=t["eG"][:], op=ALU.mult), reads=[Bt_["q"], Bt_["eG"]], writes=[Bt_["qt"]])
                            T.ck(f"C1_{l}_{hd}_{tg}")
                            for ti in range(4):
                                op(PE, lambda ti=ti: TE.transpose(PB[0][:, tcols(ti)], t["khT"][:, tcols(ti)], ident_bf[:]), reads=[Bt_["khT"], Bc], writes=[BPB[0]], inc=(ti == 3))
                            for ch in range(4):
                                if ch % 2 == 0:
                                    op(ACT, lambda ch=ch: A.activation(out=khtok[:, :, ch, :], in_=PB[0][:, 0:512].rearrange("p (a k) -> p a k", a=4), func=AF.Copy,
                                                                       scale=rowmask[:, ch:ch + 1]), reads=[BPB[0], Bc], writes=[Bkh])
                                else:
                                    op(DVE, lambda ch=ch: V.tensor_scalar(out=khtok[:, :, ch, :], in0=PB[0][:, 0:512].rearrange("p (a k) -> p a k", a=4),
                                                                          scalar1=rowmask[:, ch:ch + 1], scalar2=None, op0=ALU.mult), reads=[BPB[0], Bc], writes=[Bkh])
                            for ti in range(4):
                                op(PE, lambda ti=ti: TE.matmul(PS[1][:, tcols(ti)], lhsT=t["kt"][:, tcols(ti)], rhs=t["qt"][:, tcols(ti)], start=True, stop=True),
                                   reads=[Bt_["kt"], Bt_["qt"]], writes=[BPS[1]])
                            op(DVE, lambda: V.tensor_tensor(out=ATm[:], in0=PS[1][:].rearrange("p (a t) -> p a t", a=4),
                                                            in1=blockmask[:].unsqueeze(1).to_broadcast([128, 4, 128]), op=ALU.mult), reads=[BPS[1], Bc], writes=[BAT])
                            T.ck(f"C2_{l}_{hd}_{tg}")
                            banks_ = [3, 4, 1, 0]
                            for nchk in range(16):
                                ti, ch = nchk // 4, nchk % 4
                                bk, col = banks_[nchk // 4], (nchk % 4) * 128
                                op(PE, lambda bk=bk, col=col, ti=ti, ch=ch: TE.matmul(PS[bk][:, col:col + 128], lhsT=khtok[:, ti, ch, :], rhs=tv[:, ti, :], start=True, stop=True),
                                   reads=[Bkh, Btv], writes=[BPS[bk]], inc=(nchk % 4 == 3))
                            for half in range(2):
                                for r_ in range(8):
                                    nchk = half * 8 + r_
                                    bk, col = banks_[nchk // 4], (nchk % 4) * 128
                                    op(DVE, lambda bk=bk, col=col, nchk=nchk, r_=r_: V.scalar_tensor_tensor(
                                        out=Sring[:, r_ + 1, :], in0=Sring[:, r_, :], scalar=t["eG"][:, nchk * 32 + 31:nchk * 32 + 32], in1=PS[bk][:, col:col + 128],
                                        op0=ALU.mult, op1=ALU.add), reads=[BSr[r_], Bt_["eG"], BPS[bk]], writes=[BSr[r_ + 1]], inc=True)
                                    op(POOL, lambda r_=r_: G.tensor_copy(out=Sbf[:, r_ + 1, :], in_=Sring[:, r_ + 1, :]), reads=[BSr[r_ + 1]], writes=[BSb[r_ + 1]], inc=True)
                                for r_ in range(8):
                                    nchk = half * 8 + r_
                                    ti, ch = nchk // 4, nchk % 4
                                    cc = slice(nchk * 32, nchk * 32 + 32)
                                    op(PE, lambda ti=ti, ch=ch, cc=cc: TE.matmul(PS[2][:, cc], lhsT=tv[:, ti, :], rhs=ATm[:, ti, ch * 32:(ch + 1) * 32], start=True, stop=False),
                                       reads=[Btv, BAT], writes=[BPS[2]])
                                    op(PE, lambda r_=r_, cc=cc: TE.matmul(PS[2][:, cc], lhsT=Sbf[:, r_, :], rhs=t["qt"][:, cc], start=False, stop=True),
                                       reads=[BSb[r_], Bt_["qt"]], writes=[BPS[2]], inc=(r_ == 7))
                                op(DVE, lambda: V.tensor_copy(out=Sring[:, 0, :], in_=Sring[:, 8, :]), reads=[BSr[8]], writes=[BSr[0]])
                                op(POOL, lambda: G.tensor_copy(out=Sbf[:, 0, :], in_=Sbf[:, 8, :]), reads=[BSb[8]], writes=[BSb[0]])
                            no = 512
                        else:
                            qz_part()
                            op(PE, lambda: TE.transpose(PS[0][:, 0:128], t["kk"][:, 0:128], ident_f[:]), reads=[Bt_["kk"], Bc], writes=[BPS[0]])
                            for b in range(4):
                                op(DVE, lambda b=b: V.tensor_scalar(out=khtok[:, 0, b, :], in0=PS[0][:, 0:128], scalar1=rowmask_s[:, b:b + 1], scalar2=None, op0=ALU.mult),
                                   reads=[BPS[0], Bc], writes=[Bkh])
                            for b in range(4):
                                bj = b % 2
                                dma(SP, S0[bj], st[l, b, hd], writes=[BS0[bj]])
                                ku = 3 + bj
                                op(PE, lambda ku=ku, b=b: TE.matmul(PS[ku][:, 0:128], lhsT=khtok[:, 0, b, :], rhs=tv[:, 0, :], start=True, stop=True),
                                   reads=[Bkh, Btv], writes=[BPS[ku]])
                                op(DVE, lambda bj=bj, ku=ku, b=b: V.scalar_tensor_tensor(out=Sn[bj], in0=S0[bj], scalar=t["f"][:, 32 * b:32 * b + 1],
                                                                                       in1=PS[ku][:, 0:128], op0=ALU.mult, op1=ALU.add),
                                   reads=[BS0[bj], Bt_["f"], BPS[ku]], writes=[BSn[bj]])
                                dma(SP, ohs[l, b, hd], Sn[bj], reads=[BSn[bj]])
                                op(PE, lambda bj=bj, b=b: TE.matmul(PS[2][:, b:b + 1], lhsT=Sn[bj], rhs=t["q"][:, 32 * b:32 * b + 1], start=True, stop=True),
                                   reads=[BSn[bj], Bt_["q"]], writes=[BPS[2]])
                            no = 4
                        T.ck(f"C3_{l}_{hd}_{tg}")
                        so = slice(0, no)
                        op(ACT, lambda: A.activation(out=t["o"][:, so], in_=PS[2][:, so], func=AF.Copy), reads=[BPS[2]], writes=[Bt_["o"]])
                        op(ACT, lambda: A.activation(out=t["sq"][:, so], in_=PS[2][:, so], func=AF.Square), reads=[BPS[2]], writes=[Bt_["sq"]])
                        op(PE, lambda: TE.matmul(PS[5][:, so], lhsT=ones_f[:], rhs=t["sq"][:, so], start=True, stop=True), reads=[Bt_["sq"], Bc], writes=[BPS[5]])
                        op(ACT, lambda: A.activation(out=t["ln"][:, so], in_=PS[5][:, so], func=AF.Ln, scale=1.0 / 128, bias=epsc[:, 0:1]), reads=[BPS[5], Bc], writes=[Bt_["ln"]])
                        op(ACT, lambda: A.activation(out=t["rs"][:, so], in_=t["ln"][:, so], func=AF.Exp, scale=-0.5), reads=[Bt_["ln"]], writes=[Bt_["rs"]])
                        op(DVE, lambda: V.tensor_tensor(out=t["o"][:, so], in0=t["o"][:, so], in1=t["rs"][:, so], op=ALU.mult), reads=[Bt_["o"], Bt_["rs"]], writes=[Bt_["o"]])
                        if tg < 4:
                            op(DVE, lambda cols=cols, hd=hd: V.scalar_tensor_tensor(out=yT[:, hd, cols], in0=t["o"][:], scalar=hgc, in1=t["zs"][:], op0=ALU.mult, op1=ALU.mult),
                               reads=[Bt_["o"], Bt_["zs"], Bc], writes=[ByTp[hd]])
                        else:
                            op(DVE, lambda hd=hd: V.scalar_tensor_tensor(out=yT[:, hd, 2048:TOK:32], in0=t["o"][:, 0:4], scalar=hgc, in1=t["zs"][:, 0:128:32],
                                                                         op0=ALU.mult, op1=ALU.mult), reads=[Bt_["o"], Bt_["zs"], Bc], writes=[ByTs[hd]])
                        T.ck(f"C4_{l}_{hd}_{tg}")
                        if tg == 3:
                            dma(SP, ohp[l, hd], Sring[:, 0, :], reads=[BSr[0]])
                T.barrier()
                T.ck(f"C5_{l}")
                out_proj(l, 6)
                T.barrier()
                T.ck(f"L_{l}")

        T.force = True
        for i in range(16):
            dma(SP, yp[i * 128:(i + 1) * 128, :], xres[:, i, :], reads=[Bx[i]])
        for b in range(4):
            dma(SP, ys[b:b + 1, :], xres[32 * b:32 * b + 1, 16, :], reads=[Bx[16]])
        for Q in (SP, POOL):
            for s in Q.slots:
                if s.val > 0:
                    T._wait(SP, s.sem, s.val)
    return nc


_NC_CACHE = {}


def kernel(x_prompt, x_sample, cache_k, cache_v, state_hgrn, norm_g, w_in, q_norm_g, k_norm_g,
           sgu_ln_g, sgu_ln_b, sgu_w, sgu_b, hgrn_lb_logits, hgrn_norm_g, w_out):
    f = lambda a: np.ascontiguousarray(np.asarray(a, dtype=np.float32))
    x_prompt, x_sample, cache_k, cache_v, state_hgrn = map(f, (x_prompt, x_sample, cache_k, cache_v, state_hgrn))
    shared = {
        "norm_g": f(norm_g), "w_in": f(w_in), "q_norm_g": f(q_norm_g), "k_norm_g": f(k_norm_g),
        "sgu_ln_g": f(sgu_ln_g), "sgu_ln_b": f(sgu_ln_b), "sgu_w": f(sgu_w), "sgu_b": f(sgu_b),
        "hgrn_lb_logits": f(hgrn_lb_logits), "hgrn_norm_g": f(hgrn_norm_g), "w_out": f(w_out),
    }
    in_maps = []
    for c in range(NCORES):
        sb = slice(4 * c, 4 * c + 4)
        m = dict(shared)
        m["xp"] = np.ascontiguousarray(x_prompt[c])
        m["xs"] = np.ascontiguousarray(x_sample[sb, 0, :])
        m["ck"] = np.ascontiguousarray(cache_k[:, sb].reshape(2, 4, 2048, 768))
        m["cv"] = np.ascontiguousarray(cache_v[:, sb].reshape(2, 4, 2048, 768))
        m["st"] = np.ascontiguousarray(state_hgrn[:, sb])
        in_maps.append(m)
    if "nc" not in _NC_CACHE:
        _NC_CACHE["nc"] = build_nc()
    res = run_bass_kernel_spmd(_NC_CACHE["nc"], in_maps, core_ids=list(range(NCORES)))
    R = res.results
    y_prompt = np.stack([R[c]["yp"] for c in range(NCORES)]).astype(np.float32)
    y_sample = np.concatenate([R[c]["ys"] for c in range(NCORES)])[:, None, :].astype(np.float32)
    nkp = np.stack([R[c]["okp"] for c in range(NCORES)], axis=1).reshape(2, 8, 2048, 12, 64).astype(np.float32)
    nvp = np.stack([R[c]["ovp"] for c in range(NCORES)], axis=1).reshape(2, 8, 2048, 12, 64).astype(np.float32)
    nks = np.concatenate([R[c]["oks"] for c in range(NCORES)], axis=1).reshape(2, 32, 1, 12, 64).astype(np.float32)
    nvs = np.concatenate([R[c]["ovs"] for c in range(NCORES)], axis=1).reshape(2, 32, 1, 12, 64).astype(np.float32)
    nsg = np.concatenate([R[c]["osg"] for c in range(NCORES)], axis=1).reshape(2, 32, 1, 512).astype(np.float32)
    nhp = np.stack([R[c]["ohp"] for c in range(NCORES)], axis=1).astype(np.float32)
    nhs = np.concatenate([R[c]["ohs"] for c in range(NCORES)], axis=1).astype(np.float32)
    return (y_prompt, y_sample, nkp, nvp, nks, nvs, nsg, nhp, nhs)
```
